# Optimizing a Trainium2 kernel written in Bass

```python
import jax, jax.numpy as jnp
from jax import lax
import numpy as np

D_MODEL = 1024
BATCH = 8
SEQ = 2048
DEPTH = 4

CHUNK = 64
N_MIXERS = 3
EPS = 1e-6
NEG_INF = -1e30

GMLP_BLOCK = 128
GMLP_WIDTH = 2 * D_MODEL
GMLP_GROUPS = 8
GMLP_GROUP_DIM = GMLP_WIDTH // GMLP_GROUPS

ATT_HEADS = 16
HEAD_DIM = D_MODEL // ATT_HEADS
LEFT_CHUNKS = 8
BAND = (LEFT_CHUNKS + 1) * CHUNK
MAX_REL = 128

CONV_WIDTH = 31

D_FF = 2816
FFN_CONV_WIDTH = 3

PLE_DIM = 256

kernel_name = "hybrid_chunk_causal_encoder"


def rms_norm(x, g):
    xf = x.astype(jnp.float32)
    y = xf * lax.rsqrt(jnp.mean(xf * xf, axis=-1, keepdims=True) + EPS)
    return (y * g.astype(jnp.float32)).astype(x.dtype)


def layer_norm(x, g, b):
    xf = x.astype(jnp.float32)
    mu = jnp.mean(xf, axis=-1, keepdims=True)
    var = jnp.mean(jnp.square(xf - mu), axis=-1, keepdims=True)
    y = (xf - mu) * lax.rsqrt(var + EPS)
    return (y * g.astype(jnp.float32) + b.astype(jnp.float32)).astype(x.dtype)


def causal_dwconv(x, w):
    k, c = w.shape
    return lax.conv_general_dilated(
        x, w[:, None, :].astype(x.dtype), window_strides=(1,), padding=[(k - 1, 0)],
        dimension_numbers=("NWC", "WIO", "NWC"), feature_group_count=c)


def gmlp_mixer(x, w_in, ln_g, ln_b, w_s, b_s, w_out):
    bsz, seq, _ = x.shape
    z = jax.nn.gelu(x @ w_in)
    u, v = jnp.split(z, 2, axis=-1)
    v = layer_norm(v, ln_g, ln_b)
    v = v.reshape(bsz, seq // GMLP_BLOCK, GMLP_BLOCK, GMLP_GROUPS, GMLP_GROUP_DIM)
    pos = jnp.arange(GMLP_BLOCK)
    mask = (pos[None, :] // CHUNK) <= (pos[:, None] // CHUNK)
    w = jnp.where(mask[None], w_s, jnp.zeros((), w_s.dtype))
    v = jnp.einsum("gts,bnsgc->bntgc", w, v) + jnp.transpose(b_s)[None, None, :, :, None]
    y = u * v.reshape(bsz, seq, GMLP_WIDTH)
    return y @ w_out


def chunk_attention(x, w_qkv, rel_bias, w_out):
    bsz, seq, d = x.shape
    nc = seq // CHUNK
    qkv = (x @ w_qkv).reshape(bsz, seq, 3, ATT_HEADS, HEAD_DIM)
    q, k, v = qkv[:, :, 0], qkv[:, :, 1], qkv[:, :, 2]
    q = q.reshape(bsz, nc, CHUNK, ATT_HEADS, HEAD_DIM)
    pad = ((0, 0), (LEFT_CHUNKS * CHUNK, 0), (0, 0), (0, 0))
    kp = jnp.pad(k, pad).reshape(bsz, nc + LEFT_CHUNKS, CHUNK, ATT_HEADS, HEAD_DIM)
    vp = jnp.pad(v, pad).reshape(bsz, nc + LEFT_CHUNKS, CHUNK, ATT_HEADS, HEAD_DIM)
    kb = jnp.concatenate([kp[:, j:j + nc] for j in range(LEFT_CHUNKS + 1)], axis=2)
    vb = jnp.concatenate([vp[:, j:j + nc] for j in range(LEFT_CHUNKS + 1)], axis=2)
    qi = jnp.arange(CHUNK)[:, None]
    kj = jnp.arange(BAND)[None, :]
    rel = LEFT_CHUNKS * CHUNK + qi - kj
    bias = rel_bias[:, jnp.clip(rel, -MAX_REL, MAX_REL) + MAX_REL]
    valid = (jnp.arange(nc)[:, None] - LEFT_CHUNKS + kj // CHUNK) >= 0
    s = jnp.einsum("bnqhd,bnkhd->bnhqk", q, kb).astype(jnp.float32)
    s = s * (HEAD_DIM ** -0.5) + bias[None, None].astype(jnp.float32)
    s = jnp.where(valid[None, :, None, None, :], s, NEG_INF)
    pr = jax.nn.softmax(s, axis=-1).astype(vb.dtype)
    o = jnp.einsum("bnhqk,bnkhd->bnqhd", pr, vb).reshape(bsz, seq, d)
    return o @ w_out


def conformer_conv(x, w_in, dw, dw_b, ln_g, ln_b, w_out):
    a, g = jnp.split(x @ w_in, 2, axis=-1)
    y = a * jax.nn.sigmoid(g)
    y = causal_dwconv(y, dw) + dw_b
    y = jax.nn.silu(layer_norm(y, ln_g, ln_b))
    return y @ w_out


def conv_ffn(x, w_up, w_conv, w_down):
    h = causal_dwconv(x @ w_up, w_conv)
    g, v = jnp.split(h, 2, axis=-1)
    return (jax.nn.silu(g) * v) @ w_down


def setup_inputs(seed: int = 0) -> dict:
    key = jax.random.key(seed)
    ks = iter(jax.random.split(key, 40))
    f32 = jnp.float32

    def nrm(shape, scale):
        return jax.random.normal(next(ks), shape, f32) * scale

    def gain(shape):
        return 1.0 + nrm(shape, 0.05)

    n_a = (DEPTH + 2) // 3
    n_b = (DEPTH + 1) // 3
    n_c = DEPTH // 3
    D, F, E = D_MODEL, D_FF, GMLP_WIDTH
    return {
        "x": nrm((BATCH, SEQ, D), 1.0),
        "p": nrm((DEPTH, BATCH, SEQ, PLE_DIM), 1.0),
        "mix_pre_g": gain((DEPTH, D)),
        "mix_post_g": gain((DEPTH, D)),
        "ffn_pre_g": gain((DEPTH, D)),
        "ffn_post_g": gain((DEPTH, D)),
        "ffn_w_up": nrm((DEPTH, D, 2 * F), D ** -0.5),
        "ffn_conv": nrm((DEPTH, FFN_CONV_WIDTH, 2 * F), FFN_CONV_WIDTH ** -0.5),
        "ffn_w_down": nrm((DEPTH, F, D), F ** -0.5),
        "ple_w_gate": nrm((DEPTH, D, D), D ** -0.5),
        "ple_w_proj": nrm((DEPTH, PLE_DIM, D), PLE_DIM ** -0.5),
        "ple_norm_g": gain((DEPTH, D)),
        "a_w_in": nrm((n_a, D, 2 * E), D ** -0.5),
        "a_ln_g": gain((n_a, E)),
        "a_ln_b": nrm((n_a, E), 0.02),
        "a_w_s": nrm((n_a, GMLP_GROUPS, GMLP_BLOCK, GMLP_BLOCK), GMLP_BLOCK ** -0.5),
        "a_b_s": 1.0 + nrm((n_a, GMLP_GROUPS, GMLP_BLOCK), 0.1),
        "a_w_out": nrm((n_a, E, D), E ** -0.5),
        "b_w_qkv": nrm((n_b, D, 3 * D), D ** -0.5),
        "b_rel_bias": nrm((n_b, ATT_HEADS, 2 * MAX_REL + 1), 0.5),
        "b_w_out": nrm((n_b, D, D), D ** -0.5),
        "c_w_in": nrm((n_c, D, 2 * D), D ** -0.5),
        "c_dw": nrm((n_c, CONV_WIDTH, D), CONV_WIDTH ** -0.5),
        "c_dw_b": nrm((n_c, D), 0.02),
        "c_ln_g": gain((n_c, D)),
        "c_ln_b": nrm((n_c, D), 0.02),
        "c_w_out": nrm((n_c, D, D), D ** -0.5),
    }


def reference(x, p, mix_pre_g, mix_post_g, ffn_pre_g, ffn_post_g, ffn_w_up, ffn_conv,
              ffn_w_down, ple_w_gate, ple_w_proj, ple_norm_g, a_w_in, a_ln_g, a_ln_b,
              a_w_s, a_b_s, a_w_out, b_w_qkv, b_rel_bias, b_w_out, c_w_in, c_dw, c_dw_b,
              c_ln_g, c_ln_b, c_w_out):
    h = x
    for i in range(DEPTH):
        kind = i % N_MIXERS
        j = i // N_MIXERS
        hn = rms_norm(h, mix_pre_g[i])
        if kind == 0:
            m = gmlp_mixer(hn, a_w_in[j], a_ln_g[j], a_ln_b[j], a_w_s[j], a_b_s[j], a_w_out[j])
        elif kind == 1:
            m = chunk_attention(hn, b_w_qkv[j], b_rel_bias[j], b_w_out[j])
        else:
            m = conformer_conv(hn, c_w_in[j], c_dw[j], c_dw_b[j], c_ln_g[j], c_ln_b[j],
                               c_w_out[j])
        h = h + rms_norm(m, mix_post_g[i])
        f = conv_ffn(rms_norm(h, ffn_pre_g[i]), ffn_w_up[i], ffn_conv[i], ffn_w_down[i])
        h = h + rms_norm(f, ffn_post_g[i])
        gate = jax.nn.sigmoid(h @ ple_w_gate[i])
        e = p[i] @ ple_w_proj[i]
        h = h + rms_norm(gate * e, ple_norm_g[i])
    return h
```

```python
import numpy as np
from contextlib import ExitStack
import concourse.bass as bass
import concourse.mybir as mybir
from concourse.bass_utils import run_bass_kernel_spmd

F32 = mybir.dt.float32
BF16 = mybir.dt.bfloat16
AF = mybir.ActivationFunctionType
ALU = mybir.AluOpType
AX = mybir.AxisListType

D = 1024
S = 2048
T = 512
NT = S // T
FF = 2816
NFC = FF // 128
DEPTH = 4
EPS = 1e-6
NEG = -1e30


class Buf:
    __slots__ = ("name", "last_w", "readers", "dma_readers", "dma_sem", "dma_cnt", "const", "excl")

    fence = {}

    def __init__(self, name, const=False, excl=False):
        self.name = name
        self.excl = excl
        self.last_w = None
        self.readers = dict(Buf.fence)
        self.dma_readers = []
        self.dma_sem = None
        self.dma_cnt = 0
        self.const = const


class Op:
    __slots__ = ("eng", "fn", "deps", "sig", "sigval", "is_dma", "dma_sem", "dma_val", "idx")

    def __init__(self, eng, fn, is_dma):
        self.eng = eng
        self.fn = fn
        self.deps = []
        self.sig = False
        self.sigval = 0
        self.is_dma = is_dma
        self.dma_sem = None
        self.dma_val = 0


class Prog:
    ENGS = ("pe", "act", "dve", "pool", "sp")

    def __init__(self, nc):
        self.nc = nc
        self.ops = {e: [] for e in self.ENGS}
        self.n = 0
        self.dma_bufs = []
        self.out_dma_ops = []

    def _add(self, op, reads, writes):
        deps = []
        for b in reads:
            w = b.last_w
            if w is not None:
                deps.append((w, True))
            if b.excl:
                for e_, r in b.readers.items():
                    if e_ != op.eng:
                        deps.append((r, False))
        for b in writes:
            w = b.last_w
            if w is not None and not (op.is_dma and w.is_dma):
                deps.append((w, False))
            for r in b.readers.values():
                deps.append((r, False))
            for r in b.dma_readers:
                deps.append((r, False))
        seen = set()
        for d, raw in deps:
            if d is op:
                continue
            if (not d.is_dma) and (not op.is_dma) and d.eng == op.eng:
                if op.eng == "pe":
                    continue
            k = id(d)
            if k in seen:
                continue
            seen.add(k)
            op.deps.append(d)
            if not d.is_dma:
                d.sig = True
        for b in writes:
            b.last_w = op
            b.readers = {}
            b.dma_readers = []
        for b in reads:
            if b.const or b in writes:
                continue
            if op.is_dma:
                b.dma_readers.append(op)
            else:
                b.readers[op.eng] = op
        op.idx = self.n
        self.n += 1
        self.ops[op.eng].append(op)
        return op

    def op(self, eng, fn, reads=(), writes=()):
        return self._add(Op(eng, fn, False), list(reads), list(writes))

    def dma(self, eng, fn, reads=(), writes=(), is_output=False):
        o = Op(eng, fn, True)
        dst = writes[0]
        if dst.dma_sem is None:
            dst.dma_sem = "pending"
            self.dma_bufs.append(dst)
        dst.dma_cnt += 16
        o.dma_sem = dst
        o.dma_val = dst.dma_cnt
        self._add(o, list(reads), list(writes))
        if is_output:
            self.out_dma_ops.append(o)
        return o

    def emit(self, stack):
        nc = self.nc
        sems = {}
        for e in ("pe", "act", "dve", "pool"):
            sems[e] = stack.enter_context(nc.semaphore("s_" + e))
        for i, b in enumerate(self.dma_bufs):
            b.dma_sem = stack.enter_context(nc.semaphore("d%d" % i))
        for e in ("pe", "act", "dve", "pool"):
            c = 0
            for o in self.ops[e]:
                if o.is_dma:
                    continue
                if o.sig:
                    c += 1
                    o.sigval = c
        out_ops = self.out_dma_ops

        def run(engh, ename):
            waited = {}

            def wait(sem, val):
                k = id(sem)
                if waited.get(k, 0) >= val:
                    return
                waited[k] = val
                engh.wait_ge(sem, val)

            for o in self.ops[ename]:
                for d in o.deps:
                    if d.is_dma:
                        wait(d.dma_sem.dma_sem, d.dma_val)
                    else:
                        wait(sems[d.eng], d.sigval)
                ins = o.fn(engh)
                if o.is_dma:
                    ins.then_inc(o.dma_sem.dma_sem, 16)
                elif o.sig:
                    ins.then_inc(sems[ename], 1)
            if ename == "sp":
                for o in out_ops:
                    wait(o.dma_sem.dma_sem, o.dma_val)

        block = stack.enter_context(nc.Block())

        @block.tensor
        def _(e):
            run(e, "pe")

        @block.scalar
        def _(e):
            run(e, "act")

        @block.vector
        def _(e):
            run(e, "dve")

        @block.gpsimd
        def _(e):
            run(e, "pool")

        @block.sync
        def _(e):
            run(e, "sp")


def ptab_layout():
    off = {}
    c = 0
    for i in range(DEPTH):
        for nm in ("mix_pre", "mix_post", "ffn_pre", "ffn_post", "ple_g"):
            off[(nm, i)] = c
            c += 8
        off[("conv", i)] = c
        c += 132
    for j in range(2):
        off[("a_lng", j)] = c
        c += 16
        off[("a_lnb", j)] = c
        c += 16
    off["c_dw"] = c
    c += 248
    off["c_dwb"] = c
    c += 8
    off["c_lng"] = c
    c += 8
    off["c_lnb"] = c
    c += 8
    return off, c


class StopBuild(Exception):
    pass


class Arena:
    def __init__(self, ap_all, base, limit):
        self.all = ap_all
        self.off = base
        self.limit = limit

    def mark(self):
        return self.off

    def reset(self, m):
        self.off = m

    def alloc(self, dtype, *free):
        isz = 4 if dtype == F32 else 2
        n = 1
        for f in free:
            n *= f
        nb = n * isz
        off = (self.off + 63) // 64 * 64
        assert off + nb <= self.limit, ("SBUF arena overflow", off + nb, self.limit)
        self.off = off + nb
        v = self.all[:, off // 2:(off + nb) // 2]
        if dtype == F32:
            v = v.bitcast(F32)
        if len(free) == 2:
            v = v.rearrange("p (a b) -> p a b", a=free[0])
        elif len(free) == 3:
            v = v.rearrange("p (a b c) -> p a b c", a=free[0], b=free[1])
        return v


class Builder:
    def __init__(self, layers, dbg=()):
        self.layers = list(layers)
        self.dbg = set(dbg)
        self.nc = bass.Bass("TRN2", target_bir_lowering=False)
        Buf.fence = {}
        self.P = Prog(self.nc)
        self.poff, self.pcols = ptab_layout()
        self._bank = 0
        self._stat = 0

    def mm(self, out, lhsT, rhs, start, stop, r, w, **kw):
        self.P.op("pe", lambda e: e.matmul(out, lhsT=lhsT, rhs=rhs, start=start, stop=stop, **kw), r, w)

    def act(self, out, in_, func, r, w, bias=None, scale=None, accum=None):
        kw = {}
        if bias is not None:
            kw["bias"] = bias
        if scale is not None:
            kw["scale"] = scale
        if accum is not None:
            kw["accum_out"] = accum
        self.P.op("act", lambda e: e.activation(out=out, in_=in_, func=func, **kw), r, w)

    def ts(self, eng, out, in0, s1, s2, op0, op1, r, w):
        if op1 is None and eng == "pool":
            s2, op1 = 1.0, ALU.mult
        if op1 is None:
            self.P.op(eng, lambda e: e.tensor_scalar(out=out, in0=in0, scalar1=s1, scalar2=None, op0=op0), r, w)
        else:
            self.P.op(eng, lambda e: e.tensor_scalar(out=out, in0=in0, scalar1=s1, scalar2=s2, op0=op0, op1=op1), r, w)

    def stt(self, out, in0, scalar, in1, op0, op1, r, w):
        self.P.op("dve", lambda e: e.scalar_tensor_tensor(out=out, in0=in0, scalar=scalar, in1=in1, op0=op0, op1=op1), r, w)

    def tt(self, eng, out, in0, in1, op, r, w):
        self.P.op(eng, lambda e: e.tensor_tensor(out=out, in0=in0, in1=in1, op=op), r, w)

    def cp(self, eng, out, in_, r, w):
        if eng == "act":
            self.P.op("act", lambda e: e.copy(out=out, in_=in_), r, w)
        else:
            self.P.op(eng, lambda e: e.tensor_copy(out=out, in_=in_), r, w)

    def recip(self, out, in_, r, w):
        self.P.op("dve", lambda e: e.reciprocal(out=out, in_=in_), r, w)

    def memset(self, eng, ap, val, w):
        self.P.op(eng, lambda e: e.memset(ap, val), [], w)

    def dma_cast(self, out, in_, r, w):
        self.P.dma("pool", lambda e: e.dma_start(out=out, in_=in_), r, w)

    def dma_plain(self, out, in_, r, w, is_output=False):
        self.P.dma("sp", lambda e: e.dma_start(out=out, in_=in_), r, w, is_output=is_output)

    def bank(self):
        b = self._bank
        self._bank = (b + 1) % 6
        return b

    def statbank(self):
        b = 6 + self._stat
        self._stat ^= 1
        return b

    def psb(self, b, lo=0, hi=512):
        return self.ps[:, b * 512 + lo:b * 512 + hi]

    def build(self):
        nc = self.nc
        st = ExitStack()
        with st:
            self.declare_dram()
            self.sb_all = st.enter_context(nc.sbuf_tensor("sb_all", [128, 105900], BF16))
            self.ps = st.enter_context(nc.psum_tensor("ps_all", [128, 4096], F32))
            self.PB = [Buf("psb%d" % i, excl=True) for i in range(8)]
            self.A = Arena(self.sb_all, 0, 105900 * 2)
            self.setup_persistent()
            for i in self.layers:
                self.layer(i)
            self.store_output()
            self.P.emit(st)
        return nc

    def declare_dram(self):
        nc = self.nc
        dt = lambda n, s: nc.dram_tensor(n, s, F32, kind="ExternalInput").ap()
        self.d = {}
        self.d["xT"] = dt("xT", [8, 128, S])
        self.d["ptab"] = dt("ptab", [128, self.pcols])
        self.d["ident"] = dt("ident", [128, 128])
        for i in self.layers:
            self.d["pT%d" % i] = dt("pT%d" % i, [2, 128, S])
            self.d["wup%d" % i] = dt("wup%d" % i, [NFC, 128, 8 * 256])
            self.d["wdn%d" % i] = dt("wdn%d" % i, [8, 128, NFC * 128])
            self.d["wg%d" % i] = dt("wg%d" % i, [8, 128, 8 * 128])
            self.d["wp%d" % i] = dt("wp%d" % i, [128, 2 * D])
            kind, j = i % 3, i // 3
            if kind == 0:
                self.d["a_wu%d" % j] = dt("a_wu%d" % j, [16, 128, 8 * 128])
                self.d["a_wv%d" % j] = dt("a_wv%d" % j, [4, 128, 8 * 512])
                self.d["a_wo%d" % j] = dt("a_wo%d" % j, [8, 128, 16 * 128])
                self.d["a_ws%d" % j] = dt("a_ws%d" % j, [128, 8 * 128])
                self.d["a_bs%d" % j] = dt("a_bs%d" % j, [128, 8 * 128])
            elif kind == 1:
                self.d["b_wqk"] = dt("b_wqk", [8, 128, 8 * 256])
                self.d["b_wv"] = dt("b_wv", [2, 128, 8 * 512])
                self.d["b_wo"] = dt("b_wo", [8, 128, 8 * 128])
                self.d["b_bias"] = dt("b_bias", [128, 16 * 640])
            else:
                self.d["c_wi"] = dt("c_wi", [8, 128, 8 * 256])
                self.d["c_wo"] = dt("c_wo", [8, 128, 8 * 128])
        self.yT = nc.dram_tensor("yT", [8, 128, S], F32, kind="ExternalOutput").ap()

    def setup_persistent(self):
        A = self.A
        self.h = A.alloc(F32, 8, S)
        self.Bh = [[Buf("h%d_%d" % (c, t)) for t in range(NT)] for c in range(8)]
        self.ptab = A.alloc(F32, self.pcols)
        self.Bptab = Buf("ptab", const=True)
        self.ident = A.alloc(BF16, 128)
        self.onesb = A.alloc(BF16, 128)
        self.epsc = A.alloc(F32, 1)
        self.dummy = A.alloc(F32, 8)
        self.Bconst = Buf("const", const=True)
        self.sq = [A.alloc(BF16, T) for _ in range(4)]
        self.Bsq = [Buf("sq%d" % i) for i in range(4)]
        self._sq = 0
        self.sd = A.alloc(F32, T)
        self.Bsd = Buf("sd")
        self.rstd = A.alloc(F32, T)
        self.Brstd = Buf("rstd")
        self.mres = A.alloc(F32, 8, T)
        self.Bmres = [Buf("mres%d" % c) for c in range(8)]
        self.hn = A.alloc(BF16, 8, T)
        self.Bhn = [Buf("hn%d" % c) for c in range(8)]
        self.rtmp = [A.alloc(F32, T) for _ in range(2)]
        self.Brtmp = [Buf("rtmp%d" % i) for i in range(2)]
        self._rt = 0
        self.phase_mark = A.mark()
        self.dma_plain(self.ptab, self.d["ptab"], [], [self.Bptab])
        for c in range(8):
            self.dma_plain(self.h[:, c, :], self.d["xT"][c], [], self.Bh[c])
        self.memset("pool", self.onesb, 1.0, [self.Bconst])
        self.memset("pool", self.epsc, EPS, [self.Bconst])
        self.dma_cast(self.ident, self.d["ident"], [], [self.Bconst])

    def pcol(self, key, c, n=1):
        o = self.poff[key] + c
        return self.ptab[:, o:o + n]

    def nextsq(self):
        i = self._sq
        self._sq = (i + 1) % 4
        return self.sq[i], self.Bsq[i]

    def nextrt(self):
        i = self._rt
        self._rt ^= 1
        return self.rtmp[i], self.Brtmp[i]

    def stat_add(self, sb, src, src_bufs, c, n=8):
        sq, Bsq = self.nextsq()
        self.act(sq, src, AF.Square, src_bufs, [Bsq])
        self.mm(self.psb(sb), self.onesb, sq, c == 0, c == n - 1, [Bsq, self.Bconst], [self.PB[sb]])

    def finish_rstd(self, sb, dim=D):
        self.act(self.sd, self.psb(sb), AF.Sqrt, [self.PB[sb], self.Bconst], [self.Bsd], bias=self.epsc, scale=1.0 / dim)
        self.recip(self.rstd, self.sd, [self.Bsd], [self.Brstd])

    def pre_norm(self, t, gkey):
        sb = self.statbank()
        tsl = slice(t * T, (t + 1) * T)
        for c in range(8):
            self.stat_add(sb, self.h[:, c, tsl], [self.Bh[c][t]], c)
        if "pn1" in self.dbg:
            return
        self.finish_rstd(sb)
        if "pn2" in self.dbg:
            return
        for c in range(8):
            self.stt(self.hn[:, c, :], self.h[:, c, tsl], self.pcol(gkey, c), self.rstd, ALU.mult, ALU.mult,
                     [self.Bh[c][t], self.Bptab] + ([] if "pn3" in self.dbg else [self.Brstd]), [self.Bhn[c]])

    def post_norm_residual(self, t, gkey, sb):
        self.finish_rstd(sb)
        tsl = slice(t * T, (t + 1) * T)
        for c in range(8):
            rt, Brt = self.nextrt()
            self.stt(rt, self.mres[:, c, :], self.pcol(gkey, c), self.rstd, ALU.mult, ALU.mult,
                     [self.Bmres[c], self.Brstd, self.Bptab], [Brt])
            self.tt("pool", self.h[:, c, tsl], self.h[:, c, tsl], rt, ALU.add, [self.Bh[c][t], Brt], [self.Bh[c][t]])

    def evac_mres(self, b, dc, sb):
        self.cp("dve", self.mres[:, dc, :], self.psb(b), [self.PB[b]], [self.Bmres[dc]])
        self.stat_add(sb, self.mres[:, dc, :], [self.Bmres[dc]], dc)

    def make_slots(self, name, n, *free):
        aps = [self.A.alloc(BF16, *free) for _ in range(n)]
        bufs = [Buf("%s%d" % (name, i)) for i in range(n)]
        return {"aps": aps, "bufs": bufs, "i": 0, "n": n}

    def load_slot(self, slots, src):
        i = slots["i"]
        slots["i"] = (i + 1) % slots["n"]
        ap, bf = slots["aps"][i], slots["bufs"][i]
        flat = ap
        if len(ap.shape) == 3:
            flat = ap.rearrange("p a b -> p (a b)")
        self.dma_cast(flat, src, [], [bf])
        return ap, bf

    def stop(self, tag):
        if tag in self.dbg:
            raise StopBuild()

    def layer(self, i):
        try:
            self._layer(i)
        except StopBuild:
            pass

    def new_phase(self):
        self.A.reset(self.phase_mark)
        f = {}
        for e in ("pe", "act", "dve", "pool"):
            for o in reversed(self.P.ops[e]):
                if not o.is_dma:
                    f[e] = o
                    break
        Buf.fence = f

    def _layer(self, i):
        kind, j = i % 3, i // 3
        A = self.A
        self.new_phase()
        if "prenorm" in self.dbg:
            self.pre_norm(0, ("mix_pre", i))
            return
        if "nomix" in self.dbg:
            pass
        elif kind == 0:
            self.mixer_a(i, j)
        elif kind == 1:
            self.mixer_b(i)
        else:
            self.mixer_c(i)
        self.new_phase()
        if "noffn" not in self.dbg:
            self.ffn_phase(i)

    def ffn_phase(self, i):
        A = self.A
        actb = A.alloc(BF16, NFC, T)
        Bact = [Buf("act%d" % f) for f in range(NFC)]
        wup = self.make_slots("wup", 3, 8, 256)
        wdn = self.make_slots("wdn", 2, NFC, 128)
        cg = [A.alloc(F32, T) for _ in range(2)]
        cv = [A.alloc(F32, T) for _ in range(2)]
        sg = [A.alloc(F32, T) for _ in range(2)]
        Bcg = [Buf("cg%d" % k) for k in range(2)]
        Bcv = [Buf("cv%d" % k) for k in range(2)]
        Bsg = [Buf("sg%d" % k) for k in range(2)]
        halo = A.alloc(F32, 2 * NFC, 2)
        Bhalo = [Buf("halo%d" % q) for q in range(2 * NFC)]
        hb = A.alloc(BF16, 8, T)
        Bhb = [Buf("hb%d" % c) for c in range(8)]
        pt = [A.alloc(BF16, 2, T) for _ in range(2)]
        Bpt = [Buf("pt%d" % k) for k in range(2)]
        wg = self.make_slots("wg", 2, 8, 128)
        wp = A.alloc(BF16, 2, D)
        Bwp = Buf("wp")
        gate = [A.alloc(F32, T) for _ in range(2)]
        Bgate = [Buf("gate%d" % k) for k in range(2)]
        self.dma_cast(wp.rearrange("p a b -> p (a b)"), self.d["wp%d" % i], [], [Bwp])
        cbase = self.poff[("conv", i)]

        def tap(jj, q):
            o = cbase + jj * 44 + q
            return self.ptab[:, o:o + 1]

        for t in range(NT):
            tsl = slice(t * T, (t + 1) * T)
            ptt, Bptt = pt[t % 2], Bpt[t % 2]
            for kc in range(2):
                self.dma_cast(ptt[:, kc, :], self.d["pT%d" % i][kc][:, tsl], [], [Bptt])
            self.pre_norm(t, ("ffn_pre", i))
            for fc in range(NFC):
                w, Bw = self.load_slot(wup, self.d["wup%d" % i][fc])
                k2 = fc % 2
                bg = self.bank()
                bv = self.bank()
                for kc in range(8):
                    self.mm(self.psb(bg), w[:, kc, 0:128], self.hn[:, kc, :], kc == 0, kc == 7,
                            [Bw, self.Bhn[kc]], [self.PB[bg]])
                for kc in range(8):
                    self.mm(self.psb(bv), w[:, kc, 128:256], self.hn[:, kc, :], kc == 0, kc == 7,
                            [Bw, self.Bhn[kc]], [self.PB[bv]])
                self.stop("f1")
                for (b, q, cbuf, Bc) in ((bg, fc, cg[k2], Bcg[k2]), (bv, NFC + fc, cv[k2], Bcv[k2])):
                    pb = self.PB[b]
                    self.act(cbuf, self.psb(b), AF.Copy, [pb, self.Bptab], [Bc], scale=tap(2, q))
                    self.stt(cbuf[:, 1:T], self.psb(b, 0, T - 1), tap(1, q), cbuf[:, 1:T], ALU.mult, ALU.add,
                             [pb, Bc, self.Bptab], [Bc])
                    self.stt(cbuf[:, 2:T], self.psb(b, 0, T - 2), tap(0, q), cbuf[:, 2:T], ALU.mult, ALU.add,
                             [pb, Bc, self.Bptab], [Bc])
                    if t > 0:
                        self.stt(cbuf[:, 0:2], halo[:, q, 0:2], tap(0, q), cbuf[:, 0:2], ALU.mult, ALU.add,
                                 [Bhalo[q], Bc, self.Bptab], [Bc])
                        self.stt(cbuf[:, 0:1], halo[:, q, 1:2], tap(1, q), cbuf[:, 0:1], ALU.mult, ALU.add,
                                 [Bhalo[q], Bc, self.Bptab], [Bc])
                    if t < NT - 1:
                        self.cp("act", halo[:, q, 0:2], self.psb(b, T - 2, T), [pb], [Bhalo[q]])
                self.stop("f2")
                self.act(sg[k2], cg[k2], AF.Silu, [Bcg[k2]], [Bsg[k2]])
                self.tt("pool", actb[:, fc, :], sg[k2], cv[k2], ALU.mult, [Bsg[k2], Bcv[k2]], [Bact[fc]])
                self.stop("f3")
            self.stop("f4")
            sb = self.statbank()
            for dc in range(8):
                w, Bw = self.load_slot(wdn, self.d["wdn%d" % i][dc])
                b = self.bank()
                for fc in range(NFC):
                    self.mm(self.psb(b), w[:, fc, :], actb[:, fc, :], fc == 0, fc == NFC - 1,
                            [Bw, Bact[fc]], [self.PB[b]])
                self.evac_mres(b, dc, sb)
            self.stop("f5")
            self.post_norm_residual(t, ("ffn_post", i), sb)
            self.stop("f6")
            for c in range(8):
                self.cp("pool", hb[:, c, :], self.h[:, c, tsl], [self.Bh[c][t]], [Bhb[c]])
            sb = self.statbank()
            for dc in range(8):
                w, Bw = self.load_slot(wg, self.d["wg%d" % i][dc])
                bgt = self.bank()
                be = self.bank()
                for kc in range(8):
                    self.mm(self.psb(bgt), w[:, kc, :], hb[:, kc, :], kc == 0, kc == 7, [Bw, Bhb[kc]], [self.PB[bgt]])
                for kc in range(2):
                    self.mm(self.psb(be), wp[:, kc, dc * 128:(dc + 1) * 128], ptt[:, kc, :], kc == 0, kc == 1,
                            [Bwp, Bptt], [self.PB[be]])
                k2 = dc % 2
                self.act(gate[k2], self.psb(bgt), AF.Sigmoid, [self.PB[bgt]], [Bgate[k2]])
                self.tt("dve", self.mres[:, dc, :], gate[k2], self.psb(be), ALU.mult, [Bgate[k2], self.PB[be]],
                        [self.Bmres[dc]])
                self.stat_add(sb, self.mres[:, dc, :], [self.Bmres[dc]], dc)
            self.post_norm_residual(t, ("ple_g", i), sb)
            self.stop("f7")

    def mixer_c(self, i):
        A = self.A
        ybuf = A.alloc(BF16, 8, 30 + T)
        Byb = [Buf("yb%d" % c) for c in range(8)]
        z = A.alloc(F32, 8, T)
        Bz = [Buf("z%d" % c) for c in range(8)]
        zb = [A.alloc(BF16, T) for _ in range(2)]
        Bzb = [Buf("zb%d" % k) for k in range(2)]
        actc = A.alloc(BF16, 8, T)
        Bac = [Buf("actc%d" % c) for c in range(8)]
        dg = [A.alloc(BF16, 31, 128) for _ in range(2)]
        Bdg = [Buf("dg%d" % k) for k in range(2)]
        wi = self.make_slots("cwi", 2, 8, 256)
        wo = self.make_slots("cwo", 2, 8, 128)
        sgm = [A.alloc(F32, T) for _ in range(2)]
        Bsgm = [Buf("sgm%d" % k) for k in range(2)]
        mean = A.alloc(F32, T)
        msq = A.alloc(F32, T)
        var = A.alloc(F32, T)
        nmr = A.alloc(F32, T)
        Bmean, Bmsq, Bvar, Bnmr = Buf("mean"), Buf("msq"), Buf("var"), Buf("nmr")
        t1 = [A.alloc(F32, T) for _ in range(2)]
        Bt1 = [Buf("ct1_%d" % k) for k in range(2)]
        for c in range(8):
            self.memset("pool", ybuf[:, c, 0:30], 0.0, [Byb[c]])
        for t in range(NT):
            self.pre_norm(t, ("mix_pre", i))
            sb1 = self.statbank()
            sb2 = self.statbank()
            for c in range(8):
                w, Bw = self.load_slot(wi, self.d["c_wi"][c])
                ba = self.bank()
                bg = self.bank()
                for kc in range(8):
                    self.mm(self.psb(ba), w[:, kc, 0:128], self.hn[:, kc, :], kc == 0, kc == 7, [Bw, self.Bhn[kc]], [self.PB[ba]])
                for kc in range(8):
                    self.mm(self.psb(bg), w[:, kc, 128:256], self.hn[:, kc, :], kc == 0, kc == 7, [Bw, self.Bhn[kc]], [self.PB[bg]])
                k2 = c % 2
                self.act(sgm[k2], self.psb(bg), AF.Sigmoid, [self.PB[bg]], [Bsgm[k2]])
                self.tt("dve", ybuf[:, c, 30:30 + T], sgm[k2], self.psb(ba), ALU.mult, [Bsgm[k2], self.PB[ba]], [Byb[c]])
                for jj in range(31):
                    self.ts("pool", dg[k2][:, jj, :], self.ident, self.pcol("c_dw", jj * 8 + c), None, ALU.mult, None,
                            [self.Bconst, self.Bptab], [Bdg[k2]])
                bc = self.bank()
                for jj in range(31):
                    self.mm(self.psb(bc), dg[k2][:, jj, :], ybuf[:, c, jj:jj + T], jj == 0, jj == 30, [Bdg[k2], Byb[c]], [self.PB[bc]])
                bias = self.pcol("c_dwb", c)
                self.act(z[:, c, :], self.psb(bc), AF.Identity, [self.PB[bc], self.Bptab], [Bz[c]], bias=bias)
                self.act(zb[k2], self.psb(bc), AF.Identity, [self.PB[bc], self.Bptab], [Bzb[k2]], bias=bias)
                self.mm(self.psb(sb1), self.onesb, zb[k2], c == 0, c == 7, [Bzb[k2], self.Bconst], [self.PB[sb1]])
                sq, Bsq = self.nextsq()
                self.act(sq, self.psb(bc), AF.Square, [self.PB[bc], self.Bptab], [Bsq], bias=bias)
                self.mm(self.psb(sb2), self.onesb, sq, c == 0, c == 7, [Bsq, self.Bconst], [self.PB[sb2]])
                if t < NT - 1:
                    self.cp("pool", ybuf[:, c, 0:30], ybuf[:, c, T:T + 30], [Byb[c]], [Byb[c]])
            self.ts("dve", mean, self.psb(sb1), 1.0 / D, None, ALU.mult, None, [self.PB[sb1]], [Bmean])
            self.act(msq, mean, AF.Square, [Bmean], [Bmsq])
            self.stt(var, self.psb(sb2), 1.0 / D, msq, ALU.mult, ALU.subtract, [self.PB[sb2], Bmsq], [Bvar])
            self.act(self.sd, var, AF.Sqrt, [Bvar, self.Bconst], [self.Bsd], bias=self.epsc)
            self.recip(self.rstd, self.sd, [self.Bsd], [self.Brstd])
            self.stt(nmr, mean, -1.0, self.rstd, ALU.mult, ALU.mult, [Bmean, self.Brstd], [Bnmr])
            for c in range(8):
                k2 = c % 2
                self.tt("dve", t1[k2], z[:, c, :], self.rstd, ALU.mult, [Bz[c], self.Brstd], [Bt1[k2]])
                self.tt("pool", t1[k2], t1[k2], nmr, ALU.add, [Bt1[k2], Bnmr], [Bt1[k2]])
                self.act(actc[:, c, :], t1[k2], AF.Silu, [Bt1[k2], self.Bptab], [Bac[c]],
                         scale=self.pcol("c_lng", c), bias=self.pcol("c_lnb", c))
            sb = self.statbank()
            for dc in range(8):
                w, Bw = self.load_slot(wo, self.d["c_wo"][dc])
                b = self.bank()
                for c in range(8):
                    self.mm(self.psb(b), w[:, c, :], actc[:, c, :], c == 0, c == 7, [Bw, Bac[c]], [self.PB[b]])
                self.evac_mres(b, dc, sb)
            self.post_norm_residual(t, ("mix_post", i), sb)

    def mixer_a(self, i, j):
        A = self.A
        u = A.alloc(BF16, 16, T)
        Bu = [Buf("u%d" % c) for c in range(16)]
        vgb = A.alloc(BF16, 4, 2048)
        Bvgb = [Buf("vgb%d" % b) for b in range(4)]
        wv = self.make_slots("awv", 2, 8, 512)
        wu = self.make_slots("awu", 3, 8, 128)
        wo = self.make_slots("awo", 2, 16, 128)
        vgc = [A.alloc(F32, 512) for _ in range(3)]
        Bvgc = [Buf("vgc%d" % k) for k in range(3)]
        bst = A.alloc(F32, 4, 4, 6)
        Bbst = [Buf("bst%d" % b) for b in range(4)]
        mv = A.alloc(F32, 4, 2)
        rs = A.alloc(F32, 4, 1)
        vpe = A.alloc(F32, 4, 1)
        Bmv = [Buf("mv%d" % b) for b in range(4)]
        Brs = [Buf("rs%d" % b) for b in range(4)]
        Bvpe = [Buf("vpe%d" % b) for b in range(4)]
        nmh = A.alloc(F32, 1)
        wsT = A.alloc(BF16, 8, 128)
        Bws = Buf("wsT")
        Cc = A.alloc(F32, 16, 128)
        BCc = Buf("Cc")
        bsb = A.alloc(F32, 8, 128)
        Bbsb = Buf("bsb")
        rw = A.alloc(F32, 8, 128)
        Brw = Buf("rw")
        t1 = [A.alloc(F32, 128) for _ in range(3)]
        Bt1 = [Buf("at1_%d" % k) for k in range(3)]
        self.memset("pool", nmh, -0.5, [self.Bconst])
        self.dma_cast(wsT.rearrange("p a b -> p (a b)"), self.d["a_ws%d" % j], [], [Bws])
        self.memset("pool", wsT[64:128, :, 0:64], 0.0, [Bws])
        self.dma_plain(bsb.rearrange("p a b -> p (a b)"), self.d["a_bs%d" % j], [], [Bbsb])
        b0 = self.bank()
        b1 = self.bank()
        for g in range(8):
            bb = b0 if g < 4 else b1
            lo = (g % 4) * 128
            self.mm(self.psb(bb, lo, lo + 128), self.onesb, wsT[:, g, :], True, True, [Bws, self.Bconst], [self.PB[bb]])
        self.cp("act", rw[:, 0:4, :].rearrange("p a b -> p (a b)"), self.psb(b0), [self.PB[b0]], [Brw])
        self.cp("act", rw[:, 4:8, :].rearrange("p a b -> p (a b)"), self.psb(b1), [self.PB[b1]], [Brw])
        for uc in range(16):
            g = uc // 2
            self.stt(Cc[:, uc, :], rw[:, g, :], self.pcol(("a_lnb", j), uc), bsb[:, g, :], ALU.mult, ALU.add,
                     [Brw, Bbsb, self.Bptab], [BCc])
        for t in range(NT):
            self.pre_norm(t, ("mix_pre", i))
            for uc in range(16):
                w, Bw = self.load_slot(wu, self.d["a_wu%d" % j][uc])
                b = self.bank()
                for kc in range(8):
                    self.mm(self.psb(b), w[:, kc, :], self.hn[:, kc, :], kc == 0, kc == 7, [Bw, self.Bhn[kc]], [self.PB[b]])
                self.act(u[:, uc, :], self.psb(b), AF.Gelu_apprx_tanh, [self.PB[b]], [Bu[uc]])
            kk = 0
            for vq in range(4):
                w, Bw = self.load_slot(wv, self.d["a_wv%d" % j][vq])
                for blk in range(4):
                    b = self.bank()
                    for kc in range(8):
                        self.mm(self.psb(b), self.hn[:, kc, blk * 128:(blk + 1) * 128], w[:, kc, :], kc == 0, kc == 7,
                                [Bw, self.Bhn[kc]], [self.PB[b]])
                    k3 = kk % 3
                    kk += 1
                    self.act(vgc[k3], self.psb(b), AF.Gelu_apprx_tanh, [self.PB[b]], [Bvgc[k3]])
                    bo = bst[:, blk, vq, :]
                    src = vgc[k3]
                    self.P.op("dve", lambda e, bo=bo, src=src: e.bn_stats(out=bo, in_=src), [Bvgc[k3]], [Bbst[blk]])
                    self.cp("pool", vgb[:, blk, vq * 512:(vq + 1) * 512], vgc[k3], [Bvgc[k3]], [Bvgb[blk]])
            for blk in range(4):
                mo = mv[:, blk, :]
                bi = bst[:, blk, :, :]
                self.P.op("dve", lambda e, mo=mo, bi=bi: e.bn_aggr(out=mo, in_=bi), [Bbst[blk]], [Bmv[blk]])
                self.ts("pool", vpe[:, blk, :], mv[:, blk, 1:2], EPS, None, ALU.add, None, [Bmv[blk]], [Bvpe[blk]])
                self.tt("pool", rs[:, blk, :], vpe[:, blk, :], nmh, ALU.pow, [Bvpe[blk], self.Bconst], [Brs[blk]])
                self.ts("dve", vgb[:, blk, :], vgb[:, blk, :], mv[:, blk, 0:1], rs[:, blk, :], ALU.subtract, ALU.mult,
                        [Bvgb[blk], Bmv[blk], Brs[blk]], [Bvgb[blk]])
            kk = 0
            for blk in range(4):
                bsl = slice(blk * 128, (blk + 1) * 128)
                for uc in range(16):
                    b = self.bank()
                    self.mm(self.psb(b, 0, 128), vgb[:, blk, uc * 128:(uc + 1) * 128], wsT[:, uc // 2, :], True, True,
                            [Bvgb[blk], Bws], [self.PB[b]])
                    k3 = kk % 3
                    kk += 1
                    self.stt(t1[k3], self.psb(b, 0, 128), self.pcol(("a_lng", j), uc), Cc[:, uc, :], ALU.mult, ALU.add,
                             [self.PB[b], BCc, self.Bptab], [Bt1[k3]])
                    self.tt("dve", u[:, uc, bsl], t1[k3], u[:, uc, bsl], ALU.mult, [Bt1[k3], Bu[uc]], [Bu[uc]])
            sb = self.statbank()
            for dc in range(8):
                w, Bw = self.load_slot(wo, self.d["a_wo%d" % j][dc])
                b = self.bank()
                for uc in range(16):
                    self.mm(self.psb(b), w[:, uc, :], u[:, uc, :], uc == 0, uc == 15, [Bw, Bu[uc]], [self.PB[b]])
                self.evac_mres(b, dc, sb)
            self.post_norm_residual(t, ("mix_post", i), sb)

    def mixer_b(self, i):
        A = self.A
        kT = A.alloc(BF16, 8, 1024)
        BkT = [[Buf("kT%d_%d" % (c, s_)) for s_ in range(2)] for c in range(8)]
        V = A.alloc(BF16, 8, 1024)
        BV = [Buf("V%d" % b) for b in range(8)]
        qz = A.alloc(BF16, 8, 2, T)
        Bq = [Buf("qz%d" % c) for c in range(8)]
        Bb = A.alloc(BF16, 16, 640)
        BBb = Buf("Bb")
        Pb = [A.alloc(BF16, 640) for _ in range(2)]
        BPb = [Buf("Pb%d" % k) for k in range(2)]
        PTb = [A.alloc(BF16, 640) for _ in range(2)]
        BPTb = [Buf("PTb%d" % k) for k in range(2)]
        dgr = [A.alloc(BF16, 128) for _ in range(2)]
        Bdgr = [Buf("dgr%d" % k) for k in range(2)]
        st3 = [A.alloc(F32, 4) for _ in range(2)]
        Bst = [Buf("st%d" % k) for k in range(2)]
        wqk = self.make_slots("bwqk", 2, 8, 256)
        wvs = self.make_slots("bwv", 2, 8, 512)
        wos = self.make_slots("bwo", 2, 8, 128)
        oT, BoT = self.hn, self.Bhn
        ps = self.ps
        self.dma_cast(Bb.rearrange("p a b -> p (a b)"), self.d["b_bias"], [], [BBb])
        self.memset("pool", Bb[64:128, :, 0:64], NEG, [BBb])
        self.memset("pool", Bb[0:64, :, 576:640], NEG, [BBb])
        for c in range(8):
            self.memset("pool", qz[:, c, :, :], 0.0, [Bq[c]])
        unit = 0
        for t in range(NT):
            slot = t % 2
            self.pre_norm(t, ("mix_pre", i))
            for c in range(8):
                w, Bw = self.load_slot(wqk, self.d["b_wqk"][c])
                bq = self.bank()
                bk = self.bank()
                for kc in range(8):
                    self.mm(self.psb(bq), w[:, kc, 0:128], self.hn[:, kc, :], kc == 0, kc == 7, [Bw, self.Bhn[kc]], [self.PB[bq]])
                for kc in range(8):
                    self.mm(self.psb(bk), w[:, kc, 128:256], self.hn[:, kc, :], kc == 0, kc == 7, [Bw, self.Bhn[kc]], [self.PB[bk]])
                for hh in range(2):
                    rows = slice(hh * 64, hh * 64 + 64)
                    self.act(qz[rows, c, hh, :], ps[rows, bq * 512:(bq + 1) * 512], AF.Copy, [self.PB[bq]], [Bq[c]], scale=0.125)
                self.cp("dve", kT[:, c, slot * 512:(slot + 1) * 512], self.psb(bk), [self.PB[bk]], [BkT[c][slot]])
            for half in range(2):
                w, Bw = self.load_slot(wvs, self.d["b_wv"][half])
                for blk in range(4):
                    b = self.bank()
                    for kc in range(8):
                        self.mm(self.psb(b), self.hn[:, kc, blk * 128:(blk + 1) * 128], w[:, kc, :], kc == 0, kc == 7,
                                [Bw, self.Bhn[kc]], [self.PB[b]])
                    rb = (4 * t + blk) % 8
                    self.cp("act" if (blk % 2) else "dve", V[:, rb, half * 512:(half + 1) * 512], self.psb(b), [self.PB[b]], [BV[rb]])
            for c in range(8):
                for jq in range(4):
                    jb = 4 * t + jq
                    i0 = max(0, 4 - jb)
                    for hh in range(2):
                        hd = 2 * c + hh
                        rows = slice(hh * 64, hh * 64 + 64)
                        sp = unit % 2
                        k2 = unit % 2
                        unit += 1
                        sbase = sp * 1024
                        sbufs = [self.PB[2 * sp], self.PB[2 * sp + 1]]
                        for ii in range(i0, 5):
                            kb = jb - 4 + ii
                            rc = (kb % 8) * 128
                            ks = (kb // 4) % 2
                            sap = ps[:, sbase + ii * 128: sbase + (ii + 1) * 128]
                            wb = [sbufs[0] if ii < 4 else sbufs[1]]
                            self.mm(sap, qz[:, c, hh, jq * 128:(jq + 1) * 128], kT[:, c, rc:rc + 128], True, False,
                                    [Bq[c], BkT[c][ks]], wb)
                            self.mm(sap, self.ident, Bb[:, hd, ii * 128:(ii + 1) * 128], False, True, [self.Bconst, BBb], wb)
                        sfull = ps[:, sbase + i0 * 128: sbase + 640]
                        nmax, rsum, rinv = st3[k2][:, 0:1], st3[k2][:, 1:2], st3[k2][:, 2:3]
                        self.P.op("dve", lambda e, nmax=nmax, sfull=sfull: e.tensor_reduce(out=nmax, in_=sfull, axis=AX.X, op=ALU.max, negate=True),
                                  sbufs, [Bst[k2]])
                        self.act(Pb[k2][:, i0 * 128:640], sfull, AF.Exp, sbufs + [Bst[k2]], [BPb[k2], Bst[k2]], bias=nmax, accum=rsum)
                        self.recip(rinv, rsum, [Bst[k2]], [Bst[k2]])
                        self.ts("pool", dgr[k2], self.ident, rinv, None, ALU.mult, None, [self.Bconst, Bst[k2]], [Bdgr[k2]])
                        for ii in range(i0, 5):
                            pb_ = self.PB[4] if ii < 4 else self.PB[5]
                            self.mm(ps[:, 2048 + ii * 128: 2048 + (ii + 1) * 128], Pb[k2][:, ii * 128:(ii + 1) * 128], dgr[k2], True, True,
                                    [BPb[k2], Bdgr[k2]], [pb_])
                        self.cp("act", PTb[k2][:, i0 * 128:640], ps[:, 2048 + i0 * 128: 2048 + 640], [self.PB[4], self.PB[5]], [BPTb[k2]])
                        ob = 6 + hh
                        for ii in range(i0, 5):
                            kb = jb - 4 + ii
                            self.mm(ps[:, ob * 512 + jq * 128: ob * 512 + (jq + 1) * 128], V[:, kb % 8, c * 128:(c + 1) * 128],
                                    PTb[k2][:, ii * 128:(ii + 1) * 128], ii == i0, ii == 4, [BV[kb % 8], BPTb[k2]], [self.PB[ob]])
                for hh in range(2):
                    rows = slice(hh * 64, hh * 64 + 64)
                    ob = 6 + hh
                    self.cp("dve", oT[rows, c, :], ps[rows, ob * 512:(ob + 1) * 512], [self.PB[ob]], [BoT[c]])
            sb = self.statbank()
            for dc in range(8):
                w, Bw = self.load_slot(wos, self.d["b_wo"][dc])
                b = self.bank()
                for c in range(8):
                    self.mm(self.psb(b), w[:, c, :], oT[:, c, :], c == 0, c == 7, [Bw, BoT[c]], [self.PB[b]])
                self.evac_mres(b, dc, sb)
            self.post_norm_residual(t, ("mix_post", i), sb)

    def store_output(self):
        for c in range(8):
            self.dma_plain(self.yT[c], self.h[:, c, :], self.Bh[c], [Buf("y%d" % c)], is_output=True)


def _cols(v, n):
    return np.ascontiguousarray(np.asarray(v, np.float32).reshape(n, 128).T)


def _kc_tile(w, ncols_per_block):
    K, N = w.shape
    nb = N // ncols_per_block
    x = w.reshape(K // 128, 128, nb, ncols_per_block)
    x = x.transpose(2, 1, 0, 3)
    return np.ascontiguousarray(x).reshape(nb, 128, (K // 128) * ncols_per_block)


def host_shared(inp, layers):
    off, R = ptab_layout()
    ptab = np.zeros((128, R), np.float32)

    def put(key, arr):
        ptab[:, off[key]:off[key] + arr.shape[1]] = arr

    for i in range(DEPTH):
        put(("mix_pre", i), _cols(inp["mix_pre_g"][i], 8))
        put(("mix_post", i), _cols(inp["mix_post_g"][i], 8))
        put(("ffn_pre", i), _cols(inp["ffn_pre_g"][i], 8))
        put(("ffn_post", i), _cols(inp["ffn_post_g"][i], 8))
        put(("ple_g", i), _cols(inp["ple_norm_g"][i], 8))
        cv = np.concatenate([_cols(inp["ffn_conv"][i][jj], 44) for jj in range(3)], axis=1)
        put(("conv", i), cv)
    for j in range(2):
        put(("a_lng", j), _cols(inp["a_ln_g"][j], 16))
        put(("a_lnb", j), _cols(inp["a_ln_b"][j], 16))
    dw = np.asarray(inp["c_dw"][0], np.float32)
    dwc = dw.reshape(31, 8, 128).transpose(2, 0, 1).reshape(128, 248)
    put("c_dw", np.ascontiguousarray(dwc))
    put("c_dwb", _cols(inp["c_dw_b"][0], 8))
    put("c_lng", _cols(inp["c_ln_g"][0], 8))
    put("c_lnb", _cols(inp["c_ln_b"][0], 8))
    sh = {"ptab": ptab, "ident": np.eye(128, dtype=np.float32)}
    for i in layers:
        wu = np.asarray(inp["ffn_w_up"][i], np.float32)
        g = wu[:, :FF].reshape(D, NFC, 128)
        v = wu[:, FF:].reshape(D, NFC, 128)
        gv = np.concatenate([g, v], axis=2).reshape(D, NFC * 256)
        sh["wup%d" % i] = _kc_tile(gv, 256)
        wd = np.asarray(inp["ffn_w_down"][i], np.float32)
        x = wd.reshape(NFC, 128, 8, 128).transpose(2, 1, 0, 3)
        sh["wdn%d" % i] = np.ascontiguousarray(x).reshape(8, 128, NFC * 128)
        sh["wg%d" % i] = _kc_tile(np.asarray(inp["ple_w_gate"][i], np.float32), 128)
        wp = np.asarray(inp["ple_w_proj"][i], np.float32)
        sh["wp%d" % i] = np.ascontiguousarray(wp.reshape(2, 128, D).transpose(1, 0, 2)).reshape(128, 2 * D)
        kind, j = i % 3, i // 3
        if kind == 0:
            win = np.asarray(inp["a_w_in"][j], np.float32)
            sh["a_wu%d" % j] = _kc_tile(win[:, :2048], 128)
            sh["a_wv%d" % j] = _kc_tile(win[:, 2048:], 512)
            wo = np.asarray(inp["a_w_out"][j], np.float32)
            x = wo.reshape(16, 128, 8, 128).transpose(2, 1, 0, 3)
            sh["a_wo%d" % j] = np.ascontiguousarray(x).reshape(8, 128, 16 * 128)
            ws = np.asarray(inp["a_w_s"][j], np.float32)
            sh["a_ws%d" % j] = np.ascontiguousarray(ws.transpose(2, 0, 1)).reshape(128, 8 * 128)
            bs = np.asarray(inp["a_b_s"][j], np.float32).reshape(1, 8 * 128)
            sh["a_bs%d" % j] = np.ascontiguousarray(np.broadcast_to(bs, (128, 8 * 128)))
        elif kind == 1:
            wq = np.asarray(inp["b_w_qkv"][0], np.float32)
            q = wq[:, :D].reshape(D, 8, 128)
            k = wq[:, D:2 * D].reshape(D, 8, 128)
            qk = np.concatenate([q, k], axis=2).reshape(D, 8 * 256)
            sh["b_wqk"] = _kc_tile(qk, 256)
            sh["b_wv"] = _kc_tile(np.ascontiguousarray(wq[:, 2 * D:]), 512)
            wo = np.asarray(inp["b_w_out"][0], np.float32)
            x = wo.reshape(8, 128, 8, 128).transpose(2, 1, 0, 3)
            sh["b_wo"] = np.ascontiguousarray(x).reshape(8, 128, 8 * 128)
            rb = np.asarray(inp["b_rel_bias"][0], np.float32)
            qq = np.arange(128)[:, None]
            kk = np.arange(640)[None, :]
            idx = np.clip(qq + 512 - kk, -128, 128) + 128
            bfull = rb[:, idx]
            sh["b_bias"] = np.ascontiguousarray(bfull.transpose(1, 0, 2)).reshape(128, 16 * 640)
        else:
            wi = np.asarray(inp["c_w_in"][0], np.float32)
            a = wi[:, :D].reshape(D, 8, 128)
            g = wi[:, D:].reshape(D, 8, 128)
            ag = np.concatenate([a, g], axis=2).reshape(D, 8 * 256)
            sh["c_wi"] = _kc_tile(ag, 256)
            wo = np.asarray(inp["c_w_out"][0], np.float32)
            x = wo.reshape(8, 128, 8, 128).transpose(2, 1, 0, 3)
            sh["c_wo"] = np.ascontiguousarray(x).reshape(8, 128, 8 * 128)
    return sh


def run_layers(hT_in, p, shared, layers, trace=False, dbg=()):
    nc = Builder(layers, dbg).build()
    in_maps = []
    for b in range(8):
        m = dict(shared)
        m["xT"] = hT_in[b]
        for i in layers:
            m["pT%d" % i] = np.ascontiguousarray(p[i, b].T).reshape(2, 128, S)
        in_maps.append(m)
    res = run_bass_kernel_spmd(nc, in_maps, core_ids=list(range(8)), trace=trace)
    out = np.stack([res.results[b]["yT"] for b in range(8)])
    return out, res


def kernel(**inputs):
    inp = {k: np.asarray(v) for k, v in inputs.items()}
    layers = list(range(DEPTH))
    x = inp["x"].astype(np.float32, copy=False)
    hT = np.ascontiguousarray(x.transpose(0, 2, 1)).reshape(8, 8, 128, S)
    shared = host_shared(inp, layers)
    out, _ = run_layers(hT, inp["p"].astype(np.float32, copy=False), shared, layers)
    y = out.reshape(8, D, S).transpose(0, 2, 1)
    return np.ascontiguousarray(y).astype(np.float32, copy=False)
```

```python
import numpy as np
from contextlib import ExitStack
import concourse.bass as bass
import concourse.mybir as mybir
from concourse.bass_utils import run_bass_kernel_spmd

F32 = mybir.dt.float32
BF16 = mybir.dt.bfloat16
AF = mybir.ActivationFunctionType
ALU = mybir.AluOpType
AX = mybir.AxisListType

D = 1024
S = 2048
T = 512
NT = S // T
FF = 2816
NFC = FF // 128
DEPTH = 4
EPS = 1e-6
NEG = -1e30


class Buf:
    __slots__ = ("name", "last_w", "readers", "dma_readers", "dma_sem", "dma_cnt", "const", "excl")

    fence = {}

    def __init__(self, name, const=False, excl=False):
        self.name = name
        self.excl = excl
        self.last_w = None
        self.readers = dict(Buf.fence)
        self.dma_readers = []
        self.dma_sem = None
        self.dma_cnt = 0
        self.const = const


class Op:
    __slots__ = ("eng", "fn", "deps", "sig", "sigval", "is_dma", "dma_sem", "dma_val", "idx")

    def __init__(self, eng, fn, is_dma):
        self.eng = eng
        self.fn = fn
        self.deps = []
        self.sig = False
        self.sigval = 0
        self.is_dma = is_dma
        self.dma_sem = None
        self.dma_val = 0


class Prog:
    ENGS = ("pe", "act", "dve", "pool", "sp")

    def __init__(self, nc):
        self.nc = nc
        self.ops = {e: [] for e in self.ENGS}
        self.n = 0
        self.dma_bufs = []
        self.out_dma_ops = []

    def _add(self, op, reads, writes):
        deps = []
        for b in reads:
            w = b.last_w
            if w is not None:
                deps.append((w, True))
            if b.excl:
                for e_, r in b.readers.items():
                    if e_ != op.eng:
                        deps.append((r, False))
        for b in writes:
            w = b.last_w
            if w is not None and not (op.is_dma and w.is_dma):
                deps.append((w, False))
            for r in b.readers.values():
                deps.append((r, False))
            for r in b.dma_readers:
                deps.append((r, False))
        seen = set()
        for d, raw in deps:
            if d is op:
                continue
            if (not d.is_dma) and (not op.is_dma) and d.eng == op.eng:
                if op.eng == "pe":
                    continue
            k = id(d)
            if k in seen:
                continue
            seen.add(k)
            op.deps.append(d)
            if not d.is_dma:
                d.sig = True
        for b in writes:
            b.last_w = op
            b.readers = {}
            b.dma_readers = []
        for b in reads:
            if b.const or b in writes:
                continue
            if op.is_dma:
                b.dma_readers.append(op)
            else:
                b.readers[op.eng] = op
        op.idx = self.n
        self.n += 1
        self.ops[op.eng].append(op)
        return op

    def op(self, eng, fn, reads=(), writes=()):
        return self._add(Op(eng, fn, False), list(reads), list(writes))

    def dma(self, eng, fn, reads=(), writes=(), is_output=False):
        o = Op(eng, fn, True)
        dst = writes[0]
        if dst.dma_sem is None:
            dst.dma_sem = "pending"
            self.dma_bufs.append(dst)
        dst.dma_cnt += 16
        o.dma_sem = dst
        o.dma_val = dst.dma_cnt
        self._add(o, list(reads), list(writes))
        if is_output:
            self.out_dma_ops.append(o)
        return o

    def emit(self, stack):
        nc = self.nc
        sems = {}
        for e in ("pe", "act", "dve", "pool"):
            sems[e] = stack.enter_context(nc.semaphore("s_" + e))
        for i, b in enumerate(self.dma_bufs):
            b.dma_sem = stack.enter_context(nc.semaphore("d%d" % i))
        for e in ("pe", "act", "dve", "pool"):
            c = 0
            for o in self.ops[e]:
                if o.is_dma:
                    continue
                if o.sig:
                    c += 1
                    o.sigval = c
        out_ops = self.out_dma_ops

        def run(engh, ename):
            waited = {}

            def wait(sem, val):
                k = id(sem)
                if waited.get(k, 0) >= val:
                    return
                waited[k] = val
                engh.wait_ge(sem, val)

            for o in self.ops[ename]:
                for d in o.deps:
                    if d.is_dma:
                        wait(d.dma_sem.dma_sem, d.dma_val)
                    else:
                        wait(sems[d.eng], d.sigval)
                ins = o.fn(engh)
                if o.is_dma:
                    ins.then_inc(o.dma_sem.dma_sem, 16)
                elif o.sig:
                    ins.then_inc(sems[ename], 1)
            if ename == "sp":
                for o in out_ops:
                    wait(o.dma_sem.dma_sem, o.dma_val)

        block = stack.enter_context(nc.Block())

        @block.tensor
        def _(e):
            run(e, "pe")

        @block.scalar
        def _(e):
            run(e, "act")

        @block.vector
        def _(e):
            run(e, "dve")

        @block.gpsimd
        def _(e):
            run(e, "pool")

        @block.sync
        def _(e):
            run(e, "sp")


def ptab_layout():
    off = {}
    c = 0
    for i in range(DEPTH):
        for nm in ("mix_pre", "mix_post", "ffn_pre", "ffn_post", "ple_g"):
            off[(nm, i)] = c
            c += 8
        off[("conv", i)] = c
        c += 132
    for j in range(2):
        off[("a_lng", j)] = c
        c += 16
        off[("a_lnb", j)] = c
        c += 16
    off["c_dw"] = c
    c += 248
    off["c_dwb"] = c
    c += 8
    off["c_lng"] = c
    c += 8
    off["c_lnb"] = c
    c += 8
    return off, c


class StopBuild(Exception):
    pass


class Stream:
    def __init__(self, B, name, n, free, srcs):
        self.B = B
        self.n = n
        self.srcs = list(srcs)
        self.aps = [B.A.alloc(BF16, *free) for _ in range(n)]
        self.bufs = [Buf("%s%d" % (name, i)) for i in range(n)]
        self.issued = 0
        self.taken = 0
        for _ in range(n - 1):
            self._issue()

    def _issue(self):
        k = self.issued
        if k >= len(self.srcs):
            return
        self.issued += 1
        ap, bf = self.aps[k % self.n], self.bufs[k % self.n]
        flat = ap.rearrange("p a b -> p (a b)") if len(ap.shape) == 3 else ap
        self.B.dma_cast(flat, self.srcs[k], [], [bf])

    def next(self):
        k = self.taken
        self.taken += 1
        self._issue()
        return self.aps[k % self.n], self.bufs[k % self.n]


class Arena:
    def __init__(self, ap_all, base, limit):
        self.all = ap_all
        self.off = base
        self.limit = limit

    def mark(self):
        return self.off

    def reset(self, m):
        self.off = m

    def alloc(self, dtype, *free):
        isz = 4 if dtype == F32 else 2
        n = 1
        for f in free:
            n *= f
        nb = n * isz
        off = (self.off + 63) // 64 * 64
        assert off + nb <= self.limit, ("SBUF arena overflow", off + nb, self.limit)
        self.off = off + nb
        v = self.all[:, off // 2:(off + nb) // 2]
        if dtype == F32:
            v = v.bitcast(F32)
        if len(free) == 2:
            v = v.rearrange("p (a b) -> p a b", a=free[0])
        elif len(free) == 3:
            v = v.rearrange("p (a b c) -> p a b c", a=free[0], b=free[1])
        return v


class Builder:
    def __init__(self, layers, dbg=()):
        self.layers = list(layers)
        self.dbg = set(dbg)
        self.nc = bass.Bass("TRN2", target_bir_lowering=False)
        Buf.fence = {}
        self.P = Prog(self.nc)
        self.poff, self.pcols = ptab_layout()
        self._bank = 0
        self._stat = 0

    def mm(self, out, lhsT, rhs, start, stop, r, w, **kw):
        self.P.op("pe", lambda e: e.matmul(out, lhsT=lhsT, rhs=rhs, start=start, stop=stop, **kw), r, w)

    def act(self, out, in_, func, r, w, bias=None, scale=None, accum=None):
        kw = {}
        if bias is not None:
            kw["bias"] = bias
        if scale is not None:
            kw["scale"] = scale
        if accum is not None:
            kw["accum_out"] = accum
        self.P.op("act", lambda e: e.activation(out=out, in_=in_, func=func, **kw), r, w)

    def ts(self, eng, out, in0, s1, s2, op0, op1, r, w):
        if op1 is None and eng == "pool":
            s2, op1 = 1.0, ALU.mult
        if op1 is None:
            self.P.op(eng, lambda e: e.tensor_scalar(out=out, in0=in0, scalar1=s1, scalar2=None, op0=op0), r, w)
        else:
            self.P.op(eng, lambda e: e.tensor_scalar(out=out, in0=in0, scalar1=s1, scalar2=s2, op0=op0, op1=op1), r, w)

    def stt(self, out, in0, scalar, in1, op0, op1, r, w):
        self.P.op("dve", lambda e: e.scalar_tensor_tensor(out=out, in0=in0, scalar=scalar, in1=in1, op0=op0, op1=op1), r, w)

    def tt(self, eng, out, in0, in1, op, r, w):
        self.P.op(eng, lambda e: e.tensor_tensor(out=out, in0=in0, in1=in1, op=op), r, w)

    def cp(self, eng, out, in_, r, w):
        if eng == "act":
            self.P.op("act", lambda e: e.copy(out=out, in_=in_), r, w)
        else:
            self.P.op(eng, lambda e: e.tensor_copy(out=out, in_=in_), r, w)

    def recip(self, out, in_, r, w):
        self.P.op("dve", lambda e: e.reciprocal(out=out, in_=in_), r, w)

    def memset(self, eng, ap, val, w):
        self.P.op(eng, lambda e: e.memset(ap, val), [], w)

    def dma_cast(self, out, in_, r, w):
        self.P.dma("pool", lambda e: e.dma_start(out=out, in_=in_), r, w)

    def dma_plain(self, out, in_, r, w, is_output=False):
        self.P.dma("sp", lambda e: e.dma_start(out=out, in_=in_), r, w, is_output=is_output)

    def bank(self):
        b = self._bank
        self._bank = (b + 1) % 6
        return b

    def statbank(self):
        b = 6 + self._stat
        self._stat ^= 1
        return b

    def psb(self, b, lo=0, hi=512):
        return self.ps[:, b * 512 + lo:b * 512 + hi]

    def build(self):
        nc = self.nc
        st = ExitStack()
        with st:
            self.declare_dram()
            self.sb_all = st.enter_context(nc.sbuf_tensor("sb_all", [128, 106300], BF16))
            self.ps = st.enter_context(nc.psum_tensor("ps_all", [128, 4096], F32))
            self.PB = [Buf("psb%d" % i, excl=True) for i in range(8)]
            self.A = Arena(self.sb_all, 0, 106300 * 2)
            self.setup_persistent()
            for i in self.layers:
                self.layer(i)
            self.store_output()
            self.P.emit(st)
        return nc

    def declare_dram(self):
        nc = self.nc
        dt = lambda n, s: nc.dram_tensor(n, s, F32, kind="ExternalInput").ap()
        self.d = {}
        self.d["xT"] = dt("xT", [8, 128, S])
        self.d["ptab"] = dt("ptab", [128, self.pcols])
        self.d["ident"] = dt("ident", [128, 128])
        for i in self.layers:
            self.d["pT%d" % i] = dt("pT%d" % i, [2, 128, S])
            self.d["wup%d" % i] = dt("wup%d" % i, [NFC, 128, 8 * 256])
            self.d["wdn%d" % i] = dt("wdn%d" % i, [8, 128, NFC * 128])
            self.d["wg%d" % i] = dt("wg%d" % i, [8, 128, 8 * 128])
            self.d["wp%d" % i] = dt("wp%d" % i, [128, 2 * D])
            kind, j = i % 3, i // 3
            if kind == 0:
                self.d["a_wu%d" % j] = dt("a_wu%d" % j, [16, 128, 8 * 128])
                self.d["a_wv%d" % j] = dt("a_wv%d" % j, [4, 128, 8 * 512])
                self.d["a_wo%d" % j] = dt("a_wo%d" % j, [8, 128, 16 * 128])
                self.d["a_ws%d" % j] = dt("a_ws%d" % j, [128, 8 * 128])
                self.d["a_bs%d" % j] = dt("a_bs%d" % j, [128, 8 * 128])
            elif kind == 1:
                self.d["b_wqk"] = dt("b_wqk", [8, 128, 8 * 256])
                self.d["b_wv"] = dt("b_wv", [4, 128, 8 * 256])
                self.d["b_wo"] = dt("b_wo", [8, 128, 8 * 128])
                self.d["b_bias"] = dt("b_bias", [128, 16 * 640])
            else:
                self.d["c_wi"] = dt("c_wi", [8, 128, 8 * 256])
                self.d["c_wo"] = dt("c_wo", [8, 128, 8 * 128])
        self.yT = nc.dram_tensor("yT", [8, 128, S], F32, kind="ExternalOutput").ap()

    def setup_persistent(self):
        A = self.A
        self.h = A.alloc(F32, 8, S)
        self.Bh = [[Buf("h%d_%d" % (c, t)) for t in range(NT)] for c in range(8)]
        self.ptab = A.alloc(F32, self.pcols)
        self.Bptab = Buf("ptab", const=True)
        self.ident = A.alloc(BF16, 128)
        self.onesb = A.alloc(BF16, 128)
        self.epsc = A.alloc(F32, 1)
        self.dummy = A.alloc(F32, 8)
        self.Bconst = Buf("const", const=True)
        self.sq = [A.alloc(BF16, T) for _ in range(3)]
        self.Bsq = [Buf("sq%d" % i) for i in range(3)]
        self._sq = 0
        self.rstds = [A.alloc(F32, T) for _ in range(3)]
        self.Brstds = [Buf("rstd%d" % k) for k in range(3)]
        self.rstd, self.Brstd = self.rstds[0], self.Brstds[0]
        self.mres = A.alloc(F32, 8, T)
        self.Bmres = [Buf("mres%d" % c) for c in range(8)]
        self.hns = [A.alloc(BF16, 8, T) for _ in range(2)]
        self.Bhns = [[Buf("hn%d_%d" % (k, c)) for c in range(8)] for k in range(2)]
        self.hn, self.Bhn = self.hns[0], self.Bhns[0]
        self.rtmp = [A.alloc(F32, T) for _ in range(2)]
        self.Brtmp = [Buf("rtmp%d" % i) for i in range(2)]
        self._rt = 0
        self.phase_mark = A.mark()
        self.dma_plain(self.ptab, self.d["ptab"], [], [self.Bptab])
        for c in range(8):
            self.dma_plain(self.h[:, c, :], self.d["xT"][c], [], self.Bh[c])
        self.memset("pool", self.onesb, 1.0, [self.Bconst])
        self.memset("pool", self.epsc, EPS, [self.Bconst])
        self.dma_cast(self.ident, self.d["ident"], [], [self.Bconst])

    def pcol(self, key, c, n=1):
        o = self.poff[key] + c
        return self.ptab[:, o:o + n]

    def nextsq(self):
        i = self._sq
        self._sq = (i + 1) % 3
        return self.sq[i], self.Bsq[i]

    def nextrt(self):
        i = self._rt
        self._rt ^= 1
        return self.rtmp[i], self.Brtmp[i]

    def stat_add(self, sb, src, src_bufs, c, n=8):
        sq, Bsq = self.nextsq()
        self.act(sq, src, AF.Square, src_bufs, [Bsq])
        self.mm(self.psb(sb), self.onesb, sq, c == 0, c == n - 1, [Bsq, self.Bconst], [self.PB[sb]])

    def finish_rstd(self, sb, dim=D, role=0):
        rstd, Brstd = self.rstds[role], self.Brstds[role]
        self.act(rstd, self.psb(sb), AF.Sqrt, [self.PB[sb], self.Bconst], [Brstd], bias=self.epsc, scale=1.0 / dim)
        self.recip(rstd, rstd, [Brstd], [Brstd])
        return rstd, Brstd

    def pre_norm(self, t, gkey, k=0):
        hn, Bhn = self.hns[k], self.Bhns[k]
        sb = self.statbank()
        tsl = slice(t * T, (t + 1) * T)
        for c in range(8):
            self.stat_add(sb, self.h[:, c, tsl], [self.Bh[c][t]], c)
        rstd, Brstd = self.finish_rstd(sb, role=0)
        for c in range(8):
            self.stt(hn[:, c, :], self.h[:, c, tsl], self.pcol(gkey, c), rstd, ALU.mult, ALU.mult,
                     [self.Bh[c][t], Brstd, self.Bptab], [Bhn[c]])

    def post_norm_gen(self, t, gkey, sb, role, after=None):
        rstd, Brstd = self.finish_rstd(sb, role=role)
        tsl = slice(t * T, (t + 1) * T)
        yield
        for c in range(8):
            rt, Brt = self.nextrt()
            self.stt(rt, self.mres[:, c, :], self.pcol(gkey, c), rstd, ALU.mult, ALU.mult,
                     [self.Bmres[c], Brstd, self.Bptab], [Brt])
            self.tt("pool", self.h[:, c, tsl], self.h[:, c, tsl], rt, ALU.add, [self.Bh[c][t], Brt], [self.Bh[c][t]])
            if after is not None:
                after(c)
            yield

    def post_norm_residual(self, t, gkey, sb, role=1):
        for _ in self.post_norm_gen(t, gkey, sb, role):
            pass

    def evac_mres(self, b, dc, sb):
        self.cp("dve", self.mres[:, dc, :], self.psb(b), [self.PB[b]], [self.Bmres[dc]])
        self.stat_add(sb, self.mres[:, dc, :], [self.Bmres[dc]], dc)

    def make_slots(self, name, n, *free):
        aps = [self.A.alloc(BF16, *free) for _ in range(n)]
        bufs = [Buf("%s%d" % (name, i)) for i in range(n)]
        return {"aps": aps, "bufs": bufs, "i": 0, "n": n}

    def load_slot(self, slots, src):
        i = slots["i"]
        slots["i"] = (i + 1) % slots["n"]
        ap, bf = slots["aps"][i], slots["bufs"][i]
        flat = ap
        if len(ap.shape) == 3:
            flat = ap.rearrange("p a b -> p (a b)")
        self.dma_cast(flat, src, [], [bf])
        return ap, bf

    def stream(self, name, n, free, srcs):
        return Stream(self, name, n, free, srcs)

    def stop(self, tag):
        if tag in self.dbg:
            raise StopBuild()

    def layer(self, i):
        try:
            self._layer(i)
        except StopBuild:
            pass

    def new_phase(self):
        self.A.reset(self.phase_mark)
        f = {}
        for e in ("pe", "act", "dve", "pool"):
            for o in reversed(self.P.ops[e]):
                if not o.is_dma:
                    f[e] = o
                    break
        Buf.fence = f

    def _layer(self, i):
        kind, j = i % 3, i // 3
        A = self.A
        self.new_phase()
        if "prenorm" in self.dbg:
            self.pre_norm(0, ("mix_pre", i))
            return
        if "nomix" in self.dbg:
            pass
        elif kind == 0:
            self.mixer_a(i, j)
        elif kind == 1:
            self.mixer_b(i)
        else:
            self.mixer_c(i)
        self.new_phase()
        if "noffn" not in self.dbg:
            self.ffn_phase(i)

    def ffn_phase(self, i):
        A = self.A
        actb = A.alloc(BF16, NFC, T)
        Bact = [Buf("act%d" % f) for f in range(NFC)]
        wupS = self.stream("wup", 4, (8, 256), [self.d["wup%d" % i][fc] for _t in range(NT) for fc in range(NFC)])
        wdnS = self.stream("wdn", 2, (NFC, 128), [self.d["wdn%d" % i][dc] for _t in range(NT) for dc in range(8)])
        wgS = self.stream("wg", 2, (8, 128), [self.d["wg%d" % i][dc] for _t in range(NT) for dc in range(8)])
        cg = [A.alloc(F32, T) for _ in range(2)]
        cv = [A.alloc(F32, T) for _ in range(2)]
        sg = [A.alloc(F32, T) for _ in range(2)]
        Bcg = [Buf("cg%d" % k) for k in range(2)]
        Bcv = [Buf("cv%d" % k) for k in range(2)]
        Bsg = [Buf("sg%d" % k) for k in range(2)]
        halo = [A.alloc(F32, 2 * NFC, 2) for _ in range(2)]
        Bhalo = [[Buf("halo%d_%d" % (k, q)) for q in range(2 * NFC)] for k in range(2)]
        hb = A.alloc(BF16, 8, T)
        Bhb = [Buf("hb%d" % c) for c in range(8)]
        pt = [A.alloc(BF16, 2, T) for _ in range(2)]
        Bpt = [Buf("pt%d" % k) for k in range(2)]
        wp = A.alloc(BF16, 2, D)
        Bwp = Buf("wp")
        gate = [A.alloc(F32, T) for _ in range(2)]
        Bgate = [Buf("gate%d" % k) for k in range(2)]
        self.dma_cast(wp.rearrange("p a b -> p (a b)"), self.d["wp%d" % i], [], [Bwp])
        cbase = self.poff[("conv", i)]

        def tap(jj, q):
            o = cbase + jj * 44 + q
            return self.ptab[:, o:o + 1]

        def step(bg):
            for g in list(bg):
                try:
                    next(g)
                except StopIteration:
                    bg.remove(g)

        def drain(bg):
            while bg:
                step(bg)

        def stage_A(t):
            ptt, Bptt = pt[t % 2], Bpt[t % 2]
            tsl = slice(t * T, (t + 1) * T)
            for kc in range(2):
                self.dma_cast(ptt[:, kc, :], self.d["pT%d" % i][kc][:, tsl], [], [Bptt])
            self.pre_norm(t, ("ffn_pre", i), k=t % 2)

        def stage_B(t, bg):
            hn, Bhn = self.hns[t % 2], self.Bhns[t % 2]
            for fc in range(NFC):
                w, Bw = wupS.next()
                k2 = fc % 2
                bg_ = self.bank()
                bv_ = self.bank()
                for kc in range(8):
                    self.mm(self.psb(bg_), w[:, kc, 0:128], hn[:, kc, :], kc == 0, kc == 7, [Bw, Bhn[kc]], [self.PB[bg_]])
                for kc in range(8):
                    self.mm(self.psb(bv_), w[:, kc, 128:256], hn[:, kc, :], kc == 0, kc == 7, [Bw, Bhn[kc]], [self.PB[bv_]])
                for (b, q, cbuf, Bc) in ((bg_, fc, cg[k2], Bcg[k2]), (bv_, NFC + fc, cv[k2], Bcv[k2])):
                    pb = self.PB[b]
                    self.act(cbuf, self.psb(b), AF.Copy, [pb, self.Bptab], [Bc], scale=tap(2, q))
                    if t < NT - 1:
                        self.cp("act", halo[t % 2][:, q, 0:2], self.psb(b, T - 2, T), [pb], [Bhalo[t % 2][q]])
                    self.stt(cbuf[:, 1:T], self.psb(b, 0, T - 1), tap(1, q), cbuf[:, 1:T], ALU.mult, ALU.add,
                             [pb, Bc, self.Bptab], [Bc])
                    self.stt(cbuf[:, 2:T], self.psb(b, 0, T - 2), tap(0, q), cbuf[:, 2:T], ALU.mult, ALU.add,
                             [pb, Bc, self.Bptab], [Bc])
                    if t > 0:
                        ho, Bho = halo[(t - 1) % 2], Bhalo[(t - 1) % 2][q]
                        self.stt(cbuf[:, 0:2], ho[:, q, 0:2], tap(0, q), cbuf[:, 0:2], ALU.mult, ALU.add,
                                 [Bho, Bc, self.Bptab], [Bc])
                        self.stt(cbuf[:, 0:1], ho[:, q, 1:2], tap(1, q), cbuf[:, 0:1], ALU.mult, ALU.add,
                                 [Bho, Bc, self.Bptab], [Bc])
                self.act(sg[k2], cg[k2], AF.Silu, [Bcg[k2]], [Bsg[k2]])
                self.tt("pool", actb[:, fc, :], sg[k2], cv[k2], ALU.mult, [Bsg[k2], Bcv[k2]], [Bact[fc]])
                step(bg)
            drain(bg)

        def stage_C(t, bg):
            sb = self.statbank()
            for dc in range(8):
                w, Bw = wdnS.next()
                b = self.bank()
                for fc in range(NFC):
                    self.mm(self.psb(b), w[:, fc, :], actb[:, fc, :], fc == 0, fc == NFC - 1, [Bw, Bact[fc]], [self.PB[b]])
                step(bg)
                self.evac_mres(b, dc, sb)
            drain(bg)
            return sb

        def stage_D(t, sb):
            tsl = slice(t * T, (t + 1) * T)

            def after(c):
                self.cp("act", hb[:, c, :], self.h[:, c, tsl], [self.Bh[c][t]], [Bhb[c]])
            g = self.post_norm_gen(t, ("ffn_post", i), sb, 1, after=after)
            next(g)
            return g

        def stage_E(t):
            ptt, Bptt = pt[t % 2], Bpt[t % 2]
            sb = self.statbank()
            for dc in range(8):
                w, Bw = wgS.next()
                bgt = self.bank()
                be = self.bank()
                for kc in range(8):
                    self.mm(self.psb(bgt), w[:, kc, :], hb[:, kc, :], kc == 0, kc == 7, [Bw, Bhb[kc]], [self.PB[bgt]])
                for kc in range(2):
                    self.mm(self.psb(be), wp[:, kc, dc * 128:(dc + 1) * 128], ptt[:, kc, :], kc == 0, kc == 1,
                            [Bwp, Bptt], [self.PB[be]])
                k2 = dc % 2
                self.act(gate[k2], self.psb(bgt), AF.Sigmoid, [self.PB[bgt]], [Bgate[k2]])
                self.tt("dve", self.mres[:, dc, :], gate[k2], self.psb(be), ALU.mult, [Bgate[k2], self.PB[be]],
                        [self.Bmres[dc]])
                self.stat_add(sb, self.mres[:, dc, :], [self.Bmres[dc]], dc)
            return sb

        stage_A(0)
        stage_B(0, [])
        pend = []
        for t in range(NT):
            if t + 1 < NT:
                stage_A(t + 1)
            sb = stage_C(t, pend)
            pend = []
            dgen = [stage_D(t, sb)]
            if t + 1 < NT:
                stage_B(t + 1, dgen)
            else:
                drain(dgen)
            sbe = stage_E(t)
            g = self.post_norm_gen(t, ("ple_g", i), sbe, 2)
            next(g)
            pend = [g]
        drain(pend)

    def mixer_c(self, i):
        A = self.A
        ybuf = A.alloc(BF16, 8, 30 + T)
        Byb = [Buf("yb%d" % c) for c in range(8)]
        z = A.alloc(F32, 8, T)
        Bz = [Buf("z%d" % c) for c in range(8)]
        zb = [A.alloc(BF16, T) for _ in range(2)]
        Bzb = [Buf("zb%d" % k) for k in range(2)]
        actc = A.alloc(BF16, 8, T)
        Bac = [Buf("actc%d" % c) for c in range(8)]
        dg = [A.alloc(BF16, 31, 128) for _ in range(2)]
        Bdg = [Buf("dg%d" % k) for k in range(2)]
        wi = self.stream("cwi", 3, (8, 256), [self.d["c_wi"][c] for _t in range(NT) for c in range(8)])
        wo = self.stream("cwo", 3, (8, 128), [self.d["c_wo"][c] for _t in range(NT) for c in range(8)])
        sgm = [A.alloc(F32, T) for _ in range(2)]
        Bsgm = [Buf("sgm%d" % k) for k in range(2)]
        mean = A.alloc(F32, T)
        msq = A.alloc(F32, T)
        var = A.alloc(F32, T)
        nmr = A.alloc(F32, T)
        Bmean, Bmsq, Bvar, Bnmr = Buf("mean"), Buf("msq"), Buf("var"), Buf("nmr")
        t1 = [A.alloc(F32, T) for _ in range(2)]
        Bt1 = [Buf("ct1_%d" % k) for k in range(2)]
        for c in range(8):
            self.memset("pool", ybuf[:, c, 0:30], 0.0, [Byb[c]])
        for t in range(NT):
            self.pre_norm(t, ("mix_pre", i))
            sb1 = self.statbank()
            sb2 = self.statbank()
            for c in range(8):
                w, Bw = wi.next()
                ba = self.bank()
                bg = self.bank()
                for kc in range(8):
                    self.mm(self.psb(ba), w[:, kc, 0:128], self.hn[:, kc, :], kc == 0, kc == 7, [Bw, self.Bhn[kc]], [self.PB[ba]])
                for kc in range(8):
                    self.mm(self.psb(bg), w[:, kc, 128:256], self.hn[:, kc, :], kc == 0, kc == 7, [Bw, self.Bhn[kc]], [self.PB[bg]])
                k2 = c % 2
                self.act(sgm[k2], self.psb(bg), AF.Sigmoid, [self.PB[bg]], [Bsgm[k2]])
                self.tt("dve", ybuf[:, c, 30:30 + T], sgm[k2], self.psb(ba), ALU.mult, [Bsgm[k2], self.PB[ba]], [Byb[c]])
                for jj in range(31):
                    self.act(dg[k2][:, jj, :], self.ident, AF.Copy, [self.Bconst, self.Bptab], [Bdg[k2]],
                             scale=self.pcol("c_dw", jj * 8 + c))
                bc = self.bank()
                for jj in range(31):
                    self.mm(self.psb(bc), dg[k2][:, jj, :], ybuf[:, c, jj:jj + T], jj == 0, jj == 30, [Bdg[k2], Byb[c]], [self.PB[bc]])
                bias = self.pcol("c_dwb", c)
                self.act(z[:, c, :], self.psb(bc), AF.Identity, [self.PB[bc], self.Bptab], [Bz[c]], bias=bias)
                self.act(zb[k2], self.psb(bc), AF.Identity, [self.PB[bc], self.Bptab], [Bzb[k2]], bias=bias)
                self.mm(self.psb(sb1), self.onesb, zb[k2], c == 0, c == 7, [Bzb[k2], self.Bconst], [self.PB[sb1]])
                sq, Bsq = self.nextsq()
                self.act(sq, self.psb(bc), AF.Square, [self.PB[bc], self.Bptab], [Bsq], bias=bias)
                self.mm(self.psb(sb2), self.onesb, sq, c == 0, c == 7, [Bsq, self.Bconst], [self.PB[sb2]])
                if t < NT - 1:
                    self.cp("pool", ybuf[:, c, 0:30], ybuf[:, c, T:T + 30], [Byb[c]], [Byb[c]])
            self.ts("dve", mean, self.psb(sb1), 1.0 / D, None, ALU.mult, None, [self.PB[sb1]], [Bmean])
            self.act(msq, mean, AF.Square, [Bmean], [Bmsq])
            self.stt(var, self.psb(sb2), 1.0 / D, msq, ALU.mult, ALU.subtract, [self.PB[sb2], Bmsq], [Bvar])
            self.act(self.rstd, var, AF.Sqrt, [Bvar, self.Bconst], [self.Brstd], bias=self.epsc)
            self.recip(self.rstd, self.rstd, [self.Brstd], [self.Brstd])
            self.stt(nmr, mean, -1.0, self.rstd, ALU.mult, ALU.mult, [Bmean, self.Brstd], [Bnmr])
            for c in range(8):
                k2 = c % 2
                self.tt("dve", t1[k2], z[:, c, :], self.rstd, ALU.mult, [Bz[c], self.Brstd], [Bt1[k2]])
                self.tt("pool", t1[k2], t1[k2], nmr, ALU.add, [Bt1[k2], Bnmr], [Bt1[k2]])
                self.act(actc[:, c, :], t1[k2], AF.Silu, [Bt1[k2], self.Bptab], [Bac[c]],
                         scale=self.pcol("c_lng", c), bias=self.pcol("c_lnb", c))
            sb = self.statbank()
            for dc in range(8):
                w, Bw = wo.next()
                b = self.bank()
                for c in range(8):
                    self.mm(self.psb(b), w[:, c, :], actc[:, c, :], c == 0, c == 7, [Bw, Bac[c]], [self.PB[b]])
                self.evac_mres(b, dc, sb)
            self.post_norm_residual(t, ("mix_post", i), sb)

    def mixer_a(self, i, j):
        A = self.A
        u = A.alloc(BF16, 16, T)
        Bu = [Buf("u%d" % c) for c in range(16)]
        vgb = A.alloc(BF16, 4, 2048)
        Bvgb = [Buf("vgb%d" % b) for b in range(4)]
        wv = self.stream("awv", 2, (8, 512), [self.d["a_wv%d" % j][q] for _t in range(NT) for q in range(4)])
        wu = self.stream("awu", 3, (8, 128), [self.d["a_wu%d" % j][q] for _t in range(NT) for q in range(16)])
        wo = self.stream("awo", 2, (16, 128), [self.d["a_wo%d" % j][q] for _t in range(NT) for q in range(8)])
        vgc = [A.alloc(F32, 512) for _ in range(3)]
        Bvgc = [Buf("vgc%d" % k) for k in range(3)]
        bst = A.alloc(F32, 4, 4, 6)
        Bbst = [Buf("bst%d" % b) for b in range(4)]
        mv = A.alloc(F32, 4, 2)
        rs = A.alloc(F32, 4, 1)
        vpe = A.alloc(F32, 4, 1)
        Bmv = [Buf("mv%d" % b) for b in range(4)]
        Brs = [Buf("rs%d" % b) for b in range(4)]
        Bvpe = [Buf("vpe%d" % b) for b in range(4)]
        nmh = A.alloc(F32, 1)
        wsT = A.alloc(BF16, 8, 128)
        Bws = Buf("wsT")
        Cc = A.alloc(F32, 16, 128)
        BCc = Buf("Cc")
        bsb = A.alloc(F32, 8, 128)
        Bbsb = Buf("bsb")
        rw = A.alloc(F32, 8, 128)
        Brw = Buf("rw")
        t1 = [A.alloc(F32, 128) for _ in range(3)]
        Bt1 = [Buf("at1_%d" % k) for k in range(3)]
        self.memset("pool", nmh, -0.5, [self.Bconst])
        self.dma_cast(wsT.rearrange("p a b -> p (a b)"), self.d["a_ws%d" % j], [], [Bws])
        self.memset("pool", wsT[64:128, :, 0:64], 0.0, [Bws])
        self.dma_plain(bsb.rearrange("p a b -> p (a b)"), self.d["a_bs%d" % j], [], [Bbsb])
        b0 = self.bank()
        b1 = self.bank()
        for g in range(8):
            bb = b0 if g < 4 else b1
            lo = (g % 4) * 128
            self.mm(self.psb(bb, lo, lo + 128), self.onesb, wsT[:, g, :], True, True, [Bws, self.Bconst], [self.PB[bb]])
        self.cp("act", rw[:, 0:4, :].rearrange("p a b -> p (a b)"), self.psb(b0), [self.PB[b0]], [Brw])
        self.cp("act", rw[:, 4:8, :].rearrange("p a b -> p (a b)"), self.psb(b1), [self.PB[b1]], [Brw])
        for uc in range(16):
            g = uc // 2
            self.stt(Cc[:, uc, :], rw[:, g, :], self.pcol(("a_lnb", j), uc), bsb[:, g, :], ALU.mult, ALU.add,
                     [Brw, Bbsb, self.Bptab], [BCc])
        for t in range(NT):
            self.pre_norm(t, ("mix_pre", i))
            for uc in range(16):
                w, Bw = wu.next()
                b = self.bank()
                for kc in range(8):
                    self.mm(self.psb(b), w[:, kc, :], self.hn[:, kc, :], kc == 0, kc == 7, [Bw, self.Bhn[kc]], [self.PB[b]])
                self.act(u[:, uc, :], self.psb(b), AF.Gelu_apprx_tanh, [self.PB[b]], [Bu[uc]])
            kk = 0
            for vq in range(4):
                w, Bw = wv.next()
                for blk in range(4):
                    b = self.bank()
                    for kc in range(8):
                        self.mm(self.psb(b), self.hn[:, kc, blk * 128:(blk + 1) * 128], w[:, kc, :], kc == 0, kc == 7,
                                [Bw, self.Bhn[kc]], [self.PB[b]])
                    k3 = kk % 3
                    kk += 1
                    self.act(vgc[k3], self.psb(b), AF.Gelu_apprx_tanh, [self.PB[b]], [Bvgc[k3]])
                    bo = bst[:, blk, vq, :]
                    src = vgc[k3]
                    self.P.op("dve", lambda e, bo=bo, src=src: e.bn_stats(out=bo, in_=src), [Bvgc[k3]], [Bbst[blk]])
                    self.cp("pool", vgb[:, blk, vq * 512:(vq + 1) * 512], vgc[k3], [Bvgc[k3]], [Bvgb[blk]])
            for blk in range(4):
                mo = mv[:, blk, :]
                bi = bst[:, blk, :, :]
                self.P.op("dve", lambda e, mo=mo, bi=bi: e.bn_aggr(out=mo, in_=bi), [Bbst[blk]], [Bmv[blk]])
                self.ts("pool", vpe[:, blk, :], mv[:, blk, 1:2], EPS, None, ALU.add, None, [Bmv[blk]], [Bvpe[blk]])
                self.tt("pool", rs[:, blk, :], vpe[:, blk, :], nmh, ALU.pow, [Bvpe[blk], self.Bconst], [Brs[blk]])
                self.ts("dve", vgb[:, blk, :], vgb[:, blk, :], mv[:, blk, 0:1], rs[:, blk, :], ALU.subtract, ALU.mult,
                        [Bvgb[blk], Bmv[blk], Brs[blk]], [Bvgb[blk]])
            kk = 0
            for blk in range(4):
                bsl = slice(blk * 128, (blk + 1) * 128)
                for uc in range(16):
                    b = self.bank()
                    self.mm(self.psb(b, 0, 128), vgb[:, blk, uc * 128:(uc + 1) * 128], wsT[:, uc // 2, :], True, True,
                            [Bvgb[blk], Bws], [self.PB[b]])
                    k3 = kk % 3
                    kk += 1
                    self.stt(t1[k3], self.psb(b, 0, 128), self.pcol(("a_lng", j), uc), Cc[:, uc, :], ALU.mult, ALU.add,
                             [self.PB[b], BCc, self.Bptab], [Bt1[k3]])
                    self.tt("dve", u[:, uc, bsl], t1[k3], u[:, uc, bsl], ALU.mult, [Bt1[k3], Bu[uc]], [Bu[uc]])
            sb = self.statbank()
            for dc in range(8):
                w, Bw = wo.next()
                b = self.bank()
                for uc in range(16):
                    self.mm(self.psb(b), w[:, uc, :], u[:, uc, :], uc == 0, uc == 15, [Bw, Bu[uc]], [self.PB[b]])
                self.evac_mres(b, dc, sb)
            self.post_norm_residual(t, ("mix_post", i), sb)

    def mixer_b(self, i):
        A = self.A
        kT = A.alloc(BF16, 8, 1024)
        BkT = [[Buf("kT%d_%d" % (c, s_)) for s_ in range(2)] for c in range(8)]
        V = A.alloc(BF16, 8, 1024)
        BV = [Buf("V%d" % b) for b in range(8)]
        qz = A.alloc(BF16, 8, 2, T)
        Bq = [Buf("qz%d" % c) for c in range(8)]
        Bb = A.alloc(BF16, 16, 640)
        BBb = Buf("Bb")
        Pb = [A.alloc(BF16, 640) for _ in range(2)]
        BPb = [Buf("Pb%d" % k) for k in range(2)]
        PTb = [A.alloc(BF16, 640) for _ in range(2)]
        BPTb = [Buf("PTb%d" % k) for k in range(2)]
        dgr = [A.alloc(BF16, 128) for _ in range(2)]
        Bdgr = [Buf("dgr%d" % k) for k in range(2)]
        st3 = [A.alloc(F32, 4) for _ in range(2)]
        Bst = [Buf("st%d" % k) for k in range(2)]
        wqk = self.stream("bwqk", 2, (8, 256), [self.d["b_wqk"][q] for _t in range(NT) for q in range(8)])
        wvs = self.stream("bwv", 2, (8, 256), [self.d["b_wv"][q] for _t in range(NT) for q in range(4)])
        wos = self.stream("bwo", 2, (8, 128), [self.d["b_wo"][q] for _t in range(NT) for q in range(8)])
        oT, BoT = self.hn, self.Bhn
        ps = self.ps
        self.dma_cast(Bb.rearrange("p a b -> p (a b)"), self.d["b_bias"], [], [BBb])
        self.memset("pool", Bb[64:128, :, 0:64], NEG, [BBb])
        self.memset("pool", Bb[0:64, :, 576:640], NEG, [BBb])
        for c in range(8):
            self.memset("pool", qz[:, c, :, :], 0.0, [Bq[c]])
        unit = 0
        for t in range(NT):
            slot = t % 2
            self.pre_norm(t, ("mix_pre", i))
            for c in range(8):
                w, Bw = wqk.next()
                bq = self.bank()
                bk = self.bank()
                for kc in range(8):
                    self.mm(self.psb(bq), w[:, kc, 0:128], self.hn[:, kc, :], kc == 0, kc == 7, [Bw, self.Bhn[kc]], [self.PB[bq]])
                for kc in range(8):
                    self.mm(self.psb(bk), w[:, kc, 128:256], self.hn[:, kc, :], kc == 0, kc == 7, [Bw, self.Bhn[kc]], [self.PB[bk]])
                for hh in range(2):
                    rows = slice(hh * 64, hh * 64 + 64)
                    self.act(qz[rows, c, hh, :], ps[rows, bq * 512:(bq + 1) * 512], AF.Copy, [self.PB[bq]], [Bq[c]], scale=0.125)
                self.cp("dve", kT[:, c, slot * 512:(slot + 1) * 512], self.psb(bk), [self.PB[bk]], [BkT[c][slot]])
            for qt in range(4):
                w, Bw = wvs.next()
                for blk in range(4):
                    b = self.bank()
                    for kc in range(8):
                        self.mm(self.psb(b, 0, 256), self.hn[:, kc, blk * 128:(blk + 1) * 128], w[:, kc, :], kc == 0, kc == 7,
                                [Bw, self.Bhn[kc]], [self.PB[b]])
                    rb = (4 * t + blk) % 8
                    self.cp("act" if (blk % 2) else "dve", V[:, rb, qt * 256:(qt + 1) * 256], self.psb(b, 0, 256), [self.PB[b]], [BV[rb]])
            for c in range(8):
                for jq in range(4):
                    jb = 4 * t + jq
                    i0 = max(0, 4 - jb)
                    for hh in range(2):
                        hd = 2 * c + hh
                        rows = slice(hh * 64, hh * 64 + 64)
                        sp = unit % 2
                        k2 = unit % 2
                        unit += 1
                        sbase = sp * 1024
                        sbufs = [self.PB[2 * sp], self.PB[2 * sp + 1]]
                        for ii in range(i0, 5):
                            kb = jb - 4 + ii
                            rc = (kb % 8) * 128
                            ks = (kb // 4) % 2
                            sap = ps[:, sbase + ii * 128: sbase + (ii + 1) * 128]
                            wb = [sbufs[0] if ii < 4 else sbufs[1]]
                            self.mm(sap, qz[:, c, hh, jq * 128:(jq + 1) * 128], kT[:, c, rc:rc + 128], True, False,
                                    [Bq[c], BkT[c][ks]], wb)
                            self.mm(sap, self.ident, Bb[:, hd, ii * 128:(ii + 1) * 128], False, True, [self.Bconst, BBb], wb)
                        sfull = ps[:, sbase + i0 * 128: sbase + 640]
                        nmax, rsum, rinv = st3[k2][:, 0:1], st3[k2][:, 1:2], st3[k2][:, 2:3]
                        self.P.op("dve", lambda e, nmax=nmax, sfull=sfull: e.tensor_reduce(out=nmax, in_=sfull, axis=AX.X, op=ALU.max, negate=True),
                                  sbufs, [Bst[k2]])
                        self.act(Pb[k2][:, i0 * 128:640], sfull, AF.Exp, sbufs + [Bst[k2]], [BPb[k2], Bst[k2]], bias=nmax, accum=rsum)
                        self.recip(rinv, rsum, [Bst[k2]], [Bst[k2]])
                        self.ts("pool", dgr[k2], self.ident, rinv, None, ALU.mult, None, [self.Bconst, Bst[k2]], [Bdgr[k2]])
                        for ii in range(i0, 5):
                            pb_ = self.PB[4] if ii < 4 else self.PB[5]
                            self.mm(ps[:, 2048 + ii * 128: 2048 + (ii + 1) * 128], Pb[k2][:, ii * 128:(ii + 1) * 128], dgr[k2], True, True,
                                    [BPb[k2], Bdgr[k2]], [pb_])
                        self.cp("act", PTb[k2][:, i0 * 128:640], ps[:, 2048 + i0 * 128: 2048 + 640], [self.PB[4], self.PB[5]], [BPTb[k2]])
                        ob = 6 + hh
                        for ii in range(i0, 5):
                            kb = jb - 4 + ii
                            self.mm(ps[:, ob * 512 + jq * 128: ob * 512 + (jq + 1) * 128], V[:, kb % 8, c * 128:(c + 1) * 128],
                                    PTb[k2][:, ii * 128:(ii + 1) * 128], ii == i0, ii == 4, [BV[kb % 8], BPTb[k2]], [self.PB[ob]])
                for hh in range(2):
                    rows = slice(hh * 64, hh * 64 + 64)
                    ob = 6 + hh
                    self.cp("dve", oT[rows, c, :], ps[rows, ob * 512:(ob + 1) * 512], [self.PB[ob]], [BoT[c]])
            sb = self.statbank()
            for dc in range(8):
                w, Bw = wos.next()
                b = self.bank()
                for c in range(8):
                    self.mm(self.psb(b), w[:, c, :], oT[:, c, :], c == 0, c == 7, [Bw, BoT[c]], [self.PB[b]])
                self.evac_mres(b, dc, sb)
            self.post_norm_residual(t, ("mix_post", i), sb)

    def store_output(self):
        for c in range(8):
            self.dma_plain(self.yT[c], self.h[:, c, :], self.Bh[c], [Buf("y%d" % c)], is_output=True)


def _cols(v, n):
    return np.ascontiguousarray(np.asarray(v, np.float32).reshape(n, 128).T)


def _kc_tile(w, ncols_per_block):
    K, N = w.shape
    nb = N // ncols_per_block
    x = w.reshape(K // 128, 128, nb, ncols_per_block)
    x = x.transpose(2, 1, 0, 3)
    return np.ascontiguousarray(x).reshape(nb, 128, (K // 128) * ncols_per_block)


def host_shared(inp, layers):
    off, R = ptab_layout()
    ptab = np.zeros((128, R), np.float32)

    def put(key, arr):
        ptab[:, off[key]:off[key] + arr.shape[1]] = arr

    for i in range(DEPTH):
        put(("mix_pre", i), _cols(inp["mix_pre_g"][i], 8))
        put(("mix_post", i), _cols(inp["mix_post_g"][i], 8))
        put(("ffn_pre", i), _cols(inp["ffn_pre_g"][i], 8))
        put(("ffn_post", i), _cols(inp["ffn_post_g"][i], 8))
        put(("ple_g", i), _cols(inp["ple_norm_g"][i], 8))
        cv = np.concatenate([_cols(inp["ffn_conv"][i][jj], 44) for jj in range(3)], axis=1)
        put(("conv", i), cv)
    for j in range(2):
        put(("a_lng", j), _cols(inp["a_ln_g"][j], 16))
        put(("a_lnb", j), _cols(inp["a_ln_b"][j], 16))
    dw = np.asarray(inp["c_dw"][0], np.float32)
    dwc = dw.reshape(31, 8, 128).transpose(2, 0, 1).reshape(128, 248)
    put("c_dw", np.ascontiguousarray(dwc))
    put("c_dwb", _cols(inp["c_dw_b"][0], 8))
    put("c_lng", _cols(inp["c_ln_g"][0], 8))
    put("c_lnb", _cols(inp["c_ln_b"][0], 8))
    sh = {"ptab": ptab, "ident": np.eye(128, dtype=np.float32)}
    for i in layers:
        wu = np.asarray(inp["ffn_w_up"][i], np.float32)
        g = wu[:, :FF].reshape(D, NFC, 128)
        v = wu[:, FF:].reshape(D, NFC, 128)
        gv = np.concatenate([g, v], axis=2).reshape(D, NFC * 256)
        sh["wup%d" % i] = _kc_tile(gv, 256)
        wd = np.asarray(inp["ffn_w_down"][i], np.float32)
        x = wd.reshape(NFC, 128, 8, 128).transpose(2, 1, 0, 3)
        sh["wdn%d" % i] = np.ascontiguousarray(x).reshape(8, 128, NFC * 128)
        sh["wg%d" % i] = _kc_tile(np.asarray(inp["ple_w_gate"][i], np.float32), 128)
        wp = np.asarray(inp["ple_w_proj"][i], np.float32)
        sh["wp%d" % i] = np.ascontiguousarray(wp.reshape(2, 128, D).transpose(1, 0, 2)).reshape(128, 2 * D)
        kind, j = i % 3, i // 3
        if kind == 0:
            win = np.asarray(inp["a_w_in"][j], np.float32)
            sh["a_wu%d" % j] = _kc_tile(win[:, :2048], 128)
            sh["a_wv%d" % j] = _kc_tile(win[:, 2048:], 512)
            wo = np.asarray(inp["a_w_out"][j], np.float32)
            x = wo.reshape(16, 128, 8, 128).transpose(2, 1, 0, 3)
            sh["a_wo%d" % j] = np.ascontiguousarray(x).reshape(8, 128, 16 * 128)
            ws = np.asarray(inp["a_w_s"][j], np.float32)
            sh["a_ws%d" % j] = np.ascontiguousarray(ws.transpose(2, 0, 1)).reshape(128, 8 * 128)
            bs = np.asarray(inp["a_b_s"][j], np.float32).reshape(1, 8 * 128)
            sh["a_bs%d" % j] = np.ascontiguousarray(np.broadcast_to(bs, (128, 8 * 128)))
        elif kind == 1:
            wq = np.asarray(inp["b_w_qkv"][0], np.float32)
            q = wq[:, :D].reshape(D, 8, 128)
            k = wq[:, D:2 * D].reshape(D, 8, 128)
            qk = np.concatenate([q, k], axis=2).reshape(D, 8 * 256)
            sh["b_wqk"] = _kc_tile(qk, 256)
            sh["b_wv"] = _kc_tile(np.ascontiguousarray(wq[:, 2 * D:]), 256)
            wo = np.asarray(inp["b_w_out"][0], np.float32)
            x = wo.reshape(8, 128, 8, 128).transpose(2, 1, 0, 3)
            sh["b_wo"] = np.ascontiguousarray(x).reshape(8, 128, 8 * 128)
            rb = np.asarray(inp["b_rel_bias"][0], np.float32)
            qq = np.arange(128)[:, None]
            kk = np.arange(640)[None, :]
            idx = np.clip(qq + 512 - kk, -128, 128) + 128
            bfull = rb[:, idx]
            sh["b_bias"] = np.ascontiguousarray(bfull.transpose(1, 0, 2)).reshape(128, 16 * 640)
        else:
            wi = np.asarray(inp["c_w_in"][0], np.float32)
            a = wi[:, :D].reshape(D, 8, 128)
            g = wi[:, D:].reshape(D, 8, 128)
            ag = np.concatenate([a, g], axis=2).reshape(D, 8 * 256)
            sh["c_wi"] = _kc_tile(ag, 256)
            wo = np.asarray(inp["c_w_out"][0], np.float32)
            x = wo.reshape(8, 128, 8, 128).transpose(2, 1, 0, 3)
            sh["c_wo"] = np.ascontiguousarray(x).reshape(8, 128, 8 * 128)
    return sh


def run_layers(hT_in, p, shared, layers, trace=False, dbg=()):
    nc = Builder(layers, dbg).build()
    in_maps = []
    for b in range(8):
        m = dict(shared)
        m["xT"] = hT_in[b]
        for i in layers:
            m["pT%d" % i] = np.ascontiguousarray(p[i, b].T).reshape(2, 128, S)
        in_maps.append(m)
    res = run_bass_kernel_spmd(nc, in_maps, core_ids=list(range(8)), trace=trace)
    out = np.stack([res.results[b]["yT"] for b in range(8)])
    return out, res


def kernel(**inputs):
    inp = {k: np.asarray(v) for k, v in inputs.items()}
    layers = list(range(DEPTH))
    x = inp["x"].astype(np.float32, copy=False)
    hT = np.ascontiguousarray(x.transpose(0, 2, 1)).reshape(8, 8, 128, S)
    shared = host_shared(inp, layers)
    out, _ = run_layers(hT, inp["p"].astype(np.float32, copy=False), shared, layers)
    y = out.reshape(8, D, S).transpose(0, 2, 1)
    return np.ascontiguousarray(y).astype(np.float32, copy=False)
```

```python
import numpy as np
from contextlib import ExitStack
import concourse.bass as bass
import concourse.mybir as mybir
from concourse.bass_utils import run_bass_kernel_spmd

F32 = mybir.dt.float32
BF16 = mybir.dt.bfloat16
AF = mybir.ActivationFunctionType
ALU = mybir.AluOpType
AX = mybir.AxisListType

D = 1024
S = 2048
T = 512
NT = S // T
FF = 2816
NFC = FF // 128
DEPTH = 4
EPS = 1e-6
NEG = -1e30


class Buf:
    __slots__ = ("name", "last_w", "readers", "dma_readers", "dma_sem", "dma_cnt", "const", "excl")

    fence = {}

    def __init__(self, name, const=False, excl=False):
        self.name = name
        self.excl = excl
        self.last_w = None
        self.readers = dict(Buf.fence)
        self.dma_readers = []
        self.dma_sem = None
        self.dma_cnt = 0
        self.const = const


class Op:
    __slots__ = ("eng", "fn", "deps", "sig", "sigval", "is_dma", "dma_sem", "dma_val", "idx")

    def __init__(self, eng, fn, is_dma):
        self.eng = eng
        self.fn = fn
        self.deps = []
        self.sig = False
        self.sigval = 0
        self.is_dma = is_dma
        self.dma_sem = None
        self.dma_val = 0


class Prog:
    ENGS = ("pe", "act", "dve", "pool", "sp")

    def __init__(self, nc):
        self.nc = nc
        self.ops = {e: [] for e in self.ENGS}
        self.n = 0
        self.dma_bufs = []
        self.out_dma_ops = []

    def _add(self, op, reads, writes):
        deps = []
        for b in reads:
            w = b.last_w
            if w is not None:
                deps.append((w, True))
            if b.excl:
                for e_, r in b.readers.items():
                    if e_ != op.eng:
                        deps.append((r, False))
        for b in writes:
            w = b.last_w
            if w is not None and not (op.is_dma and w.is_dma):
                deps.append((w, False))
            for r in b.readers.values():
                deps.append((r, False))
            for r in b.dma_readers:
                deps.append((r, False))
        seen = set()
        for d, raw in deps:
            if d is op:
                continue
            if (not d.is_dma) and (not op.is_dma) and d.eng == op.eng:
                if op.eng == "pe":
                    continue
            k = id(d)
            if k in seen:
                continue
            seen.add(k)
            op.deps.append(d)
            if not d.is_dma:
                d.sig = True
        for b in writes:
            b.last_w = op
            b.readers = {}
            b.dma_readers = []
        for b in reads:
            if b.const or b in writes:
                continue
            if op.is_dma:
                b.dma_readers.append(op)
            else:
                b.readers[op.eng] = op
        op.idx = self.n
        self.n += 1
        self.ops[op.eng].append(op)
        return op

    def op(self, eng, fn, reads=(), writes=()):
        return self._add(Op(eng, fn, False), list(reads), list(writes))

    def dma(self, eng, fn, reads=(), writes=(), is_output=False):
        o = Op(eng, fn, True)
        dst = writes[0]
        if dst.dma_sem is None:
            dst.dma_sem = "pending"
            self.dma_bufs.append(dst)
        dst.dma_cnt += 16
        o.dma_sem = dst
        o.dma_val = dst.dma_cnt
        self._add(o, list(reads), list(writes))
        if is_output:
            self.out_dma_ops.append(o)
        return o

    def emit(self, stack):
        nc = self.nc
        sems = {}
        for e in ("pe", "act", "dve", "pool"):
            sems[e] = stack.enter_context(nc.semaphore("s_" + e))
        for i, b in enumerate(self.dma_bufs):
            b.dma_sem = stack.enter_context(nc.semaphore("d%d" % i))
        for e in ("pe", "act", "dve", "pool"):
            c = 0
            for o in self.ops[e]:
                if o.is_dma:
                    continue
                if o.sig:
                    c += 1
                    o.sigval = c
        out_ops = self.out_dma_ops

        def run(engh, ename):
            waited = {}

            def wait(sem, val):
                k = id(sem)
                if waited.get(k, 0) >= val:
                    return
                waited[k] = val
                engh.wait_ge(sem, val)

            for o in self.ops[ename]:
                for d in o.deps:
                    if d.is_dma:
                        wait(d.dma_sem.dma_sem, d.dma_val)
                    else:
                        wait(sems[d.eng], d.sigval)
                ins = o.fn(engh)
                if o.is_dma:
                    ins.then_inc(o.dma_sem.dma_sem, 16)
                elif o.sig:
                    ins.then_inc(sems[ename], 1)
            if ename == "sp":
                for o in out_ops:
                    wait(o.dma_sem.dma_sem, o.dma_val)

        block = stack.enter_context(nc.Block())

        @block.tensor
        def _(e):
            run(e, "pe")

        @block.scalar
        def _(e):
            run(e, "act")

        @block.vector
        def _(e):
            run(e, "dve")

        @block.gpsimd
        def _(e):
            run(e, "pool")

        @block.sync
        def _(e):
            run(e, "sp")


def ptab_layout():
    off = {}
    c = 0
    for i in range(DEPTH):
        for nm in ("mix_pre", "mix_post", "ffn_pre", "ffn_post", "ple_g"):
            off[(nm, i)] = c
            c += 8
        off[("conv", i)] = c
        c += 132
    for j in range(2):
        off[("a_lng", j)] = c
        c += 16
        off[("a_lnb", j)] = c
        c += 16
    off["c_dw"] = c
    c += 248
    off["c_dwb"] = c
    c += 8
    off["c_lng"] = c
    c += 8
    off["c_lnb"] = c
    c += 8
    return off, c


class StopBuild(Exception):
    pass


class Stream:
    def __init__(self, B, name, n, free, srcs):
        self.B = B
        self.n = n
        self.srcs = list(srcs)
        self.aps = [B.A.alloc(BF16, *free) for _ in range(n)]
        self.bufs = [Buf("%s%d" % (name, i)) for i in range(n)]
        self.issued = 0
        self.taken = 0
        for _ in range(n - 1):
            self._issue()

    def _issue(self):
        k = self.issued
        if k >= len(self.srcs):
            return
        self.issued += 1
        ap, bf = self.aps[k % self.n], self.bufs[k % self.n]
        flat = ap.rearrange("p a b -> p (a b)") if len(ap.shape) == 3 else ap
        self.B.dma_cast(flat, self.srcs[k], [], [bf])

    def next(self):
        k = self.taken
        self.taken += 1
        self._issue()
        return self.aps[k % self.n], self.bufs[k % self.n]


class Arena:
    def __init__(self, ap_all, base, limit):
        self.all = ap_all
        self.off = base
        self.limit = limit

    def mark(self):
        return self.off

    def reset(self, m):
        self.off = m

    def alloc(self, dtype, *free):
        isz = 4 if dtype == F32 else 2
        n = 1
        for f in free:
            n *= f
        nb = n * isz
        off = (self.off + 63) // 64 * 64
        assert off + nb <= self.limit, ("SBUF arena overflow", off + nb, self.limit)
        self.off = off + nb
        v = self.all[:, off // 2:(off + nb) // 2]
        if dtype == F32:
            v = v.bitcast(F32)
        if len(free) == 2:
            v = v.rearrange("p (a b) -> p a b", a=free[0])
        elif len(free) == 3:
            v = v.rearrange("p (a b c) -> p a b c", a=free[0], b=free[1])
        return v


class Builder:
    def __init__(self, layers, dbg=()):
        self.layers = list(layers)
        self.dbg = set(dbg)
        self.nc = bass.Bass("TRN2", target_bir_lowering=False)
        Buf.fence = {}
        self.P = Prog(self.nc)
        self.poff, self.pcols = ptab_layout()
        self._bank = 0
        self._stat = 0

    def mm(self, out, lhsT, rhs, start, stop, r, w, **kw):
        self.P.op("pe", lambda e: e.matmul(out, lhsT=lhsT, rhs=rhs, start=start, stop=stop, **kw), r, w)

    def act(self, out, in_, func, r, w, bias=None, scale=None, accum=None):
        kw = {}
        if bias is not None:
            kw["bias"] = bias
        if scale is not None:
            kw["scale"] = scale
        if accum is not None:
            kw["accum_out"] = accum
        self.P.op("act", lambda e: e.activation(out=out, in_=in_, func=func, **kw), r, w)

    def ts(self, eng, out, in0, s1, s2, op0, op1, r, w):
        if op1 is None and eng == "pool":
            s2, op1 = 1.0, ALU.mult
        if op1 is None:
            self.P.op(eng, lambda e: e.tensor_scalar(out=out, in0=in0, scalar1=s1, scalar2=None, op0=op0), r, w)
        else:
            self.P.op(eng, lambda e: e.tensor_scalar(out=out, in0=in0, scalar1=s1, scalar2=s2, op0=op0, op1=op1), r, w)

    def stt(self, out, in0, scalar, in1, op0, op1, r, w):
        self.P.op("dve", lambda e: e.scalar_tensor_tensor(out=out, in0=in0, scalar=scalar, in1=in1, op0=op0, op1=op1), r, w)

    def tt(self, eng, out, in0, in1, op, r, w):
        self.P.op(eng, lambda e: e.tensor_tensor(out=out, in0=in0, in1=in1, op=op), r, w)

    def cp(self, eng, out, in_, r, w):
        if eng == "act":
            self.P.op("act", lambda e: e.copy(out=out, in_=in_), r, w)
        else:
            self.P.op(eng, lambda e: e.tensor_copy(out=out, in_=in_), r, w)

    def recip(self, out, in_, r, w):
        self.P.op("dve", lambda e: e.reciprocal(out=out, in_=in_), r, w)

    def memset(self, eng, ap, val, w):
        self.P.op(eng, lambda e: e.memset(ap, val), [], w)

    def dma_cast(self, out, in_, r, w):
        self.P.dma("pool", lambda e: e.dma_start(out=out, in_=in_), r, w)

    def dma_plain(self, out, in_, r, w, is_output=False):
        self.P.dma("sp", lambda e: e.dma_start(out=out, in_=in_), r, w, is_output=is_output)

    def bank(self):
        b = self._bank
        self._bank = (b + 1) % 6
        return b

    def statbank(self):
        b = 6 + self._stat
        self._stat ^= 1
        return b

    def psb(self, b, lo=0, hi=512):
        return self.ps[:, b * 512 + lo:b * 512 + hi]

    def build(self):
        nc = self.nc
        st = ExitStack()
        with st:
            self.declare_dram()
            self.sb_all = st.enter_context(nc.sbuf_tensor("sb_all", [128, 106300], BF16))
            self.ps = st.enter_context(nc.psum_tensor("ps_all", [128, 4096], F32))
            self.PB = [Buf("psb%d" % i, excl=True) for i in range(8)]
            self.A = Arena(self.sb_all, 0, 106300 * 2)
            self.setup_persistent()
            for i in self.layers:
                self.layer(i)
            self.store_output()
            self.P.emit(st)
        return nc

    def declare_dram(self):
        nc = self.nc
        dt = lambda n, s: nc.dram_tensor(n, s, F32, kind="ExternalInput").ap()
        self.d = {}
        self.d["xT"] = dt("xT", [8, 128, S])
        self.d["ptab"] = dt("ptab", [128, self.pcols])
        self.d["ident"] = dt("ident", [128, 128])
        for i in self.layers:
            self.d["pT%d" % i] = dt("pT%d" % i, [2, 128, S])
            self.d["wup%d" % i] = dt("wup%d" % i, [NFC, 128, 8 * 256])
            self.d["wdn%d" % i] = dt("wdn%d" % i, [8, 128, NFC * 128])
            self.d["wg%d" % i] = dt("wg%d" % i, [8, 128, 8 * 128])
            self.d["wp%d" % i] = dt("wp%d" % i, [128, 2 * D])
            kind, j = i % 3, i // 3
            if kind == 0:
                self.d["a_wu%d" % j] = dt("a_wu%d" % j, [16, 128, 8 * 128])
                self.d["a_wv%d" % j] = dt("a_wv%d" % j, [4, 128, 8 * 512])
                self.d["a_wo%d" % j] = dt("a_wo%d" % j, [8, 128, 16 * 128])
                self.d["a_ws%d" % j] = dt("a_ws%d" % j, [128, 8 * 128])
                self.d["a_bs%d" % j] = dt("a_bs%d" % j, [128, 8 * 128])
            elif kind == 1:
                self.d["b_wqk"] = dt("b_wqk", [8, 128, 8 * 256])
                self.d["b_wv"] = dt("b_wv", [4, 128, 8 * 256])
                self.d["b_wo"] = dt("b_wo", [8, 128, 8 * 128])
                self.d["b_bias"] = dt("b_bias", [128, 16 * 640])
            else:
                self.d["c_wi"] = dt("c_wi", [8, 128, 8 * 256])
                self.d["c_wo"] = dt("c_wo", [8, 128, 8 * 128])
        self.yT = nc.dram_tensor("yT", [8, 128, S], F32, kind="ExternalOutput").ap()

    def setup_persistent(self):
        A = self.A
        self.h = A.alloc(F32, 8, S)
        self.Bh = [[Buf("h%d_%d" % (c, t)) for t in range(NT)] for c in range(8)]
        self.ptab = A.alloc(F32, self.pcols)
        self.Bptab = Buf("ptab", const=True)
        self.ident = A.alloc(BF16, 128)
        self.onesb = A.alloc(BF16, 128)
        self.epsc = A.alloc(F32, 1)
        self.dummy = A.alloc(F32, 8)
        self.Bconst = Buf("const", const=True)
        self.sq = [A.alloc(BF16, T) for _ in range(3)]
        self.Bsq = [Buf("sq%d" % i) for i in range(3)]
        self._sq = 0
        self.rstds = [A.alloc(F32, T) for _ in range(3)]
        self.Brstds = [Buf("rstd%d" % k) for k in range(3)]
        self.rstd, self.Brstd = self.rstds[0], self.Brstds[0]
        self.mres = A.alloc(F32, 8, T)
        self.Bmres = [Buf("mres%d" % c) for c in range(8)]
        self.hns = [A.alloc(BF16, 8, T) for _ in range(2)]
        self.Bhns = [[Buf("hn%d_%d" % (k, c)) for c in range(8)] for k in range(2)]
        self.hn, self.Bhn = self.hns[0], self.Bhns[0]
        self.rtmp = [A.alloc(F32, T) for _ in range(2)]
        self.Brtmp = [Buf("rtmp%d" % i) for i in range(2)]
        self._rt = 0
        self.phase_mark = A.mark()
        self.dma_plain(self.ptab, self.d["ptab"], [], [self.Bptab])
        for c in range(8):
            self.dma_plain(self.h[:, c, :], self.d["xT"][c], [], self.Bh[c])
        self.memset("pool", self.onesb, 1.0, [self.Bconst])
        self.memset("pool", self.epsc, EPS, [self.Bconst])
        self.dma_cast(self.ident, self.d["ident"], [], [self.Bconst])

    def pcol(self, key, c, n=1):
        o = self.poff[key] + c
        return self.ptab[:, o:o + n]

    def nextsq(self):
        i = self._sq
        self._sq = (i + 1) % 3
        return self.sq[i], self.Bsq[i]

    def nextrt(self):
        i = self._rt
        self._rt ^= 1
        return self.rtmp[i], self.Brtmp[i]

    def defer_mm(self, *args, **kw):
        self.flush_mm()
        self._pend_mm = (args, kw)

    def flush_mm(self):
        p = getattr(self, "_pend_mm", None)
        if p is not None:
            self._pend_mm = None
            self.mm(*p[0], **p[1])

    def stat_add(self, sb, src, src_bufs, c, n=8):
        sq, Bsq = self.nextsq()
        self.act(sq, src, AF.Square, src_bufs, [Bsq])
        self.defer_mm(self.psb(sb), self.onesb, sq, c == 0, c == n - 1, [Bsq, self.Bconst], [self.PB[sb]])

    def finish_rstd(self, sb, dim=D, role=0):
        self.flush_mm()
        rstd, Brstd = self.rstds[role], self.Brstds[role]
        self.act(rstd, self.psb(sb), AF.Sqrt, [self.PB[sb], self.Bconst], [Brstd], bias=self.epsc, scale=1.0 / dim)
        self.recip(rstd, rstd, [Brstd], [Brstd])
        return rstd, Brstd

    def pre_norm_gen(self, t, gkey, k=0):
        hn, Bhn = self.hns[k], self.Bhns[k]
        sb = self.statbank()
        tsl = slice(t * T, (t + 1) * T)
        for c in range(8):
            self.stat_add(sb, self.h[:, c, tsl], [self.Bh[c][t]], c)
            yield
        rstd, Brstd = self.finish_rstd(sb, role=0)
        yield
        for c in range(8):
            self.stt(hn[:, c, :], self.h[:, c, tsl], self.pcol(gkey, c), rstd, ALU.mult, ALU.mult,
                     [self.Bh[c][t], Brstd, self.Bptab], [Bhn[c]])
            if c % 2:
                yield

    def pre_norm(self, t, gkey, k=0):
        for _ in self.pre_norm_gen(t, gkey, k):
            pass

    def post_norm_gen(self, t, gkey, sb, role, after=None):
        rstd, Brstd = self.finish_rstd(sb, role=role)
        tsl = slice(t * T, (t + 1) * T)
        yield
        for c in range(8):
            rt, Brt = self.nextrt()
            self.stt(rt, self.mres[:, c, :], self.pcol(gkey, c), rstd, ALU.mult, ALU.mult,
                     [self.Bmres[c], Brstd, self.Bptab], [Brt])
            self.tt("pool", self.h[:, c, tsl], self.h[:, c, tsl], rt, ALU.add, [self.Bh[c][t], Brt], [self.Bh[c][t]])
            if after is not None:
                after(c)
            yield

    def post_norm_residual(self, t, gkey, sb, role=1):
        for _ in self.post_norm_gen(t, gkey, sb, role):
            pass

    def evac_mres(self, b, dc, sb):
        self.cp("dve", self.mres[:, dc, :], self.psb(b), [self.PB[b]], [self.Bmres[dc]])
        self.stat_add(sb, self.mres[:, dc, :], [self.Bmres[dc]], dc)

    def make_slots(self, name, n, *free):
        aps = [self.A.alloc(BF16, *free) for _ in range(n)]
        bufs = [Buf("%s%d" % (name, i)) for i in range(n)]
        return {"aps": aps, "bufs": bufs, "i": 0, "n": n}

    def load_slot(self, slots, src):
        i = slots["i"]
        slots["i"] = (i + 1) % slots["n"]
        ap, bf = slots["aps"][i], slots["bufs"][i]
        flat = ap
        if len(ap.shape) == 3:
            flat = ap.rearrange("p a b -> p (a b)")
        self.dma_cast(flat, src, [], [bf])
        return ap, bf

    def stream(self, name, n, free, srcs):
        return Stream(self, name, n, free, srcs)

    def stop(self, tag):
        if tag in self.dbg:
            raise StopBuild()

    def layer(self, i):
        try:
            self._layer(i)
        except StopBuild:
            pass

    def new_phase(self):
        self.flush_mm()
        self.A.reset(self.phase_mark)
        f = {}
        for e in ("pe", "act", "dve", "pool"):
            for o in reversed(self.P.ops[e]):
                if not o.is_dma:
                    f[e] = o
                    break
        Buf.fence = f

    def _layer(self, i):
        kind, j = i % 3, i // 3
        A = self.A
        self.new_phase()
        if "prenorm" in self.dbg:
            self.pre_norm(0, ("mix_pre", i))
            return
        if "nomix" in self.dbg:
            pass
        elif kind == 0:
            self.mixer_a(i, j)
        elif kind == 1:
            self.mixer_b(i)
        else:
            self.mixer_c(i)
        self.new_phase()
        if "noffn" not in self.dbg:
            self.ffn_phase(i)

    def ffn_phase(self, i):
        A = self.A
        actb = A.alloc(BF16, NFC, T)
        Bact = [Buf("act%d" % f) for f in range(NFC)]
        wupS = self.stream("wup", 4, (8, 256), [self.d["wup%d" % i][fc] for _t in range(NT) for fc in range(NFC)])
        wdnS = self.stream("wdn", 2, (NFC, 128), [self.d["wdn%d" % i][dc] for _t in range(NT) for dc in range(8)])
        wgS = self.stream("wg", 2, (8, 128), [self.d["wg%d" % i][dc] for _t in range(NT) for dc in range(8)])
        cg = [A.alloc(F32, T) for _ in range(2)]
        cv = [A.alloc(F32, T) for _ in range(2)]
        sg = [A.alloc(F32, T) for _ in range(2)]
        Bcg = [Buf("cg%d" % k) for k in range(2)]
        Bcv = [Buf("cv%d" % k) for k in range(2)]
        Bsg = [Buf("sg%d" % k) for k in range(2)]
        halo = [A.alloc(F32, 2 * NFC, 2) for _ in range(2)]
        Bhalo = [[Buf("halo%d_%d" % (k, q)) for q in range(2 * NFC)] for k in range(2)]
        bnd = [A.alloc(F32, 3, 2 * NFC) for _ in range(2)]
        Bbnd = [Buf("bnd%d" % k) for k in range(2)]
        hb = A.alloc(BF16, 8, T)
        Bhb = [Buf("hb%d" % c) for c in range(8)]
        pt = [A.alloc(BF16, 2, T) for _ in range(2)]
        Bpt = [Buf("pt%d" % k) for k in range(2)]
        wp = A.alloc(BF16, 2, D)
        Bwp = Buf("wp")
        gate = [A.alloc(F32, T) for _ in range(2)]
        Bgate = [Buf("gate%d" % k) for k in range(2)]
        self.dma_cast(wp.rearrange("p a b -> p (a b)"), self.d["wp%d" % i], [], [Bwp])
        cbase = self.poff[("conv", i)]

        def tap(jj, q):
            o = cbase + jj * 44 + q
            return self.ptab[:, o:o + 1]

        def step(bg):
            for g in list(bg):
                try:
                    next(g)
                except StopIteration:
                    bg.remove(g)

        def drain(bg):
            while bg:
                step(bg)

        def load_pt(t):
            ptt, Bptt = pt[t % 2], Bpt[t % 2]
            tsl = slice(t * T, (t + 1) * T)
            for kc in range(2):
                self.dma_cast(ptt[:, kc, :], self.d["pT%d" % i][kc][:, tsl], [], [Bptt])

        def stage_A(t):
            return self.pre_norm_gen(t, ("ffn_pre", i), k=t % 2)

        def stage_B(t, bg):
            hn, Bhn = self.hns[t % 2], self.Bhns[t % 2]
            if t > 0:
                ho, Bho = halo[(t - 1) % 2], Bhalo[(t - 1) % 2]
                bd, Bbd = bnd[t % 2], Bbnd[t % 2]
                W0 = self.ptab[:, cbase:cbase + 44]
                W1 = self.ptab[:, cbase + 44:cbase + 88]
                self.tt("dve", bd[:, 2, :], ho[:, :, 1], W1, ALU.mult, Bho + [self.Bptab], [Bbd])
                self.tt("dve", bd[:, 0, :], ho[:, :, 0], W0, ALU.mult, Bho + [self.Bptab], [Bbd])
                self.tt("dve", bd[:, 0, :], bd[:, 0, :], bd[:, 2, :], ALU.add, [Bbd], [Bbd])
                self.tt("dve", bd[:, 1, :], ho[:, :, 1], W0, ALU.mult, Bho + [self.Bptab], [Bbd])
            for fc in range(NFC):
                w, Bw = wupS.next()
                k2 = fc % 2
                bg_ = self.bank()
                bv_ = self.bank()
                for kc in range(8):
                    self.mm(self.psb(bg_), w[:, kc, 0:128], hn[:, kc, :], kc == 0, kc == 7, [Bw, Bhn[kc]], [self.PB[bg_]])
                for kc in range(8):
                    self.mm(self.psb(bv_), w[:, kc, 128:256], hn[:, kc, :], kc == 0, kc == 7, [Bw, Bhn[kc]], [self.PB[bv_]])
                for (b, q, cbuf, Bc) in ((bg_, fc, cg[k2], Bcg[k2]), (bv_, NFC + fc, cv[k2], Bcv[k2])):
                    pb = self.PB[b]
                    if t == 0:
                        self.act(cbuf, self.psb(b), AF.Copy, [pb, self.Bptab], [Bc], scale=tap(2, q))
                    else:
                        bd, Bbd = bnd[t % 2], Bbnd[t % 2]
                        self.act(cbuf[:, 2:T], self.psb(b, 2, T), AF.Copy, [pb, self.Bptab], [Bc], scale=tap(2, q))
                        self.act(cbuf[:, 0:1], self.psb(b, 0, 1), AF.Identity, [pb, self.Bptab, Bbd], [Bc],
                                 scale=tap(2, q), bias=bd[:, 0, q:q + 1])
                        self.act(cbuf[:, 1:2], self.psb(b, 1, 2), AF.Identity, [pb, self.Bptab, Bbd], [Bc],
                                 scale=tap(2, q), bias=bd[:, 1, q:q + 1])
                    if t < NT - 1:
                        self.cp("act", halo[t % 2][:, q, 0:2], self.psb(b, T - 2, T), [pb], [Bhalo[t % 2][q]])
                    self.stt(cbuf[:, 1:T], self.psb(b, 0, T - 1), tap(1, q), cbuf[:, 1:T], ALU.mult, ALU.add,
                             [pb, Bc, self.Bptab], [Bc])
                    self.stt(cbuf[:, 2:T], self.psb(b, 0, T - 2), tap(0, q), cbuf[:, 2:T], ALU.mult, ALU.add,
                             [pb, Bc, self.Bptab], [Bc])
                self.act(sg[k2], cg[k2], AF.Silu, [Bcg[k2]], [Bsg[k2]])
                self.tt("pool", actb[:, fc, :], sg[k2], cv[k2], ALU.mult, [Bsg[k2], Bcv[k2]], [Bact[fc]])
                step(bg)
            drain(bg)

        def stage_C(t, bg):
            sb = self.statbank()
            for dc in range(8):
                w, Bw = wdnS.next()
                b = self.bank()
                for fc in range(NFC):
                    self.mm(self.psb(b), w[:, fc, :], actb[:, fc, :], fc == 0, fc == NFC - 1, [Bw, Bact[fc]], [self.PB[b]])
                step(bg)
                self.evac_mres(b, dc, sb)
            drain(bg)
            return sb

        def stage_D(t, sb):
            tsl = slice(t * T, (t + 1) * T)

            def after(c):
                self.cp("act", hb[:, c, :], self.h[:, c, tsl], [self.Bh[c][t]], [Bhb[c]])
            g = self.post_norm_gen(t, ("ffn_post", i), sb, 1, after=after)
            next(g)
            return g

        def stage_E(t):
            ptt, Bptt = pt[t % 2], Bpt[t % 2]
            if t + 1 < NT:
                load_pt(t + 1)
            sb = self.statbank()
            for dc in range(8):
                w, Bw = wgS.next()
                bgt = self.bank()
                be = self.bank()
                for kc in range(8):
                    self.mm(self.psb(bgt), w[:, kc, :], hb[:, kc, :], kc == 0, kc == 7, [Bw, Bhb[kc]], [self.PB[bgt]])
                for kc in range(2):
                    self.mm(self.psb(be), wp[:, kc, dc * 128:(dc + 1) * 128], ptt[:, kc, :], kc == 0, kc == 1,
                            [Bwp, Bptt], [self.PB[be]])
                k2 = dc % 2
                self.act(gate[k2], self.psb(bgt), AF.Sigmoid, [self.PB[bgt]], [Bgate[k2]])
                self.tt("dve", self.mres[:, dc, :], gate[k2], self.psb(be), ALU.mult, [Bgate[k2], self.PB[be]],
                        [self.Bmres[dc]])
                self.stat_add(sb, self.mres[:, dc, :], [self.Bmres[dc]], dc)
            return sb

        load_pt(0)
        drain([stage_A(0)])
        stage_B(0, [stage_A(1)])
        pend = []
        for t in range(NT):
            sb = stage_C(t, pend)
            pend = []
            bgl = [stage_D(t, sb)]
            if t + 2 < NT:
                bgl.append(stage_A(t + 2))
            if t + 1 < NT:
                stage_B(t + 1, bgl)
            else:
                drain(bgl)
            sbe = stage_E(t)
            g = self.post_norm_gen(t, ("ple_g", i), sbe, 2)
            next(g)
            pend = [g]
        drain(pend)
        self.flush_mm()

    def mixer_c(self, i):
        A = self.A
        ybuf = A.alloc(BF16, 8, 30 + T)
        Byb = [Buf("yb%d" % c) for c in range(8)]
        z = A.alloc(F32, 8, T)
        Bz = [Buf("z%d" % c) for c in range(8)]
        zb = [A.alloc(BF16, T) for _ in range(2)]
        Bzb = [Buf("zb%d" % k) for k in range(2)]
        actc = A.alloc(BF16, 8, T)
        Bac = [Buf("actc%d" % c) for c in range(8)]
        dg = [A.alloc(BF16, 31, 128) for _ in range(2)]
        Bdg = [Buf("dg%d" % k) for k in range(2)]
        wi = self.stream("cwi", 3, (8, 256), [self.d["c_wi"][c] for _t in range(NT) for c in range(8)])
        wo = self.stream("cwo", 3, (8, 128), [self.d["c_wo"][c] for _t in range(NT) for c in range(8)])
        sgm = [A.alloc(F32, T) for _ in range(2)]
        Bsgm = [Buf("sgm%d" % k) for k in range(2)]
        mean = A.alloc(F32, T)
        msq = A.alloc(F32, T)
        var = A.alloc(F32, T)
        nmr = A.alloc(F32, T)
        Bmean, Bmsq, Bvar, Bnmr = Buf("mean"), Buf("msq"), Buf("var"), Buf("nmr")
        t1 = [A.alloc(F32, T) for _ in range(2)]
        Bt1 = [Buf("ct1_%d" % k) for k in range(2)]
        for c in range(8):
            self.memset("pool", ybuf[:, c, 0:30], 0.0, [Byb[c]])
        for t in range(NT):
            self.pre_norm(t, ("mix_pre", i))
            sb1 = self.statbank()
            sb2 = self.statbank()
            for c in range(8):
                w, Bw = wi.next()
                ba = self.bank()
                bg = self.bank()
                for kc in range(8):
                    self.mm(self.psb(ba), w[:, kc, 0:128], self.hn[:, kc, :], kc == 0, kc == 7, [Bw, self.Bhn[kc]], [self.PB[ba]])
                for kc in range(8):
                    self.mm(self.psb(bg), w[:, kc, 128:256], self.hn[:, kc, :], kc == 0, kc == 7, [Bw, self.Bhn[kc]], [self.PB[bg]])
                k2 = c % 2
                self.act(sgm[k2], self.psb(bg), AF.Sigmoid, [self.PB[bg]], [Bsgm[k2]])
                self.tt("dve", ybuf[:, c, 30:30 + T], sgm[k2], self.psb(ba), ALU.mult, [Bsgm[k2], self.PB[ba]], [Byb[c]])
                o_ = self.poff["c_dw"] + c * 31
                in1 = self.ptab[:, o_:o_ + 31].unsqueeze(2).to_broadcast([128, 31, 128])
                in0 = self.ident.unsqueeze(1).to_broadcast([128, 31, 128])
                self.tt("dve", dg[k2], in0, in1, ALU.mult, [self.Bconst, self.Bptab], [Bdg[k2]])
                bc = self.bank()
                for jj in range(31):
                    self.mm(self.psb(bc), dg[k2][:, jj, :], ybuf[:, c, jj:jj + T], jj == 0, jj == 30, [Bdg[k2], Byb[c]], [self.PB[bc]])
                bias = self.pcol("c_dwb", c)
                self.act(z[:, c, :], self.psb(bc), AF.Identity, [self.PB[bc], self.Bptab], [Bz[c]], bias=bias)
                self.act(zb[k2], self.psb(bc), AF.Identity, [self.PB[bc], self.Bptab], [Bzb[k2]], bias=bias)
                self.flush_mm()
                self.mm(self.psb(sb1), self.onesb, zb[k2], c == 0, c == 7, [Bzb[k2], self.Bconst], [self.PB[sb1]])
                sq, Bsq = self.nextsq()
                self.act(sq, self.psb(bc), AF.Square, [self.PB[bc], self.Bptab], [Bsq], bias=bias)
                self.defer_mm(self.psb(sb2), self.onesb, sq, c == 0, c == 7, [Bsq, self.Bconst], [self.PB[sb2]])
                if t < NT - 1:
                    self.cp("pool", ybuf[:, c, 0:30], ybuf[:, c, T:T + 30], [Byb[c]], [Byb[c]])
            self.flush_mm()
            self.ts("dve", mean, self.psb(sb1), 1.0 / D, None, ALU.mult, None, [self.PB[sb1]], [Bmean])
            self.act(msq, mean, AF.Square, [Bmean], [Bmsq])
            self.stt(var, self.psb(sb2), 1.0 / D, msq, ALU.mult, ALU.subtract, [self.PB[sb2], Bmsq], [Bvar])
            self.act(self.rstd, var, AF.Sqrt, [Bvar, self.Bconst], [self.Brstd], bias=self.epsc)
            self.recip(self.rstd, self.rstd, [self.Brstd], [self.Brstd])
            self.stt(nmr, mean, -1.0, self.rstd, ALU.mult, ALU.mult, [Bmean, self.Brstd], [Bnmr])
            for c in range(8):
                k2 = c % 2
                self.tt("dve", t1[k2], z[:, c, :], self.rstd, ALU.mult, [Bz[c], self.Brstd], [Bt1[k2]])
                self.tt("pool", t1[k2], t1[k2], nmr, ALU.add, [Bt1[k2], Bnmr], [Bt1[k2]])
                self.act(actc[:, c, :], t1[k2], AF.Silu, [Bt1[k2], self.Bptab], [Bac[c]],
                         scale=self.pcol("c_lng", c), bias=self.pcol("c_lnb", c))
            sb = self.statbank()
            for dc in range(8):
                w, Bw = wo.next()
                b = self.bank()
                for c in range(8):
                    self.mm(self.psb(b), w[:, c, :], actc[:, c, :], c == 0, c == 7, [Bw, Bac[c]], [self.PB[b]])
                self.evac_mres(b, dc, sb)
            self.post_norm_residual(t, ("mix_post", i), sb)

    def mixer_a(self, i, j):
        A = self.A
        u = A.alloc(BF16, 16, T)
        Bu = [Buf("u%d" % c) for c in range(16)]
        vgb = A.alloc(BF16, 4, 2048)
        Bvgb = [Buf("vgb%d" % b) for b in range(4)]
        wv = self.stream("awv", 2, (8, 512), [self.d["a_wv%d" % j][q] for _t in range(NT) for q in range(4)])
        wu = self.stream("awu", 3, (8, 128), [self.d["a_wu%d" % j][q] for _t in range(NT) for q in range(16)])
        wo = self.stream("awo", 2, (16, 128), [self.d["a_wo%d" % j][q] for _t in range(NT) for q in range(8)])
        vgc = [A.alloc(F32, 512) for _ in range(3)]
        Bvgc = [Buf("vgc%d" % k) for k in range(3)]
        bst = A.alloc(F32, 4, 4, 6)
        Bbst = [Buf("bst%d" % b) for b in range(4)]
        mv = A.alloc(F32, 4, 2)
        rs = A.alloc(F32, 4, 1)
        vpe = A.alloc(F32, 4, 1)
        Bmv = [Buf("mv%d" % b) for b in range(4)]
        Brs = [Buf("rs%d" % b) for b in range(4)]
        Bvpe = [Buf("vpe%d" % b) for b in range(4)]
        nmh = A.alloc(F32, 1)
        wsT = A.alloc(BF16, 8, 128)
        Bws = Buf("wsT")
        Cc = A.alloc(F32, 16, 128)
        BCc = Buf("Cc")
        bsb = A.alloc(F32, 8, 128)
        Bbsb = Buf("bsb")
        rw = A.alloc(F32, 8, 128)
        Brw = Buf("rw")
        t1 = [A.alloc(F32, 128) for _ in range(3)]
        Bt1 = [Buf("at1_%d" % k) for k in range(3)]
        self.memset("pool", nmh, -0.5, [self.Bconst])
        self.dma_cast(wsT.rearrange("p a b -> p (a b)"), self.d["a_ws%d" % j], [], [Bws])
        self.memset("pool", wsT[64:128, :, 0:64], 0.0, [Bws])
        self.dma_plain(bsb.rearrange("p a b -> p (a b)"), self.d["a_bs%d" % j], [], [Bbsb])
        b0 = self.bank()
        b1 = self.bank()
        for g in range(8):
            bb = b0 if g < 4 else b1
            lo = (g % 4) * 128
            self.mm(self.psb(bb, lo, lo + 128), self.onesb, wsT[:, g, :], True, True, [Bws, self.Bconst], [self.PB[bb]])
        self.cp("act", rw[:, 0:4, :].rearrange("p a b -> p (a b)"), self.psb(b0), [self.PB[b0]], [Brw])
        self.cp("act", rw[:, 4:8, :].rearrange("p a b -> p (a b)"), self.psb(b1), [self.PB[b1]], [Brw])
        for uc in range(16):
            g = uc // 2
            self.stt(Cc[:, uc, :], rw[:, g, :], self.pcol(("a_lnb", j), uc), bsb[:, g, :], ALU.mult, ALU.add,
                     [Brw, Bbsb, self.Bptab], [BCc])
        for t in range(NT):
            self.pre_norm(t, ("mix_pre", i))
            for uc in range(16):
                w, Bw = wu.next()
                b = self.bank()
                for kc in range(8):
                    self.mm(self.psb(b), w[:, kc, :], self.hn[:, kc, :], kc == 0, kc == 7, [Bw, self.Bhn[kc]], [self.PB[b]])
                self.act(u[:, uc, :], self.psb(b), AF.Gelu_apprx_tanh, [self.PB[b]], [Bu[uc]])
            kk = 0
            for vq in range(4):
                w, Bw = wv.next()
                for blk in range(4):
                    b = self.bank()
                    for kc in range(8):
                        self.mm(self.psb(b), self.hn[:, kc, blk * 128:(blk + 1) * 128], w[:, kc, :], kc == 0, kc == 7,
                                [Bw, self.Bhn[kc]], [self.PB[b]])
                    k3 = kk % 3
                    kk += 1
                    self.act(vgc[k3], self.psb(b), AF.Gelu_apprx_tanh, [self.PB[b]], [Bvgc[k3]])
                    bo = bst[:, blk, vq, :]
                    src = vgc[k3]
                    self.P.op("dve", lambda e, bo=bo, src=src: e.bn_stats(out=bo, in_=src), [Bvgc[k3]], [Bbst[blk]])
                    self.cp("pool", vgb[:, blk, vq * 512:(vq + 1) * 512], vgc[k3], [Bvgc[k3]], [Bvgb[blk]])
            for blk in range(4):
                mo = mv[:, blk, :]
                bi = bst[:, blk, :, :]
                self.P.op("dve", lambda e, mo=mo, bi=bi: e.bn_aggr(out=mo, in_=bi), [Bbst[blk]], [Bmv[blk]])
                self.ts("pool", vpe[:, blk, :], mv[:, blk, 1:2], EPS, None, ALU.add, None, [Bmv[blk]], [Bvpe[blk]])
                self.tt("pool", rs[:, blk, :], vpe[:, blk, :], nmh, ALU.pow, [Bvpe[blk], self.Bconst], [Brs[blk]])
                self.ts("dve", vgb[:, blk, :], vgb[:, blk, :], mv[:, blk, 0:1], rs[:, blk, :], ALU.subtract, ALU.mult,
                        [Bvgb[blk], Bmv[blk], Brs[blk]], [Bvgb[blk]])
            kk = 0
            for blk in range(4):
                bsl = slice(blk * 128, (blk + 1) * 128)
                for uc in range(16):
                    b = self.bank()
                    self.mm(self.psb(b, 0, 128), vgb[:, blk, uc * 128:(uc + 1) * 128], wsT[:, uc // 2, :], True, True,
                            [Bvgb[blk], Bws], [self.PB[b]])
                    k3 = kk % 3
                    kk += 1
                    self.stt(t1[k3], self.psb(b, 0, 128), self.pcol(("a_lng", j), uc), Cc[:, uc, :], ALU.mult, ALU.add,
                             [self.PB[b], BCc, self.Bptab], [Bt1[k3]])
                    self.tt("dve", u[:, uc, bsl], t1[k3], u[:, uc, bsl], ALU.mult, [Bt1[k3], Bu[uc]], [Bu[uc]])
            sb = self.statbank()
            for dc in range(8):
                w, Bw = wo.next()
                b = self.bank()
                for uc in range(16):
                    self.mm(self.psb(b), w[:, uc, :], u[:, uc, :], uc == 0, uc == 15, [Bw, Bu[uc]], [self.PB[b]])
                self.evac_mres(b, dc, sb)
            self.post_norm_residual(t, ("mix_post", i), sb)

    def mixer_b(self, i):
        A = self.A
        kT = A.alloc(BF16, 8, 1024)
        BkT = [[Buf("kT%d_%d" % (c, s_)) for s_ in range(2)] for c in range(8)]
        V = A.alloc(BF16, 8, 1024)
        BV = [Buf("V%d" % b) for b in range(8)]
        qz = A.alloc(BF16, 8, 2, T)
        Bq = [Buf("qz%d" % c) for c in range(8)]
        Bb = A.alloc(BF16, 16, 640)
        BBb = Buf("Bb")
        Pb = [A.alloc(BF16, 640) for _ in range(2)]
        BPb = [Buf("Pb%d" % k) for k in range(2)]
        PTb = [A.alloc(BF16, 640) for _ in range(2)]
        BPTb = [Buf("PTb%d" % k) for k in range(2)]
        dgr = [A.alloc(BF16, 128) for _ in range(2)]
        Bdgr = [Buf("dgr%d" % k) for k in range(2)]
        st3 = [A.alloc(F32, 4) for _ in range(2)]
        Bst = [Buf("st%d" % k) for k in range(2)]
        wqk = self.stream("bwqk", 2, (8, 256), [self.d["b_wqk"][q] for _t in range(NT) for q in range(8)])
        wvs = self.stream("bwv", 2, (8, 256), [self.d["b_wv"][q] for _t in range(NT) for q in range(4)])
        wos = self.stream("bwo", 2, (8, 128), [self.d["b_wo"][q] for _t in range(NT) for q in range(8)])
        oT, BoT = self.hn, self.Bhn
        ps = self.ps
        self.dma_cast(Bb.rearrange("p a b -> p (a b)"), self.d["b_bias"], [], [BBb])
        self.memset("pool", Bb[64:128, :, 0:64], NEG, [BBb])
        self.memset("pool", Bb[0:64, :, 576:640], NEG, [BBb])
        for c in range(8):
            self.memset("pool", qz[:, c, :, :], 0.0, [Bq[c]])
        unit = 0
        for t in range(NT):
            slot = t % 2
            self.pre_norm(t, ("mix_pre", i))
            for c in range(8):
                w, Bw = wqk.next()
                bq = self.bank()
                bk = self.bank()
                for kc in range(8):
                    self.mm(self.psb(bq), w[:, kc, 0:128], self.hn[:, kc, :], kc == 0, kc == 7, [Bw, self.Bhn[kc]], [self.PB[bq]])
                for kc in range(8):
                    self.mm(self.psb(bk), w[:, kc, 128:256], self.hn[:, kc, :], kc == 0, kc == 7, [Bw, self.Bhn[kc]], [self.PB[bk]])
                for hh in range(2):
                    rows = slice(hh * 64, hh * 64 + 64)
                    self.act(qz[rows, c, hh, :], ps[rows, bq * 512:(bq + 1) * 512], AF.Copy, [self.PB[bq]], [Bq[c]], scale=0.125)
                self.cp("dve", kT[:, c, slot * 512:(slot + 1) * 512], self.psb(bk), [self.PB[bk]], [BkT[c][slot]])
            for qt in range(4):
                w, Bw = wvs.next()
                for blk in range(4):
                    b = self.bank()
                    for kc in range(8):
                        self.mm(self.psb(b, 0, 256), self.hn[:, kc, blk * 128:(blk + 1) * 128], w[:, kc, :], kc == 0, kc == 7,
                                [Bw, self.Bhn[kc]], [self.PB[b]])
                    rb = (4 * t + blk) % 8
                    self.cp("act" if (blk % 2) else "dve", V[:, rb, qt * 256:(qt + 1) * 256], self.psb(b, 0, 256), [self.PB[b]], [BV[rb]])
            for c in range(8):
                for jq in range(4):
                    jb = 4 * t + jq
                    i0 = max(0, 4 - jb)
                    for hh in range(2):
                        hd = 2 * c + hh
                        rows = slice(hh * 64, hh * 64 + 64)
                        sp = unit % 2
                        k2 = unit % 2
                        unit += 1
                        sbase = sp * 1024
                        sbufs = [self.PB[2 * sp], self.PB[2 * sp + 1]]
                        for ii in range(i0, 5):
                            kb = jb - 4 + ii
                            rc = (kb % 8) * 128
                            ks = (kb // 4) % 2
                            sap = ps[:, sbase + ii * 128: sbase + (ii + 1) * 128]
                            wb = [sbufs[0] if ii < 4 else sbufs[1]]
                            self.mm(sap, qz[:, c, hh, jq * 128:(jq + 1) * 128], kT[:, c, rc:rc + 128], True, False,
                                    [Bq[c], BkT[c][ks]], wb)
                            self.mm(sap, self.ident, Bb[:, hd, ii * 128:(ii + 1) * 128], False, True, [self.Bconst, BBb], wb)
                        sfull = ps[:, sbase + i0 * 128: sbase + 640]
                        nmax, rsum, rinv = st3[k2][:, 0:1], st3[k2][:, 1:2], st3[k2][:, 2:3]
                        self.P.op("dve", lambda e, nmax=nmax, sfull=sfull: e.tensor_reduce(out=nmax, in_=sfull, axis=AX.X, op=ALU.max, negate=True),
                                  sbufs, [Bst[k2]])
                        self.act(Pb[k2][:, i0 * 128:640], sfull, AF.Exp, sbufs + [Bst[k2]], [BPb[k2], Bst[k2]], bias=nmax, accum=rsum)
                        self.recip(rinv, rsum, [Bst[k2]], [Bst[k2]])
                        self.ts("pool", dgr[k2], self.ident, rinv, None, ALU.mult, None, [self.Bconst, Bst[k2]], [Bdgr[k2]])
                        for ii in range(i0, 5):
                            pb_ = self.PB[4] if ii < 4 else self.PB[5]
                            self.mm(ps[:, 2048 + ii * 128: 2048 + (ii + 1) * 128], Pb[k2][:, ii * 128:(ii + 1) * 128], dgr[k2], True, True,
                                    [BPb[k2], Bdgr[k2]], [pb_])
                        self.cp("act", PTb[k2][:, i0 * 128:640], ps[:, 2048 + i0 * 128: 2048 + 640], [self.PB[4], self.PB[5]], [BPTb[k2]])
                        ob = 6 + hh
                        for ii in range(i0, 5):
                            kb = jb - 4 + ii
                            self.mm(ps[:, ob * 512 + jq * 128: ob * 512 + (jq + 1) * 128], V[:, kb % 8, c * 128:(c + 1) * 128],
                                    PTb[k2][:, ii * 128:(ii + 1) * 128], ii == i0, ii == 4, [BV[kb % 8], BPTb[k2]], [self.PB[ob]])
                for hh in range(2):
                    rows = slice(hh * 64, hh * 64 + 64)
                    ob = 6 + hh
                    self.cp("dve", oT[rows, c, :], ps[rows, ob * 512:(ob + 1) * 512], [self.PB[ob]], [BoT[c]])
            sb = self.statbank()
            for dc in range(8):
                w, Bw = wos.next()
                b = self.bank()
                for c in range(8):
                    self.mm(self.psb(b), w[:, c, :], oT[:, c, :], c == 0, c == 7, [Bw, BoT[c]], [self.PB[b]])
                self.evac_mres(b, dc, sb)
            self.post_norm_residual(t, ("mix_post", i), sb)

    def store_output(self):
        for c in range(8):
            self.dma_plain(self.yT[c], self.h[:, c, :], self.Bh[c], [Buf("y%d" % c)], is_output=True)


def _cols(v, n):
    return np.ascontiguousarray(np.asarray(v, np.float32).reshape(n, 128).T)


def _kc_tile(w, ncols_per_block):
    K, N = w.shape
    nb = N // ncols_per_block
    x = w.reshape(K // 128, 128, nb, ncols_per_block)
    x = x.transpose(2, 1, 0, 3)
    return np.ascontiguousarray(x).reshape(nb, 128, (K // 128) * ncols_per_block)


def host_shared(inp, layers):
    off, R = ptab_layout()
    ptab = np.zeros((128, R), np.float32)

    def put(key, arr):
        ptab[:, off[key]:off[key] + arr.shape[1]] = arr

    for i in range(DEPTH):
        put(("mix_pre", i), _cols(inp["mix_pre_g"][i], 8))
        put(("mix_post", i), _cols(inp["mix_post_g"][i], 8))
        put(("ffn_pre", i), _cols(inp["ffn_pre_g"][i], 8))
        put(("ffn_post", i), _cols(inp["ffn_post_g"][i], 8))
        put(("ple_g", i), _cols(inp["ple_norm_g"][i], 8))
        cv = np.concatenate([_cols(inp["ffn_conv"][i][jj], 44) for jj in range(3)], axis=1)
        put(("conv", i), cv)
    for j in range(2):
        put(("a_lng", j), _cols(inp["a_ln_g"][j], 16))
        put(("a_lnb", j), _cols(inp["a_ln_b"][j], 16))
    dw = np.asarray(inp["c_dw"][0], np.float32)
    dwc = dw.reshape(31, 8, 128).transpose(2, 1, 0).reshape(128, 248)
    put("c_dw", np.ascontiguousarray(dwc))
    put("c_dwb", _cols(inp["c_dw_b"][0], 8))
    put("c_lng", _cols(inp["c_ln_g"][0], 8))
    put("c_lnb", _cols(inp["c_ln_b"][0], 8))
    sh = {"ptab": ptab, "ident": np.eye(128, dtype=np.float32)}
    for i in layers:
        wu = np.asarray(inp["ffn_w_up"][i], np.float32)
        g = wu[:, :FF].reshape(D, NFC, 128)
        v = wu[:, FF:].reshape(D, NFC, 128)
        gv = np.concatenate([g, v], axis=2).reshape(D, NFC * 256)
        sh["wup%d" % i] = _kc_tile(gv, 256)
        wd = np.asarray(inp["ffn_w_down"][i], np.float32)
        x = wd.reshape(NFC, 128, 8, 128).transpose(2, 1, 0, 3)
        sh["wdn%d" % i] = np.ascontiguousarray(x).reshape(8, 128, NFC * 128)
        sh["wg%d" % i] = _kc_tile(np.asarray(inp["ple_w_gate"][i], np.float32), 128)
        wp = np.asarray(inp["ple_w_proj"][i], np.float32)
        sh["wp%d" % i] = np.ascontiguousarray(wp.reshape(2, 128, D).transpose(1, 0, 2)).reshape(128, 2 * D)
        kind, j = i % 3, i // 3
        if kind == 0:
            win = np.asarray(inp["a_w_in"][j], np.float32)
            sh["a_wu%d" % j] = _kc_tile(win[:, :2048], 128)
            sh["a_wv%d" % j] = _kc_tile(win[:, 2048:], 512)
            wo = np.asarray(inp["a_w_out"][j], np.float32)
            x = wo.reshape(16, 128, 8, 128).transpose(2, 1, 0, 3)
            sh["a_wo%d" % j] = np.ascontiguousarray(x).reshape(8, 128, 16 * 128)
            ws = np.asarray(inp["a_w_s"][j], np.float32)
            sh["a_ws%d" % j] = np.ascontiguousarray(ws.transpose(2, 0, 1)).reshape(128, 8 * 128)
            bs = np.asarray(inp["a_b_s"][j], np.float32).reshape(1, 8 * 128)
            sh["a_bs%d" % j] = np.ascontiguousarray(np.broadcast_to(bs, (128, 8 * 128)))
        elif kind == 1:
            wq = np.asarray(inp["b_w_qkv"][0], np.float32)
            q = wq[:, :D].reshape(D, 8, 128)
            k = wq[:, D:2 * D].reshape(D, 8, 128)
            qk = np.concatenate([q, k], axis=2).reshape(D, 8 * 256)
            sh["b_wqk"] = _kc_tile(qk, 256)
            sh["b_wv"] = _kc_tile(np.ascontiguousarray(wq[:, 2 * D:]), 256)
            wo = np.asarray(inp["b_w_out"][0], np.float32)
            x = wo.reshape(8, 128, 8, 128).transpose(2, 1, 0, 3)
            sh["b_wo"] = np.ascontiguousarray(x).reshape(8, 128, 8 * 128)
            rb = np.asarray(inp["b_rel_bias"][0], np.float32)
            qq = np.arange(128)[:, None]
            kk = np.arange(640)[None, :]
            idx = np.clip(qq + 512 - kk, -128, 128) + 128
            bfull = rb[:, idx]
            sh["b_bias"] = np.ascontiguousarray(bfull.transpose(1, 0, 2)).reshape(128, 16 * 640)
        else:
            wi = np.asarray(inp["c_w_in"][0], np.float32)
            a = wi[:, :D].reshape(D, 8, 128)
            g = wi[:, D:].reshape(D, 8, 128)
            ag = np.concatenate([a, g], axis=2).reshape(D, 8 * 256)
            sh["c_wi"] = _kc_tile(ag, 256)
            wo = np.asarray(inp["c_w_out"][0], np.float32)
            x = wo.reshape(8, 128, 8, 128).transpose(2, 1, 0, 3)
            sh["c_wo"] = np.ascontiguousarray(x).reshape(8, 128, 8 * 128)
    return sh


def run_layers(hT_in, p, shared, layers, trace=False, dbg=()):
    nc = Builder(layers, dbg).build()
    in_maps = []
    for b in range(8):
        m = dict(shared)
        m["xT"] = hT_in[b]
        for i in layers:
            m["pT%d" % i] = np.ascontiguousarray(p[i, b].T).reshape(2, 128, S)
        in_maps.append(m)
    res = run_bass_kernel_spmd(nc, in_maps, core_ids=list(range(8)), trace=trace)
    out = np.stack([res.results[b]["yT"] for b in range(8)])
    return out, res


def kernel(**inputs):
    inp = {k: np.asarray(v) for k, v in inputs.items()}
    layers = list(range(DEPTH))
    x = inp["x"].astype(np.float32, copy=False)
    hT = np.ascontiguousarray(x.transpose(0, 2, 1)).reshape(8, 8, 128, S)
    shared = host_shared(inp, layers)
    out, _ = run_layers(hT, inp["p"].astype(np.float32, copy=False), shared, layers)
    y = out.reshape(8, D, S).transpose(0, 2, 1)
    return np.ascontiguousarray(y).astype(np.float32, copy=False)
```

```python
import numpy as np
from contextlib import ExitStack
import concourse.bass as bass
import concourse.mybir as mybir
from concourse.bass_utils import run_bass_kernel_spmd

F32 = mybir.dt.float32
BF16 = mybir.dt.bfloat16
AF = mybir.ActivationFunctionType
ALU = mybir.AluOpType
AX = mybir.AxisListType

D = 1024
S = 2048
T = 512
NT = S // T
FF = 2816
NFC = FF // 128
DEPTH = 4
EPS = 1e-6
NEG = -1e30


class Buf:
    __slots__ = ("name", "last_w", "readers", "dma_readers", "dma_sem", "dma_cnt", "const", "excl")

    fence = {}

    def __init__(self, name, const=False, excl=False):
        self.name = name
        self.excl = excl
        self.last_w = None
        self.readers = dict(Buf.fence)
        self.dma_readers = []
        self.dma_sem = None
        self.dma_cnt = 0
        self.const = const


class Op:
    __slots__ = ("eng", "fn", "deps", "sig", "sigval", "is_dma", "dma_sem", "dma_val", "idx")

    def __init__(self, eng, fn, is_dma):
        self.eng = eng
        self.fn = fn
        self.deps = []
        self.sig = False
        self.sigval = 0
        self.is_dma = is_dma
        self.dma_sem = None
        self.dma_val = 0


class Prog:
    ENGS = ("pe", "act", "dve", "pool", "sp")

    def __init__(self, nc):
        self.nc = nc
        self.ops = {e: [] for e in self.ENGS}
        self.n = 0
        self.dma_bufs = []
        self.out_dma_ops = []

    def _add(self, op, reads, writes):
        deps = []
        for b in reads:
            w = b.last_w
            if w is not None:
                deps.append((w, True))
            if b.excl:
                for e_, r in b.readers.items():
                    if e_ != op.eng:
                        deps.append((r, False))
        for b in writes:
            w = b.last_w
            if w is not None and not (op.is_dma and w.is_dma):
                deps.append((w, False))
            for r in b.readers.values():
                deps.append((r, False))
            for r in b.dma_readers:
                deps.append((r, False))
        seen = set()
        for d, raw in deps:
            if d is op:
                continue
            if (not d.is_dma) and (not op.is_dma) and d.eng == op.eng:
                if op.eng == "pe":
                    continue
            k = id(d)
            if k in seen:
                continue
            seen.add(k)
            op.deps.append(d)
            if not d.is_dma:
                d.sig = True
        for b in writes:
            b.last_w = op
            b.readers = {}
            b.dma_readers = []
        for b in reads:
            if b.const or b in writes:
                continue
            if op.is_dma:
                b.dma_readers.append(op)
            else:
                b.readers[op.eng] = op
        op.idx = self.n
        self.n += 1
        self.ops[op.eng].append(op)
        return op

    @staticmethod
    def _flat(xs):
        out = []
        for x in xs:
            if isinstance(x, (list, tuple)):
                out.extend(Prog._flat(x))
            else:
                out.append(x)
        return out

    def op(self, eng, fn, reads=(), writes=()):
        return self._add(Op(eng, fn, False), self._flat(reads), self._flat(writes))

    def dma(self, eng, fn, reads=(), writes=(), is_output=False):
        o = Op(eng, fn, True)
        reads, writes = self._flat(reads), self._flat(writes)
        dst = writes[0]
        if dst.dma_sem is None:
            dst.dma_sem = "pending"
            self.dma_bufs.append(dst)
        dst.dma_cnt += 16
        o.dma_sem = dst
        o.dma_val = dst.dma_cnt
        self._add(o, list(reads), list(writes))
        if is_output:
            self.out_dma_ops.append(o)
        return o

    def emit(self, stack):
        nc = self.nc
        sems = {}
        for e in ("pe", "act", "dve", "pool"):
            sems[e] = stack.enter_context(nc.semaphore("s_" + e))
        for i, b in enumerate(self.dma_bufs):
            b.dma_sem = stack.enter_context(nc.semaphore("d%d" % i))
        for e in ("pe", "act", "dve", "pool"):
            c = 0
            for o in self.ops[e]:
                if o.is_dma:
                    continue
                if o.sig:
                    c += 1
                    o.sigval = c
        out_ops = self.out_dma_ops

        def run(engh, ename):
            waited = {}

            def wait(sem, val):
                k = id(sem)
                if waited.get(k, 0) >= val:
                    return
                waited[k] = val
                engh.wait_ge(sem, val)

            for o in self.ops[ename]:
                for d in o.deps:
                    if d.is_dma:
                        wait(d.dma_sem.dma_sem, d.dma_val)
                    else:
                        wait(sems[d.eng], d.sigval)
                ins = o.fn(engh)
                if o.is_dma:
                    ins.then_inc(o.dma_sem.dma_sem, 16)
                elif o.sig:
                    ins.then_inc(sems[ename], 1)
            if ename == "sp":
                for o in out_ops:
                    wait(o.dma_sem.dma_sem, o.dma_val)

        block = stack.enter_context(nc.Block())

        @block.tensor
        def _(e):
            run(e, "pe")

        @block.scalar
        def _(e):
            run(e, "act")

        @block.vector
        def _(e):
            run(e, "dve")

        @block.gpsimd
        def _(e):
            run(e, "pool")

        @block.sync
        def _(e):
            run(e, "sp")


def ptab_layout():
    off = {}
    c = 0
    for i in range(DEPTH):
        for nm in ("mix_pre", "mix_post", "ffn_pre", "ffn_post", "ple_g"):
            off[(nm, i)] = c
            c += 8
        off[("conv", i)] = c
        c += 132
    for j in range(2):
        off[("a_lng", j)] = c
        c += 16
        off[("a_lnb", j)] = c
        c += 16
    off["c_dw"] = c
    c += 248
    off["c_dwb"] = c
    c += 8
    off["c_lng"] = c
    c += 8
    off["c_lnb"] = c
    c += 8
    return off, c


class StopBuild(Exception):
    pass


class Stream:
    def __init__(self, B, name, n, free, srcs):
        self.B = B
        self.n = n
        self.srcs = list(srcs)
        self.aps = [B.A.alloc(BF16, *free) for _ in range(n)]
        self.bufs = [Buf("%s%d" % (name, i)) for i in range(n)]
        self.issued = 0
        self.taken = 0
        for _ in range(n - 1):
            self._issue()

    def _issue(self):
        k = self.issued
        if k >= len(self.srcs):
            return
        self.issued += 1
        ap, bf = self.aps[k % self.n], self.bufs[k % self.n]
        flat = ap.rearrange("p a b -> p (a b)") if len(ap.shape) == 3 else ap
        self.B.dma_cast(flat, self.srcs[k], [], [bf])

    def next(self):
        k = self.taken
        self.taken += 1
        self._issue()
        return self.aps[k % self.n], self.bufs[k % self.n]


class Arena:
    def __init__(self, ap_all, base, limit):
        self.all = ap_all
        self.off = base
        self.limit = limit

    def mark(self):
        return self.off

    def reset(self, m):
        self.off = m

    def alloc(self, dtype, *free):
        isz = 4 if dtype == F32 else 2
        n = 1
        for f in free:
            n *= f
        nb = n * isz
        off = (self.off + 63) // 64 * 64
        assert off + nb <= self.limit, ("SBUF arena overflow", off + nb, self.limit)
        self.off = off + nb
        v = self.all[:, off // 2:(off + nb) // 2]
        if dtype == F32:
            v = v.bitcast(F32)
        if len(free) == 2:
            v = v.rearrange("p (a b) -> p a b", a=free[0])
        elif len(free) == 3:
            v = v.rearrange("p (a b c) -> p a b c", a=free[0], b=free[1])
        return v


class Builder:
    def __init__(self, layers, dbg=()):
        self.layers = list(layers)
        self.dbg = set(dbg)
        self.nc = bass.Bass("TRN2", target_bir_lowering=False)
        Buf.fence = {}
        self.P = Prog(self.nc)
        self.poff, self.pcols = ptab_layout()
        self._bank = 0
        self._stat = 0

    def mm(self, out, lhsT, rhs, start, stop, r, w, **kw):
        self.P.op("pe", lambda e: e.matmul(out, lhsT=lhsT, rhs=rhs, start=start, stop=stop, **kw), r, w)

    def act(self, out, in_, func, r, w, bias=None, scale=None, accum=None):
        kw = {}
        if bias is not None:
            kw["bias"] = bias
        if scale is not None:
            kw["scale"] = scale
        if accum is not None:
            kw["accum_out"] = accum
        self.P.op("act", lambda e: e.activation(out=out, in_=in_, func=func, **kw), r, w)

    def ts(self, eng, out, in0, s1, s2, op0, op1, r, w):
        if op1 is None and eng == "pool":
            s2, op1 = 1.0, ALU.mult
        if op1 is None:
            self.P.op(eng, lambda e: e.tensor_scalar(out=out, in0=in0, scalar1=s1, scalar2=None, op0=op0), r, w)
        else:
            self.P.op(eng, lambda e: e.tensor_scalar(out=out, in0=in0, scalar1=s1, scalar2=s2, op0=op0, op1=op1), r, w)

    def stt(self, out, in0, scalar, in1, op0, op1, r, w):
        self.P.op("dve", lambda e: e.scalar_tensor_tensor(out=out, in0=in0, scalar=scalar, in1=in1, op0=op0, op1=op1), r, w)

    def tt(self, eng, out, in0, in1, op, r, w):
        self.P.op(eng, lambda e: e.tensor_tensor(out=out, in0=in0, in1=in1, op=op), r, w)

    def cp(self, eng, out, in_, r, w):
        if eng == "act":
            self.P.op("act", lambda e: e.copy(out=out, in_=in_), r, w)
        else:
            self.P.op(eng, lambda e: e.tensor_copy(out=out, in_=in_), r, w)

    def recip(self, out, in_, r, w):
        self.P.op("dve", lambda e: e.reciprocal(out=out, in_=in_), r, w)

    def memset(self, eng, ap, val, w):
        self.P.op(eng, lambda e: e.memset(ap, val), [], w)

    def dma_cast(self, out, in_, r, w):
        self.P.dma("pool", lambda e: e.dma_start(out=out, in_=in_), r, w)

    def dma_plain(self, out, in_, r, w, is_output=False):
        self.P.dma("sp", lambda e: e.dma_start(out=out, in_=in_), r, w, is_output=is_output)

    def bank(self):
        b = self._bank
        self._bank = (b + 1) % 6
        return b

    def statbank(self):
        b = 6 + self._stat
        self._stat ^= 1
        return b

    def psb(self, b, lo=0, hi=512):
        return self.ps[:, b * 512 + lo:b * 512 + hi]

    def build(self):
        nc = self.nc
        st = ExitStack()
        with st:
            self.declare_dram()
            self.sb_all = st.enter_context(nc.sbuf_tensor("sb_all", [128, 106300], BF16))
            self.ps = st.enter_context(nc.psum_tensor("ps_all", [128, 4096], F32))
            self.PBK = [Buf("psk%d" % i, excl=True) for i in range(32)]
            self.PB = [self.PBK[4 * i:4 * i + 4] for i in range(8)]
            self.A = Arena(self.sb_all, 0, 106300 * 2)
            self.setup_persistent()
            for i in self.layers:
                self.layer(i)
            self.store_output()
            self.P.emit(st)
        return nc

    def declare_dram(self):
        nc = self.nc
        dt = lambda n, s: nc.dram_tensor(n, s, F32, kind="ExternalInput").ap()
        self.d = {}
        self.d["xT"] = dt("xT", [8, 128, S])
        self.d["ptab"] = dt("ptab", [128, self.pcols])
        self.d["ident"] = dt("ident", [128, 128])
        for i in self.layers:
            self.d["pT%d" % i] = dt("pT%d" % i, [2, 128, S])
            self.d["wup%d" % i] = dt("wup%d" % i, [NFC, 128, 8 * 256])
            self.d["wdn%d" % i] = dt("wdn%d" % i, [8, 128, NFC * 128])
            self.d["wg%d" % i] = dt("wg%d" % i, [8, 128, 8 * 128])
            self.d["wp%d" % i] = dt("wp%d" % i, [128, 2 * D])
            kind, j = i % 3, i // 3
            if kind == 0:
                self.d["a_wu%d" % j] = dt("a_wu%d" % j, [16, 128, 8 * 128])
                self.d["a_wv%d" % j] = dt("a_wv%d" % j, [4, 128, 8 * 512])
                self.d["a_wo%d" % j] = dt("a_wo%d" % j, [8, 128, 16 * 128])
                self.d["a_ws%d" % j] = dt("a_ws%d" % j, [128, 8 * 128])
                self.d["a_bs%d" % j] = dt("a_bs%d" % j, [128, 8 * 128])
            elif kind == 1:
                self.d["b_wqk"] = dt("b_wqk", [8, 128, 8 * 256])
                self.d["b_wv"] = dt("b_wv", [4, 128, 8 * 256])
                self.d["b_wo"] = dt("b_wo", [8, 128, 8 * 128])
                self.d["b_bias"] = dt("b_bias", [128, 16 * 640])
            else:
                self.d["c_wi"] = dt("c_wi", [8, 128, 8 * 256])
                self.d["c_wo"] = dt("c_wo", [8, 128, 8 * 128])
        self.yT = nc.dram_tensor("yT", [8, 128, S], F32, kind="ExternalOutput").ap()

    def setup_persistent(self):
        A = self.A
        self.h = A.alloc(F32, 8, S)
        self.Bh = [[Buf("h%d_%d" % (c, t)) for t in range(NT)] for c in range(8)]
        self.ptab = A.alloc(F32, self.pcols)
        self.Bptab = Buf("ptab", const=True)
        self.ident = A.alloc(BF16, 128)
        self.onesb = A.alloc(BF16, 128)
        self.epsc = A.alloc(F32, 1)
        self.dummy = A.alloc(F32, 8)
        self.Bconst = Buf("const", const=True)
        self.sq = [A.alloc(BF16, T) for _ in range(3)]
        self.Bsq = [Buf("sq%d" % i) for i in range(3)]
        self._sq = 0
        self.rstds = [A.alloc(F32, T) for _ in range(3)]
        self.Brstds = [Buf("rstd%d" % k) for k in range(3)]
        self.rstd, self.Brstd = self.rstds[0], self.Brstds[0]
        self.mres = A.alloc(F32, 8, T)
        self.Bmres = [Buf("mres%d" % c) for c in range(8)]
        self.hns = [A.alloc(BF16, 8, T) for _ in range(2)]
        self.hn1_off = A.off - 8 * T * 2
        self.Bhns = [[Buf("hn%d_%d" % (k, c)) for c in range(8)] for k in range(2)]
        self.hn, self.Bhn = self.hns[0], self.Bhns[0]
        self.rtmp = [A.alloc(F32, T) for _ in range(3)]
        self.Brtmp = [Buf("rtmp%d" % i) for i in range(3)]
        self._rt = 0
        self.phase_mark = A.mark()
        self.dma_plain(self.ptab, self.d["ptab"], [], [self.Bptab])
        for c in range(8):
            self.dma_plain(self.h[:, c, :], self.d["xT"][c], [], self.Bh[c])
        self.memset("pool", self.onesb, 1.0, [self.Bconst])
        self.memset("pool", self.epsc, EPS, [self.Bconst])
        self.dma_cast(self.ident, self.d["ident"], [], [self.Bconst])

    def pcol(self, key, c, n=1):
        o = self.poff[key] + c
        return self.ptab[:, o:o + n]

    def nextsq(self):
        i = self._sq
        self._sq = (i + 1) % 3
        return self.sq[i], self.Bsq[i]

    def nextrt(self):
        i = self._rt
        self._rt = (i + 1) % 3
        return self.rtmp[i], self.Brtmp[i]

    def defer_mm(self, *args, **kw):
        self.flush_mm()
        self._pend_mm = (args, kw)

    def flush_mm(self):
        p = getattr(self, "_pend_mm", None)
        if p is not None:
            self._pend_mm = None
            self.mm(*p[0], **p[1])

    def stat_add(self, sb, src, src_bufs, c, n=8):
        sq, Bsq = self.nextsq()
        self.act(sq, src, AF.Square, src_bufs, [Bsq])
        self.defer_mm(self.psb(sb), self.onesb, sq, c == 0, c == n - 1, [Bsq, self.Bconst], [self.PB[sb]])

    def finish_rstd(self, sb, dim=D, role=0):
        self.flush_mm()
        rstd, Brstd = self.rstds[role], self.Brstds[role]
        self.act(rstd, self.psb(sb), AF.Sqrt, [self.PB[sb], self.Bconst], [Brstd], bias=self.epsc, scale=1.0 / dim)
        self.recip(rstd, rstd, [Brstd], [Brstd])
        return rstd, Brstd

    def pre_norm_gen(self, t, gkey, k=0):
        hn, Bhn = self.hns[k], self.Bhns[k]
        sb = self.statbank()
        tsl = slice(t * T, (t + 1) * T)
        for c in range(8):
            self.stat_add(sb, self.h[:, c, tsl], [self.Bh[c][t]], c)
            yield
        rstd, Brstd = self.finish_rstd(sb, role=0)
        yield
        for c in range(8):
            self.stt(hn[:, c, :], self.h[:, c, tsl], self.pcol(gkey, c), rstd, ALU.mult, ALU.mult,
                     [self.Bh[c][t], Brstd, self.Bptab], [Bhn[c]])
            if c % 2:
                yield

    def pre_norm(self, t, gkey, k=0):
        for _ in self.pre_norm_gen(t, gkey, k):
            pass

    def post_norm_gen(self, t, gkey, sb, role, after=None):
        rstd, Brstd = self.finish_rstd(sb, role=role)
        tsl = slice(t * T, (t + 1) * T)
        yield
        rts = {}
        for i in range(10):
            if i < 8:
                rt, Brt = self.nextrt()
                rts[i] = (rt, Brt)
                self.stt(rt, self.mres[:, i, :], self.pcol(gkey, i), rstd, ALU.mult, ALU.mult,
                         [self.Bmres[i], Brstd, self.Bptab], [Brt])
            if 1 <= i < 9:
                c = i - 1
                rt, Brt = rts.pop(c)
                self.tt("pool", self.h[:, c, tsl], self.h[:, c, tsl], rt, ALU.add, [self.Bh[c][t], Brt], [self.Bh[c][t]])
            if 2 <= i < 10 and after is not None:
                after(i - 2)
            yield

    def post_norm_residual(self, t, gkey, sb, role=1):
        for _ in self.post_norm_gen(t, gkey, sb, role):
            pass

    def evac_mres(self, b, dc, sb):
        self.cp("dve", self.mres[:, dc, :], self.psb(b), [self.PB[b]], [self.Bmres[dc]])
        self.stat_add(sb, self.mres[:, dc, :], [self.Bmres[dc]], dc)

    def make_slots(self, name, n, *free):
        aps = [self.A.alloc(BF16, *free) for _ in range(n)]
        bufs = [Buf("%s%d" % (name, i)) for i in range(n)]
        return {"aps": aps, "bufs": bufs, "i": 0, "n": n}

    def load_slot(self, slots, src):
        i = slots["i"]
        slots["i"] = (i + 1) % slots["n"]
        ap, bf = slots["aps"][i], slots["bufs"][i]
        flat = ap
        if len(ap.shape) == 3:
            flat = ap.rearrange("p a b -> p (a b)")
        self.dma_cast(flat, src, [], [bf])
        return ap, bf

    def stream(self, name, n, free, srcs):
        return Stream(self, name, n, free, srcs)

    def stop(self, tag):
        if tag in self.dbg:
            raise StopBuild()

    def layer(self, i):
        try:
            self._layer(i)
        except StopBuild:
            pass

    def new_phase(self):
        self.flush_mm()
        self.A.reset(self.phase_mark)
        f = {}
        for e in ("pe", "act", "dve", "pool"):
            for o in reversed(self.P.ops[e]):
                if not o.is_dma:
                    f[e] = o
                    break
        Buf.fence = f

    def _layer(self, i):
        kind, j = i % 3, i // 3
        A = self.A
        self.new_phase()
        if "prenorm" in self.dbg:
            self.pre_norm(0, ("mix_pre", i))
            return
        if "nomix" in self.dbg:
            pass
        elif kind == 0:
            self.mixer_a(i, j)
        elif kind == 1:
            self.mixer_b(i)
        else:
            self.mixer_c(i)
        self.new_phase()
        if "noffn" not in self.dbg:
            self.ffn_phase(i)

    def ffn_phase(self, i):
        A = self.A
        actb = A.alloc(BF16, NFC, T)
        Bact = [Buf("act%d" % f) for f in range(NFC)]
        wupS = self.stream("wup", 4, (8, 256), [self.d["wup%d" % i][fc] for _t in range(NT) for fc in range(NFC)])
        wdnS = self.stream("wdn", 2, (NFC, 128), [self.d["wdn%d" % i][dc] for _t in range(NT) for dc in range(8)])
        wgS = self.stream("wg", 2, (8, 128), [self.d["wg%d" % i][dc] for _t in range(NT) for dc in range(8)])
        cg = [A.alloc(F32, T) for _ in range(2)]
        cv = [A.alloc(F32, T) for _ in range(2)]
        sg = [A.alloc(F32, T) for _ in range(2)]
        Bcg = [Buf("cg%d" % k) for k in range(2)]
        Bcv = [Buf("cv%d" % k) for k in range(2)]
        Bsg = [Buf("sg%d" % k) for k in range(2)]
        halo = [A.alloc(F32, 2 * NFC, 2) for _ in range(2)]
        Bhalo = [[Buf("halo%d_%d" % (k, q)) for q in range(2 * NFC)] for k in range(2)]
        bnd = [A.alloc(F32, 3, 2 * NFC) for _ in range(2)]
        Bbnd = [Buf("bnd%d" % k) for k in range(2)]
        hb = A.alloc(BF16, 8, T)
        Bhb = [Buf("hb%d" % c) for c in range(8)]
        pt = [A.alloc(BF16, 2, T) for _ in range(2)]
        Bpt = [Buf("pt%d" % k) for k in range(2)]
        wp = A.alloc(BF16, 2, D)
        Bwp = Buf("wp")
        gate = [A.alloc(F32, T) for _ in range(2)]
        Bgate = [Buf("gate%d" % k) for k in range(2)]
        self.dma_cast(wp.rearrange("p a b -> p (a b)"), self.d["wp%d" % i], [], [Bwp])
        cbase = self.poff[("conv", i)]

        def tap(jj, q):
            o = cbase + jj * 44 + q
            return self.ptab[:, o:o + 1]

        def step(bg):
            for g in list(bg):
                try:
                    next(g)
                except StopIteration:
                    bg.remove(g)

        def drain(bg):
            while bg:
                step(bg)

        def load_pt(t):
            ptt, Bptt = pt[t % 2], Bpt[t % 2]
            tsl = slice(t * T, (t + 1) * T)
            for kc in range(2):
                self.dma_cast(ptt[:, kc, :], self.d["pT%d" % i][kc][:, tsl], [], [Bptt])

        def stage_A(t):
            return self.pre_norm_gen(t, ("ffn_pre", i), k=t % 2)

        def stage_B(t, bg):
            hn, Bhn = self.hns[t % 2], self.Bhns[t % 2]
            if t > 0:
                ho, Bho = halo[(t - 1) % 2], Bhalo[(t - 1) % 2]
                bd, Bbd = bnd[t % 2], Bbnd[t % 2]
                W0 = self.ptab[:, cbase:cbase + 44]
                W1 = self.ptab[:, cbase + 44:cbase + 88]
                self.tt("dve", bd[:, 2, :], ho[:, :, 1], W1, ALU.mult, Bho + [self.Bptab], [Bbd])
                self.tt("dve", bd[:, 0, :], ho[:, :, 0], W0, ALU.mult, Bho + [self.Bptab], [Bbd])
                self.tt("dve", bd[:, 0, :], bd[:, 0, :], bd[:, 2, :], ALU.add, [Bbd], [Bbd])
                self.tt("dve", bd[:, 1, :], ho[:, :, 1], W0, ALU.mult, Bho + [self.Bptab], [Bbd])
            for fc in range(NFC):
                w, Bw = wupS.next()
                k2 = fc % 2
                bg_ = self.bank()
                bv_ = self.bank()
                for kc in range(8):
                    self.mm(self.psb(bg_), w[:, kc, 0:128], hn[:, kc, :], kc == 0, kc == 7, [Bw, Bhn[kc]], [self.PB[bg_]])
                for kc in range(8):
                    self.mm(self.psb(bv_), w[:, kc, 128:256], hn[:, kc, :], kc == 0, kc == 7, [Bw, Bhn[kc]], [self.PB[bv_]])
                for (b, q, cbuf, Bc) in ((bg_, fc, cg[k2], Bcg[k2]), (bv_, NFC + fc, cv[k2], Bcv[k2])):
                    pb = self.PB[b]
                    if t == 0:
                        self.act(cbuf, self.psb(b), AF.Copy, [pb, self.Bptab], [Bc], scale=tap(2, q))
                    else:
                        bd, Bbd = bnd[t % 2], Bbnd[t % 2]
                        self.act(cbuf[:, 2:T], self.psb(b, 2, T), AF.Copy, [pb, self.Bptab], [Bc], scale=tap(2, q))
                        self.act(cbuf[:, 0:1], self.psb(b, 0, 1), AF.Identity, [pb, self.Bptab, Bbd], [Bc],
                                 scale=tap(2, q), bias=bd[:, 0, q:q + 1])
                        self.act(cbuf[:, 1:2], self.psb(b, 1, 2), AF.Identity, [pb, self.Bptab, Bbd], [Bc],
                                 scale=tap(2, q), bias=bd[:, 1, q:q + 1])
                    if t < NT - 1:
                        self.cp("act", halo[t % 2][:, q, 0:2], self.psb(b, T - 2, T), [pb], [Bhalo[t % 2][q]])
                    self.stt(cbuf[:, 1:T], self.psb(b, 0, T - 1), tap(1, q), cbuf[:, 1:T], ALU.mult, ALU.add,
                             [pb, Bc, self.Bptab], [Bc])
                    self.stt(cbuf[:, 2:T], self.psb(b, 0, T - 2), tap(0, q), cbuf[:, 2:T], ALU.mult, ALU.add,
                             [pb, Bc, self.Bptab], [Bc])
                self.act(sg[k2], cg[k2], AF.Silu, [Bcg[k2]], [Bsg[k2]])
                self.tt("pool", actb[:, fc, :], sg[k2], cv[k2], ALU.mult, [Bsg[k2], Bcv[k2]], [Bact[fc]])
                step(bg)
            drain(bg)

        def stage_C(t, bg):
            sb = self.statbank()
            for dc in range(8):
                w, Bw = wdnS.next()
                b = self.bank()
                for fc in range(NFC):
                    self.mm(self.psb(b), w[:, fc, :], actb[:, fc, :], fc == 0, fc == NFC - 1, [Bw, Bact[fc]], [self.PB[b]])
                step(bg)
                self.evac_mres(b, dc, sb)
            drain(bg)
            return sb

        def stage_D(t, sb):
            tsl = slice(t * T, (t + 1) * T)

            def after(c):
                self.cp("act", hb[:, c, :], self.h[:, c, tsl], [self.Bh[c][t]], [Bhb[c]])
            g = self.post_norm_gen(t, ("ffn_post", i), sb, 1, after=after)
            next(g)
            return g

        def stage_E(t):
            ptt, Bptt = pt[t % 2], Bpt[t % 2]
            if t + 1 < NT:
                load_pt(t + 1)
            sb = self.statbank()
            for dc in range(8):
                w, Bw = wgS.next()
                bgt = self.bank()
                be = self.bank()
                for kc in range(8):
                    self.mm(self.psb(bgt), w[:, kc, :], hb[:, kc, :], kc == 0, kc == 7, [Bw, Bhb[kc]], [self.PB[bgt]])
                for kc in range(2):
                    self.mm(self.psb(be), wp[:, kc, dc * 128:(dc + 1) * 128], ptt[:, kc, :], kc == 0, kc == 1,
                            [Bwp, Bptt], [self.PB[be]])
                k2 = dc % 2
                self.act(gate[k2], self.psb(bgt), AF.Sigmoid, [self.PB[bgt]], [Bgate[k2]])
                self.tt("dve", self.mres[:, dc, :], gate[k2], self.psb(be), ALU.mult, [Bgate[k2], self.PB[be]],
                        [self.Bmres[dc]])
                self.stat_add(sb, self.mres[:, dc, :], [self.Bmres[dc]], dc)
            return sb

        load_pt(0)
        drain([stage_A(0)])
        stage_B(0, [stage_A(1)])
        pend = []
        for t in range(NT):
            sb = stage_C(t, pend)
            pend = []
            bgl = [stage_D(t, sb)]
            if t + 2 < NT:
                bgl.append(stage_A(t + 2))
            if t + 1 < NT:
                stage_B(t + 1, bgl)
            else:
                drain(bgl)
            sbe = stage_E(t)
            g = self.post_norm_gen(t, ("ple_g", i), sbe, 2)
            next(g)
            pend = [g]
        drain(pend)
        self.flush_mm()

    def mixer_c(self, i):
        A = self.A
        ybuf = A.alloc(BF16, 8, 30 + T)
        Byb = [Buf("yb%d" % c) for c in range(8)]
        z = A.alloc(F32, 8, T)
        Bz = [Buf("z%d" % c) for c in range(8)]
        zb = [A.alloc(BF16, T) for _ in range(2)]
        Bzb = [Buf("zb%d" % k) for k in range(2)]
        actc = A.alloc(BF16, 8, T)
        Bac = [Buf("actc%d" % c) for c in range(8)]
        dg = [A.alloc(BF16, 31, 128) for _ in range(2)]
        Bdg = [Buf("dg%d" % k) for k in range(2)]
        wi = self.stream("cwi", 3, (8, 256), [self.d["c_wi"][c] for _t in range(NT) for c in range(8)])
        wo = self.stream("cwo", 3, (8, 128), [self.d["c_wo"][c] for _t in range(NT) for c in range(8)])
        sgm = [A.alloc(F32, T) for _ in range(2)]
        Bsgm = [Buf("sgm%d" % k) for k in range(2)]
        mean = A.alloc(F32, T)
        msq = A.alloc(F32, T)
        var = A.alloc(F32, T)
        nmr = A.alloc(F32, T)
        Bmean, Bmsq, Bvar, Bnmr = Buf("mean"), Buf("msq"), Buf("var"), Buf("nmr")
        t1 = [A.alloc(F32, T) for _ in range(2)]
        Bt1 = [Buf("ct1_%d" % k) for k in range(2)]
        for c in range(8):
            self.memset("pool", ybuf[:, c, 0:30], 0.0, [Byb[c]])
        for t in range(NT):
            self.pre_norm(t, ("mix_pre", i))
            sb1 = self.statbank()
            sb2 = self.statbank()
            for c in range(8):
                w, Bw = wi.next()
                ba = self.bank()
                bg = self.bank()
                for kc in range(8):
                    self.mm(self.psb(ba), w[:, kc, 0:128], self.hn[:, kc, :], kc == 0, kc == 7, [Bw, self.Bhn[kc]], [self.PB[ba]])
                for kc in range(8):
                    self.mm(self.psb(bg), w[:, kc, 128:256], self.hn[:, kc, :], kc == 0, kc == 7, [Bw, self.Bhn[kc]], [self.PB[bg]])
                k2 = c % 2
                self.act(sgm[k2], self.psb(bg), AF.Sigmoid, [self.PB[bg]], [Bsgm[k2]])
                self.tt("dve", ybuf[:, c, 30:30 + T], sgm[k2], self.psb(ba), ALU.mult, [Bsgm[k2], self.PB[ba]], [Byb[c]])
                o_ = self.poff["c_dw"] + c * 31
                in1 = self.ptab[:, o_:o_ + 31].unsqueeze(2).to_broadcast([128, 31, 128])
                in0 = self.ident.unsqueeze(1).to_broadcast([128, 31, 128])
                self.tt("dve", dg[k2], in0, in1, ALU.mult, [self.Bconst, self.Bptab], [Bdg[k2]])
                bc = self.bank()
                for jj in range(31):
                    self.mm(self.psb(bc), dg[k2][:, jj, :], ybuf[:, c, jj:jj + T], jj == 0, jj == 30, [Bdg[k2], Byb[c]], [self.PB[bc]])
                bias = self.pcol("c_dwb", c)
                self.act(z[:, c, :], self.psb(bc), AF.Identity, [self.PB[bc], self.Bptab], [Bz[c]], bias=bias)
                self.act(zb[k2], self.psb(bc), AF.Identity, [self.PB[bc], self.Bptab], [Bzb[k2]], bias=bias)
                self.flush_mm()
                self.mm(self.psb(sb1), self.onesb, zb[k2], c == 0, c == 7, [Bzb[k2], self.Bconst], [self.PB[sb1]])
                sq, Bsq = self.nextsq()
                self.act(sq, self.psb(bc), AF.Square, [self.PB[bc], self.Bptab], [Bsq], bias=bias)
                self.defer_mm(self.psb(sb2), self.onesb, sq, c == 0, c == 7, [Bsq, self.Bconst], [self.PB[sb2]])
                if t < NT - 1:
                    self.cp("pool", ybuf[:, c, 0:30], ybuf[:, c, T:T + 30], [Byb[c]], [Byb[c]])
            self.flush_mm()
            self.ts("dve", mean, self.psb(sb1), 1.0 / D, None, ALU.mult, None, [self.PB[sb1]], [Bmean])
            self.act(msq, mean, AF.Square, [Bmean], [Bmsq])
            self.stt(var, self.psb(sb2), 1.0 / D, msq, ALU.mult, ALU.subtract, [self.PB[sb2], Bmsq], [Bvar])
            self.act(self.rstd, var, AF.Sqrt, [Bvar, self.Bconst], [self.Brstd], bias=self.epsc)
            self.recip(self.rstd, self.rstd, [self.Brstd], [self.Brstd])
            self.stt(nmr, mean, -1.0, self.rstd, ALU.mult, ALU.mult, [Bmean, self.Brstd], [Bnmr])
            for c in range(8):
                k2 = c % 2
                self.tt("dve", t1[k2], z[:, c, :], self.rstd, ALU.mult, [Bz[c], self.Brstd], [Bt1[k2]])
                self.tt("pool", t1[k2], t1[k2], nmr, ALU.add, [Bt1[k2], Bnmr], [Bt1[k2]])
                self.act(actc[:, c, :], t1[k2], AF.Silu, [Bt1[k2], self.Bptab], [Bac[c]],
                         scale=self.pcol("c_lng", c), bias=self.pcol("c_lnb", c))
            sb = self.statbank()
            for dc in range(8):
                w, Bw = wo.next()
                b = self.bank()
                for c in range(8):
                    self.mm(self.psb(b), w[:, c, :], actc[:, c, :], c == 0, c == 7, [Bw, Bac[c]], [self.PB[b]])
                self.evac_mres(b, dc, sb)
            self.post_norm_residual(t, ("mix_post", i), sb)

    def mixer_a(self, i, j):
        A = self.A
        u = A.alloc(BF16, 16, T)
        Bu = [Buf("u%d" % c) for c in range(16)]
        vgb = A.alloc(BF16, 4, 2048)
        Bvgb = [Buf("vgb%d" % b) for b in range(4)]
        wv = self.stream("awv", 2, (8, 512), [self.d["a_wv%d" % j][q] for _t in range(NT) for q in range(4)])
        wu = self.stream("awu", 3, (8, 128), [self.d["a_wu%d" % j][q] for _t in range(NT) for q in range(16)])
        wo = self.stream("awo", 2, (16, 128), [self.d["a_wo%d" % j][q] for _t in range(NT) for q in range(8)])
        vgc = [A.alloc(F32, 512) for _ in range(3)]
        Bvgc = [Buf("vgc%d" % k) for k in range(3)]
        bst = A.alloc(F32, 4, 4, 6)
        Bbst = [Buf("bst%d" % b) for b in range(4)]
        mv = A.alloc(F32, 4, 2)
        rs = A.alloc(F32, 4, 1)
        vpe = A.alloc(F32, 4, 1)
        Bmv = [Buf("mv%d" % b) for b in range(4)]
        Brs = [Buf("rs%d" % b) for b in range(4)]
        Bvpe = [Buf("vpe%d" % b) for b in range(4)]
        nmh = A.alloc(F32, 1)
        wsT = A.alloc(BF16, 8, 128)
        Bws = Buf("wsT")
        Cc = A.alloc(F32, 16, 128)
        BCc = Buf("Cc")
        bsb = A.alloc(F32, 8, 128)
        Bbsb = Buf("bsb")
        rw = A.alloc(F32, 8, 128)
        Brw = Buf("rw")
        t1 = [A.alloc(F32, 128) for _ in range(3)]
        Bt1 = [Buf("at1_%d" % k) for k in range(3)]
        self.memset("pool", nmh, -0.5, [self.Bconst])
        self.dma_cast(wsT.rearrange("p a b -> p (a b)"), self.d["a_ws%d" % j], [], [Bws])
        self.memset("pool", wsT[64:128, :, 0:64], 0.0, [Bws])
        self.dma_plain(bsb.rearrange("p a b -> p (a b)"), self.d["a_bs%d" % j], [], [Bbsb])
        b0 = self.bank()
        b1 = self.bank()
        for g in range(8):
            bb = b0 if g < 4 else b1
            lo = (g % 4) * 128
            self.mm(self.psb(bb, lo, lo + 128), self.onesb, wsT[:, g, :], True, True, [Bws, self.Bconst], [self.PB[bb]])
        self.cp("act", rw[:, 0:4, :].rearrange("p a b -> p (a b)"), self.psb(b0), [self.PB[b0]], [Brw])
        self.cp("act", rw[:, 4:8, :].rearrange("p a b -> p (a b)"), self.psb(b1), [self.PB[b1]], [Brw])
        for uc in range(16):
            g = uc // 2
            self.stt(Cc[:, uc, :], rw[:, g, :], self.pcol(("a_lnb", j), uc), bsb[:, g, :], ALU.mult, ALU.add,
                     [Brw, Bbsb, self.Bptab], [BCc])
        for t in range(NT):
            self.pre_norm(t, ("mix_pre", i))
            for uc in range(16):
                w, Bw = wu.next()
                b = self.bank()
                for kc in range(8):
                    self.mm(self.psb(b), w[:, kc, :], self.hn[:, kc, :], kc == 0, kc == 7, [Bw, self.Bhn[kc]], [self.PB[b]])
                self.act(u[:, uc, :], self.psb(b), AF.Gelu_apprx_tanh, [self.PB[b]], [Bu[uc]])
            kk = 0
            for vq in range(4):
                w, Bw = wv.next()
                for blk in range(4):
                    b = self.bank()
                    for kc in range(8):
                        self.mm(self.psb(b), self.hn[:, kc, blk * 128:(blk + 1) * 128], w[:, kc, :], kc == 0, kc == 7,
                                [Bw, self.Bhn[kc]], [self.PB[b]])
                    k3 = kk % 3
                    kk += 1
                    self.act(vgc[k3], self.psb(b), AF.Gelu_apprx_tanh, [self.PB[b]], [Bvgc[k3]])
                    bo = bst[:, blk, vq, :]
                    src = vgc[k3]
                    self.P.op("dve", lambda e, bo=bo, src=src: e.bn_stats(out=bo, in_=src), [Bvgc[k3]], [Bbst[blk]])
                    self.cp("pool", vgb[:, blk, vq * 512:(vq + 1) * 512], vgc[k3], [Bvgc[k3]], [Bvgb[blk]])
            for blk in range(4):
                mo = mv[:, blk, :]
                bi = bst[:, blk, :, :]
                self.P.op("dve", lambda e, mo=mo, bi=bi: e.bn_aggr(out=mo, in_=bi), [Bbst[blk]], [Bmv[blk]])
                self.ts("pool", vpe[:, blk, :], mv[:, blk, 1:2], EPS, None, ALU.add, None, [Bmv[blk]], [Bvpe[blk]])
                self.tt("pool", rs[:, blk, :], vpe[:, blk, :], nmh, ALU.pow, [Bvpe[blk], self.Bconst], [Brs[blk]])
                self.ts("dve", vgb[:, blk, :], vgb[:, blk, :], mv[:, blk, 0:1], rs[:, blk, :], ALU.subtract, ALU.mult,
                        [Bvgb[blk], Bmv[blk], Brs[blk]], [Bvgb[blk]])
            kk = 0
            for blk in range(4):
                bsl = slice(blk * 128, (blk + 1) * 128)
                for uc in range(16):
                    b = self.bank()
                    self.mm(self.psb(b, 0, 128), vgb[:, blk, uc * 128:(uc + 1) * 128], wsT[:, uc // 2, :], True, True,
                            [Bvgb[blk], Bws], [self.PB[b]])
                    k3 = kk % 3
                    kk += 1
                    self.stt(t1[k3], self.psb(b, 0, 128), self.pcol(("a_lng", j), uc), Cc[:, uc, :], ALU.mult, ALU.add,
                             [self.PB[b], BCc, self.Bptab], [Bt1[k3]])
                    self.tt("dve", u[:, uc, bsl], t1[k3], u[:, uc, bsl], ALU.mult, [Bt1[k3], Bu[uc]], [Bu[uc]])
            sb = self.statbank()
            for dc in range(8):
                w, Bw = wo.next()
                b = self.bank()
                for uc in range(16):
                    self.mm(self.psb(b), w[:, uc, :], u[:, uc, :], uc == 0, uc == 15, [Bw, Bu[uc]], [self.PB[b]])
                self.evac_mres(b, dc, sb)
            self.post_norm_residual(t, ("mix_post", i), sb)

    def mixer_b(self, i):
        A = self.A
        kT = A.alloc(BF16, 8, 1024)
        BkT = [[Buf("kT%d_%d" % (c, s_)) for s_ in range(2)] for c in range(8)]
        V = A.alloc(BF16, 8, 1024)
        BV = [Buf("V%d" % b) for b in range(8)]
        qz = A.alloc(BF16, 8, 2, T)
        Bq = [Buf("qz%d" % c) for c in range(8)]
        Bb = A.alloc(BF16, 16, 640)
        BBb = Buf("Bb")
        A2 = Arena(self.sb_all, self.hn1_off, self.hn1_off + 8 * T * 2)
        Pb = [A2.alloc(BF16, 640) for _ in range(2)]
        BPb = [Buf("Pb%d" % k) for k in range(2)]
        PTb = [A2.alloc(BF16, 640) for _ in range(3)]
        BPTb = [Buf("PTb%d" % k) for k in range(3)]
        dgr = [A2.alloc(BF16, 128) for _ in range(3)]
        Bdgr = [Buf("dgr%d" % k) for k in range(3)]
        st3 = [A.alloc(F32, 4) for _ in range(3)]
        Bst = [Buf("st%d" % k) for k in range(3)]
        wqk = self.stream("bwqk", 2, (8, 256), [self.d["b_wqk"][q] for _t in range(NT) for q in range(8)])
        wvs = self.stream("bwv", 2, (8, 256), [self.d["b_wv"][q] for _t in range(NT) for q in range(4)])
        wos = self.stream("bwo", 2, (8, 128), [self.d["b_wo"][q] for _t in range(NT) for q in range(8)])
        oT, BoT = self.hn, self.Bhn
        ps = self.ps
        self.dma_cast(Bb.rearrange("p a b -> p (a b)"), self.d["b_bias"], [], [BBb])
        self.memset("pool", Bb[64:128, :, 0:64], NEG, [BBb])
        self.memset("pool", Bb[0:64, :, 576:640], NEG, [BBb])
        for c in range(8):
            self.memset("pool", qz[:, c, :, :], 0.0, [Bq[c]])
        unit = 0
        for t in range(NT):
            slot = t % 2
            self.pre_norm(t, ("mix_pre", i))
            for c in range(8):
                w, Bw = wqk.next()
                bq = self.bank()
                bk = self.bank()
                for kc in range(8):
                    self.mm(self.psb(bq), w[:, kc, 0:128], self.hn[:, kc, :], kc == 0, kc == 7, [Bw, self.Bhn[kc]], [self.PB[bq]])
                for kc in range(8):
                    self.mm(self.psb(bk), w[:, kc, 128:256], self.hn[:, kc, :], kc == 0, kc == 7, [Bw, self.Bhn[kc]], [self.PB[bk]])
                for hh in range(2):
                    rows = slice(hh * 64, hh * 64 + 64)
                    self.act(qz[rows, c, hh, :], ps[rows, bq * 512:(bq + 1) * 512], AF.Copy, [self.PB[bq]], [Bq[c]], scale=0.125)
                self.cp("dve", kT[:, c, slot * 512:(slot + 1) * 512], self.psb(bk), [self.PB[bk]], [BkT[c][slot]])
            for qt in range(4):
                w, Bw = wvs.next()
                for blk in range(4):
                    b = self.bank()
                    for kc in range(8):
                        self.mm(self.psb(b, 0, 256), self.hn[:, kc, blk * 128:(blk + 1) * 128], w[:, kc, :], kc == 0, kc == 7,
                                [Bw, self.Bhn[kc]], [self.PB[b]])
                    rb = (4 * t + blk) % 8
                    self.cp("act" if (blk % 2) else "dve", V[:, rb, qt * 256:(qt + 1) * 256], self.psb(b, 0, 256), [self.PB[b]], [BV[rb]])
            units = [(c, jq, hh) for c in range(8) for jq in range(4) for hh in range(2)]
            NU = len(units)

            def geom(u):
                c, jq, hh = units[u]
                jb = 4 * t + jq
                return c, jq, hh, jb, max(0, 4 - jb)

            def st_scores(u):
                c, jq, hh, jb, i0 = geom(u)
                hd = 2 * c + hh
                sl = u % 2
                k3 = u % 3
                sbase = sl * 1024
                for ii in range(i0, 5):
                    kb = jb - 4 + ii
                    rc = (kb % 8) * 128
                    ks = (kb // 4) % 2
                    sap = ps[:, sbase + ii * 128: sbase + (ii + 1) * 128]
                    wb = self.PB[2 * sl] if ii < 4 else self.PB[2 * sl + 1]
                    self.mm(sap, qz[:, c, hh, jq * 128:(jq + 1) * 128], kT[:, c, rc:rc + 128], True, False,
                            [Bq[c], BkT[c][ks]], wb)
                    self.mm(sap, self.ident, Bb[:, hd, ii * 128:(ii + 1) * 128], False, True, [self.Bconst, BBb], wb)
                sbufs = [self.PB[2 * sl], self.PB[2 * sl + 1]]
                sfull = ps[:, sbase + i0 * 128: sbase + 640]
                nmax, rsum, rinv = st3[k3][:, 0:1], st3[k3][:, 1:2], st3[k3][:, 2:3]
                self.P.op("dve", lambda e, nmax=nmax, sfull=sfull: e.tensor_reduce(out=nmax, in_=sfull, axis=AX.X, op=ALU.max, negate=True),
                          sbufs, [Bst[k3]])
                self.act(Pb[sl][:, i0 * 128:640], sfull, AF.Exp, sbufs + [Bst[k3]], [BPb[sl], Bst[k3]], bias=nmax, accum=rsum)
                self.recip(rinv, rsum, [Bst[k3]], [Bst[k3]])
                self.ts("dve", dgr[k3], self.ident, rinv, None, ALU.mult, None, [self.Bconst, Bst[k3]], [Bdgr[k3]])

            def st_pt(u):
                c, jq, hh, jb, i0 = geom(u)
                sl = u % 2
                k3 = u % 3
                pbase = 2048
                for ii in range(i0, 5):
                    self.mm(ps[:, pbase + ii * 128: pbase + (ii + 1) * 128], Pb[sl][:, ii * 128:(ii + 1) * 128], dgr[k3], True, True,
                            [BPb[sl], Bdgr[k3]], [self.PB[4] if ii < 4 else self.PB[5]])
                self.cp("act" if (u % 2) else "dve", PTb[k3][:, i0 * 128:640], ps[:, pbase + i0 * 128: pbase + 640],
                        [self.PB[4], self.PB[5]], [BPTb[k3]])

            def st_pv(u):
                c, jq, hh, jb, i0 = geom(u)
                k3 = u % 3
                ob = 6 + hh
                for ii in range(i0, 5):
                    kb = jb - 4 + ii
                    self.mm(ps[:, ob * 512 + jq * 128: ob * 512 + (jq + 1) * 128], V[:, kb % 8, c * 128:(c + 1) * 128],
                            PTb[k3][:, ii * 128:(ii + 1) * 128], ii == i0, ii == 4, [BV[kb % 8], BPTb[k3]], [self.PB[ob]])
                if jq == 3 and hh == 1:
                    for h2 in range(2):
                        rows = slice(h2 * 64, h2 * 64 + 64)
                        o2 = 6 + h2
                        self.cp("dve" if h2 else "act", oT[rows, c, :], ps[rows, o2 * 512:(o2 + 1) * 512], self.PB[o2], [BoT[c]])

            for u in range(NU + 2):
                if u < NU:
                    st_scores(u)
                if 0 <= u - 1 < NU:
                    st_pt(u - 1)
                if 0 <= u - 2 < NU:
                    st_pv(u - 2)
            sb = self.statbank()
            for dc in range(8):
                w, Bw = wos.next()
                b = self.bank()
                for c in range(8):
                    self.mm(self.psb(b), w[:, c, :], oT[:, c, :], c == 0, c == 7, [Bw, BoT[c]], [self.PB[b]])
                self.evac_mres(b, dc, sb)
            self.post_norm_residual(t, ("mix_post", i), sb)

    def store_output(self):
        for c in range(8):
            self.dma_plain(self.yT[c], self.h[:, c, :], self.Bh[c], [Buf("y%d" % c)], is_output=True)


def _cols(v, n):
    return np.ascontiguousarray(np.asarray(v, np.float32).reshape(n, 128).T)


def _kc_tile(w, ncols_per_block):
    K, N = w.shape
    nb = N // ncols_per_block
    x = w.reshape(K // 128, 128, nb, ncols_per_block)
    x = x.transpose(2, 1, 0, 3)
    return np.ascontiguousarray(x).reshape(nb, 128, (K // 128) * ncols_per_block)


def host_shared(inp, layers):
    off, R = ptab_layout()
    ptab = np.zeros((128, R), np.float32)

    def put(key, arr):
        ptab[:, off[key]:off[key] + arr.shape[1]] = arr

    for i in range(DEPTH):
        put(("mix_pre", i), _cols(inp["mix_pre_g"][i], 8))
        put(("mix_post", i), _cols(inp["mix_post_g"][i], 8))
        put(("ffn_pre", i), _cols(inp["ffn_pre_g"][i], 8))
        put(("ffn_post", i), _cols(inp["ffn_post_g"][i], 8))
        put(("ple_g", i), _cols(inp["ple_norm_g"][i], 8))
        cv = np.concatenate([_cols(inp["ffn_conv"][i][jj], 44) for jj in range(3)], axis=1)
        put(("conv", i), cv)
    for j in range(2):
        put(("a_lng", j), _cols(inp["a_ln_g"][j], 16))
        put(("a_lnb", j), _cols(inp["a_ln_b"][j], 16))
    dw = np.asarray(inp["c_dw"][0], np.float32)
    dwc = dw.reshape(31, 8, 128).transpose(2, 1, 0).reshape(128, 248)
    put("c_dw", np.ascontiguousarray(dwc))
    put("c_dwb", _cols(inp["c_dw_b"][0], 8))
    put("c_lng", _cols(inp["c_ln_g"][0], 8))
    put("c_lnb", _cols(inp["c_ln_b"][0], 8))
    sh = {"ptab": ptab, "ident": np.eye(128, dtype=np.float32)}
    for i in layers:
        wu = np.asarray(inp["ffn_w_up"][i], np.float32)
        g = wu[:, :FF].reshape(D, NFC, 128)
        v = wu[:, FF:].reshape(D, NFC, 128)
        gv = np.concatenate([g, v], axis=2).reshape(D, NFC * 256)
        sh["wup%d" % i] = _kc_tile(gv, 256)
        wd = np.asarray(inp["ffn_w_down"][i], np.float32)
        x = wd.reshape(NFC, 128, 8, 128).transpose(2, 1, 0, 3)
        sh["wdn%d" % i] = np.ascontiguousarray(x).reshape(8, 128, NFC * 128)
        sh["wg%d" % i] = _kc_tile(np.asarray(inp["ple_w_gate"][i], np.float32), 128)
        wp = np.asarray(inp["ple_w_proj"][i], np.float32)
        sh["wp%d" % i] = np.ascontiguousarray(wp.reshape(2, 128, D).transpose(1, 0, 2)).reshape(128, 2 * D)
        kind, j = i % 3, i // 3
        if kind == 0:
            win = np.asarray(inp["a_w_in"][j], np.float32)
            sh["a_wu%d" % j] = _kc_tile(win[:, :2048], 128)
            sh["a_wv%d" % j] = _kc_tile(win[:, 2048:], 512)
            wo = np.asarray(inp["a_w_out"][j], np.float32)
            x = wo.reshape(16, 128, 8, 128).transpose(2, 1, 0, 3)
            sh["a_wo%d" % j] = np.ascontiguousarray(x).reshape(8, 128, 16 * 128)
            ws = np.asarray(inp["a_w_s"][j], np.float32)
            sh["a_ws%d" % j] = np.ascontiguousarray(ws.transpose(2, 0, 1)).reshape(128, 8 * 128)
            bs = np.asarray(inp["a_b_s"][j], np.float32).reshape(1, 8 * 128)
            sh["a_bs%d" % j] = np.ascontiguousarray(np.broadcast_to(bs, (128, 8 * 128)))
        elif kind == 1:
            wq = np.asarray(inp["b_w_qkv"][0], np.float32)
            q = wq[:, :D].reshape(D, 8, 128)
            k = wq[:, D:2 * D].reshape(D, 8, 128)
            qk = np.concatenate([q, k], axis=2).reshape(D, 8 * 256)
            sh["b_wqk"] = _kc_tile(qk, 256)
            sh["b_wv"] = _kc_tile(np.ascontiguousarray(wq[:, 2 * D:]), 256)
            wo = np.asarray(inp["b_w_out"][0], np.float32)
            x = wo.reshape(8, 128, 8, 128).transpose(2, 1, 0, 3)
            sh["b_wo"] = np.ascontiguousarray(x).reshape(8, 128, 8 * 128)
            rb = np.asarray(inp["b_rel_bias"][0], np.float32)
            qq = np.arange(128)[:, None]
            kk = np.arange(640)[None, :]
            idx = np.clip(qq + 512 - kk, -128, 128) + 128
            bfull = rb[:, idx]
            sh["b_bias"] = np.ascontiguousarray(bfull.transpose(1, 0, 2)).reshape(128, 16 * 640)
        else:
            wi = np.asarray(inp["c_w_in"][0], np.float32)
            a = wi[:, :D].reshape(D, 8, 128)
            g = wi[:, D:].reshape(D, 8, 128)
            ag = np.concatenate([a, g], axis=2).reshape(D, 8 * 256)
            sh["c_wi"] = _kc_tile(ag, 256)
            wo = np.asarray(inp["c_w_out"][0], np.float32)
            x = wo.reshape(8, 128, 8, 128).transpose(2, 1, 0, 3)
            sh["c_wo"] = np.ascontiguousarray(x).reshape(8, 128, 8 * 128)
    return sh


def run_layers(hT_in, p, shared, layers, trace=False, dbg=()):
    nc = Builder(layers, dbg).build()
    in_maps = []
    for b in range(8):
        m = dict(shared)
        m["xT"] = hT_in[b]
        for i in layers:
            m["pT%d" % i] = np.ascontiguousarray(p[i, b].T).reshape(2, 128, S)
        in_maps.append(m)
    res = run_bass_kernel_spmd(nc, in_maps, core_ids=list(range(8)), trace=trace)
    out = np.stack([res.results[b]["yT"] for b in range(8)])
    return out, res


def kernel(**inputs):
    inp = {k: np.asarray(v) for k, v in inputs.items()}
    layers = list(range(DEPTH))
    x = inp["x"].astype(np.float32, copy=False)
    hT = np.ascontiguousarray(x.transpose(0, 2, 1)).reshape(8, 8, 128, S)
    shared = host_shared(inp, layers)
    out, _ = run_layers(hT, inp["p"].astype(np.float32, copy=False), shared, layers)
    y = out.reshape(8, D, S).transpose(0, 2, 1)
    return np.ascontiguousarray(y).astype(np.float32, copy=False)
```

```python
import numpy as np
from contextlib import ExitStack
import concourse.bass as bass
import concourse.mybir as mybir
from concourse.bass_utils import run_bass_kernel_spmd

F32 = mybir.dt.float32
BF16 = mybir.dt.bfloat16
AF = mybir.ActivationFunctionType
ALU = mybir.AluOpType
AX = mybir.AxisListType

D = 1024
S = 2048
T = 512
NT = S // T
FF = 2816
NFC = FF // 128
DEPTH = 4
EPS = 1e-6
NEG = -1e30


class Buf:
    __slots__ = ("name", "last_w", "readers", "dma_readers", "dma_sem", "dma_cnt", "const", "excl")

    fence = {}

    def __init__(self, name, const=False, excl=False):
        self.name = name
        self.excl = excl
        self.last_w = None
        self.readers = dict(Buf.fence)
        self.dma_readers = []
        self.dma_sem = None
        self.dma_cnt = 0
        self.const = const


class Op:
    __slots__ = ("eng", "fn", "deps", "sig", "sigval", "is_dma", "dma_sem", "dma_val", "idx")

    def __init__(self, eng, fn, is_dma):
        self.eng = eng
        self.fn = fn
        self.deps = []
        self.sig = False
        self.sigval = 0
        self.is_dma = is_dma
        self.dma_sem = None
        self.dma_val = 0


class Prog:
    ENGS = ("pe", "act", "dve", "pool", "sp")

    def __init__(self, nc):
        self.nc = nc
        self.ops = {e: [] for e in self.ENGS}
        self.n = 0
        self.dma_bufs = []
        self.out_dma_ops = []

    def _add(self, op, reads, writes):
        deps = []
        for b in reads:
            w = b.last_w
            if w is not None:
                deps.append((w, True))
            if b.excl:
                for e_, r in b.readers.items():
                    if e_ != op.eng:
                        deps.append((r, False))
        for b in writes:
            w = b.last_w
            if w is not None and not (op.is_dma and w.is_dma):
                deps.append((w, False))
            for r in b.readers.values():
                deps.append((r, False))
            for r in b.dma_readers:
                deps.append((r, False))
        seen = set()
        for d, raw in deps:
            if d is op:
                continue
            if (not d.is_dma) and (not op.is_dma) and d.eng == op.eng:
                if op.eng == "pe":
                    continue
            k = id(d)
            if k in seen:
                continue
            seen.add(k)
            op.deps.append(d)
            if not d.is_dma:
                d.sig = True
        for b in writes:
            b.last_w = op
            b.readers = {}
            b.dma_readers = []
        for b in reads:
            if b.const or b in writes:
                continue
            if op.is_dma:
                b.dma_readers.append(op)
            else:
                b.readers[op.eng] = op
        op.idx = self.n
        self.n += 1
        self.ops[op.eng].append(op)
        return op

    @staticmethod
    def _flat(xs):
        out = []
        for x in xs:
            if isinstance(x, (list, tuple)):
                out.extend(Prog._flat(x))
            else:
                out.append(x)
        return out

    def op(self, eng, fn, reads=(), writes=()):
        return self._add(Op(eng, fn, False), self._flat(reads), self._flat(writes))

    def dma(self, eng, fn, reads=(), writes=(), is_output=False):
        o = Op(eng, fn, True)
        reads, writes = self._flat(reads), self._flat(writes)
        dst = writes[0]
        if dst.dma_sem is None:
            dst.dma_sem = "pending"
            self.dma_bufs.append(dst)
        dst.dma_cnt += 16
        o.dma_sem = dst
        o.dma_val = dst.dma_cnt
        self._add(o, list(reads), list(writes))
        if is_output:
            self.out_dma_ops.append(o)
        return o

    def emit(self, stack):
        nc = self.nc
        sems = {}
        for e in ("pe", "act", "dve", "pool"):
            sems[e] = stack.enter_context(nc.semaphore("s_" + e))
        for i, b in enumerate(self.dma_bufs):
            b.dma_sem = stack.enter_context(nc.semaphore("d%d" % i))
        for e in ("pe", "act", "dve", "pool"):
            c = 0
            for o in self.ops[e]:
                if o.is_dma:
                    continue
                if o.sig:
                    c += 1
                    o.sigval = c
        out_ops = self.out_dma_ops

        def run(engh, ename):
            waited = {}

            def wait(sem, val):
                k = id(sem)
                if waited.get(k, 0) >= val:
                    return
                waited[k] = val
                engh.wait_ge(sem, val)

            for o in self.ops[ename]:
                for d in o.deps:
                    if d.is_dma:
                        wait(d.dma_sem.dma_sem, d.dma_val)
                    else:
                        wait(sems[d.eng], d.sigval)
                ins = o.fn(engh)
                if o.is_dma:
                    ins.then_inc(o.dma_sem.dma_sem, 16)
                elif o.sig:
                    ins.then_inc(sems[ename], 1)
            if ename == "sp":
                for o in out_ops:
                    wait(o.dma_sem.dma_sem, o.dma_val)

        block = stack.enter_context(nc.Block())

        @block.tensor
        def _(e):
            run(e, "pe")

        @block.scalar
        def _(e):
            run(e, "act")

        @block.vector
        def _(e):
            run(e, "dve")

        @block.gpsimd
        def _(e):
            run(e, "pool")

        @block.sync
        def _(e):
            run(e, "sp")


def ptab_layout():
    off = {}
    c = 0
    for i in range(DEPTH):
        for nm in ("mix_pre", "mix_post", "ffn_pre", "ffn_post", "ple_g"):
            off[(nm, i)] = c
            c += 8
        off[("conv", i)] = c
        c += 132
    for j in range(2):
        off[("a_lng", j)] = c
        c += 16
        off[("a_lnb", j)] = c
        c += 16
    off["c_dw"] = c
    c += 248
    off["c_dwb"] = c
    c += 8
    off["c_lng"] = c
    c += 8
    off["c_lnb"] = c
    c += 8
    return off, c


class StopBuild(Exception):
    pass


class Stream:
    def __init__(self, B, name, n, free, srcs):
        self.B = B
        self.n = n
        self.srcs = list(srcs)
        self.aps = [B.A.alloc(BF16, *free) for _ in range(n)]
        self.bufs = [Buf("%s%d" % (name, i)) for i in range(n)]
        self.issued = 0
        self.taken = 0
        for _ in range(n - 1):
            self._issue()

    def _issue(self):
        k = self.issued
        if k >= len(self.srcs):
            return
        self.issued += 1
        ap, bf = self.aps[k % self.n], self.bufs[k % self.n]
        flat = ap.rearrange("p a b -> p (a b)") if len(ap.shape) == 3 else ap
        self.B.dma_cast(flat, self.srcs[k], [], [bf])

    def next(self):
        k = self.taken
        self.taken += 1
        self._issue()
        return self.aps[k % self.n], self.bufs[k % self.n]


class Arena:
    def __init__(self, ap_all, base, limit):
        self.all = ap_all
        self.off = base
        self.limit = limit

    def mark(self):
        return self.off

    def reset(self, m):
        self.off = m

    def alloc(self, dtype, *free):
        isz = 4 if dtype == F32 else 2
        n = 1
        for f in free:
            n *= f
        nb = n * isz
        off = (self.off + 63) // 64 * 64
        assert off + nb <= self.limit, ("SBUF arena overflow", off + nb, self.limit)
        self.off = off + nb
        v = self.all[:, off // 2:(off + nb) // 2]
        if dtype == F32:
            v = v.bitcast(F32)
        if len(free) == 2:
            v = v.rearrange("p (a b) -> p a b", a=free[0])
        elif len(free) == 3:
            v = v.rearrange("p (a b c) -> p a b c", a=free[0], b=free[1])
        return v


class Builder:
    def __init__(self, layers, dbg=()):
        self.layers = list(layers)
        self.dbg = set(dbg)
        self.nc = bass.Bass("TRN2", target_bir_lowering=False)
        Buf.fence = {}
        self.P = Prog(self.nc)
        self.poff, self.pcols = ptab_layout()
        self._bank = 0
        self._stat = 0

    def mm(self, out, lhsT, rhs, start, stop, r, w, **kw):
        self.P.op("pe", lambda e: e.matmul(out, lhsT=lhsT, rhs=rhs, start=start, stop=stop, **kw), r, w)

    def act(self, out, in_, func, r, w, bias=None, scale=None, accum=None):
        kw = {}
        if bias is not None:
            kw["bias"] = bias
        if scale is not None:
            kw["scale"] = scale
        if accum is not None:
            kw["accum_out"] = accum
        self.P.op("act", lambda e: e.activation(out=out, in_=in_, func=func, **kw), r, w)

    def ts(self, eng, out, in0, s1, s2, op0, op1, r, w):
        if op1 is None and eng == "pool":
            s2, op1 = 1.0, ALU.mult
        if op1 is None:
            self.P.op(eng, lambda e: e.tensor_scalar(out=out, in0=in0, scalar1=s1, scalar2=None, op0=op0), r, w)
        else:
            self.P.op(eng, lambda e: e.tensor_scalar(out=out, in0=in0, scalar1=s1, scalar2=s2, op0=op0, op1=op1), r, w)

    def stt(self, out, in0, scalar, in1, op0, op1, r, w):
        self.P.op("dve", lambda e: e.scalar_tensor_tensor(out=out, in0=in0, scalar=scalar, in1=in1, op0=op0, op1=op1), r, w)

    def tt(self, eng, out, in0, in1, op, r, w):
        self.P.op(eng, lambda e: e.tensor_tensor(out=out, in0=in0, in1=in1, op=op), r, w)

    def cp(self, eng, out, in_, r, w):
        if eng == "act":
            self.P.op("act", lambda e: e.copy(out=out, in_=in_), r, w)
        else:
            self.P.op(eng, lambda e: e.tensor_copy(out=out, in_=in_), r, w)

    def recip(self, out, in_, r, w):
        self.P.op("dve", lambda e: e.reciprocal(out=out, in_=in_), r, w)

    def memset(self, eng, ap, val, w):
        self.P.op(eng, lambda e: e.memset(ap, val), [], w)

    def dma_cast(self, out, in_, r, w):
        self.P.dma("pool", lambda e: e.dma_start(out=out, in_=in_), r, w)

    def dma_plain(self, out, in_, r, w, is_output=False):
        self.P.dma("sp", lambda e: e.dma_start(out=out, in_=in_), r, w, is_output=is_output)

    def bank(self):
        b = self._bank
        self._bank = (b + 1) % 6
        return b

    def statbank(self):
        b = 6 + self._stat
        self._stat ^= 1
        return b

    def psb(self, b, lo=0, hi=512):
        return self.ps[:, b * 512 + lo:b * 512 + hi]

    def build(self):
        nc = self.nc
        st = ExitStack()
        with st:
            self.declare_dram()
            self.sb_all = st.enter_context(nc.sbuf_tensor("sb_all", [128, 106300], BF16))
            self.ps = st.enter_context(nc.psum_tensor("ps_all", [128, 4096], F32))
            self.PBK = [Buf("psk%d" % i, excl=True) for i in range(32)]
            self.PB = [self.PBK[4 * i:4 * i + 4] for i in range(8)]
            self.A = Arena(self.sb_all, 0, 106300 * 2)
            self.setup_persistent()
            for i in self.layers:
                self.layer(i)
            self.store_output()
            self.P.emit(st)
        return nc

    def declare_dram(self):
        nc = self.nc
        dt = lambda n, s: nc.dram_tensor(n, s, F32, kind="ExternalInput").ap()
        self.d = {}
        self.d["xT"] = dt("xT", [8, 128, S])
        self.d["ptab"] = dt("ptab", [128, self.pcols])
        self.d["ident"] = dt("ident", [128, 128])
        for i in self.layers:
            self.d["pT%d" % i] = dt("pT%d" % i, [2, 128, S])
            self.d["wup%d" % i] = dt("wup%d" % i, [NFC, 128, 8 * 256])
            self.d["wdn%d" % i] = dt("wdn%d" % i, [8, 128, NFC * 128])
            self.d["wg%d" % i] = dt("wg%d" % i, [8, 128, 8 * 128])
            self.d["wp%d" % i] = dt("wp%d" % i, [128, 2 * D])
            kind, j = i % 3, i // 3
            if kind == 0:
                self.d["a_wu%d" % j] = dt("a_wu%d" % j, [16, 128, 8 * 128])
                self.d["a_wv%d" % j] = dt("a_wv%d" % j, [4, 128, 8 * 512])
                self.d["a_wo%d" % j] = dt("a_wo%d" % j, [8, 128, 16 * 128])
                self.d["a_ws%d" % j] = dt("a_ws%d" % j, [128, 8 * 128])
                self.d["a_bs%d" % j] = dt("a_bs%d" % j, [128, 8 * 128])
            elif kind == 1:
                self.d["b_wqk"] = dt("b_wqk", [8, 128, 8 * 256])
                self.d["b_wv"] = dt("b_wv", [4, 128, 8 * 256])
                self.d["b_wo"] = dt("b_wo", [8, 128, 8 * 128])
                self.d["b_bias"] = dt("b_bias", [128, 16 * 640])
            else:
                self.d["c_wi"] = dt("c_wi", [8, 128, 8 * 256])
                self.d["c_wo"] = dt("c_wo", [8, 128, 8 * 128])
        self.yT = nc.dram_tensor("yT", [8, 128, S], F32, kind="ExternalOutput").ap()

    def setup_persistent(self):
        A = self.A
        self.h = A.alloc(F32, 8, S)
        self.Bh = [[Buf("h%d_%d" % (c, t)) for t in range(NT)] for c in range(8)]
        self.ptab = A.alloc(F32, self.pcols)
        self.Bptab = Buf("ptab", const=True)
        self.ident = A.alloc(BF16, 128)
        self.onesb = A.alloc(BF16, 128)
        self.epsc = A.alloc(F32, 1)
        self.dummy = A.alloc(F32, 8)
        self.Bconst = Buf("const", const=True)
        self.sq = [A.alloc(BF16, T) for _ in range(3)]
        self.Bsq = [Buf("sq%d" % i) for i in range(3)]
        self._sq = 0
        self.rstds = [A.alloc(F32, T) for _ in range(3)]
        self.Brstds = [Buf("rstd%d" % k) for k in range(3)]
        self.rstd, self.Brstd = self.rstds[0], self.Brstds[0]
        self.mres = A.alloc(F32, 8, T)
        self.Bmres = [Buf("mres%d" % c) for c in range(8)]
        self.hns = [A.alloc(BF16, 8, T) for _ in range(2)]
        self.hn1_off = A.off - 8 * T * 2
        self.Bhns = [[Buf("hn%d_%d" % (k, c)) for c in range(8)] for k in range(2)]
        self.hn, self.Bhn = self.hns[0], self.Bhns[0]
        self.rtmp = [A.alloc(F32, T) for _ in range(3)]
        self.Brtmp = [Buf("rtmp%d" % i) for i in range(3)]
        self._rt = 0
        self.phase_mark = A.mark()
        self.dma_plain(self.ptab, self.d["ptab"], [], [self.Bptab])
        for c in range(8):
            self.dma_plain(self.h[:, c, :], self.d["xT"][c], [], self.Bh[c])
        self.memset("pool", self.onesb, 1.0, [self.Bconst])
        self.memset("pool", self.epsc, EPS, [self.Bconst])
        self.dma_cast(self.ident, self.d["ident"], [], [self.Bconst])

    def pcol(self, key, c, n=1):
        o = self.poff[key] + c
        return self.ptab[:, o:o + n]

    def nextsq(self):
        i = self._sq
        self._sq = (i + 1) % 3
        return self.sq[i], self.Bsq[i]

    def nextrt(self):
        i = self._rt
        self._rt = (i + 1) % 3
        return self.rtmp[i], self.Brtmp[i]

    def defer_mm(self, *args, **kw):
        self.flush_mm()
        self._pend_mm = (args, kw)

    def flush_mm(self):
        p = getattr(self, "_pend_mm", None)
        if p is not None:
            self._pend_mm = None
            self.mm(*p[0], **p[1])

    def stat_add(self, sb, src, src_bufs, c, n=8):
        sq, Bsq = self.nextsq()
        self.act(sq, src, AF.Square, src_bufs, [Bsq])
        self.defer_mm(self.psb(sb), self.onesb, sq, c == 0, c == n - 1, [Bsq, self.Bconst], [self.PB[sb]])

    def finish_rstd(self, sb, dim=D, role=0):
        self.flush_mm()
        rstd, Brstd = self.rstds[role], self.Brstds[role]
        self.act(rstd, self.psb(sb), AF.Sqrt, [self.PB[sb], self.Bconst], [Brstd], bias=self.epsc, scale=1.0 / dim)
        self.recip(rstd, rstd, [Brstd], [Brstd])
        return rstd, Brstd

    def pre_norm_gen(self, t, gkey, k=0):
        hn, Bhn = self.hns[k], self.Bhns[k]
        sb = self.statbank()
        tsl = slice(t * T, (t + 1) * T)
        for c in range(8):
            self.stat_add(sb, self.h[:, c, tsl], [self.Bh[c][t]], c)
            yield
        rstd, Brstd = self.finish_rstd(sb, role=0)
        yield
        for c in range(8):
            self.stt(hn[:, c, :], self.h[:, c, tsl], self.pcol(gkey, c), rstd, ALU.mult, ALU.mult,
                     [self.Bh[c][t], Brstd, self.Bptab], [Bhn[c]])
            if c % 2:
                yield

    def pre_norm(self, t, gkey, k=0):
        for _ in self.pre_norm_gen(t, gkey, k):
            pass

    def post_norm_gen(self, t, gkey, sb, role, after=None):
        rstd, Brstd = self.finish_rstd(sb, role=role)
        tsl = slice(t * T, (t + 1) * T)
        yield
        rts = {}
        for i in range(10):
            if i < 8:
                rt, Brt = self.nextrt()
                rts[i] = (rt, Brt)
                self.stt(rt, self.mres[:, i, :], self.pcol(gkey, i), rstd, ALU.mult, ALU.mult,
                         [self.Bmres[i], Brstd, self.Bptab], [Brt])
            if 1 <= i < 9:
                c = i - 1
                rt, Brt = rts.pop(c)
                self.tt("pool", self.h[:, c, tsl], self.h[:, c, tsl], rt, ALU.add, [self.Bh[c][t], Brt], [self.Bh[c][t]])
            if 2 <= i < 10 and after is not None:
                after(i - 2)
            yield

    def post_norm_residual(self, t, gkey, sb, role=1):
        for _ in self.post_norm_gen(t, gkey, sb, role):
            pass

    def evac_mres(self, b, dc, sb):
        self.cp("dve", self.mres[:, dc, :], self.psb(b), [self.PB[b]], [self.Bmres[dc]])
        self.stat_add(sb, self.mres[:, dc, :], [self.Bmres[dc]], dc)

    def make_slots(self, name, n, *free):
        aps = [self.A.alloc(BF16, *free) for _ in range(n)]
        bufs = [Buf("%s%d" % (name, i)) for i in range(n)]
        return {"aps": aps, "bufs": bufs, "i": 0, "n": n}

    def load_slot(self, slots, src):
        i = slots["i"]
        slots["i"] = (i + 1) % slots["n"]
        ap, bf = slots["aps"][i], slots["bufs"][i]
        flat = ap
        if len(ap.shape) == 3:
            flat = ap.rearrange("p a b -> p (a b)")
        self.dma_cast(flat, src, [], [bf])
        return ap, bf

    def stream(self, name, n, free, srcs):
        return Stream(self, name, n, free, srcs)

    def stop(self, tag):
        if tag in self.dbg:
            raise StopBuild()

    def layer(self, i):
        try:
            self._layer(i)
        except StopBuild:
            pass

    def new_phase(self):
        self.flush_mm()
        self.A.reset(self.phase_mark)
        f = {}
        for e in ("pe", "act", "dve", "pool"):
            for o in reversed(self.P.ops[e]):
                if not o.is_dma:
                    f[e] = o
                    break
        Buf.fence = f

    def _layer(self, i):
        kind, j = i % 3, i // 3
        A = self.A
        self.new_phase()
        if "prenorm" in self.dbg:
            self.pre_norm(0, ("mix_pre", i))
            return
        if "nomix" in self.dbg:
            pass
        elif kind == 0:
            self.mixer_a(i, j)
        elif kind == 1:
            self.mixer_b(i)
        else:
            self.mixer_c(i)
        self.new_phase()
        if "noffn" not in self.dbg:
            self.ffn_phase(i)

    def ffn_phase(self, i):
        A = self.A
        actb = A.alloc(BF16, NFC, T)
        Bact = [Buf("act%d" % f) for f in range(NFC)]
        wupS = self.stream("wup", 4, (8, 256), [self.d["wup%d" % i][fc] for _t in range(NT) for fc in range(NFC)])
        wdnS = self.stream("wdn", 2, (NFC, 128), [self.d["wdn%d" % i][dc] for _t in range(NT) for dc in range(8)])
        wgS = self.stream("wg", 2, (8, 128), [self.d["wg%d" % i][dc] for _t in range(NT) for dc in range(8)])
        cg = [A.alloc(F32, T) for _ in range(2)]
        cv = [A.alloc(F32, T) for _ in range(2)]
        sg = [A.alloc(F32, T) for _ in range(2)]
        Bcg = [Buf("cg%d" % k) for k in range(2)]
        Bcv = [Buf("cv%d" % k) for k in range(2)]
        Bsg = [Buf("sg%d" % k) for k in range(2)]
        halo = [A.alloc(F32, 2 * NFC, 2) for _ in range(2)]
        Bhalo = [[Buf("halo%d_%d" % (k, q)) for q in range(2 * NFC)] for k in range(2)]
        bnd = [A.alloc(F32, 3, 2 * NFC) for _ in range(2)]
        Bbnd = [Buf("bnd%d" % k) for k in range(2)]
        hb = A.alloc(BF16, 8, T)
        Bhb = [Buf("hb%d" % c) for c in range(8)]
        pt = [A.alloc(BF16, 2, T) for _ in range(2)]
        Bpt = [Buf("pt%d" % k) for k in range(2)]
        wp = A.alloc(BF16, 2, D)
        Bwp = Buf("wp")
        gate = [A.alloc(F32, T) for _ in range(2)]
        Bgate = [Buf("gate%d" % k) for k in range(2)]
        self.dma_cast(wp.rearrange("p a b -> p (a b)"), self.d["wp%d" % i], [], [Bwp])
        cbase = self.poff[("conv", i)]

        def tap(jj, q):
            o = cbase + jj * 44 + q
            return self.ptab[:, o:o + 1]

        def step(bg):
            for g in list(bg):
                try:
                    next(g)
                except StopIteration:
                    bg.remove(g)

        def drain(bg):
            while bg:
                step(bg)

        def load_pt(t):
            ptt, Bptt = pt[t % 2], Bpt[t % 2]
            tsl = slice(t * T, (t + 1) * T)
            for kc in range(2):
                self.dma_cast(ptt[:, kc, :], self.d["pT%d" % i][kc][:, tsl], [], [Bptt])

        def stage_A(t):
            return self.pre_norm_gen(t, ("ffn_pre", i), k=t % 2)

        def stage_B(t, bg):
            hn, Bhn = self.hns[t % 2], self.Bhns[t % 2]
            if t > 0:
                ho, Bho = halo[(t - 1) % 2], Bhalo[(t - 1) % 2]
                bd, Bbd = bnd[t % 2], Bbnd[t % 2]
                W0 = self.ptab[:, cbase:cbase + 44]
                W1 = self.ptab[:, cbase + 44:cbase + 88]
                self.tt("dve", bd[:, 2, :], ho[:, :, 1], W1, ALU.mult, Bho + [self.Bptab], [Bbd])
                self.tt("dve", bd[:, 0, :], ho[:, :, 0], W0, ALU.mult, Bho + [self.Bptab], [Bbd])
                self.tt("dve", bd[:, 0, :], bd[:, 0, :], bd[:, 2, :], ALU.add, [Bbd], [Bbd])
                self.tt("dve", bd[:, 1, :], ho[:, :, 1], W0, ALU.mult, Bho + [self.Bptab], [Bbd])
            for fc in range(NFC):
                w, Bw = wupS.next()
                k2 = fc % 2
                bg_ = self.bank()
                bv_ = self.bank()
                for kc in range(8):
                    self.mm(self.psb(bg_), w[:, kc, 0:128], hn[:, kc, :], kc == 0, kc == 7, [Bw, Bhn[kc]], [self.PB[bg_]])
                for kc in range(8):
                    self.mm(self.psb(bv_), w[:, kc, 128:256], hn[:, kc, :], kc == 0, kc == 7, [Bw, Bhn[kc]], [self.PB[bv_]])
                for (b, q, cbuf, Bc) in ((bg_, fc, cg[k2], Bcg[k2]), (bv_, NFC + fc, cv[k2], Bcv[k2])):
                    pb = self.PB[b]
                    if t == 0:
                        self.act(cbuf, self.psb(b), AF.Copy, [pb, self.Bptab], [Bc], scale=tap(2, q))
                    else:
                        bd, Bbd = bnd[t % 2], Bbnd[t % 2]
                        self.act(cbuf[:, 2:T], self.psb(b, 2, T), AF.Copy, [pb, self.Bptab], [Bc], scale=tap(2, q))
                        self.act(cbuf[:, 0:1], self.psb(b, 0, 1), AF.Identity, [pb, self.Bptab, Bbd], [Bc],
                                 scale=tap(2, q), bias=bd[:, 0, q:q + 1])
                        self.act(cbuf[:, 1:2], self.psb(b, 1, 2), AF.Identity, [pb, self.Bptab, Bbd], [Bc],
                                 scale=tap(2, q), bias=bd[:, 1, q:q + 1])
                    if t < NT - 1:
                        self.cp("act", halo[t % 2][:, q, 0:2], self.psb(b, T - 2, T), [pb], [Bhalo[t % 2][q]])
                    self.stt(cbuf[:, 1:T], self.psb(b, 0, T - 1), tap(1, q), cbuf[:, 1:T], ALU.mult, ALU.add,
                             [pb, Bc, self.Bptab], [Bc])
                    self.stt(cbuf[:, 2:T], self.psb(b, 0, T - 2), tap(0, q), cbuf[:, 2:T], ALU.mult, ALU.add,
                             [pb, Bc, self.Bptab], [Bc])
                self.act(sg[k2], cg[k2], AF.Silu, [Bcg[k2]], [Bsg[k2]])
                self.tt("pool", actb[:, fc, :], sg[k2], cv[k2], ALU.mult, [Bsg[k2], Bcv[k2]], [Bact[fc]])
                step(bg)
            drain(bg)

        def stage_C(t, bg):
            sb = self.statbank()
            for dc in range(8):
                w, Bw = wdnS.next()
                b = self.bank()
                for fc in range(NFC):
                    self.mm(self.psb(b), w[:, fc, :], actb[:, fc, :], fc == 0, fc == NFC - 1, [Bw, Bact[fc]], [self.PB[b]])
                step(bg)
                self.evac_mres(b, dc, sb)
            drain(bg)
            return sb

        def stage_D(t, sb):
            tsl = slice(t * T, (t + 1) * T)

            def after(c):
                self.cp("act", hb[:, c, :], self.h[:, c, tsl], [self.Bh[c][t]], [Bhb[c]])
            g = self.post_norm_gen(t, ("ffn_post", i), sb, 1, after=after)
            next(g)
            return g

        def stage_E(t):
            ptt, Bptt = pt[t % 2], Bpt[t % 2]
            if t + 1 < NT:
                load_pt(t + 1)
            sb = self.statbank()
            for dc in range(8):
                w, Bw = wgS.next()
                bgt = self.bank()
                be = self.bank()
                for kc in range(8):
                    self.mm(self.psb(bgt), w[:, kc, :], hb[:, kc, :], kc == 0, kc == 7, [Bw, Bhb[kc]], [self.PB[bgt]])
                for kc in range(2):
                    self.mm(self.psb(be), wp[:, kc, dc * 128:(dc + 1) * 128], ptt[:, kc, :], kc == 0, kc == 1,
                            [Bwp, Bptt], [self.PB[be]])
                k2 = dc % 2
                self.act(gate[k2], self.psb(bgt), AF.Sigmoid, [self.PB[bgt]], [Bgate[k2]])
                self.tt("dve", self.mres[:, dc, :], gate[k2], self.psb(be), ALU.mult, [Bgate[k2], self.PB[be]],
                        [self.Bmres[dc]])
                self.stat_add(sb, self.mres[:, dc, :], [self.Bmres[dc]], dc)
            return sb

        load_pt(0)
        drain([stage_A(0)])
        stage_B(0, [stage_A(1)])
        pend = []
        for t in range(NT):
            sb = stage_C(t, pend)
            pend = []
            bgl = [stage_D(t, sb)]
            if t + 2 < NT:
                bgl.append(stage_A(t + 2))
            if t + 1 < NT:
                stage_B(t + 1, bgl)
            else:
                drain(bgl)
            sbe = stage_E(t)
            g = self.post_norm_gen(t, ("ple_g", i), sbe, 2)
            next(g)
            pend = [g]
        drain(pend)
        self.flush_mm()

    def mixer_c(self, i):
        A = self.A
        ybuf = A.alloc(BF16, 8, 30 + T)
        Byb = [Buf("yb%d" % c) for c in range(8)]
        z = A.alloc(F32, 8, T)
        Bz = [Buf("z%d" % c) for c in range(8)]
        zb = [A.alloc(BF16, T) for _ in range(2)]
        Bzb = [Buf("zb%d" % k) for k in range(2)]
        actc = A.alloc(BF16, 8, T)
        Bac = [Buf("actc%d" % c) for c in range(8)]
        dg = [A.alloc(BF16, 31, 128) for _ in range(2)]
        Bdg = [Buf("dg%d" % k) for k in range(2)]
        wi = self.stream("cwi", 3, (8, 256), [self.d["c_wi"][c] for _t in range(NT) for c in range(8)])
        wo = self.stream("cwo", 3, (8, 128), [self.d["c_wo"][c] for _t in range(NT) for c in range(8)])
        sgm = [A.alloc(F32, T) for _ in range(2)]
        Bsgm = [Buf("sgm%d" % k) for k in range(2)]
        mean = A.alloc(F32, T)
        msq = A.alloc(F32, T)
        var = A.alloc(F32, T)
        nmr = A.alloc(F32, T)
        Bmean, Bmsq, Bvar, Bnmr = Buf("mean"), Buf("msq"), Buf("var"), Buf("nmr")
        t1 = [A.alloc(F32, T) for _ in range(2)]
        Bt1 = [Buf("ct1_%d" % k) for k in range(2)]
        for c in range(8):
            self.memset("pool", ybuf[:, c, 0:30], 0.0, [Byb[c]])

        def build_diag(c):
            k2_ = c % 2
            o_ = self.poff["c_dw"] + c * 31
            in1 = self.ptab[:, o_:o_ + 31].unsqueeze(2).to_broadcast([128, 31, 128])
            in0 = self.ident.unsqueeze(1).to_broadcast([128, 31, 128])
            self.tt("dve", dg[k2_], in0, in1, ALU.mult, [self.Bconst, self.Bptab], [Bdg[k2_]])

        for t in range(NT):
            self.pre_norm(t, ("mix_pre", i))
            sb1 = self.statbank()
            sb2 = self.statbank()
            build_diag(0)
            for c in range(8):
                if c + 1 < 8:
                    build_diag(c + 1)
                w, Bw = wi.next()
                ba = self.bank()
                bg = self.bank()
                for kc in range(8):
                    self.mm(self.psb(ba), w[:, kc, 0:128], self.hn[:, kc, :], kc == 0, kc == 7, [Bw, self.Bhn[kc]], [self.PB[ba]])
                for kc in range(8):
                    self.mm(self.psb(bg), w[:, kc, 128:256], self.hn[:, kc, :], kc == 0, kc == 7, [Bw, self.Bhn[kc]], [self.PB[bg]])
                k2 = c % 2
                self.act(sgm[k2], self.psb(bg), AF.Sigmoid, [self.PB[bg]], [Bsgm[k2]])
                self.tt("dve", ybuf[:, c, 30:30 + T], sgm[k2], self.psb(ba), ALU.mult, [Bsgm[k2], self.PB[ba]], [Byb[c]])
                bc = self.bank()
                for jj in range(31):
                    self.mm(self.psb(bc), dg[k2][:, jj, :], ybuf[:, c, jj:jj + T], jj == 0, jj == 30, [Bdg[k2], Byb[c]], [self.PB[bc]])
                bias = self.pcol("c_dwb", c)
                self.act(z[:, c, :], self.psb(bc), AF.Identity, [self.PB[bc], self.Bptab], [Bz[c]], bias=bias)
                self.act(zb[k2], self.psb(bc), AF.Identity, [self.PB[bc], self.Bptab], [Bzb[k2]], bias=bias)
                self.flush_mm()
                self.mm(self.psb(sb1), self.onesb, zb[k2], c == 0, c == 7, [Bzb[k2], self.Bconst], [self.PB[sb1]])
                sq, Bsq = self.nextsq()
                self.act(sq, self.psb(bc), AF.Square, [self.PB[bc], self.Bptab], [Bsq], bias=bias)
                self.defer_mm(self.psb(sb2), self.onesb, sq, c == 0, c == 7, [Bsq, self.Bconst], [self.PB[sb2]])
                if t < NT - 1:
                    self.cp("pool", ybuf[:, c, 0:30], ybuf[:, c, T:T + 30], [Byb[c]], [Byb[c]])
            self.flush_mm()
            self.ts("dve", mean, self.psb(sb1), 1.0 / D, None, ALU.mult, None, [self.PB[sb1]], [Bmean])
            self.act(msq, mean, AF.Square, [Bmean], [Bmsq])
            self.stt(var, self.psb(sb2), 1.0 / D, msq, ALU.mult, ALU.subtract, [self.PB[sb2], Bmsq], [Bvar])
            self.act(self.rstd, var, AF.Sqrt, [Bvar, self.Bconst], [self.Brstd], bias=self.epsc)
            self.recip(self.rstd, self.rstd, [self.Brstd], [self.Brstd])
            self.stt(nmr, mean, -1.0, self.rstd, ALU.mult, ALU.mult, [Bmean, self.Brstd], [Bnmr])
            for c in range(8):
                k2 = c % 2
                self.tt("dve", t1[k2], z[:, c, :], self.rstd, ALU.mult, [Bz[c], self.Brstd], [Bt1[k2]])
                self.tt("pool", t1[k2], t1[k2], nmr, ALU.add, [Bt1[k2], Bnmr], [Bt1[k2]])
                self.act(actc[:, c, :], t1[k2], AF.Silu, [Bt1[k2], self.Bptab], [Bac[c]],
                         scale=self.pcol("c_lng", c), bias=self.pcol("c_lnb", c))
            sb = self.statbank()
            for dc in range(8):
                w, Bw = wo.next()
                b = self.bank()
                for c in range(8):
                    self.mm(self.psb(b), w[:, c, :], actc[:, c, :], c == 0, c == 7, [Bw, Bac[c]], [self.PB[b]])
                self.evac_mres(b, dc, sb)
            self.post_norm_residual(t, ("mix_post", i), sb)

    def mixer_a(self, i, j):
        A = self.A
        u = A.alloc(BF16, 16, T)
        Bu = [Buf("u%d" % c) for c in range(16)]
        vgb = A.alloc(BF16, 4, 2048)
        Bvgb = [Buf("vgb%d" % b) for b in range(4)]
        wv = self.stream("awv", 2, (8, 512), [self.d["a_wv%d" % j][q] for _t in range(NT) for q in range(4)])
        wu = self.stream("awu", 3, (8, 128), [self.d["a_wu%d" % j][q] for _t in range(NT) for q in range(16)])
        wo = self.stream("awo", 2, (16, 128), [self.d["a_wo%d" % j][q] for _t in range(NT) for q in range(8)])
        bst = A.alloc(F32, 4, 4, 6)
        Bbst = [Buf("bst%d" % b) for b in range(4)]
        mv = A.alloc(F32, 4, 2)
        rs = A.alloc(F32, 4, 1)
        vpe = A.alloc(F32, 4, 1)
        Bmv = [Buf("mv%d" % b) for b in range(4)]
        Brs = [Buf("rs%d" % b) for b in range(4)]
        Bvpe = [Buf("vpe%d" % b) for b in range(4)]
        nmh = A.alloc(F32, 1)
        wsT = A.alloc(BF16, 8, 128)
        Bws = Buf("wsT")
        Cc = A.alloc(F32, 16, 128)
        BCc = Buf("Cc")
        bsb = A.alloc(F32, 8, 128)
        Bbsb = Buf("bsb")
        rw = A.alloc(F32, 8, 128)
        Brw = Buf("rw")
        t1 = [A.alloc(F32, 4, 128) for _ in range(3)]
        Bt1 = [Buf("at1_%d" % k) for k in range(3)]
        self.memset("pool", nmh, -0.5, [self.Bconst])
        self.dma_cast(wsT.rearrange("p a b -> p (a b)"), self.d["a_ws%d" % j], [], [Bws])
        self.memset("pool", wsT[64:128, :, 0:64], 0.0, [Bws])
        self.dma_plain(bsb.rearrange("p a b -> p (a b)"), self.d["a_bs%d" % j], [], [Bbsb])
        b0 = self.bank()
        b1 = self.bank()
        for g in range(8):
            bb = b0 if g < 4 else b1
            lo = (g % 4) * 128
            self.mm(self.psb(bb, lo, lo + 128), self.onesb, wsT[:, g, :], True, True, [Bws, self.Bconst], [self.PB[bb]])
        self.cp("act", rw[:, 0:4, :].rearrange("p a b -> p (a b)"), self.psb(b0), [self.PB[b0]], [Brw])
        self.cp("act", rw[:, 4:8, :].rearrange("p a b -> p (a b)"), self.psb(b1), [self.PB[b1]], [Brw])
        for uc in range(16):
            g = uc // 2
            self.stt(Cc[:, uc, :], rw[:, g, :], self.pcol(("a_lnb", j), uc), bsb[:, g, :], ALU.mult, ALU.add,
                     [Brw, Bbsb, self.Bptab], [BCc])
        self.pre_norm(0, ("mix_pre", i), k=0)
        for t in range(NT):
            hn, Bhn = self.hns[t % 2], self.Bhns[t % 2]
            for vq in range(4):
                w, Bw = wv.next()
                for blk in range(4):
                    b = self.bank()
                    for kc in range(8):
                        self.mm(self.psb(b), hn[:, kc, blk * 128:(blk + 1) * 128], w[:, kc, :], kc == 0, kc == 7,
                                [Bw, Bhn[kc]], [self.PB[b]])
                    dst = vgb[:, blk, vq * 512:(vq + 1) * 512]
                    self.act(dst, self.psb(b), AF.Gelu_apprx_tanh, [self.PB[b]], [Bvgb[blk]])
                    bo = bst[:, blk, vq, :]
                    self.P.op("dve", lambda e, bo=bo, src=dst: e.bn_stats(out=bo, in_=src), [Bvgb[blk]], [Bbst[blk]])
            for blk in range(4):
                mo = mv[:, blk, :]
                bi = bst[:, blk, :, :]
                self.P.op("dve", lambda e, mo=mo, bi=bi: e.bn_aggr(out=mo, in_=bi), [Bbst[blk]], [Bmv[blk]])
                self.ts("pool", vpe[:, blk, :], mv[:, blk, 1:2], EPS, None, ALU.add, None, [Bmv[blk]], [Bvpe[blk]])
                self.tt("pool", rs[:, blk, :], vpe[:, blk, :], nmh, ALU.pow, [Bvpe[blk], self.Bconst], [Brs[blk]])
                self.ts("dve", vgb[:, blk, :], vgb[:, blk, :], mv[:, blk, 0:1], rs[:, blk, :], ALU.subtract, ALU.mult,
                        [Bvgb[blk], Bmv[blk], Brs[blk]], [Bvgb[blk]])
            kk = 0
            for u4 in range(4):
                for j4 in range(4):
                    uc = u4 * 4 + j4
                    w, Bw = wu.next()
                    b = self.bank()
                    for kc in range(8):
                        self.mm(self.psb(b), w[:, kc, :], hn[:, kc, :], kc == 0, kc == 7, [Bw, Bhn[kc]], [self.PB[b]])
                    self.act(u[:, uc, :], self.psb(b), AF.Gelu_apprx_tanh, [self.PB[b]], [Bu[uc]])
                for blk in range(4):
                    bsl = slice(blk * 128, (blk + 1) * 128)
                    b = self.bank()
                    for j4 in range(4):
                        uc = u4 * 4 + j4
                        self.mm(self.psb(b, j4 * 128, (j4 + 1) * 128), vgb[:, blk, uc * 128:(uc + 1) * 128], wsT[:, uc // 2, :], True, True,
                                [Bvgb[blk], Bws], [self.PB[b]])
                    k3 = kk % 3
                    kk += 1
                    for j4 in range(4):
                        uc = u4 * 4 + j4
                        self.stt(t1[k3][:, j4, :], self.psb(b, j4 * 128, (j4 + 1) * 128), self.pcol(("a_lng", j), uc), Cc[:, uc, :],
                                 ALU.mult, ALU.add, [self.PB[b], BCc, self.Bptab], [Bt1[k3]])
                    uv = u[:, u4 * 4:(u4 + 1) * 4, bsl]
                    self.tt("dve", uv, t1[k3], uv, ALU.mult, [Bt1[k3]] + Bu[u4 * 4:(u4 + 1) * 4], Bu[u4 * 4:(u4 + 1) * 4])
            if t + 1 < NT:
                self.pre_norm(t + 1, ("mix_pre", i), k=(t + 1) % 2)
            sb = self.statbank()
            for dc in range(8):
                w, Bw = wo.next()
                b = self.bank()
                for uc in range(16):
                    self.mm(self.psb(b), w[:, uc, :], u[:, uc, :], uc == 0, uc == 15, [Bw, Bu[uc]], [self.PB[b]])
                self.evac_mres(b, dc, sb)
            self.post_norm_residual(t, ("mix_post", i), sb)

    def mixer_b(self, i):
        A = self.A
        kT = A.alloc(BF16, 8, 1024)
        BkT = [[Buf("kT%d_%d" % (c, s_)) for s_ in range(2)] for c in range(8)]
        V = A.alloc(BF16, 8, 1024)
        BV = [Buf("V%d" % b) for b in range(8)]
        qz = A.alloc(BF16, 8, 2, T)
        Bq = [Buf("qz%d" % c) for c in range(8)]
        Bb = A.alloc(BF16, 16, 640)
        BBb = Buf("Bb")
        A2 = Arena(self.sb_all, self.hn1_off, self.hn1_off + 8 * T * 2)
        Pb = [A2.alloc(BF16, 640) for _ in range(2)]
        BPb = [Buf("Pb%d" % k) for k in range(2)]
        PTb = [A2.alloc(BF16, 640) for _ in range(3)]
        BPTb = [Buf("PTb%d" % k) for k in range(3)]
        dgr = [A2.alloc(BF16, 128) for _ in range(3)]
        Bdgr = [Buf("dgr%d" % k) for k in range(3)]
        st3 = [A.alloc(F32, 4) for _ in range(3)]
        Bst = [Buf("st%d" % k) for k in range(3)]
        wqk = self.stream("bwqk", 2, (8, 256), [self.d["b_wqk"][q] for _t in range(NT) for q in range(8)])
        wvs = self.stream("bwv", 2, (8, 256), [self.d["b_wv"][q] for _t in range(NT) for q in range(4)])
        wos = self.stream("bwo", 2, (8, 128), [self.d["b_wo"][q] for _t in range(NT) for q in range(8)])
        oT, BoT = self.hn, self.Bhn
        ps = self.ps
        self.dma_cast(Bb.rearrange("p a b -> p (a b)"), self.d["b_bias"], [], [BBb])
        self.memset("pool", Bb[64:128, :, 0:64], NEG, [BBb])
        self.memset("pool", Bb[0:64, :, 576:640], NEG, [BBb])
        for c in range(8):
            self.memset("pool", qz[:, c, :, :], 0.0, [Bq[c]])
        unit = 0
        for t in range(NT):
            slot = t % 2
            self.pre_norm(t, ("mix_pre", i))
            for c in range(8):
                w, Bw = wqk.next()
                bq = self.bank()
                bk = self.bank()
                for kc in range(8):
                    self.mm(self.psb(bq), w[:, kc, 0:128], self.hn[:, kc, :], kc == 0, kc == 7, [Bw, self.Bhn[kc]], [self.PB[bq]])
                for kc in range(8):
                    self.mm(self.psb(bk), w[:, kc, 128:256], self.hn[:, kc, :], kc == 0, kc == 7, [Bw, self.Bhn[kc]], [self.PB[bk]])
                for hh in range(2):
                    rows = slice(hh * 64, hh * 64 + 64)
                    self.act(qz[rows, c, hh, :], ps[rows, bq * 512:(bq + 1) * 512], AF.Copy, [self.PB[bq]], [Bq[c]], scale=0.125)
                self.cp("dve", kT[:, c, slot * 512:(slot + 1) * 512], self.psb(bk), [self.PB[bk]], [BkT[c][slot]])
            for qt in range(4):
                w, Bw = wvs.next()
                for blk in range(4):
                    b = self.bank()
                    for kc in range(8):
                        self.mm(self.psb(b, 0, 256), self.hn[:, kc, blk * 128:(blk + 1) * 128], w[:, kc, :], kc == 0, kc == 7,
                                [Bw, self.Bhn[kc]], [self.PB[b]])
                    rb = (4 * t + blk) % 8
                    self.cp("act" if (blk % 2) else "dve", V[:, rb, qt * 256:(qt + 1) * 256], self.psb(b, 0, 256), [self.PB[b]], [BV[rb]])
            units = [(c, jq, hh) for c in range(8) for jq in range(4) for hh in range(2)]
            NU = len(units)

            def geom(u):
                c, jq, hh = units[u]
                jb = 4 * t + jq
                return c, jq, hh, jb, max(0, 4 - jb)

            def st_scores(u):
                c, jq, hh, jb, i0 = geom(u)
                hd = 2 * c + hh
                sl = u % 2
                k3 = u % 3
                sbase = sl * 1024
                for ii in range(i0, 5):
                    kb = jb - 4 + ii
                    rc = (kb % 8) * 128
                    ks = (kb // 4) % 2
                    sap = ps[:, sbase + ii * 128: sbase + (ii + 1) * 128]
                    wb = self.PB[2 * sl] if ii < 4 else self.PB[2 * sl + 1]
                    self.mm(sap, qz[:, c, hh, jq * 128:(jq + 1) * 128], kT[:, c, rc:rc + 128], True, False,
                            [Bq[c], BkT[c][ks]], wb)
                    self.mm(sap, self.ident, Bb[:, hd, ii * 128:(ii + 1) * 128], False, True, [self.Bconst, BBb], wb)
                sbufs = [self.PB[2 * sl], self.PB[2 * sl + 1]]
                sfull = ps[:, sbase + i0 * 128: sbase + 640]
                nmax, rsum, rinv = st3[k3][:, 0:1], st3[k3][:, 1:2], st3[k3][:, 2:3]
                self.P.op("dve", lambda e, nmax=nmax, sfull=sfull: e.tensor_reduce(out=nmax, in_=sfull, axis=AX.X, op=ALU.max, negate=True),
                          sbufs, [Bst[k3]])
                self.act(Pb[sl][:, i0 * 128:640], sfull, AF.Exp, sbufs + [Bst[k3]], [BPb[sl], Bst[k3]], bias=nmax, accum=rsum)
                self.recip(rinv, rsum, [Bst[k3]], [Bst[k3]])
                self.ts("dve", dgr[k3], self.ident, rinv, None, ALU.mult, None, [self.Bconst, Bst[k3]], [Bdgr[k3]])

            def st_pt(u):
                c, jq, hh, jb, i0 = geom(u)
                sl = u % 2
                k3 = u % 3
                pbase = 2048
                for ii in range(i0, 5):
                    self.mm(ps[:, pbase + ii * 128: pbase + (ii + 1) * 128], Pb[sl][:, ii * 128:(ii + 1) * 128], dgr[k3], True, True,
                            [BPb[sl], Bdgr[k3]], [self.PB[4] if ii < 4 else self.PB[5]])
                self.cp("act" if (u % 2) else "dve", PTb[k3][:, i0 * 128:640], ps[:, pbase + i0 * 128: pbase + 640],
                        [self.PB[4], self.PB[5]], [BPTb[k3]])

            def st_pv(u):
                c, jq, hh, jb, i0 = geom(u)
                k3 = u % 3
                ob = 6 + hh
                for ii in range(i0, 5):
                    kb = jb - 4 + ii
                    self.mm(ps[:, ob * 512 + jq * 128: ob * 512 + (jq + 1) * 128], V[:, kb % 8, c * 128:(c + 1) * 128],
                            PTb[k3][:, ii * 128:(ii + 1) * 128], ii == i0, ii == 4, [BV[kb % 8], BPTb[k3]], [self.PB[ob]])
                if jq == 3 and hh == 1:
                    for h2 in range(2):
                        rows = slice(h2 * 64, h2 * 64 + 64)
                        o2 = 6 + h2
                        self.cp("dve" if h2 else "act", oT[rows, c, :], ps[rows, o2 * 512:(o2 + 1) * 512], self.PB[o2], [BoT[c]])

            for u in range(NU + 2):
                if u < NU:
                    st_scores(u)
                if 0 <= u - 1 < NU:
                    st_pt(u - 1)
                if 0 <= u - 2 < NU:
                    st_pv(u - 2)
            sb = self.statbank()
            for dc in range(8):
                w, Bw = wos.next()
                b = self.bank()
                for c in range(8):
                    self.mm(self.psb(b), w[:, c, :], oT[:, c, :], c == 0, c == 7, [Bw, BoT[c]], [self.PB[b]])
                self.evac_mres(b, dc, sb)
            self.post_norm_residual(t, ("mix_post", i), sb)

    def store_output(self):
        for c in range(8):
            self.dma_plain(self.yT[c], self.h[:, c, :], self.Bh[c], [Buf("y%d" % c)], is_output=True)


def _cols(v, n):
    return np.ascontiguousarray(np.asarray(v, np.float32).reshape(n, 128).T)


def _kc_tile(w, ncols_per_block):
    K, N = w.shape
    nb = N // ncols_per_block
    x = w.reshape(K // 128, 128, nb, ncols_per_block)
    x = x.transpose(2, 1, 0, 3)
    return np.ascontiguousarray(x).reshape(nb, 128, (K // 128) * ncols_per_block)


def host_shared(inp, layers):
    off, R = ptab_layout()
    ptab = np.zeros((128, R), np.float32)

    def put(key, arr):
        ptab[:, off[key]:off[key] + arr.shape[1]] = arr

    for i in range(DEPTH):
        put(("mix_pre", i), _cols(inp["mix_pre_g"][i], 8))
        put(("mix_post", i), _cols(inp["mix_post_g"][i], 8))
        put(("ffn_pre", i), _cols(inp["ffn_pre_g"][i], 8))
        put(("ffn_post", i), _cols(inp["ffn_post_g"][i], 8))
        put(("ple_g", i), _cols(inp["ple_norm_g"][i], 8))
        cv = np.concatenate([_cols(inp["ffn_conv"][i][jj], 44) for jj in range(3)], axis=1)
        put(("conv", i), cv)
    for j in range(2):
        put(("a_lng", j), _cols(inp["a_ln_g"][j], 16))
        put(("a_lnb", j), _cols(inp["a_ln_b"][j], 16))
    dw = np.asarray(inp["c_dw"][0], np.float32)
    dwc = dw.reshape(31, 8, 128).transpose(2, 1, 0).reshape(128, 248)
    put("c_dw", np.ascontiguousarray(dwc))
    put("c_dwb", _cols(inp["c_dw_b"][0], 8))
    put("c_lng", _cols(inp["c_ln_g"][0], 8))
    put("c_lnb", _cols(inp["c_ln_b"][0], 8))
    sh = {"ptab": ptab, "ident": np.eye(128, dtype=np.float32)}
    for i in layers:
        wu = np.asarray(inp["ffn_w_up"][i], np.float32)
        g = wu[:, :FF].reshape(D, NFC, 128)
        v = wu[:, FF:].reshape(D, NFC, 128)
        gv = np.concatenate([g, v], axis=2).reshape(D, NFC * 256)
        sh["wup%d" % i] = _kc_tile(gv, 256)
        wd = np.asarray(inp["ffn_w_down"][i], np.float32)
        x = wd.reshape(NFC, 128, 8, 128).transpose(2, 1, 0, 3)
        sh["wdn%d" % i] = np.ascontiguousarray(x).reshape(8, 128, NFC * 128)
        sh["wg%d" % i] = _kc_tile(np.asarray(inp["ple_w_gate"][i], np.float32), 128)
        wp = np.asarray(inp["ple_w_proj"][i], np.float32)
        sh["wp%d" % i] = np.ascontiguousarray(wp.reshape(2, 128, D).transpose(1, 0, 2)).reshape(128, 2 * D)
        kind, j = i % 3, i // 3
        if kind == 0:
            win = np.asarray(inp["a_w_in"][j], np.float32)
            sh["a_wu%d" % j] = _kc_tile(win[:, :2048], 128)
            sh["a_wv%d" % j] = _kc_tile(win[:, 2048:], 512)
            wo = np.asarray(inp["a_w_out"][j], np.float32)
            x = wo.reshape(16, 128, 8, 128).transpose(2, 1, 0, 3)
            sh["a_wo%d" % j] = np.ascontiguousarray(x).reshape(8, 128, 16 * 128)
            ws = np.asarray(inp["a_w_s"][j], np.float32)
            sh["a_ws%d" % j] = np.ascontiguousarray(ws.transpose(2, 0, 1)).reshape(128, 8 * 128)
            bs = np.asarray(inp["a_b_s"][j], np.float32).reshape(1, 8 * 128)
            sh["a_bs%d" % j] = np.ascontiguousarray(np.broadcast_to(bs, (128, 8 * 128)))
        elif kind == 1:
            wq = np.asarray(inp["b_w_qkv"][0], np.float32)
            q = wq[:, :D].reshape(D, 8, 128)
            k = wq[:, D:2 * D].reshape(D, 8, 128)
            qk = np.concatenate([q, k], axis=2).reshape(D, 8 * 256)
            sh["b_wqk"] = _kc_tile(qk, 256)
            sh["b_wv"] = _kc_tile(np.ascontiguousarray(wq[:, 2 * D:]), 256)
            wo = np.asarray(inp["b_w_out"][0], np.float32)
            x = wo.reshape(8, 128, 8, 128).transpose(2, 1, 0, 3)
            sh["b_wo"] = np.ascontiguousarray(x).reshape(8, 128, 8 * 128)
            rb = np.asarray(inp["b_rel_bias"][0], np.float32)
            qq = np.arange(128)[:, None]
            kk = np.arange(640)[None, :]
            idx = np.clip(qq + 512 - kk, -128, 128) + 128
            bfull = rb[:, idx]
            sh["b_bias"] = np.ascontiguousarray(bfull.transpose(1, 0, 2)).reshape(128, 16 * 640)
        else:
            wi = np.asarray(inp["c_w_in"][0], np.float32)
            a = wi[:, :D].reshape(D, 8, 128)
            g = wi[:, D:].reshape(D, 8, 128)
            ag = np.concatenate([a, g], axis=2).reshape(D, 8 * 256)
            sh["c_wi"] = _kc_tile(ag, 256)
            wo = np.asarray(inp["c_w_out"][0], np.float32)
            x = wo.reshape(8, 128, 8, 128).transpose(2, 1, 0, 3)
            sh["c_wo"] = np.ascontiguousarray(x).reshape(8, 128, 8 * 128)
    return sh


def run_layers(hT_in, p, shared, layers, trace=False, dbg=()):
    nc = Builder(layers, dbg).build()
    in_maps = []
    for b in range(8):
        m = dict(shared)
        m["xT"] = hT_in[b]
        for i in layers:
            m["pT%d" % i] = np.ascontiguousarray(p[i, b].T).reshape(2, 128, S)
        in_maps.append(m)
    res = run_bass_kernel_spmd(nc, in_maps, core_ids=list(range(8)), trace=trace)
    out = np.stack([res.results[b]["yT"] for b in range(8)])
    return out, res


def kernel(**inputs):
    inp = {k: np.asarray(v) for k, v in inputs.items()}
    layers = list(range(DEPTH))
    x = inp["x"].astype(np.float32, copy=False)
    hT = np.ascontiguousarray(x.transpose(0, 2, 1)).reshape(8, 8, 128, S)
    shared = host_shared(inp, layers)
    out, _ = run_layers(hT, inp["p"].astype(np.float32, copy=False), shared, layers)
    y = out.reshape(8, D, S).transpose(0, 2, 1)
    return np.ascontiguousarray(y).astype(np.float32, copy=False)
```

```python
import numpy as np
from contextlib import ExitStack
import concourse.bass as bass
import concourse.mybir as mybir
from concourse.bass_utils import run_bass_kernel_spmd

F32 = mybir.dt.float32
BF16 = mybir.dt.bfloat16
AF = mybir.ActivationFunctionType
ALU = mybir.AluOpType
AX = mybir.AxisListType

D = 1024
S = 2048
T = 512
NT = S // T
FF = 2816
NFC = FF // 128
DEPTH = 4
EPS = 1e-6
NEG = -1e30


class Buf:
    __slots__ = ("name", "last_w", "readers", "dma_readers", "dma_sem", "dma_cnt", "const", "excl")

    fence = {}

    def __init__(self, name, const=False, excl=False):
        self.name = name
        self.excl = excl
        self.last_w = None
        self.readers = dict(Buf.fence)
        self.dma_readers = []
        self.dma_sem = None
        self.dma_cnt = 0
        self.const = const


class Op:
    __slots__ = ("eng", "fn", "deps", "sig", "sigval", "is_dma", "dma_sem", "dma_val", "idx")

    def __init__(self, eng, fn, is_dma):
        self.eng = eng
        self.fn = fn
        self.deps = []
        self.sig = False
        self.sigval = 0
        self.is_dma = is_dma
        self.dma_sem = None
        self.dma_val = 0


class Prog:
    ENGS = ("pe", "act", "dve", "pool", "sp")

    def __init__(self, nc):
        self.nc = nc
        self.ops = {e: [] for e in self.ENGS}
        self.n = 0
        self.dma_bufs = []
        self.out_dma_ops = []

    def _add(self, op, reads, writes):
        deps = []
        for b in reads:
            w = b.last_w
            if w is not None:
                deps.append((w, True))
            if b.excl:
                for e_, r in b.readers.items():
                    if e_ != op.eng:
                        deps.append((r, False))
        for b in writes:
            w = b.last_w
            if w is not None and not (op.is_dma and w.is_dma):
                deps.append((w, False))
            for r in b.readers.values():
                deps.append((r, False))
            for r in b.dma_readers:
                deps.append((r, False))
        seen = set()
        for d, raw in deps:
            if d is op:
                continue
            if (not d.is_dma) and (not op.is_dma) and d.eng == op.eng:
                if op.eng == "pe":
                    continue
            k = id(d)
            if k in seen:
                continue
            seen.add(k)
            op.deps.append(d)
            if not d.is_dma:
                d.sig = True
        for b in writes:
            b.last_w = op
            b.readers = {}
            b.dma_readers = []
        for b in reads:
            if b.const or b in writes:
                continue
            if op.is_dma:
                b.dma_readers.append(op)
            else:
                b.readers[op.eng] = op
        op.idx = self.n
        self.n += 1
        self.ops[op.eng].append(op)
        return op

    @staticmethod
    def _flat(xs):
        out = []
        for x in xs:
            if isinstance(x, (list, tuple)):
                out.extend(Prog._flat(x))
            else:
                out.append(x)
        return out

    def op(self, eng, fn, reads=(), writes=()):
        return self._add(Op(eng, fn, False), self._flat(reads), self._flat(writes))

    def dma(self, eng, fn, reads=(), writes=(), is_output=False):
        o = Op(eng, fn, True)
        reads, writes = self._flat(reads), self._flat(writes)
        dst = writes[0]
        if dst.dma_sem is None:
            dst.dma_sem = "pending"
            self.dma_bufs.append(dst)
        dst.dma_cnt += 16
        o.dma_sem = dst
        o.dma_val = dst.dma_cnt
        self._add(o, list(reads), list(writes))
        if is_output:
            self.out_dma_ops.append(o)
        return o

    def emit(self, stack):
        nc = self.nc
        sems = {}
        for e in ("pe", "act", "dve", "pool"):
            sems[e] = stack.enter_context(nc.semaphore("s_" + e))
        for i, b in enumerate(self.dma_bufs):
            b.dma_sem = stack.enter_context(nc.semaphore("d%d" % i))
        for e in ("pe", "act", "dve", "pool"):
            c = 0
            for o in self.ops[e]:
                if o.is_dma:
                    continue
                if o.sig:
                    c += 1
                    o.sigval = c
        out_ops = self.out_dma_ops

        def run(engh, ename):
            waited = {}

            def wait(sem, val):
                k = id(sem)
                if waited.get(k, 0) >= val:
                    return
                waited[k] = val
                engh.wait_ge(sem, val)

            for o in self.ops[ename]:
                for d in o.deps:
                    if d.is_dma:
                        wait(d.dma_sem.dma_sem, d.dma_val)
                    else:
                        wait(sems[d.eng], d.sigval)
                ins = o.fn(engh)
                if o.is_dma:
                    ins.then_inc(o.dma_sem.dma_sem, 16)
                elif o.sig:
                    ins.then_inc(sems[ename], 1)
            if ename == "sp":
                for o in out_ops:
                    wait(o.dma_sem.dma_sem, o.dma_val)

        block = stack.enter_context(nc.Block())

        @block.tensor
        def _(e):
            run(e, "pe")

        @block.scalar
        def _(e):
            run(e, "act")

        @block.vector
        def _(e):
            run(e, "dve")

        @block.gpsimd
        def _(e):
            run(e, "pool")

        @block.sync
        def _(e):
            run(e, "sp")


def ptab_layout():
    off = {}
    c = 0
    for i in range(DEPTH):
        for nm in ("mix_pre", "mix_post", "ffn_pre", "ffn_post", "ple_g"):
            off[(nm, i)] = c
            c += 8
        off[("conv", i)] = c
        c += 132
    for j in range(2):
        off[("a_lng", j)] = c
        c += 16
        off[("a_lnb", j)] = c
        c += 16
    off["c_dw"] = c
    c += 248
    off["c_dwb"] = c
    c += 8
    off["c_lng"] = c
    c += 8
    off["c_lnb"] = c
    c += 8
    return off, c


class StopBuild(Exception):
    pass


class Stream:
    def __init__(self, B, name, n, free, srcs):
        self.B = B
        self.n = n
        self.srcs = list(srcs)
        self.aps = [B.A.alloc(BF16, *free) for _ in range(n)]
        self.bufs = [Buf("%s%d" % (name, i)) for i in range(n)]
        self.issued = 0
        self.taken = 0
        for _ in range(n - 1):
            self._issue()

    def _issue(self):
        k = self.issued
        if k >= len(self.srcs):
            return
        self.issued += 1
        ap, bf = self.aps[k % self.n], self.bufs[k % self.n]
        flat = ap.rearrange("p a b -> p (a b)") if len(ap.shape) == 3 else ap
        self.B.dma_cast(flat, self.srcs[k], [], [bf])

    def next(self):
        k = self.taken
        self.taken += 1
        self._issue()
        return self.aps[k % self.n], self.bufs[k % self.n]


class Arena:
    def __init__(self, ap_all, base, limit):
        self.all = ap_all
        self.off = base
        self.limit = limit

    def mark(self):
        return self.off

    def reset(self, m):
        self.off = m

    def alloc(self, dtype, *free):
        isz = 4 if dtype == F32 else 2
        n = 1
        for f in free:
            n *= f
        nb = n * isz
        off = (self.off + 63) // 64 * 64
        assert off + nb <= self.limit, ("SBUF arena overflow", off + nb, self.limit)
        self.off = off + nb
        v = self.all[:, off // 2:(off + nb) // 2]
        if dtype == F32:
            v = v.bitcast(F32)
        if len(free) == 2:
            v = v.rearrange("p (a b) -> p a b", a=free[0])
        elif len(free) == 3:
            v = v.rearrange("p (a b c) -> p a b c", a=free[0], b=free[1])
        return v


class Builder:
    def __init__(self, layers, dbg=()):
        self.layers = list(layers)
        self.dbg = set(dbg)
        self.nc = bass.Bass("TRN2", target_bir_lowering=False)
        Buf.fence = {}
        self.P = Prog(self.nc)
        self.poff, self.pcols = ptab_layout()
        self._bank = 0
        self._stat = 0

    def mm(self, out, lhsT, rhs, start, stop, r, w, **kw):
        self.P.op("pe", lambda e: e.matmul(out, lhsT=lhsT, rhs=rhs, start=start, stop=stop, **kw), r, w)

    def act(self, out, in_, func, r, w, bias=None, scale=None, accum=None):
        kw = {}
        if bias is not None:
            kw["bias"] = bias
        if scale is not None:
            kw["scale"] = scale
        if accum is not None:
            kw["accum_out"] = accum
        self.P.op("act", lambda e: e.activation(out=out, in_=in_, func=func, **kw), r, w)

    def ts(self, eng, out, in0, s1, s2, op0, op1, r, w):
        if op1 is None and eng == "pool":
            s2, op1 = 1.0, ALU.mult
        if op1 is None:
            self.P.op(eng, lambda e: e.tensor_scalar(out=out, in0=in0, scalar1=s1, scalar2=None, op0=op0), r, w)
        else:
            self.P.op(eng, lambda e: e.tensor_scalar(out=out, in0=in0, scalar1=s1, scalar2=s2, op0=op0, op1=op1), r, w)

    def stt(self, out, in0, scalar, in1, op0, op1, r, w):
        self.P.op("dve", lambda e: e.scalar_tensor_tensor(out=out, in0=in0, scalar=scalar, in1=in1, op0=op0, op1=op1), r, w)

    def tt(self, eng, out, in0, in1, op, r, w):
        self.P.op(eng, lambda e: e.tensor_tensor(out=out, in0=in0, in1=in1, op=op), r, w)

    def cp(self, eng, out, in_, r, w):
        if eng == "act":
            self.P.op("act", lambda e: e.copy(out=out, in_=in_), r, w)
        else:
            self.P.op(eng, lambda e: e.tensor_copy(out=out, in_=in_), r, w)

    def recip(self, out, in_, r, w):
        self.P.op("dve", lambda e: e.reciprocal(out=out, in_=in_), r, w)

    def memset(self, eng, ap, val, w):
        self.P.op(eng, lambda e: e.memset(ap, val), [], w)

    def dma_cast(self, out, in_, r, w):
        self.P.dma("pool", lambda e: e.dma_start(out=out, in_=in_), r, w)

    def dma_plain(self, out, in_, r, w, is_output=False):
        self.P.dma("sp", lambda e: e.dma_start(out=out, in_=in_), r, w, is_output=is_output)

    def bank(self):
        b = self._bank
        self._bank = (b + 1) % 6
        return b

    def statbank(self):
        b = 6 + self._stat
        self._stat ^= 1
        return b

    def psb(self, b, lo=0, hi=512):
        return self.ps[:, b * 512 + lo:b * 512 + hi]

    def build(self):
        nc = self.nc
        st = ExitStack()
        with st:
            self.declare_dram()
            self.sb_all = st.enter_context(nc.sbuf_tensor("sb_all", [128, 106300], BF16))
            self.ps = st.enter_context(nc.psum_tensor("ps_all", [128, 4096], F32))
            self.PBK = [Buf("psk%d" % i, excl=True) for i in range(32)]
            self.PB = [self.PBK[4 * i:4 * i + 4] for i in range(8)]
            self.A = Arena(self.sb_all, 0, 106300 * 2)
            self.setup_persistent()
            for i in self.layers:
                self.layer(i)
            self.store_output()
            self.P.emit(st)
        return nc

    def declare_dram(self):
        nc = self.nc
        dt = lambda n, s: nc.dram_tensor(n, s, F32, kind="ExternalInput").ap()
        self.d = {}
        self.d["xT"] = dt("xT", [8, 128, S])
        self.d["ptab"] = dt("ptab", [128, self.pcols])
        self.d["ident"] = dt("ident", [128, 128])
        for i in self.layers:
            self.d["pT%d" % i] = dt("pT%d" % i, [2, 128, S])
            self.d["wup%d" % i] = dt("wup%d" % i, [NFC, 128, 8 * 256])
            self.d["wdn%d" % i] = dt("wdn%d" % i, [8, 128, NFC * 128])
            self.d["wg%d" % i] = dt("wg%d" % i, [8, 128, 8 * 128])
            self.d["wp%d" % i] = dt("wp%d" % i, [128, 2 * D])
            kind, j = i % 3, i // 3
            if kind == 0:
                self.d["a_wu%d" % j] = dt("a_wu%d" % j, [16, 128, 8 * 128])
                self.d["a_wv%d" % j] = dt("a_wv%d" % j, [4, 128, 8 * 512])
                self.d["a_wo%d" % j] = dt("a_wo%d" % j, [8, 128, 16 * 128])
                self.d["a_ws%d" % j] = dt("a_ws%d" % j, [128, 8 * 128])
                self.d["a_bs%d" % j] = dt("a_bs%d" % j, [128, 8 * 128])
            elif kind == 1:
                self.d["b_wqk"] = dt("b_wqk", [8, 128, 8 * 256])
                self.d["b_wv"] = dt("b_wv", [4, 128, 8 * 256])
                self.d["b_wo"] = dt("b_wo", [8, 128, 8 * 128])
                self.d["b_bias"] = dt("b_bias", [128, 16 * 640])
            else:
                self.d["c_wi"] = dt("c_wi", [8, 128, 8 * 256])
                self.d["c_wo"] = dt("c_wo", [8, 128, 8 * 128])
        self.yT = nc.dram_tensor("yT", [8, 128, S], F32, kind="ExternalOutput").ap()

    def setup_persistent(self):
        A = self.A
        self.h = A.alloc(F32, 8, S)
        self.Bh = [[Buf("h%d_%d" % (c, t)) for t in range(NT)] for c in range(8)]
        self.ptab = A.alloc(F32, self.pcols)
        self.Bptab = Buf("ptab", const=True)
        self.ident = A.alloc(BF16, 128)
        self.onesb = A.alloc(BF16, 128)
        self.epsc = A.alloc(F32, 1)
        self.dummy = A.alloc(F32, 8)
        self.Bconst = Buf("const", const=True)
        self.sq = [A.alloc(BF16, T) for _ in range(3)]
        self.Bsq = [Buf("sq%d" % i) for i in range(3)]
        self._sq = 0
        self.rstds = [A.alloc(F32, T) for _ in range(3)]
        self.Brstds = [Buf("rstd%d" % k) for k in range(3)]
        self.rstd, self.Brstd = self.rstds[0], self.Brstds[0]
        self.mres = A.alloc(F32, 8, T)
        self.Bmres = [Buf("mres%d" % c) for c in range(8)]
        self.hns = [A.alloc(BF16, 8, T) for _ in range(2)]
        self.hn1_off = A.off - 8 * T * 2
        self.Bhns = [[Buf("hn%d_%d" % (k, c)) for c in range(8)] for k in range(2)]
        self.hn, self.Bhn = self.hns[0], self.Bhns[0]
        self.rtmp = [A.alloc(F32, T) for _ in range(3)]
        self.Brtmp = [Buf("rtmp%d" % i) for i in range(3)]
        self._rt = 0
        self.phase_mark = A.mark()
        self.dma_plain(self.ptab, self.d["ptab"], [], [self.Bptab])
        for c in range(8):
            self.dma_plain(self.h[:, c, :], self.d["xT"][c], [], self.Bh[c])
        self.memset("pool", self.onesb, 1.0, [self.Bconst])
        self.memset("pool", self.epsc, EPS, [self.Bconst])
        self.dma_cast(self.ident, self.d["ident"], [], [self.Bconst])

    def pcol(self, key, c, n=1):
        o = self.poff[key] + c
        return self.ptab[:, o:o + n]

    def nextsq(self):
        i = self._sq
        self._sq = (i + 1) % 3
        return self.sq[i], self.Bsq[i]

    def nextrt(self):
        i = self._rt
        self._rt = (i + 1) % 3
        return self.rtmp[i], self.Brtmp[i]

    def defer_mm(self, *args, **kw):
        self.flush_mm()
        self._pend_mm = (args, kw)

    def flush_mm(self):
        p = getattr(self, "_pend_mm", None)
        if p is not None:
            self._pend_mm = None
            self.mm(*p[0], **p[1])

    def stat_add(self, sb, src, src_bufs, c, n=8):
        sq, Bsq = self.nextsq()
        self.act(sq, src, AF.Square, src_bufs, [Bsq])
        self.defer_mm(self.psb(sb), self.onesb, sq, c == 0, c == n - 1, [Bsq, self.Bconst], [self.PB[sb]])

    def finish_rstd(self, sb, dim=D, role=0):
        self.flush_mm()
        rstd, Brstd = self.rstds[role], self.Brstds[role]
        self.act(rstd, self.psb(sb), AF.Sqrt, [self.PB[sb], self.Bconst], [Brstd], bias=self.epsc, scale=1.0 / dim)
        self.recip(rstd, rstd, [Brstd], [Brstd])
        return rstd, Brstd

    def pre_norm_gen(self, t, gkey, k=0):
        hn, Bhn = self.hns[k], self.Bhns[k]
        sb = self.statbank()
        tsl = slice(t * T, (t + 1) * T)
        for c in range(8):
            self.stat_add(sb, self.h[:, c, tsl], [self.Bh[c][t]], c)
            yield
        rstd, Brstd = self.finish_rstd(sb, role=0)
        yield
        for c in range(8):
            self.stt(hn[:, c, :], self.h[:, c, tsl], self.pcol(gkey, c), rstd, ALU.mult, ALU.mult,
                     [self.Bh[c][t], Brstd, self.Bptab], [Bhn[c]])
            if c % 2:
                yield

    def pre_norm(self, t, gkey, k=0):
        for _ in self.pre_norm_gen(t, gkey, k):
            pass

    def post_norm_gen(self, t, gkey, sb, role, after=None):
        rstd, Brstd = self.finish_rstd(sb, role=role)
        tsl = slice(t * T, (t + 1) * T)
        yield
        rts = {}
        for i in range(10):
            if i < 8:
                rt, Brt = self.nextrt()
                rts[i] = (rt, Brt)
                self.stt(rt, self.mres[:, i, :], self.pcol(gkey, i), rstd, ALU.mult, ALU.mult,
                         [self.Bmres[i], Brstd, self.Bptab], [Brt])
            if 1 <= i < 9:
                c = i - 1
                rt, Brt = rts.pop(c)
                self.tt("pool", self.h[:, c, tsl], self.h[:, c, tsl], rt, ALU.add, [self.Bh[c][t], Brt], [self.Bh[c][t]])
            if 2 <= i < 10 and after is not None:
                after(i - 2)
            yield

    def post_norm_residual(self, t, gkey, sb, role=1):
        for _ in self.post_norm_gen(t, gkey, sb, role):
            pass

    def evac_mres(self, b, dc, sb):
        self.cp("dve", self.mres[:, dc, :], self.psb(b), [self.PB[b]], [self.Bmres[dc]])
        self.stat_add(sb, self.mres[:, dc, :], [self.Bmres[dc]], dc)

    def make_slots(self, name, n, *free):
        aps = [self.A.alloc(BF16, *free) for _ in range(n)]
        bufs = [Buf("%s%d" % (name, i)) for i in range(n)]
        return {"aps": aps, "bufs": bufs, "i": 0, "n": n}

    def load_slot(self, slots, src):
        i = slots["i"]
        slots["i"] = (i + 1) % slots["n"]
        ap, bf = slots["aps"][i], slots["bufs"][i]
        flat = ap
        if len(ap.shape) == 3:
            flat = ap.rearrange("p a b -> p (a b)")
        self.dma_cast(flat, src, [], [bf])
        return ap, bf

    def stream(self, name, n, free, srcs):
        return Stream(self, name, n, free, srcs)

    def stop(self, tag):
        if tag in self.dbg:
            raise StopBuild()

    def layer(self, i):
        try:
            self._layer(i)
        except StopBuild:
            pass

    def new_phase(self):
        self.flush_mm()
        self.A.reset(self.phase_mark)
        f = {}
        for e in ("pe", "act", "dve", "pool"):
            for o in reversed(self.P.ops[e]):
                if not o.is_dma:
                    f[e] = o
                    break
        Buf.fence = f

    def _layer(self, i):
        kind, j = i % 3, i // 3
        A = self.A
        self.new_phase()
        if "prenorm" in self.dbg:
            self.pre_norm(0, ("mix_pre", i))
            return
        if "nomix" in self.dbg:
            pass
        elif kind == 0:
            self.mixer_a(i, j)
        elif kind == 1:
            self.mixer_b(i)
        else:
            self.mixer_c(i)
        self.new_phase()
        if "noffn" not in self.dbg:
            self.ffn_phase(i)

    def ffn_phase(self, i):
        A = self.A
        actb = A.alloc(BF16, NFC, T)
        Bact = [Buf("act%d" % f) for f in range(NFC)]
        wupS = self.stream("wup", 3, (8, 256), [self.d["wup%d" % i][fc] for _t in range(NT) for fc in range(NFC)])
        wdnS = self.stream("wdn", 3, (NFC, 128), [self.d["wdn%d" % i][dc] for _t in range(NT) for dc in range(8)])
        wgS = self.stream("wg", 2, (8, 128), [self.d["wg%d" % i][dc] for _t in range(NT) for dc in range(8)])
        cg = [A.alloc(F32, T) for _ in range(2)]
        cv = [A.alloc(F32, T) for _ in range(2)]
        sg = [A.alloc(F32, T) for _ in range(2)]
        Bcg = [Buf("cg%d" % k) for k in range(2)]
        Bcv = [Buf("cv%d" % k) for k in range(2)]
        Bsg = [Buf("sg%d" % k) for k in range(2)]
        halo = [A.alloc(F32, 2 * NFC, 2) for _ in range(2)]
        Bhalo = [[Buf("halo%d_%d" % (k, q)) for q in range(2 * NFC)] for k in range(2)]
        bnd = [A.alloc(F32, 3, 2 * NFC) for _ in range(2)]
        Bbnd = [Buf("bnd%d" % k) for k in range(2)]
        hb = A.alloc(BF16, 8, T)
        Bhb = [Buf("hb%d" % c) for c in range(8)]
        pt = [A.alloc(BF16, 2, T) for _ in range(2)]
        Bpt = [Buf("pt%d" % k) for k in range(2)]
        wp = A.alloc(BF16, 2, D)
        Bwp = Buf("wp")
        gate = [A.alloc(F32, T) for _ in range(2)]
        Bgate = [Buf("gate%d" % k) for k in range(2)]
        self.dma_cast(wp.rearrange("p a b -> p (a b)"), self.d["wp%d" % i], [], [Bwp])
        cbase = self.poff[("conv", i)]

        def tap(jj, q):
            o = cbase + jj * 44 + q
            return self.ptab[:, o:o + 1]

        def step(bg):
            for g in list(bg):
                try:
                    next(g)
                except StopIteration:
                    bg.remove(g)

        def drain(bg):
            while bg:
                step(bg)

        def load_pt(t):
            ptt, Bptt = pt[t % 2], Bpt[t % 2]
            tsl = slice(t * T, (t + 1) * T)
            for kc in range(2):
                self.dma_cast(ptt[:, kc, :], self.d["pT%d" % i][kc][:, tsl], [], [Bptt])

        def stage_A(t):
            return self.pre_norm_gen(t, ("ffn_pre", i), k=t % 2)

        def stage_B(t, bg):
            hn, Bhn = self.hns[t % 2], self.Bhns[t % 2]
            if t > 0:
                ho, Bho = halo[(t - 1) % 2], Bhalo[(t - 1) % 2]
                bd, Bbd = bnd[t % 2], Bbnd[t % 2]
                W0 = self.ptab[:, cbase:cbase + 44]
                W1 = self.ptab[:, cbase + 44:cbase + 88]
                self.tt("dve", bd[:, 2, :], ho[:, :, 1], W1, ALU.mult, Bho + [self.Bptab], [Bbd])
                self.tt("dve", bd[:, 0, :], ho[:, :, 0], W0, ALU.mult, Bho + [self.Bptab], [Bbd])
                self.tt("dve", bd[:, 0, :], bd[:, 0, :], bd[:, 2, :], ALU.add, [Bbd], [Bbd])
                self.tt("dve", bd[:, 1, :], ho[:, :, 1], W0, ALU.mult, Bho + [self.Bptab], [Bbd])
            for fc in range(NFC):
                w, Bw = wupS.next()
                k2 = fc % 2
                bg_ = self.bank()
                bv_ = self.bank()
                for kc in range(8):
                    self.mm(self.psb(bg_), w[:, kc, 0:128], hn[:, kc, :], kc == 0, kc == 7, [Bw, Bhn[kc]], [self.PB[bg_]])
                for kc in range(8):
                    self.mm(self.psb(bv_), w[:, kc, 128:256], hn[:, kc, :], kc == 0, kc == 7, [Bw, Bhn[kc]], [self.PB[bv_]])
                for (b, q, cbuf, Bc) in ((bg_, fc, cg[k2], Bcg[k2]), (bv_, NFC + fc, cv[k2], Bcv[k2])):
                    pb = self.PB[b]
                    if t == 0:
                        self.act(cbuf, self.psb(b), AF.Copy, [pb, self.Bptab], [Bc], scale=tap(2, q))
                    else:
                        bd, Bbd = bnd[t % 2], Bbnd[t % 2]
                        self.act(cbuf[:, 2:T], self.psb(b, 2, T), AF.Copy, [pb, self.Bptab], [Bc], scale=tap(2, q))
                        self.act(cbuf[:, 0:1], self.psb(b, 0, 1), AF.Identity, [pb, self.Bptab, Bbd], [Bc],
                                 scale=tap(2, q), bias=bd[:, 0, q:q + 1])
                        self.act(cbuf[:, 1:2], self.psb(b, 1, 2), AF.Identity, [pb, self.Bptab, Bbd], [Bc],
                                 scale=tap(2, q), bias=bd[:, 1, q:q + 1])
                    if t < NT - 1:
                        self.cp("act", halo[t % 2][:, q, 0:2], self.psb(b, T - 2, T), [pb], [Bhalo[t % 2][q]])
                    self.stt(cbuf[:, 1:T], self.psb(b, 0, T - 1), tap(1, q), cbuf[:, 1:T], ALU.mult, ALU.add,
                             [pb, Bc, self.Bptab], [Bc])
                    self.stt(cbuf[:, 2:T], self.psb(b, 0, T - 2), tap(0, q), cbuf[:, 2:T], ALU.mult, ALU.add,
                             [pb, Bc, self.Bptab], [Bc])
                self.act(sg[k2], cg[k2], AF.Silu, [Bcg[k2]], [Bsg[k2]])
                self.tt("pool", actb[:, fc, :], sg[k2], cv[k2], ALU.mult, [Bsg[k2], Bcv[k2]], [Bact[fc]])
                step(bg)
            drain(bg)

        def stage_C(t, bg):
            sb = self.statbank()
            for dc in range(8):
                w, Bw = wdnS.next()
                b = self.bank()
                for fc in range(NFC):
                    self.mm(self.psb(b), w[:, fc, :], actb[:, fc, :], fc == 0, fc == NFC - 1, [Bw, Bact[fc]], [self.PB[b]])
                step(bg)
                self.evac_mres(b, dc, sb)
            drain(bg)
            return sb

        def stage_D(t, sb):
            tsl = slice(t * T, (t + 1) * T)

            def after(c):
                self.cp("act", hb[:, c, :], self.h[:, c, tsl], [self.Bh[c][t]], [Bhb[c]])
            g = self.post_norm_gen(t, ("ffn_post", i), sb, 1, after=after)
            next(g)
            return g

        def stage_E(t):
            ptt, Bptt = pt[t % 2], Bpt[t % 2]
            if t + 1 < NT:
                load_pt(t + 1)
            sb = self.statbank()
            for dc in range(8):
                w, Bw = wgS.next()
                bgt = self.bank()
                be = self.bank()
                for kc in range(8):
                    self.mm(self.psb(bgt), w[:, kc, :], hb[:, kc, :], kc == 0, kc == 7, [Bw, Bhb[kc]], [self.PB[bgt]])
                for kc in range(2):
                    self.mm(self.psb(be), wp[:, kc, dc * 128:(dc + 1) * 128], ptt[:, kc, :], kc == 0, kc == 1,
                            [Bwp, Bptt], [self.PB[be]])
                k2 = dc % 2
                self.act(gate[k2], self.psb(bgt), AF.Sigmoid, [self.PB[bgt]], [Bgate[k2]])
                self.tt("dve", self.mres[:, dc, :], gate[k2], self.psb(be), ALU.mult, [Bgate[k2], self.PB[be]],
                        [self.Bmres[dc]])
                self.stat_add(sb, self.mres[:, dc, :], [self.Bmres[dc]], dc)
            return sb

        load_pt(0)
        drain([stage_A(0)])
        stage_B(0, [stage_A(1)])
        pend = []
        for t in range(NT):
            sb = stage_C(t, pend)
            pend = []
            bgl = [stage_D(t, sb)]
            if t + 2 < NT:
                bgl.append(stage_A(t + 2))
            if t + 1 < NT:
                stage_B(t + 1, bgl)
            else:
                drain(bgl)
            sbe = stage_E(t)
            g = self.post_norm_gen(t, ("ple_g", i), sbe, 2)
            next(g)
            pend = [g]
        drain(pend)
        self.flush_mm()

    def mixer_c(self, i):
        A = self.A
        ybuf = A.alloc(BF16, 8, 30 + T)
        Byb = [Buf("yb%d" % c) for c in range(8)]
        z = A.alloc(F32, 8, T)
        Bz = [Buf("z%d" % c) for c in range(8)]
        zb = [A.alloc(BF16, T) for _ in range(2)]
        Bzb = [Buf("zb%d" % k) for k in range(2)]
        actc = A.alloc(BF16, 8, T)
        Bac = [Buf("actc%d" % c) for c in range(8)]
        dg = [A.alloc(BF16, 31, 128) for _ in range(2)]
        Bdg = [Buf("dg%d" % k) for k in range(2)]
        wi = self.stream("cwi", 3, (8, 256), [self.d["c_wi"][c] for _t in range(NT) for c in range(8)])
        wo = self.stream("cwo", 3, (8, 128), [self.d["c_wo"][c] for _t in range(NT) for c in range(8)])
        sgm = [A.alloc(F32, T) for _ in range(2)]
        Bsgm = [Buf("sgm%d" % k) for k in range(2)]
        mean = A.alloc(F32, T)
        msq = A.alloc(F32, T)
        var = A.alloc(F32, T)
        nmr = A.alloc(F32, T)
        Bmean, Bmsq, Bvar, Bnmr = Buf("mean"), Buf("msq"), Buf("var"), Buf("nmr")
        t1 = [A.alloc(F32, T) for _ in range(2)]
        Bt1 = [Buf("ct1_%d" % k) for k in range(2)]
        for c in range(8):
            self.memset("pool", ybuf[:, c, 0:30], 0.0, [Byb[c]])

        def build_diag(c):
            k2_ = c % 2
            o_ = self.poff["c_dw"] + c * 31
            in1 = self.ptab[:, o_:o_ + 31].unsqueeze(2).to_broadcast([128, 31, 128])
            in0 = self.ident.unsqueeze(1).to_broadcast([128, 31, 128])
            self.tt("dve", dg[k2_], in0, in1, ALU.mult, [self.Bconst, self.Bptab], [Bdg[k2_]])

        self.pre_norm(0, ("mix_pre", i), k=0)
        for t in range(NT):
            hn, Bhn = self.hns[t % 2], self.Bhns[t % 2]
            sb1 = self.statbank()
            sb2 = self.statbank()
            build_diag(0)
            for c in range(8):
                if c + 1 < 8:
                    build_diag(c + 1)
                w, Bw = wi.next()
                ba = self.bank()
                bg = self.bank()
                for kc in range(8):
                    self.mm(self.psb(ba), w[:, kc, 0:128], hn[:, kc, :], kc == 0, kc == 7, [Bw, Bhn[kc]], [self.PB[ba]])
                for kc in range(8):
                    self.mm(self.psb(bg), w[:, kc, 128:256], hn[:, kc, :], kc == 0, kc == 7, [Bw, Bhn[kc]], [self.PB[bg]])
                k2 = c % 2
                self.act(sgm[k2], self.psb(bg), AF.Sigmoid, [self.PB[bg]], [Bsgm[k2]])
                self.tt("dve", ybuf[:, c, 30:30 + T], sgm[k2], self.psb(ba), ALU.mult, [Bsgm[k2], self.PB[ba]], [Byb[c]])
                bc = self.bank()
                for jj in range(31):
                    self.mm(self.psb(bc), dg[k2][:, jj, :], ybuf[:, c, jj:jj + T], jj == 0, jj == 30, [Bdg[k2], Byb[c]], [self.PB[bc]])
                bias = self.pcol("c_dwb", c)
                self.act(z[:, c, :], self.psb(bc), AF.Identity, [self.PB[bc], self.Bptab], [Bz[c]], bias=bias)
                self.act(zb[k2], self.psb(bc), AF.Identity, [self.PB[bc], self.Bptab], [Bzb[k2]], bias=bias)
                self.flush_mm()
                self.mm(self.psb(sb1), self.onesb, zb[k2], c == 0, c == 7, [Bzb[k2], self.Bconst], [self.PB[sb1]])
                sq, Bsq = self.nextsq()
                self.act(sq, self.psb(bc), AF.Square, [self.PB[bc], self.Bptab], [Bsq], bias=bias)
                self.defer_mm(self.psb(sb2), self.onesb, sq, c == 0, c == 7, [Bsq, self.Bconst], [self.PB[sb2]])
                if t < NT - 1:
                    self.cp("pool", ybuf[:, c, 0:30], ybuf[:, c, T:T + 30], [Byb[c]], [Byb[c]])
            self.flush_mm()
            self.ts("dve", mean, self.psb(sb1), 1.0 / D, None, ALU.mult, None, [self.PB[sb1]], [Bmean])
            self.act(msq, mean, AF.Square, [Bmean], [Bmsq])
            self.stt(var, self.psb(sb2), 1.0 / D, msq, ALU.mult, ALU.subtract, [self.PB[sb2], Bmsq], [Bvar])
            self.act(self.rstd, var, AF.Sqrt, [Bvar, self.Bconst], [self.Brstd], bias=self.epsc)
            self.recip(self.rstd, self.rstd, [self.Brstd], [self.Brstd])
            self.stt(nmr, mean, -1.0, self.rstd, ALU.mult, ALU.mult, [Bmean, self.Brstd], [Bnmr])
            for c in range(8):
                k2 = c % 2
                self.tt("dve", t1[k2], z[:, c, :], self.rstd, ALU.mult, [Bz[c], self.Brstd], [Bt1[k2]])
                self.tt("pool", t1[k2], t1[k2], nmr, ALU.add, [Bt1[k2], Bnmr], [Bt1[k2]])
                self.act(actc[:, c, :], t1[k2], AF.Silu, [Bt1[k2], self.Bptab], [Bac[c]],
                         scale=self.pcol("c_lng", c), bias=self.pcol("c_lnb", c))
            if t + 1 < NT:
                self.pre_norm(t + 1, ("mix_pre", i), k=(t + 1) % 2)
            sb = self.statbank()
            for dc in range(8):
                w, Bw = wo.next()
                b = self.bank()
                for c in range(8):
                    self.mm(self.psb(b), w[:, c, :], actc[:, c, :], c == 0, c == 7, [Bw, Bac[c]], [self.PB[b]])
                self.evac_mres(b, dc, sb)
            self.post_norm_residual(t, ("mix_post", i), sb)

    def mixer_a(self, i, j):
        A = self.A
        u = A.alloc(BF16, 16, T)
        Bu = [Buf("u%d" % c) for c in range(16)]
        vgb = A.alloc(BF16, 4, 2048)
        Bvgb = [Buf("vgb%d" % b) for b in range(4)]
        wv = self.stream("awv", 2, (8, 512), [self.d["a_wv%d" % j][q] for _t in range(NT) for q in range(4)])
        wu = self.stream("awu", 3, (8, 128), [self.d["a_wu%d" % j][q] for _t in range(NT) for q in range(16)])
        wo = self.stream("awo", 2, (16, 128), [self.d["a_wo%d" % j][q] for _t in range(NT) for q in range(8)])
        bst = A.alloc(F32, 4, 4, 6)
        Bbst = [Buf("bst%d" % b) for b in range(4)]
        mv = A.alloc(F32, 4, 2)
        rs = A.alloc(F32, 4, 1)
        vpe = A.alloc(F32, 4, 1)
        Bmv = [Buf("mv%d" % b) for b in range(4)]
        Brs = [Buf("rs%d" % b) for b in range(4)]
        Bvpe = [Buf("vpe%d" % b) for b in range(4)]
        nmh = A.alloc(F32, 1)
        wsT = A.alloc(BF16, 8, 128)
        Bws = Buf("wsT")
        Cc = A.alloc(F32, 16, 128)
        BCc = Buf("Cc")
        bsb = A.alloc(F32, 8, 128)
        Bbsb = Buf("bsb")
        rw = A.alloc(F32, 8, 128)
        Brw = Buf("rw")
        t1 = [A.alloc(F32, 4, 128) for _ in range(3)]
        Bt1 = [Buf("at1_%d" % k) for k in range(3)]
        self.memset("pool", nmh, -0.5, [self.Bconst])
        self.dma_cast(wsT.rearrange("p a b -> p (a b)"), self.d["a_ws%d" % j], [], [Bws])
        self.memset("pool", wsT[64:128, :, 0:64], 0.0, [Bws])
        self.dma_plain(bsb.rearrange("p a b -> p (a b)"), self.d["a_bs%d" % j], [], [Bbsb])
        b0 = self.bank()
        b1 = self.bank()
        for g in range(8):
            bb = b0 if g < 4 else b1
            lo = (g % 4) * 128
            self.mm(self.psb(bb, lo, lo + 128), self.onesb, wsT[:, g, :], True, True, [Bws, self.Bconst], [self.PB[bb]])
        self.cp("act", rw[:, 0:4, :].rearrange("p a b -> p (a b)"), self.psb(b0), [self.PB[b0]], [Brw])
        self.cp("act", rw[:, 4:8, :].rearrange("p a b -> p (a b)"), self.psb(b1), [self.PB[b1]], [Brw])
        for uc in range(16):
            g = uc // 2
            self.stt(Cc[:, uc, :], rw[:, g, :], self.pcol(("a_lnb", j), uc), bsb[:, g, :], ALU.mult, ALU.add,
                     [Brw, Bbsb, self.Bptab], [BCc])
        self.pre_norm(0, ("mix_pre", i), k=0)
        for t in range(NT):
            hn, Bhn = self.hns[t % 2], self.Bhns[t % 2]
            for vq in range(4):
                w, Bw = wv.next()
                for blk in range(4):
                    b = self.bank()
                    for kc in range(8):
                        self.mm(self.psb(b), hn[:, kc, blk * 128:(blk + 1) * 128], w[:, kc, :], kc == 0, kc == 7,
                                [Bw, Bhn[kc]], [self.PB[b]])
                    dst = vgb[:, blk, vq * 512:(vq + 1) * 512]
                    self.act(dst, self.psb(b), AF.Gelu_apprx_tanh, [self.PB[b]], [Bvgb[blk]])
                    bo = bst[:, blk, vq, :]
                    self.P.op("dve", lambda e, bo=bo, src=dst: e.bn_stats(out=bo, in_=src), [Bvgb[blk]], [Bbst[blk]])
            for blk in range(4):
                mo = mv[:, blk, :]
                bi = bst[:, blk, :, :]
                self.P.op("dve", lambda e, mo=mo, bi=bi: e.bn_aggr(out=mo, in_=bi), [Bbst[blk]], [Bmv[blk]])
                self.ts("pool", vpe[:, blk, :], mv[:, blk, 1:2], EPS, None, ALU.add, None, [Bmv[blk]], [Bvpe[blk]])
                self.tt("pool", rs[:, blk, :], vpe[:, blk, :], nmh, ALU.pow, [Bvpe[blk], self.Bconst], [Brs[blk]])
                self.ts("dve", vgb[:, blk, :], vgb[:, blk, :], mv[:, blk, 0:1], rs[:, blk, :], ALU.subtract, ALU.mult,
                        [Bvgb[blk], Bmv[blk], Brs[blk]], [Bvgb[blk]])
            kk = 0
            for u4 in range(4):
                for j4 in range(4):
                    uc = u4 * 4 + j4
                    w, Bw = wu.next()
                    b = self.bank()
                    for kc in range(8):
                        self.mm(self.psb(b), w[:, kc, :], hn[:, kc, :], kc == 0, kc == 7, [Bw, Bhn[kc]], [self.PB[b]])
                    self.act(u[:, uc, :], self.psb(b), AF.Gelu_apprx_tanh, [self.PB[b]], [Bu[uc]])
                for blk in range(4):
                    bsl = slice(blk * 128, (blk + 1) * 128)
                    b = self.bank()
                    for j4 in range(4):
                        uc = u4 * 4 + j4
                        self.mm(self.psb(b, j4 * 128, (j4 + 1) * 128), vgb[:, blk, uc * 128:(uc + 1) * 128], wsT[:, uc // 2, :], True, True,
                                [Bvgb[blk], Bws], [self.PB[b]])
                    k3 = kk % 3
                    kk += 1
                    for j4 in range(4):
                        uc = u4 * 4 + j4
                        self.stt(t1[k3][:, j4, :], self.psb(b, j4 * 128, (j4 + 1) * 128), self.pcol(("a_lng", j), uc), Cc[:, uc, :],
                                 ALU.mult, ALU.add, [self.PB[b], BCc, self.Bptab], [Bt1[k3]])
                    uv = u[:, u4 * 4:(u4 + 1) * 4, bsl]
                    self.tt("dve", uv, t1[k3], uv, ALU.mult, [Bt1[k3]] + Bu[u4 * 4:(u4 + 1) * 4], Bu[u4 * 4:(u4 + 1) * 4])
            if t + 1 < NT:
                self.pre_norm(t + 1, ("mix_pre", i), k=(t + 1) % 2)
            sb = self.statbank()
            for dc in range(8):
                w, Bw = wo.next()
                b = self.bank()
                for uc in range(16):
                    self.mm(self.psb(b), w[:, uc, :], u[:, uc, :], uc == 0, uc == 15, [Bw, Bu[uc]], [self.PB[b]])
                self.evac_mres(b, dc, sb)
            self.post_norm_residual(t, ("mix_post", i), sb)

    def mixer_b(self, i):
        A = self.A
        kT = A.alloc(BF16, 8, 1024)
        BkT = [[Buf("kT%d_%d" % (c, s_)) for s_ in range(2)] for c in range(8)]
        V = A.alloc(BF16, 8, 1024)
        BV = [Buf("V%d" % b) for b in range(8)]
        qz = A.alloc(BF16, 8, 2, T)
        Bq = [Buf("qz%d" % c) for c in range(8)]
        Bb = A.alloc(BF16, 16, 640)
        BBb = Buf("Bb")
        A2 = Arena(self.sb_all, self.hn1_off, self.hn1_off + 8 * T * 2)
        Pb = [A2.alloc(BF16, 640) for _ in range(2)]
        BPb = [Buf("Pb%d" % k) for k in range(2)]
        PTb = [A2.alloc(BF16, 640) for _ in range(3)]
        BPTb = [Buf("PTb%d" % k) for k in range(3)]
        dgr = [A2.alloc(BF16, 128) for _ in range(3)]
        Bdgr = [Buf("dgr%d" % k) for k in range(3)]
        st3 = [A.alloc(F32, 4) for _ in range(3)]
        Bst = [Buf("st%d" % k) for k in range(3)]
        wqk = self.stream("bwqk", 2, (8, 256), [self.d["b_wqk"][q] for _t in range(NT) for q in range(8)])
        wvs = self.stream("bwv", 2, (8, 256), [self.d["b_wv"][q] for _t in range(NT) for q in range(4)])
        wos = self.stream("bwo", 2, (8, 128), [self.d["b_wo"][q] for _t in range(NT) for q in range(8)])
        oT, BoT = self.hn, self.Bhn
        ps = self.ps
        self.dma_cast(Bb.rearrange("p a b -> p (a b)"), self.d["b_bias"], [], [BBb])
        self.memset("pool", Bb[64:128, :, 0:64], NEG, [BBb])
        self.memset("pool", Bb[0:64, :, 576:640], NEG, [BBb])
        for c in range(8):
            self.memset("pool", qz[:, c, :, :], 0.0, [Bq[c]])
        unit = 0
        for t in range(NT):
            slot = t % 2
            self.pre_norm(t, ("mix_pre", i))
            for c in range(8):
                w, Bw = wqk.next()
                bq = self.bank()
                bk = self.bank()
                for kc in range(8):
                    self.mm(self.psb(bq), w[:, kc, 0:128], self.hn[:, kc, :], kc == 0, kc == 7, [Bw, self.Bhn[kc]], [self.PB[bq]])
                for kc in range(8):
                    self.mm(self.psb(bk), w[:, kc, 128:256], self.hn[:, kc, :], kc == 0, kc == 7, [Bw, self.Bhn[kc]], [self.PB[bk]])
                for hh in range(2):
                    rows = slice(hh * 64, hh * 64 + 64)
                    self.act(qz[rows, c, hh, :], ps[rows, bq * 512:(bq + 1) * 512], AF.Copy, [self.PB[bq]], [Bq[c]], scale=0.125)
                self.cp("dve", kT[:, c, slot * 512:(slot + 1) * 512], self.psb(bk), [self.PB[bk]], [BkT[c][slot]])
            for qt in range(4):
                w, Bw = wvs.next()
                for blk in range(4):
                    b = self.bank()
                    for kc in range(8):
                        self.mm(self.psb(b, 0, 256), self.hn[:, kc, blk * 128:(blk + 1) * 128], w[:, kc, :], kc == 0, kc == 7,
                                [Bw, self.Bhn[kc]], [self.PB[b]])
                    rb = (4 * t + blk) % 8
                    self.cp("act" if (blk % 2) else "dve", V[:, rb, qt * 256:(qt + 1) * 256], self.psb(b, 0, 256), [self.PB[b]], [BV[rb]])
            units = [(c, jq, hh) for c in range(8) for jq in range(4) for hh in range(2)]
            NU = len(units)

            def geom(u):
                c, jq, hh = units[u]
                jb = 4 * t + jq
                return c, jq, hh, jb, max(0, 4 - jb)

            def st_scores(u):
                c, jq, hh, jb, i0 = geom(u)
                hd = 2 * c + hh
                sl = u % 2
                k3 = u % 3
                sbase = sl * 1024
                for ii in range(i0, 5):
                    kb = jb - 4 + ii
                    rc = (kb % 8) * 128
                    ks = (kb // 4) % 2
                    sap = ps[:, sbase + ii * 128: sbase + (ii + 1) * 128]
                    wb = self.PB[2 * sl] if ii < 4 else self.PB[2 * sl + 1]
                    self.mm(sap, qz[:, c, hh, jq * 128:(jq + 1) * 128], kT[:, c, rc:rc + 128], True, False,
                            [Bq[c], BkT[c][ks]], wb)
                    self.mm(sap, self.ident, Bb[:, hd, ii * 128:(ii + 1) * 128], False, True, [self.Bconst, BBb], wb)
                sbufs = [self.PB[2 * sl], self.PB[2 * sl + 1]]
                sfull = ps[:, sbase + i0 * 128: sbase + 640]
                nmax, rsum, rinv = st3[k3][:, 0:1], st3[k3][:, 1:2], st3[k3][:, 2:3]
                self.P.op("dve", lambda e, nmax=nmax, sfull=sfull: e.tensor_reduce(out=nmax, in_=sfull, axis=AX.X, op=ALU.max, negate=True),
                          sbufs, [Bst[k3]])
                self.act(Pb[sl][:, i0 * 128:640], sfull, AF.Exp, sbufs + [Bst[k3]], [BPb[sl], Bst[k3]], bias=nmax, accum=rsum)
                self.recip(rinv, rsum, [Bst[k3]], [Bst[k3]])
                self.ts("dve", dgr[k3], self.ident, rinv, None, ALU.mult, None, [self.Bconst, Bst[k3]], [Bdgr[k3]])

            def st_pt(u):
                c, jq, hh, jb, i0 = geom(u)
                sl = u % 2
                k3 = u % 3
                pbase = 2048
                for ii in range(i0, 5):
                    self.mm(ps[:, pbase + ii * 128: pbase + (ii + 1) * 128], Pb[sl][:, ii * 128:(ii + 1) * 128], dgr[k3], True, True,
                            [BPb[sl], Bdgr[k3]], [self.PB[4] if ii < 4 else self.PB[5]])
                self.cp("act" if (u % 2) else "dve", PTb[k3][:, i0 * 128:640], ps[:, pbase + i0 * 128: pbase + 640],
                        [self.PB[4], self.PB[5]], [BPTb[k3]])

            def st_pv(u):
                c, jq, hh, jb, i0 = geom(u)
                k3 = u % 3
                ob = 6 + hh
                for ii in range(i0, 5):
                    kb = jb - 4 + ii
                    self.mm(ps[:, ob * 512 + jq * 128: ob * 512 + (jq + 1) * 128], V[:, kb % 8, c * 128:(c + 1) * 128],
                            PTb[k3][:, ii * 128:(ii + 1) * 128], ii == i0, ii == 4, [BV[kb % 8], BPTb[k3]], [self.PB[ob]])
                if jq == 3 and hh == 1:
                    for h2 in range(2):
                        rows = slice(h2 * 64, h2 * 64 + 64)
                        o2 = 6 + h2
                        self.cp("dve" if h2 else "act", oT[rows, c, :], ps[rows, o2 * 512:(o2 + 1) * 512], self.PB[o2], [BoT[c]])

            for u in range(NU + 2):
                if u < NU:
                    st_scores(u)
                if 0 <= u - 1 < NU:
                    st_pt(u - 1)
                if 0 <= u - 2 < NU:
                    st_pv(u - 2)
            sb = self.statbank()
            for dc in range(8):
                w, Bw = wos.next()
                b = self.bank()
                for c in range(8):
                    self.mm(self.psb(b), w[:, c, :], oT[:, c, :], c == 0, c == 7, [Bw, BoT[c]], [self.PB[b]])
                self.evac_mres(b, dc, sb)
            self.post_norm_residual(t, ("mix_post", i), sb)

    def store_output(self):
        for c in range(8):
            self.dma_plain(self.yT[c], self.h[:, c, :], self.Bh[c], [Buf("y%d" % c)], is_output=True)


def _cols(v, n):
    return np.ascontiguousarray(np.asarray(v, np.float32).reshape(n, 128).T)


def _kc_tile(w, ncols_per_block):
    K, N = w.shape
    nb = N // ncols_per_block
    x = w.reshape(K // 128, 128, nb, ncols_per_block)
    x = x.transpose(2, 1, 0, 3)
    return np.ascontiguousarray(x).reshape(nb, 128, (K // 128) * ncols_per_block)


def host_shared(inp, layers):
    off, R = ptab_layout()
    ptab = np.zeros((128, R), np.float32)

    def put(key, arr):
        ptab[:, off[key]:off[key] + arr.shape[1]] = arr

    for i in range(DEPTH):
        put(("mix_pre", i), _cols(inp["mix_pre_g"][i], 8))
        put(("mix_post", i), _cols(inp["mix_post_g"][i], 8))
        put(("ffn_pre", i), _cols(inp["ffn_pre_g"][i], 8))
        put(("ffn_post", i), _cols(inp["ffn_post_g"][i], 8))
        put(("ple_g", i), _cols(inp["ple_norm_g"][i], 8))
        cv = np.concatenate([_cols(inp["ffn_conv"][i][jj], 44) for jj in range(3)], axis=1)
        put(("conv", i), cv)
    for j in range(2):
        put(("a_lng", j), _cols(inp["a_ln_g"][j], 16))
        put(("a_lnb", j), _cols(inp["a_ln_b"][j], 16))
    dw = np.asarray(inp["c_dw"][0], np.float32)
    dwc = dw.reshape(31, 8, 128).transpose(2, 1, 0).reshape(128, 248)
    put("c_dw", np.ascontiguousarray(dwc))
    put("c_dwb", _cols(inp["c_dw_b"][0], 8))
    put("c_lng", _cols(inp["c_ln_g"][0], 8))
    put("c_lnb", _cols(inp["c_ln_b"][0], 8))
    sh = {"ptab": ptab, "ident": np.eye(128, dtype=np.float32)}
    for i in layers:
        wu = np.asarray(inp["ffn_w_up"][i], np.float32)
        g = wu[:, :FF].reshape(D, NFC, 128)
        v = wu[:, FF:].reshape(D, NFC, 128)
        gv = np.concatenate([g, v], axis=2).reshape(D, NFC * 256)
        sh["wup%d" % i] = _kc_tile(gv, 256)
        wd = np.asarray(inp["ffn_w_down"][i], np.float32)
        x = wd.reshape(NFC, 128, 8, 128).transpose(2, 1, 0, 3)
        sh["wdn%d" % i] = np.ascontiguousarray(x).reshape(8, 128, NFC * 128)
        sh["wg%d" % i] = _kc_tile(np.asarray(inp["ple_w_gate"][i], np.float32), 128)
        wp = np.asarray(inp["ple_w_proj"][i], np.float32)
        sh["wp%d" % i] = np.ascontiguousarray(wp.reshape(2, 128, D).transpose(1, 0, 2)).reshape(128, 2 * D)
        kind, j = i % 3, i // 3
        if kind == 0:
            win = np.asarray(inp["a_w_in"][j], np.float32)
            sh["a_wu%d" % j] = _kc_tile(win[:, :2048], 128)
            sh["a_wv%d" % j] = _kc_tile(win[:, 2048:], 512)
            wo = np.asarray(inp["a_w_out"][j], np.float32)
            x = wo.reshape(16, 128, 8, 128).transpose(2, 1, 0, 3)
            sh["a_wo%d" % j] = np.ascontiguousarray(x).reshape(8, 128, 16 * 128)
            ws = np.asarray(inp["a_w_s"][j], np.float32)
            sh["a_ws%d" % j] = np.ascontiguousarray(ws.transpose(2, 0, 1)).reshape(128, 8 * 128)
            bs = np.asarray(inp["a_b_s"][j], np.float32).reshape(1, 8 * 128)
            sh["a_bs%d" % j] = np.ascontiguousarray(np.broadcast_to(bs, (128, 8 * 128)))
        elif kind == 1:
            wq = np.asarray(inp["b_w_qkv"][0], np.float32)
            q = wq[:, :D].reshape(D, 8, 128)
            k = wq[:, D:2 * D].reshape(D, 8, 128)
            qk = np.concatenate([q, k], axis=2).reshape(D, 8 * 256)
            sh["b_wqk"] = _kc_tile(qk, 256)
            sh["b_wv"] = _kc_tile(np.ascontiguousarray(wq[:, 2 * D:]), 256)
            wo = np.asarray(inp["b_w_out"][0], np.float32)
            x = wo.reshape(8, 128, 8, 128).transpose(2, 1, 0, 3)
            sh["b_wo"] = np.ascontiguousarray(x).reshape(8, 128, 8 * 128)
            rb = np.asarray(inp["b_rel_bias"][0], np.float32)
            qq = np.arange(128)[:, None]
            kk = np.arange(640)[None, :]
            idx = np.clip(qq + 512 - kk, -128, 128) + 128
            bfull = rb[:, idx]
            sh["b_bias"] = np.ascontiguousarray(bfull.transpose(1, 0, 2)).reshape(128, 16 * 640)
        else:
            wi = np.asarray(inp["c_w_in"][0], np.float32)
            a = wi[:, :D].reshape(D, 8, 128)
            g = wi[:, D:].reshape(D, 8, 128)
            ag = np.concatenate([a, g], axis=2).reshape(D, 8 * 256)
            sh["c_wi"] = _kc_tile(ag, 256)
            wo = np.asarray(inp["c_w_out"][0], np.float32)
            x = wo.reshape(8, 128, 8, 128).transpose(2, 1, 0, 3)
            sh["c_wo"] = np.ascontiguousarray(x).reshape(8, 128, 8 * 128)
    return sh


def run_layers(hT_in, p, shared, layers, trace=False, dbg=()):
    nc = Builder(layers, dbg).build()
    in_maps = []
    for b in range(8):
        m = dict(shared)
        m["xT"] = hT_in[b]
        for i in layers:
            m["pT%d" % i] = np.ascontiguousarray(p[i, b].T).reshape(2, 128, S)
        in_maps.append(m)
    res = run_bass_kernel_spmd(nc, in_maps, core_ids=list(range(8)), trace=trace)
    out = np.stack([res.results[b]["yT"] for b in range(8)])
    return out, res


def kernel(**inputs):
    inp = {k: np.asarray(v) for k, v in inputs.items()}
    layers = list(range(DEPTH))
    x = inp["x"].astype(np.float32, copy=False)
    hT = np.ascontiguousarray(x.transpose(0, 2, 1)).reshape(8, 8, 128, S)
    shared = host_shared(inp, layers)
    out, _ = run_layers(hT, inp["p"].astype(np.float32, copy=False), shared, layers)
    y = out.reshape(8, D, S).transpose(0, 2, 1)
    return np.ascontiguousarray(y).astype(np.float32, copy=False)
```

```python
import numpy as np
from contextlib import ExitStack
import concourse.bass as bass
import concourse.mybir as mybir
from concourse.bass_utils import run_bass_kernel_spmd

F32 = mybir.dt.float32
BF16 = mybir.dt.bfloat16
AF = mybir.ActivationFunctionType
ALU = mybir.AluOpType
AX = mybir.AxisListType

D = 1024
S = 2048
T = 512
NT = S // T
FF = 2816
NFC = FF // 128
DEPTH = 4
EPS = 1e-6
NEG = -1e30


class Buf:
    __slots__ = ("name", "last_w", "readers", "dma_readers", "dma_sem", "dma_cnt", "const", "excl")

    fence = {}

    def __init__(self, name, const=False, excl=False):
        self.name = name
        self.excl = excl
        self.last_w = None
        self.readers = dict(Buf.fence)
        self.dma_readers = []
        self.dma_sem = None
        self.dma_cnt = 0
        self.const = const


class Op:
    __slots__ = ("eng", "fn", "deps", "sig", "sigval", "is_dma", "dma_sem", "dma_val", "idx")

    def __init__(self, eng, fn, is_dma):
        self.eng = eng
        self.fn = fn
        self.deps = []
        self.sig = False
        self.sigval = 0
        self.is_dma = is_dma
        self.dma_sem = None
        self.dma_val = 0


class Prog:
    ENGS = ("pe", "act", "dve", "pool", "sp")

    def __init__(self, nc):
        self.nc = nc
        self.ops = {e: [] for e in self.ENGS}
        self.n = 0
        self.dma_bufs = []
        self.out_dma_ops = []

    def _add(self, op, reads, writes):
        deps = []
        for b in reads:
            w = b.last_w
            if w is not None:
                deps.append((w, True))
            if b.excl:
                for e_, r in b.readers.items():
                    if e_ != op.eng:
                        deps.append((r, False))
        for b in writes:
            w = b.last_w
            if w is not None and not (op.is_dma and w.is_dma):
                deps.append((w, False))
            for r in b.readers.values():
                deps.append((r, False))
            for r in b.dma_readers:
                deps.append((r, False))
        seen = set()
        for d, raw in deps:
            if d is op:
                continue
            if (not d.is_dma) and (not op.is_dma) and d.eng == op.eng:
                if op.eng == "pe":
                    continue
            k = id(d)
            if k in seen:
                continue
            seen.add(k)
            op.deps.append(d)
            if not d.is_dma:
                d.sig = True
        for b in writes:
            b.last_w = op
            b.readers = {}
            b.dma_readers = []
        for b in reads:
            if b.const or b in writes:
                continue
            if op.is_dma:
                b.dma_readers.append(op)
            else:
                b.readers[op.eng] = op
        op.idx = self.n
        self.n += 1
        self.ops[op.eng].append(op)
        return op

    @staticmethod
    def _flat(xs):
        out = []
        for x in xs:
            if isinstance(x, (list, tuple)):
                out.extend(Prog._flat(x))
            else:
                out.append(x)
        return out

    def op(self, eng, fn, reads=(), writes=()):
        return self._add(Op(eng, fn, False), self._flat(reads), self._flat(writes))

    def dma(self, eng, fn, reads=(), writes=(), is_output=False):
        o = Op(eng, fn, True)
        reads, writes = self._flat(reads), self._flat(writes)
        dst = writes[0]
        if dst.dma_sem is None:
            dst.dma_sem = "pending"
            self.dma_bufs.append(dst)
        dst.dma_cnt += 16
        o.dma_sem = dst
        o.dma_val = dst.dma_cnt
        self._add(o, list(reads), list(writes))
        if is_output:
            self.out_dma_ops.append(o)
        return o

    def emit(self, stack):
        nc = self.nc
        sems = {}
        for e in ("pe", "act", "dve", "pool"):
            sems[e] = stack.enter_context(nc.semaphore("s_" + e))
        for i, b in enumerate(self.dma_bufs):
            b.dma_sem = stack.enter_context(nc.semaphore("d%d" % i))
        for e in ("pe", "act", "dve", "pool"):
            c = 0
            for o in self.ops[e]:
                if o.is_dma:
                    continue
                if o.sig:
                    c += 1
                    o.sigval = c
        out_ops = self.out_dma_ops

        def run(engh, ename):
            waited = {}

            def wait(sem, val):
                k = id(sem)
                if waited.get(k, 0) >= val:
                    return
                waited[k] = val
                engh.wait_ge(sem, val)

            for o in self.ops[ename]:
                for d in o.deps:
                    if d.is_dma:
                        wait(d.dma_sem.dma_sem, d.dma_val)
                    else:
                        wait(sems[d.eng], d.sigval)
                ins = o.fn(engh)
                if o.is_dma:
                    ins.then_inc(o.dma_sem.dma_sem, 16)
                elif o.sig:
                    ins.then_inc(sems[ename], 1)
            if ename == "sp":
                for o in out_ops:
                    wait(o.dma_sem.dma_sem, o.dma_val)

        block = stack.enter_context(nc.Block())

        @block.tensor
        def _(e):
            run(e, "pe")

        @block.scalar
        def _(e):
            run(e, "act")

        @block.vector
        def _(e):
            run(e, "dve")

        @block.gpsimd
        def _(e):
            run(e, "pool")

        @block.sync
        def _(e):
            run(e, "sp")


def ptab_layout():
    off = {}
    c = 0
    for i in range(DEPTH):
        for nm in ("mix_pre", "mix_post", "ffn_pre", "ffn_post", "ple_g"):
            off[(nm, i)] = c
            c += 8
        off[("conv", i)] = c
        c += 132
    for j in range(2):
        off[("a_lng", j)] = c
        c += 16
        off[("a_lnb", j)] = c
        c += 16
    off["c_dw"] = c
    c += 248
    off["c_dwb"] = c
    c += 8
    off["c_lng"] = c
    c += 8
    off["c_lnb"] = c
    c += 8
    return off, c


class StopBuild(Exception):
    pass


class Stream:
    def __init__(self, B, name, n, free, srcs):
        self.B = B
        self.n = n
        self.srcs = list(srcs)
        self.aps = [B.A.alloc(BF16, *free) for _ in range(n)]
        self.bufs = [Buf("%s%d" % (name, i)) for i in range(n)]
        self.issued = 0
        self.taken = 0
        for _ in range(n - 1):
            self._issue()

    def _issue(self):
        k = self.issued
        if k >= len(self.srcs):
            return
        self.issued += 1
        ap, bf = self.aps[k % self.n], self.bufs[k % self.n]
        flat = ap.rearrange("p a b -> p (a b)") if len(ap.shape) == 3 else ap
        self.B.dma_cast(flat, self.srcs[k], [], [bf])

    def next(self):
        k = self.taken
        self.taken += 1
        self._issue()
        return self.aps[k % self.n], self.bufs[k % self.n]


class Arena:
    def __init__(self, ap_all, base, limit):
        self.all = ap_all
        self.off = base
        self.limit = limit

    def mark(self):
        return self.off

    def reset(self, m):
        self.off = m

    def alloc(self, dtype, *free):
        isz = 4 if dtype == F32 else 2
        n = 1
        for f in free:
            n *= f
        nb = n * isz
        off = (self.off + 63) // 64 * 64
        assert off + nb <= self.limit, ("SBUF arena overflow", off + nb, self.limit)
        self.off = off + nb
        v = self.all[:, off // 2:(off + nb) // 2]
        if dtype == F32:
            v = v.bitcast(F32)
        if len(free) == 2:
            v = v.rearrange("p (a b) -> p a b", a=free[0])
        elif len(free) == 3:
            v = v.rearrange("p (a b c) -> p a b c", a=free[0], b=free[1])
        return v


class Builder:
    def __init__(self, layers, dbg=()):
        self.layers = list(layers)
        self.dbg = set(dbg)
        self.nc = bass.Bass("TRN2", target_bir_lowering=False)
        Buf.fence = {}
        self.P = Prog(self.nc)
        self.poff, self.pcols = ptab_layout()
        self._bank = 0
        self._stat = 0

    def mm(self, out, lhsT, rhs, start, stop, r, w, **kw):
        self.P.op("pe", lambda e: e.matmul(out, lhsT=lhsT, rhs=rhs, start=start, stop=stop, **kw), r, w)

    def act(self, out, in_, func, r, w, bias=None, scale=None, accum=None):
        kw = {}
        if bias is not None:
            kw["bias"] = bias
        if scale is not None:
            kw["scale"] = scale
        if accum is not None:
            kw["accum_out"] = accum
        self.P.op("act", lambda e: e.activation(out=out, in_=in_, func=func, **kw), r, w)

    def ts(self, eng, out, in0, s1, s2, op0, op1, r, w):
        if op1 is None and eng == "pool":
            s2, op1 = 1.0, ALU.mult
        if op1 is None:
            self.P.op(eng, lambda e: e.tensor_scalar(out=out, in0=in0, scalar1=s1, scalar2=None, op0=op0), r, w)
        else:
            self.P.op(eng, lambda e: e.tensor_scalar(out=out, in0=in0, scalar1=s1, scalar2=s2, op0=op0, op1=op1), r, w)

    def stt(self, out, in0, scalar, in1, op0, op1, r, w):
        self.P.op("dve", lambda e: e.scalar_tensor_tensor(out=out, in0=in0, scalar=scalar, in1=in1, op0=op0, op1=op1), r, w)

    def tt(self, eng, out, in0, in1, op, r, w):
        self.P.op(eng, lambda e: e.tensor_tensor(out=out, in0=in0, in1=in1, op=op), r, w)

    def cp(self, eng, out, in_, r, w):
        if eng == "act":
            self.P.op("act", lambda e: e.copy(out=out, in_=in_), r, w)
        else:
            self.P.op(eng, lambda e: e.tensor_copy(out=out, in_=in_), r, w)

    def recip(self, out, in_, r, w):
        self.P.op("dve", lambda e: e.reciprocal(out=out, in_=in_), r, w)

    def memset(self, eng, ap, val, w):
        self.P.op(eng, lambda e: e.memset(ap, val), [], w)

    def dma_cast(self, out, in_, r, w):
        self.P.dma("pool", lambda e: e.dma_start(out=out, in_=in_), r, w)

    def dma_plain(self, out, in_, r, w, is_output=False):
        self.P.dma("sp", lambda e: e.dma_start(out=out, in_=in_), r, w, is_output=is_output)

    def bank(self):
        b = self._bank
        self._bank = (b + 1) % 6
        return b

    def statbank(self):
        b = 6 + self._stat
        self._stat ^= 1
        return b

    def psb(self, b, lo=0, hi=512):
        return self.ps[:, b * 512 + lo:b * 512 + hi]

    def build(self):
        nc = self.nc
        st = ExitStack()
        with st:
            self.declare_dram()
            self.sb_all = st.enter_context(nc.sbuf_tensor("sb_all", [128, 106300], BF16))
            self.ps = st.enter_context(nc.psum_tensor("ps_all", [128, 4096], F32))
            self.PBK = [Buf("psk%d" % i, excl=True) for i in range(32)]
            self.PB = [self.PBK[4 * i:4 * i + 4] for i in range(8)]
            self.A = Arena(self.sb_all, 0, 106300 * 2)
            self.setup_persistent()
            for i in self.layers:
                self.layer(i)
            self.store_output()
            self.P.emit(st)
        return nc

    def declare_dram(self):
        nc = self.nc
        dt = lambda n, s: nc.dram_tensor(n, s, F32, kind="ExternalInput").ap()
        self.d = {}
        self.d["xT"] = dt("xT", [8, 128, S])
        self.d["ptab"] = dt("ptab", [128, self.pcols])
        self.d["ident"] = dt("ident", [128, 128])
        for i in self.layers:
            self.d["pT%d" % i] = dt("pT%d" % i, [2, 128, S])
            self.d["wup%d" % i] = dt("wup%d" % i, [NFC, 128, 8 * 256])
            self.d["wdn%d" % i] = dt("wdn%d" % i, [8, 128, NFC * 128])
            self.d["wg%d" % i] = dt("wg%d" % i, [8, 128, 8 * 128])
            self.d["wp%d" % i] = dt("wp%d" % i, [128, 2 * D])
            kind, j = i % 3, i // 3
            if kind == 0:
                self.d["a_wu%d" % j] = dt("a_wu%d" % j, [16, 128, 8 * 128])
                self.d["a_wv%d" % j] = dt("a_wv%d" % j, [4, 128, 8 * 512])
                self.d["a_wo%d" % j] = dt("a_wo%d" % j, [8, 128, 16 * 128])
                self.d["a_ws%d" % j] = dt("a_ws%d" % j, [128, 8 * 128])
                self.d["a_bs%d" % j] = dt("a_bs%d" % j, [128, 8 * 128])
            elif kind == 1:
                self.d["b_wqk"] = dt("b_wqk", [8, 128, 8 * 256])
                self.d["b_wv"] = dt("b_wv", [4, 128, 8 * 256])
                self.d["b_wo"] = dt("b_wo", [8, 128, 8 * 128])
                self.d["b_bias"] = dt("b_bias", [128, 16 * 640])
            else:
                self.d["c_wi"] = dt("c_wi", [8, 128, 8 * 256])
                self.d["c_wo"] = dt("c_wo", [8, 128, 8 * 128])
        self.yT = nc.dram_tensor("yT", [8, 128, S], F32, kind="ExternalOutput").ap()

    def setup_persistent(self):
        A = self.A
        self.h = A.alloc(F32, 8, S)
        self.Bh = [[Buf("h%d_%d" % (c, t)) for t in range(NT)] for c in range(8)]
        self.ptab = A.alloc(F32, self.pcols)
        self.Bptab = Buf("ptab", const=True)
        self.ident = A.alloc(BF16, 128)
        self.onesb = A.alloc(BF16, 128)
        self.epsc = A.alloc(F32, 1)
        self.dummy = A.alloc(F32, 8)
        self.Bconst = Buf("const", const=True)
        self.sq = [A.alloc(BF16, T) for _ in range(3)]
        self.Bsq = [Buf("sq%d" % i) for i in range(3)]
        self._sq = 0
        self.rstds = [A.alloc(F32, T) for _ in range(3)]
        self.Brstds = [Buf("rstd%d" % k) for k in range(3)]
        self.rstd, self.Brstd = self.rstds[0], self.Brstds[0]
        self.mres = A.alloc(F32, 8, T)
        self.Bmres = [Buf("mres%d" % c) for c in range(8)]
        self.hns = [A.alloc(BF16, 8, T) for _ in range(2)]
        self.hn1_off = A.off - 8 * T * 2
        self.Bhns = [[Buf("hn%d_%d" % (k, c)) for c in range(8)] for k in range(2)]
        self.hn, self.Bhn = self.hns[0], self.Bhns[0]
        self.rtmp = [A.alloc(F32, T) for _ in range(3)]
        self.Brtmp = [Buf("rtmp%d" % i) for i in range(3)]
        self._rt = 0
        self.phase_mark = A.mark()
        self.dma_plain(self.ptab, self.d["ptab"], [], [self.Bptab])
        for c in range(8):
            self.dma_plain(self.h[:, c, :], self.d["xT"][c], [], self.Bh[c])
        self.memset("pool", self.onesb, 1.0, [self.Bconst])
        self.memset("pool", self.epsc, EPS, [self.Bconst])
        self.dma_cast(self.ident, self.d["ident"], [], [self.Bconst])

    def pcol(self, key, c, n=1):
        o = self.poff[key] + c
        return self.ptab[:, o:o + n]

    def nextsq(self):
        i = self._sq
        self._sq = (i + 1) % 3
        return self.sq[i], self.Bsq[i]

    def nextrt(self):
        i = self._rt
        self._rt = (i + 1) % 3
        return self.rtmp[i], self.Brtmp[i]

    def defer_mm(self, *args, **kw):
        self.flush_mm()
        self._pend_mm = (args, kw)

    def flush_mm(self):
        p = getattr(self, "_pend_mm", None)
        if p is not None:
            self._pend_mm = None
            self.mm(*p[0], **p[1])

    def stat_add(self, sb, src, src_bufs, c, n=8):
        sq, Bsq = self.nextsq()
        self.act(sq, src, AF.Square, src_bufs, [Bsq])
        self.defer_mm(self.psb(sb), self.onesb, sq, c == 0, c == n - 1, [Bsq, self.Bconst], [self.PB[sb]])

    def finish_rstd(self, sb, dim=D, role=0):
        self.flush_mm()
        rstd, Brstd = self.rstds[role], self.Brstds[role]
        self.act(rstd, self.psb(sb), AF.Sqrt, [self.PB[sb], self.Bconst], [Brstd], bias=self.epsc, scale=1.0 / dim)
        self.recip(rstd, rstd, [Brstd], [Brstd])
        return rstd, Brstd

    def pre_norm_gen(self, t, gkey, k=0):
        hn, Bhn = self.hns[k], self.Bhns[k]
        sb = self.statbank()
        tsl = slice(t * T, (t + 1) * T)
        for c in range(8):
            self.stat_add(sb, self.h[:, c, tsl], [self.Bh[c][t]], c)
            yield
        rstd, Brstd = self.finish_rstd(sb, role=0)
        yield
        for c in range(8):
            self.stt(hn[:, c, :], self.h[:, c, tsl], self.pcol(gkey, c), rstd, ALU.mult, ALU.mult,
                     [self.Bh[c][t], Brstd, self.Bptab], [Bhn[c]])
            if c % 2:
                yield

    def pre_norm(self, t, gkey, k=0):
        for _ in self.pre_norm_gen(t, gkey, k):
            pass

    def post_norm_gen(self, t, gkey, sb, role, after=None):
        rstd, Brstd = self.finish_rstd(sb, role=role)
        tsl = slice(t * T, (t + 1) * T)
        yield
        rts = {}
        for i in range(10):
            if i < 8:
                rt, Brt = self.nextrt()
                rts[i] = (rt, Brt)
                self.stt(rt, self.mres[:, i, :], self.pcol(gkey, i), rstd, ALU.mult, ALU.mult,
                         [self.Bmres[i], Brstd, self.Bptab], [Brt])
            if 1 <= i < 9:
                c = i - 1
                rt, Brt = rts.pop(c)
                self.tt("pool", self.h[:, c, tsl], self.h[:, c, tsl], rt, ALU.add, [self.Bh[c][t], Brt], [self.Bh[c][t]])
            if 2 <= i < 10 and after is not None:
                after(i - 2)
            yield

    def post_norm_residual(self, t, gkey, sb, role=1):
        for _ in self.post_norm_gen(t, gkey, sb, role):
            pass

    def evac_mres(self, b, dc, sb):
        self.cp("dve", self.mres[:, dc, :], self.psb(b), [self.PB[b]], [self.Bmres[dc]])
        self.stat_add(sb, self.mres[:, dc, :], [self.Bmres[dc]], dc)

    def make_slots(self, name, n, *free):
        aps = [self.A.alloc(BF16, *free) for _ in range(n)]
        bufs = [Buf("%s%d" % (name, i)) for i in range(n)]
        return {"aps": aps, "bufs": bufs, "i": 0, "n": n}

    def load_slot(self, slots, src):
        i = slots["i"]
        slots["i"] = (i + 1) % slots["n"]
        ap, bf = slots["aps"][i], slots["bufs"][i]
        flat = ap
        if len(ap.shape) == 3:
            flat = ap.rearrange("p a b -> p (a b)")
        self.dma_cast(flat, src, [], [bf])
        return ap, bf

    def stream(self, name, n, free, srcs):
        return Stream(self, name, n, free, srcs)

    def stop(self, tag):
        if tag in self.dbg:
            raise StopBuild()

    def layer(self, i):
        try:
            self._layer(i)
        except StopBuild:
            pass

    def new_phase(self):
        self.flush_mm()
        self.A.reset(self.phase_mark)
        f = {}
        for e in ("pe", "act", "dve", "pool"):
            for o in reversed(self.P.ops[e]):
                if not o.is_dma:
                    f[e] = o
                    break
        Buf.fence = f

    def _layer(self, i):
        kind, j = i % 3, i // 3
        A = self.A
        self.new_phase()
        if "prenorm" in self.dbg:
            self.pre_norm(0, ("mix_pre", i))
            return
        if "nomix" in self.dbg:
            pass
        elif kind == 0:
            self.mixer_a(i, j)
        elif kind == 1:
            self.mixer_b(i)
        else:
            self.mixer_c(i)
        self.new_phase()
        if "noffn" not in self.dbg:
            self.ffn_phase(i)

    def ffn_phase(self, i):
        A = self.A
        actb = A.alloc(BF16, NFC, T)
        Bact = [Buf("act%d" % f) for f in range(NFC)]
        wupS = self.stream("wup", 4, (8, 256), [self.d["wup%d" % i][fc] for _t in range(NT) for fc in range(NFC)])
        wdnS = self.stream("wdn", 2, (NFC, 128), [self.d["wdn%d" % i][dc] for _t in range(NT) for dc in range(8)])
        wgS = self.stream("wg", 2, (8, 128), [self.d["wg%d" % i][dc] for _t in range(NT) for dc in range(8)])
        cg = [A.alloc(F32, T) for _ in range(2)]
        cv = [A.alloc(F32, T) for _ in range(2)]
        sg = [A.alloc(F32, T) for _ in range(2)]
        Bcg = [Buf("cg%d" % k) for k in range(2)]
        Bcv = [Buf("cv%d" % k) for k in range(2)]
        Bsg = [Buf("sg%d" % k) for k in range(2)]
        halo = [A.alloc(F32, 2 * NFC, 2) for _ in range(2)]
        Bhalo = [[Buf("halo%d_%d" % (k, q)) for q in range(2 * NFC)] for k in range(2)]
        bnd = [A.alloc(F32, 3, 2 * NFC) for _ in range(2)]
        Bbnd = [Buf("bnd%d" % k) for k in range(2)]
        hb = A.alloc(BF16, 8, T)
        Bhb = [Buf("hb%d" % c) for c in range(8)]
        pt = [A.alloc(BF16, 2, T) for _ in range(2)]
        Bpt = [Buf("pt%d" % k) for k in range(2)]
        wp = A.alloc(BF16, 2, D)
        Bwp = Buf("wp")
        gate = [A.alloc(F32, T) for _ in range(2)]
        Bgate = [Buf("gate%d" % k) for k in range(2)]
        self.dma_cast(wp.rearrange("p a b -> p (a b)"), self.d["wp%d" % i], [], [Bwp])
        cbase = self.poff[("conv", i)]

        def tap(jj, q):
            o = cbase + jj * 44 + q
            return self.ptab[:, o:o + 1]

        def step(bg):
            for g in list(bg):
                try:
                    next(g)
                except StopIteration:
                    bg.remove(g)

        def drain(bg):
            while bg:
                step(bg)

        def load_pt(t):
            ptt, Bptt = pt[t % 2], Bpt[t % 2]
            tsl = slice(t * T, (t + 1) * T)
            for kc in range(2):
                self.dma_cast(ptt[:, kc, :], self.d["pT%d" % i][kc][:, tsl], [], [Bptt])

        def stage_A(t):
            return self.pre_norm_gen(t, ("ffn_pre", i), k=t % 2)

        def stage_B(t, bg):
            hn, Bhn = self.hns[t % 2], self.Bhns[t % 2]
            if t > 0:
                ho, Bho = halo[(t - 1) % 2], Bhalo[(t - 1) % 2]
                bd, Bbd = bnd[t % 2], Bbnd[t % 2]
                W0 = self.ptab[:, cbase:cbase + 44]
                W1 = self.ptab[:, cbase + 44:cbase + 88]
                self.tt("dve", bd[:, 2, :], ho[:, :, 1], W1, ALU.mult, Bho + [self.Bptab], [Bbd])
                self.tt("dve", bd[:, 0, :], ho[:, :, 0], W0, ALU.mult, Bho + [self.Bptab], [Bbd])
                self.tt("dve", bd[:, 0, :], bd[:, 0, :], bd[:, 2, :], ALU.add, [Bbd], [Bbd])
                self.tt("dve", bd[:, 1, :], ho[:, :, 1], W0, ALU.mult, Bho + [self.Bptab], [Bbd])
            for fc in range(NFC):
                w, Bw = wupS.next()
                k2 = fc % 2
                bg_ = self.bank()
                bv_ = self.bank()
                for kc in range(8):
                    self.mm(self.psb(bg_), w[:, kc, 0:128], hn[:, kc, :], kc == 0, kc == 7, [Bw, Bhn[kc]], [self.PB[bg_]])
                for kc in range(8):
                    self.mm(self.psb(bv_), w[:, kc, 128:256], hn[:, kc, :], kc == 0, kc == 7, [Bw, Bhn[kc]], [self.PB[bv_]])
                for (b, q, cbuf, Bc) in ((bg_, fc, cg[k2], Bcg[k2]), (bv_, NFC + fc, cv[k2], Bcv[k2])):
                    pb = self.PB[b]
                    if t == 0:
                        self.act(cbuf, self.psb(b), AF.Copy, [pb, self.Bptab], [Bc], scale=tap(2, q))
                    else:
                        bd, Bbd = bnd[t % 2], Bbnd[t % 2]
                        self.act(cbuf[:, 2:T], self.psb(b, 2, T), AF.Copy, [pb, self.Bptab], [Bc], scale=tap(2, q))
                        self.act(cbuf[:, 0:1], self.psb(b, 0, 1), AF.Identity, [pb, self.Bptab, Bbd], [Bc],
                                 scale=tap(2, q), bias=bd[:, 0, q:q + 1])
                        self.act(cbuf[:, 1:2], self.psb(b, 1, 2), AF.Identity, [pb, self.Bptab, Bbd], [Bc],
                                 scale=tap(2, q), bias=bd[:, 1, q:q + 1])
                    if t < NT - 1:
                        self.cp("act", halo[t % 2][:, q, 0:2], self.psb(b, T - 2, T), [pb], [Bhalo[t % 2][q]])
                    self.stt(cbuf[:, 1:T], self.psb(b, 0, T - 1), tap(1, q), cbuf[:, 1:T], ALU.mult, ALU.add,
                             [pb, Bc, self.Bptab], [Bc])
                    self.stt(cbuf[:, 2:T], self.psb(b, 0, T - 2), tap(0, q), cbuf[:, 2:T], ALU.mult, ALU.add,
                             [pb, Bc, self.Bptab], [Bc])
                self.act(sg[k2], cg[k2], AF.Silu, [Bcg[k2]], [Bsg[k2]])
                self.tt("pool", actb[:, fc, :], sg[k2], cv[k2], ALU.mult, [Bsg[k2], Bcv[k2]], [Bact[fc]])
                step(bg)
            drain(bg)

        def stage_C(t, bg):
            sb = self.statbank()
            for dc in range(8):
                w, Bw = wdnS.next()
                b = self.bank()
                for fc in range(NFC):
                    self.mm(self.psb(b), w[:, fc, :], actb[:, fc, :], fc == 0, fc == NFC - 1, [Bw, Bact[fc]], [self.PB[b]])
                step(bg)
                self.evac_mres(b, dc, sb)
            drain(bg)
            return sb

        def stage_D(t, sb):
            tsl = slice(t * T, (t + 1) * T)

            def after(c):
                self.cp("act", hb[:, c, :], self.h[:, c, tsl], [self.Bh[c][t]], [Bhb[c]])
            g = self.post_norm_gen(t, ("ffn_post", i), sb, 1, after=after)
            next(g)
            return g

        def stage_E(t):
            ptt, Bptt = pt[t % 2], Bpt[t % 2]
            if t + 1 < NT:
                load_pt(t + 1)
            sb = self.statbank()
            for dc in range(8):
                w, Bw = wgS.next()
                bgt = self.bank()
                be = self.bank()
                for kc in range(8):
                    self.mm(self.psb(bgt), w[:, kc, :], hb[:, kc, :], kc == 0, kc == 7, [Bw, Bhb[kc]], [self.PB[bgt]])
                for kc in range(2):
                    self.mm(self.psb(be), wp[:, kc, dc * 128:(dc + 1) * 128], ptt[:, kc, :], kc == 0, kc == 1,
                            [Bwp, Bptt], [self.PB[be]])
                k2 = dc % 2
                self.act(gate[k2], self.psb(bgt), AF.Sigmoid, [self.PB[bgt]], [Bgate[k2]])
                self.tt("dve", self.mres[:, dc, :], gate[k2], self.psb(be), ALU.mult, [Bgate[k2], self.PB[be]],
                        [self.Bmres[dc]])
                self.stat_add(sb, self.mres[:, dc, :], [self.Bmres[dc]], dc)
            return sb

        load_pt(0)
        drain([stage_A(0)])
        stage_B(0, [stage_A(1)])
        pend = []
        for t in range(NT):
            sb = stage_C(t, pend)
            pend = []
            bgl = [stage_D(t, sb)]
            if t + 2 < NT:
                bgl.append(stage_A(t + 2))
            if t + 1 < NT:
                stage_B(t + 1, bgl)
            else:
                drain(bgl)
            sbe = stage_E(t)
            g = self.post_norm_gen(t, ("ple_g", i), sbe, 2)
            next(g)
            pend = [g]
        drain(pend)
        self.flush_mm()

    def mixer_c(self, i):
        A = self.A
        ybuf = A.alloc(BF16, 8, 30 + T)
        Byb = [Buf("yb%d" % c) for c in range(8)]
        z = A.alloc(F32, 8, T)
        Bz = [Buf("z%d" % c) for c in range(8)]
        zb = [A.alloc(BF16, T) for _ in range(2)]
        Bzb = [Buf("zb%d" % k) for k in range(2)]
        actc = A.alloc(BF16, 8, T)
        Bac = [Buf("actc%d" % c) for c in range(8)]
        dg = [A.alloc(BF16, 31, 128) for _ in range(2)]
        Bdg = [Buf("dg%d" % k) for k in range(2)]
        wi = self.stream("cwi", 3, (8, 256), [self.d["c_wi"][c] for _t in range(NT) for c in range(8)])
        wo = self.stream("cwo", 3, (8, 128), [self.d["c_wo"][c] for _t in range(NT) for c in range(8)])
        sgm = [A.alloc(F32, T) for _ in range(2)]
        Bsgm = [Buf("sgm%d" % k) for k in range(2)]
        mean = A.alloc(F32, T)
        msq = A.alloc(F32, T)
        var = A.alloc(F32, T)
        nmr = A.alloc(F32, T)
        Bmean, Bmsq, Bvar, Bnmr = Buf("mean"), Buf("msq"), Buf("var"), Buf("nmr")
        t1 = [A.alloc(F32, T) for _ in range(2)]
        Bt1 = [Buf("ct1_%d" % k) for k in range(2)]
        for c in range(8):
            self.memset("pool", ybuf[:, c, 0:30], 0.0, [Byb[c]])

        def build_diag(c):
            k2_ = c % 2
            o_ = self.poff["c_dw"] + c * 31
            in1 = self.ptab[:, o_:o_ + 31].unsqueeze(2).to_broadcast([128, 31, 128])
            in0 = self.ident.unsqueeze(1).to_broadcast([128, 31, 128])
            self.tt("dve", dg[k2_], in0, in1, ALU.mult, [self.Bconst, self.Bptab], [Bdg[k2_]])

        self.pre_norm(0, ("mix_pre", i), k=0)
        for t in range(NT):
            hn, Bhn = self.hns[t % 2], self.Bhns[t % 2]
            sb1 = self.statbank()
            sb2 = self.statbank()
            build_diag(0)
            for c in range(8):
                if c + 1 < 8:
                    build_diag(c + 1)
                w, Bw = wi.next()
                ba = self.bank()
                bg = self.bank()
                for kc in range(8):
                    self.mm(self.psb(ba), w[:, kc, 0:128], hn[:, kc, :], kc == 0, kc == 7, [Bw, Bhn[kc]], [self.PB[ba]])
                for kc in range(8):
                    self.mm(self.psb(bg), w[:, kc, 128:256], hn[:, kc, :], kc == 0, kc == 7, [Bw, Bhn[kc]], [self.PB[bg]])
                k2 = c % 2
                self.act(sgm[k2], self.psb(bg), AF.Sigmoid, [self.PB[bg]], [Bsgm[k2]])
                self.tt("dve", ybuf[:, c, 30:30 + T], sgm[k2], self.psb(ba), ALU.mult, [Bsgm[k2], self.PB[ba]], [Byb[c]])
                bc = self.bank()
                for jj in range(31):
                    self.mm(self.psb(bc), dg[k2][:, jj, :], ybuf[:, c, jj:jj + T], jj == 0, jj == 30, [Bdg[k2], Byb[c]], [self.PB[bc]])
                bias = self.pcol("c_dwb", c)
                self.act(z[:, c, :], self.psb(bc), AF.Identity, [self.PB[bc], self.Bptab], [Bz[c]], bias=bias)
                self.act(zb[k2], self.psb(bc), AF.Identity, [self.PB[bc], self.Bptab], [Bzb[k2]], bias=bias)
                self.flush_mm()
                self.mm(self.psb(sb1), self.onesb, zb[k2], c == 0, c == 7, [Bzb[k2], self.Bconst], [self.PB[sb1]])
                sq, Bsq = self.nextsq()
                self.act(sq, self.psb(bc), AF.Square, [self.PB[bc], self.Bptab], [Bsq], bias=bias)
                self.defer_mm(self.psb(sb2), self.onesb, sq, c == 0, c == 7, [Bsq, self.Bconst], [self.PB[sb2]])
                if t < NT - 1:
                    self.cp("pool", ybuf[:, c, 0:30], ybuf[:, c, T:T + 30], [Byb[c]], [Byb[c]])
            self.flush_mm()
            self.ts("dve", mean, self.psb(sb1), 1.0 / D, None, ALU.mult, None, [self.PB[sb1]], [Bmean])
            self.act(msq, mean, AF.Square, [Bmean], [Bmsq])
            self.stt(var, self.psb(sb2), 1.0 / D, msq, ALU.mult, ALU.subtract, [self.PB[sb2], Bmsq], [Bvar])
            self.act(self.rstd, var, AF.Sqrt, [Bvar, self.Bconst], [self.Brstd], bias=self.epsc)
            self.recip(self.rstd, self.rstd, [self.Brstd], [self.Brstd])
            self.stt(nmr, mean, -1.0, self.rstd, ALU.mult, ALU.mult, [Bmean, self.Brstd], [Bnmr])
            for c in range(8):
                k2 = c % 2
                self.tt("dve", t1[k2], z[:, c, :], self.rstd, ALU.mult, [Bz[c], self.Brstd], [Bt1[k2]])
                self.tt("pool", t1[k2], t1[k2], nmr, ALU.add, [Bt1[k2], Bnmr], [Bt1[k2]])
                self.act(actc[:, c, :], t1[k2], AF.Silu, [Bt1[k2], self.Bptab], [Bac[c]],
                         scale=self.pcol("c_lng", c), bias=self.pcol("c_lnb", c))
            if t + 1 < NT:
                self.pre_norm(t + 1, ("mix_pre", i), k=(t + 1) % 2)
            sb = self.statbank()
            for dc in range(8):
                w, Bw = wo.next()
                b = self.bank()
                for c in range(8):
                    self.mm(self.psb(b), w[:, c, :], actc[:, c, :], c == 0, c == 7, [Bw, Bac[c]], [self.PB[b]])
                self.evac_mres(b, dc, sb)
            self.post_norm_residual(t, ("mix_post", i), sb)

    def mixer_a(self, i, j):
        A = self.A
        u = A.alloc(BF16, 16, T)
        Bu = [Buf("u%d" % c) for c in range(16)]
        vgb = A.alloc(BF16, 4, 2048)
        Bvgb = [Buf("vgb%d" % b) for b in range(4)]
        wv = self.stream("awv", 2, (8, 512), [self.d["a_wv%d" % j][q] for _t in range(NT) for q in range(4)])
        wu = self.stream("awu", 3, (8, 128), [self.d["a_wu%d" % j][q] for _t in range(NT) for q in range(16)])
        wo = self.stream("awo", 2, (16, 128), [self.d["a_wo%d" % j][q] for _t in range(NT) for q in range(8)])
        bst = A.alloc(F32, 4, 4, 6)
        Bbst = [Buf("bst%d" % b) for b in range(4)]
        mv = A.alloc(F32, 4, 2)
        rs = A.alloc(F32, 4, 1)
        vpe = A.alloc(F32, 4, 1)
        Bmv = [Buf("mv%d" % b) for b in range(4)]
        Brs = [Buf("rs%d" % b) for b in range(4)]
        Bvpe = [Buf("vpe%d" % b) for b in range(4)]
        nmh = A.alloc(F32, 1)
        wsT = A.alloc(BF16, 8, 128)
        Bws = Buf("wsT")
        Cc = A.alloc(F32, 16, 128)
        BCc = Buf("Cc")
        bsb = A.alloc(F32, 8, 128)
        Bbsb = Buf("bsb")
        rw = A.alloc(F32, 8, 128)
        Brw = Buf("rw")
        t1 = [A.alloc(F32, 4, 128) for _ in range(3)]
        Bt1 = [Buf("at1_%d" % k) for k in range(3)]
        self.memset("pool", nmh, -0.5, [self.Bconst])
        self.dma_cast(wsT.rearrange("p a b -> p (a b)"), self.d["a_ws%d" % j], [], [Bws])
        self.memset("pool", wsT[64:128, :, 0:64], 0.0, [Bws])
        self.dma_plain(bsb.rearrange("p a b -> p (a b)"), self.d["a_bs%d" % j], [], [Bbsb])
        b0 = self.bank()
        b1 = self.bank()
        for g in range(8):
            bb = b0 if g < 4 else b1
            lo = (g % 4) * 128
            self.mm(self.psb(bb, lo, lo + 128), self.onesb, wsT[:, g, :], True, True, [Bws, self.Bconst], [self.PB[bb]])
        self.cp("act", rw[:, 0:4, :].rearrange("p a b -> p (a b)"), self.psb(b0), [self.PB[b0]], [Brw])
        self.cp("act", rw[:, 4:8, :].rearrange("p a b -> p (a b)"), self.psb(b1), [self.PB[b1]], [Brw])
        for uc in range(16):
            g = uc // 2
            self.stt(Cc[:, uc, :], rw[:, g, :], self.pcol(("a_lnb", j), uc), bsb[:, g, :], ALU.mult, ALU.add,
                     [Brw, Bbsb, self.Bptab], [BCc])
        self.pre_norm(0, ("mix_pre", i), k=0)
        for t in range(NT):
            hn, Bhn = self.hns[t % 2], self.Bhns[t % 2]
            for vq in range(4):
                w, Bw = wv.next()
                for blk in range(4):
                    b = self.bank()
                    for kc in range(8):
                        self.mm(self.psb(b), hn[:, kc, blk * 128:(blk + 1) * 128], w[:, kc, :], kc == 0, kc == 7,
                                [Bw, Bhn[kc]], [self.PB[b]])
                    dst = vgb[:, blk, vq * 512:(vq + 1) * 512]
                    self.act(dst, self.psb(b), AF.Gelu_apprx_tanh, [self.PB[b]], [Bvgb[blk]])
                    bo = bst[:, blk, vq, :]
                    self.P.op("dve", lambda e, bo=bo, src=dst: e.bn_stats(out=bo, in_=src), [Bvgb[blk]], [Bbst[blk]])
            for blk in range(4):
                mo = mv[:, blk, :]
                bi = bst[:, blk, :, :]
                self.P.op("dve", lambda e, mo=mo, bi=bi: e.bn_aggr(out=mo, in_=bi), [Bbst[blk]], [Bmv[blk]])
                self.ts("pool", vpe[:, blk, :], mv[:, blk, 1:2], EPS, None, ALU.add, None, [Bmv[blk]], [Bvpe[blk]])
                self.tt("pool", rs[:, blk, :], vpe[:, blk, :], nmh, ALU.pow, [Bvpe[blk], self.Bconst], [Brs[blk]])
                self.ts("dve", vgb[:, blk, :], vgb[:, blk, :], mv[:, blk, 0:1], rs[:, blk, :], ALU.subtract, ALU.mult,
                        [Bvgb[blk], Bmv[blk], Brs[blk]], [Bvgb[blk]])
            kk = 0
            for u4 in range(4):
                for j4 in range(4):
                    uc = u4 * 4 + j4
                    w, Bw = wu.next()
                    b = self.bank()
                    for kc in range(8):
                        self.mm(self.psb(b), w[:, kc, :], hn[:, kc, :], kc == 0, kc == 7, [Bw, Bhn[kc]], [self.PB[b]])
                    self.act(u[:, uc, :], self.psb(b), AF.Gelu_apprx_tanh, [self.PB[b]], [Bu[uc]])
                for blk in range(4):
                    bsl = slice(blk * 128, (blk + 1) * 128)
                    b = self.bank()
                    for j4 in range(4):
                        uc = u4 * 4 + j4
                        self.mm(self.psb(b, j4 * 128, (j4 + 1) * 128), vgb[:, blk, uc * 128:(uc + 1) * 128], wsT[:, uc // 2, :], True, True,
                                [Bvgb[blk], Bws], [self.PB[b]])
                    k3 = kk % 3
                    kk += 1
                    for j4 in range(4):
                        uc = u4 * 4 + j4
                        self.stt(t1[k3][:, j4, :], self.psb(b, j4 * 128, (j4 + 1) * 128), self.pcol(("a_lng", j), uc), Cc[:, uc, :],
                                 ALU.mult, ALU.add, [self.PB[b], BCc, self.Bptab], [Bt1[k3]])
                    uv = u[:, u4 * 4:(u4 + 1) * 4, bsl]
                    self.tt("dve", uv, t1[k3], uv, ALU.mult, [Bt1[k3]] + Bu[u4 * 4:(u4 + 1) * 4], Bu[u4 * 4:(u4 + 1) * 4])
            if t + 1 < NT:
                self.pre_norm(t + 1, ("mix_pre", i), k=(t + 1) % 2)
            sb = self.statbank()
            for dc in range(8):
                w, Bw = wo.next()
                b = self.bank()
                for uc in range(16):
                    self.mm(self.psb(b), w[:, uc, :], u[:, uc, :], uc == 0, uc == 15, [Bw, Bu[uc]], [self.PB[b]])
                self.evac_mres(b, dc, sb)
            self.post_norm_residual(t, ("mix_post", i), sb)

    def mixer_b(self, i):
        A = self.A
        kT = A.alloc(BF16, 8, 1024)
        BkT = [[Buf("kT%d_%d" % (c, s_)) for s_ in range(2)] for c in range(8)]
        V = A.alloc(BF16, 8, 1024)
        BV = [Buf("V%d" % b) for b in range(8)]
        qz = A.alloc(BF16, 8, 2, T)
        Bq = [Buf("qz%d" % c) for c in range(8)]
        Bb = A.alloc(BF16, 16, 640)
        BBb = Buf("Bb")
        A2 = Arena(self.sb_all, self.hn1_off, self.hn1_off + 8 * T * 2)
        Pb = [A2.alloc(BF16, 640) for _ in range(2)]
        BPb = [Buf("Pb%d" % k) for k in range(2)]
        PTb = [A2.alloc(BF16, 640) for _ in range(3)]
        BPTb = [Buf("PTb%d" % k) for k in range(3)]
        dgr = [A2.alloc(BF16, 128) for _ in range(3)]
        Bdgr = [Buf("dgr%d" % k) for k in range(3)]
        st3 = [A.alloc(F32, 4) for _ in range(3)]
        Bst = [Buf("st%d" % k) for k in range(3)]
        wqk = self.stream("bwqk", 2, (8, 256), [self.d["b_wqk"][q] for _t in range(NT) for q in range(8)])
        wvs = self.stream("bwv", 2, (8, 256), [self.d["b_wv"][q] for _t in range(NT) for q in range(4)])
        wos = self.stream("bwo", 2, (8, 128), [self.d["b_wo"][q] for _t in range(NT) for q in range(8)])
        oT, BoT = self.hn, self.Bhn
        ps = self.ps
        self.dma_cast(Bb.rearrange("p a b -> p (a b)"), self.d["b_bias"], [], [BBb])
        self.memset("pool", Bb[64:128, :, 0:64], NEG, [BBb])
        self.memset("pool", Bb[0:64, :, 576:640], NEG, [BBb])
        for c in range(8):
            self.memset("pool", qz[:, c, :, :], 0.0, [Bq[c]])
        unit = 0
        for t in range(NT):
            slot = t % 2
            self.pre_norm(t, ("mix_pre", i))
            for c in range(8):
                w, Bw = wqk.next()
                bq = self.bank()
                bk = self.bank()
                for kc in range(8):
                    self.mm(self.psb(bq), w[:, kc, 0:128], self.hn[:, kc, :], kc == 0, kc == 7, [Bw, self.Bhn[kc]], [self.PB[bq]])
                for kc in range(8):
                    self.mm(self.psb(bk), w[:, kc, 128:256], self.hn[:, kc, :], kc == 0, kc == 7, [Bw, self.Bhn[kc]], [self.PB[bk]])
                for hh in range(2):
                    rows = slice(hh * 64, hh * 64 + 64)
                    self.act(qz[rows, c, hh, :], ps[rows, bq * 512:(bq + 1) * 512], AF.Copy, [self.PB[bq]], [Bq[c]], scale=0.125)
                self.cp("dve", kT[:, c, slot * 512:(slot + 1) * 512], self.psb(bk), [self.PB[bk]], [BkT[c][slot]])
            for qt in range(4):
                w, Bw = wvs.next()
                for blk in range(4):
                    b = self.bank()
                    for kc in range(8):
                        self.mm(self.psb(b, 0, 256), self.hn[:, kc, blk * 128:(blk + 1) * 128], w[:, kc, :], kc == 0, kc == 7,
                                [Bw, self.Bhn[kc]], [self.PB[b]])
                    rb = (4 * t + blk) % 8
                    self.cp("act" if (blk % 2) else "dve", V[:, rb, qt * 256:(qt + 1) * 256], self.psb(b, 0, 256), [self.PB[b]], [BV[rb]])
            units = [(c, jq, hh) for c in range(8) for jq in range(4) for hh in range(2)]
            NU = len(units)

            def geom(u):
                c, jq, hh = units[u]
                jb = 4 * t + jq
                return c, jq, hh, jb, max(0, 4 - jb)

            def st_scores(u):
                c, jq, hh, jb, i0 = geom(u)
                hd = 2 * c + hh
                sl = u % 2
                k3 = u % 3
                sbase = sl * 1024
                for ii in range(i0, 5):
                    kb = jb - 4 + ii
                    rc = (kb % 8) * 128
                    ks = (kb // 4) % 2
                    sap = ps[:, sbase + ii * 128: sbase + (ii + 1) * 128]
                    wb = self.PB[2 * sl] if ii < 4 else self.PB[2 * sl + 1]
                    self.mm(sap, qz[:, c, hh, jq * 128:(jq + 1) * 128], kT[:, c, rc:rc + 128], True, False,
                            [Bq[c], BkT[c][ks]], wb)
                    self.mm(sap, self.ident, Bb[:, hd, ii * 128:(ii + 1) * 128], False, True, [self.Bconst, BBb], wb)
                sbufs = [self.PB[2 * sl], self.PB[2 * sl + 1]]
                sfull = ps[:, sbase + i0 * 128: sbase + 640]
                nmax, rsum, rinv = st3[k3][:, 0:1], st3[k3][:, 1:2], st3[k3][:, 2:3]
                self.P.op("dve", lambda e, nmax=nmax, sfull=sfull: e.tensor_reduce(out=nmax, in_=sfull, axis=AX.X, op=ALU.max, negate=True),
                          sbufs, [Bst[k3]])
                self.act(Pb[sl][:, i0 * 128:640], sfull, AF.Exp, sbufs + [Bst[k3]], [BPb[sl], Bst[k3]], bias=nmax, accum=rsum)
                self.recip(rinv, rsum, [Bst[k3]], [Bst[k3]])
                self.ts("dve", dgr[k3], self.ident, rinv, None, ALU.mult, None, [self.Bconst, Bst[k3]], [Bdgr[k3]])

            def st_pt(u):
                c, jq, hh, jb, i0 = geom(u)
                sl = u % 2
                k3 = u % 3
                pbase = 2048
                for ii in range(i0, 5):
                    self.mm(ps[:, pbase + ii * 128: pbase + (ii + 1) * 128], Pb[sl][:, ii * 128:(ii + 1) * 128], dgr[k3], True, True,
                            [BPb[sl], Bdgr[k3]], [self.PB[4] if ii < 4 else self.PB[5]])
                self.cp("act" if (u % 2) else "dve", PTb[k3][:, i0 * 128:640], ps[:, pbase + i0 * 128: pbase + 640],
                        [self.PB[4], self.PB[5]], [BPTb[k3]])

            def st_pv(u):
                c, jq, hh, jb, i0 = geom(u)
                k3 = u % 3
                ob = 6 + hh
                for ii in range(i0, 5):
                    kb = jb - 4 + ii
                    self.mm(ps[:, ob * 512 + jq * 128: ob * 512 + (jq + 1) * 128], V[:, kb % 8, c * 128:(c + 1) * 128],
                            PTb[k3][:, ii * 128:(ii + 1) * 128], ii == i0, ii == 4, [BV[kb % 8], BPTb[k3]], [self.PB[ob]])
                if jq == 3 and hh == 1:
                    for h2 in range(2):
                        rows = slice(h2 * 64, h2 * 64 + 64)
                        o2 = 6 + h2
                        self.cp("dve" if h2 else "act", oT[rows, c, :], ps[rows, o2 * 512:(o2 + 1) * 512], self.PB[o2], [BoT[c]])

            for u in range(NU + 2):
                if u < NU:
                    st_scores(u)
                if 0 <= u - 1 < NU:
                    st_pt(u - 1)
                if 0 <= u - 2 < NU:
                    st_pv(u - 2)
            sb = self.statbank()
            for dc in range(8):
                w, Bw = wos.next()
                b = self.bank()
                for c in range(8):
                    self.mm(self.psb(b), w[:, c, :], oT[:, c, :], c == 0, c == 7, [Bw, BoT[c]], [self.PB[b]])
                self.evac_mres(b, dc, sb)
            self.post_norm_residual(t, ("mix_post", i), sb)

    def store_output(self):
        for c in range(8):
            self.dma_plain(self.yT[c], self.h[:, c, :], self.Bh[c], [Buf("y%d" % c)], is_output=True)


def _cols(v, n):
    return np.ascontiguousarray(np.asarray(v, np.float32).reshape(n, 128).T)


def _kc_tile(w, ncols_per_block):
    K, N = w.shape
    nb = N // ncols_per_block
    x = w.reshape(K // 128, 128, nb, ncols_per_block)
    x = x.transpose(2, 1, 0, 3)
    return np.ascontiguousarray(x).reshape(nb, 128, (K // 128) * ncols_per_block)


def host_shared(inp, layers):
    off, R = ptab_layout()
    ptab = np.zeros((128, R), np.float32)

    def put(key, arr):
        ptab[:, off[key]:off[key] + arr.shape[1]] = arr

    for i in range(DEPTH):
        put(("mix_pre", i), _cols(inp["mix_pre_g"][i], 8))
        put(("mix_post", i), _cols(inp["mix_post_g"][i], 8))
        put(("ffn_pre", i), _cols(inp["ffn_pre_g"][i], 8))
        put(("ffn_post", i), _cols(inp["ffn_post_g"][i], 8))
        put(("ple_g", i), _cols(inp["ple_norm_g"][i], 8))
        cv = np.concatenate([_cols(inp["ffn_conv"][i][jj], 44) for jj in range(3)], axis=1)
        put(("conv", i), cv)
    for j in range(2):
        put(("a_lng", j), _cols(inp["a_ln_g"][j], 16))
        put(("a_lnb", j), _cols(inp["a_ln_b"][j], 16))
    dw = np.asarray(inp["c_dw"][0], np.float32)
    dwc = dw.reshape(31, 8, 128).transpose(2, 1, 0).reshape(128, 248)
    put("c_dw", np.ascontiguousarray(dwc))
    put("c_dwb", _cols(inp["c_dw_b"][0], 8))
    put("c_lng", _cols(inp["c_ln_g"][0], 8))
    put("c_lnb", _cols(inp["c_ln_b"][0], 8))
    sh = {"ptab": ptab, "ident": np.eye(128, dtype=np.float32)}
    for i in layers:
        wu = np.asarray(inp["ffn_w_up"][i], np.float32)
        g = wu[:, :FF].reshape(D, NFC, 128)
        v = wu[:, FF:].reshape(D, NFC, 128)
        gv = np.concatenate([g, v], axis=2).reshape(D, NFC * 256)
        sh["wup%d" % i] = _kc_tile(gv, 256)
        wd = np.asarray(inp["ffn_w_down"][i], np.float32)
        x = wd.reshape(NFC, 128, 8, 128).transpose(2, 1, 0, 3)
        sh["wdn%d" % i] = np.ascontiguousarray(x).reshape(8, 128, NFC * 128)
        sh["wg%d" % i] = _kc_tile(np.asarray(inp["ple_w_gate"][i], np.float32), 128)
        wp = np.asarray(inp["ple_w_proj"][i], np.float32)
        sh["wp%d" % i] = np.ascontiguousarray(wp.reshape(2, 128, D).transpose(1, 0, 2)).reshape(128, 2 * D)
        kind, j = i % 3, i // 3
        if kind == 0:
            win = np.asarray(inp["a_w_in"][j], np.float32)
            sh["a_wu%d" % j] = _kc_tile(win[:, :2048], 128)
            sh["a_wv%d" % j] = _kc_tile(win[:, 2048:], 512)
            wo = np.asarray(inp["a_w_out"][j], np.float32)
            x = wo.reshape(16, 128, 8, 128).transpose(2, 1, 0, 3)
            sh["a_wo%d" % j] = np.ascontiguousarray(x).reshape(8, 128, 16 * 128)
            ws = np.asarray(inp["a_w_s"][j], np.float32)
            sh["a_ws%d" % j] = np.ascontiguousarray(ws.transpose(2, 0, 1)).reshape(128, 8 * 128)
            bs = np.asarray(inp["a_b_s"][j], np.float32).reshape(1, 8 * 128)
            sh["a_bs%d" % j] = np.ascontiguousarray(np.broadcast_to(bs, (128, 8 * 128)))
        elif kind == 1:
            wq = np.asarray(inp["b_w_qkv"][0], np.float32)
            q = wq[:, :D].reshape(D, 8, 128)
            k = wq[:, D:2 * D].reshape(D, 8, 128)
            qk = np.concatenate([q, k], axis=2).reshape(D, 8 * 256)
            sh["b_wqk"] = _kc_tile(qk, 256)
            sh["b_wv"] = _kc_tile(np.ascontiguousarray(wq[:, 2 * D:]), 256)
            wo = np.asarray(inp["b_w_out"][0], np.float32)
            x = wo.reshape(8, 128, 8, 128).transpose(2, 1, 0, 3)
            sh["b_wo"] = np.ascontiguousarray(x).reshape(8, 128, 8 * 128)
            rb = np.asarray(inp["b_rel_bias"][0], np.float32)
            qq = np.arange(128)[:, None]
            kk = np.arange(640)[None, :]
            idx = np.clip(qq + 512 - kk, -128, 128) + 128
            bfull = rb[:, idx]
            sh["b_bias"] = np.ascontiguousarray(bfull.transpose(1, 0, 2)).reshape(128, 16 * 640)
        else:
            wi = np.asarray(inp["c_w_in"][0], np.float32)
            a = wi[:, :D].reshape(D, 8, 128)
            g = wi[:, D:].reshape(D, 8, 128)
            ag = np.concatenate([a, g], axis=2).reshape(D, 8 * 256)
            sh["c_wi"] = _kc_tile(ag, 256)
            wo = np.asarray(inp["c_w_out"][0], np.float32)
            x = wo.reshape(8, 128, 8, 128).transpose(2, 1, 0, 3)
            sh["c_wo"] = np.ascontiguousarray(x).reshape(8, 128, 8 * 128)
    return sh


def run_layers(hT_in, p, shared, layers, trace=False, dbg=()):
    nc = Builder(layers, dbg).build()
    in_maps = []
    for b in range(8):
        m = dict(shared)
        m["xT"] = hT_in[b]
        for i in layers:
            m["pT%d" % i] = np.ascontiguousarray(p[i, b].T).reshape(2, 128, S)
        in_maps.append(m)
    res = run_bass_kernel_spmd(nc, in_maps, core_ids=list(range(8)), trace=trace)
    out = np.stack([res.results[b]["yT"] for b in range(8)])
    return out, res


def kernel(**inputs):
    inp = {k: np.asarray(v) for k, v in inputs.items()}
    layers = list(range(DEPTH))
    x = inp["x"].astype(np.float32, copy=False)
    hT = np.ascontiguousarray(x.transpose(0, 2, 1)).reshape(8, 8, 128, S)
    shared = host_shared(inp, layers)
    out, _ = run_layers(hT, inp["p"].astype(np.float32, copy=False), shared, layers)
    y = out.reshape(8, D, S).transpose(0, 2, 1)
    return np.ascontiguousarray(y).astype(np.float32, copy=False)
```

```python
import numpy as np
from contextlib import ExitStack
import concourse.bass as bass
import concourse.mybir as mybir
from concourse.bass_utils import run_bass_kernel_spmd

F32 = mybir.dt.float32
BF16 = mybir.dt.bfloat16
AF = mybir.ActivationFunctionType
ALU = mybir.AluOpType
AX = mybir.AxisListType

D = 1024
S = 2048
T = 512
NT = S // T
FF = 2816
NFC = FF // 128
DEPTH = 4
EPS = 1e-6
NEG = -1e30


class Buf:
    __slots__ = ("name", "last_w", "readers", "dma_readers", "dma_sem", "dma_cnt", "const", "excl")

    fence = {}

    def __init__(self, name, const=False, excl=False):
        self.name = name
        self.excl = excl
        self.last_w = None
        self.readers = dict(Buf.fence)
        self.dma_readers = []
        self.dma_sem = None
        self.dma_cnt = 0
        self.const = const


class Op:
    __slots__ = ("eng", "fn", "deps", "sig", "sigval", "is_dma", "dma_sem", "dma_val", "idx")

    def __init__(self, eng, fn, is_dma):
        self.eng = eng
        self.fn = fn
        self.deps = []
        self.sig = False
        self.sigval = 0
        self.is_dma = is_dma
        self.dma_sem = None
        self.dma_val = 0


class Prog:
    ENGS = ("pe", "act", "dve", "pool", "sp")

    def __init__(self, nc):
        self.nc = nc
        self.ops = {e: [] for e in self.ENGS}
        self.n = 0
        self.dma_bufs = []
        self.out_dma_ops = []

    def _add(self, op, reads, writes):
        deps = []
        for b in reads:
            w = b.last_w
            if w is not None:
                deps.append((w, True))
            if b.excl:
                for e_, r in b.readers.items():
                    if e_ != op.eng:
                        deps.append((r, False))
        for b in writes:
            w = b.last_w
            if w is not None and not (op.is_dma and w.is_dma):
                deps.append((w, False))
            for r in b.readers.values():
                deps.append((r, False))
            for r in b.dma_readers:
                deps.append((r, False))
        seen = set()
        for d, raw in deps:
            if d is op:
                continue
            if (not d.is_dma) and (not op.is_dma) and d.eng == op.eng:
                if op.eng == "pe":
                    continue
            k = id(d)
            if k in seen:
                continue
            seen.add(k)
            op.deps.append(d)
            if not d.is_dma:
                d.sig = True
        for b in writes:
            b.last_w = op
            b.readers = {}
            b.dma_readers = []
        for b in reads:
            if b.const or b in writes:
                continue
            if op.is_dma:
                b.dma_readers.append(op)
            else:
                b.readers[op.eng] = op
        op.idx = self.n
        self.n += 1
        self.ops[op.eng].append(op)
        return op

    @staticmethod
    def _flat(xs):
        out = []
        for x in xs:
            if isinstance(x, (list, tuple)):
                out.extend(Prog._flat(x))
            else:
                out.append(x)
        return out

    def op(self, eng, fn, reads=(), writes=()):
        return self._add(Op(eng, fn, False), self._flat(reads), self._flat(writes))

    def dma(self, eng, fn, reads=(), writes=(), is_output=False):
        o = Op(eng, fn, True)
        reads, writes = self._flat(reads), self._flat(writes)
        dst = writes[0]
        if dst.dma_sem is None:
            dst.dma_sem = "pending"
            self.dma_bufs.append(dst)
        dst.dma_cnt += 16
        o.dma_sem = dst
        o.dma_val = dst.dma_cnt
        self._add(o, list(reads), list(writes))
        if is_output:
            self.out_dma_ops.append(o)
        return o

    def emit(self, stack):
        nc = self.nc
        sems = {}
        for e in ("pe", "act", "dve", "pool"):
            sems[e] = stack.enter_context(nc.semaphore("s_" + e))
        for i, b in enumerate(self.dma_bufs):
            b.dma_sem = stack.enter_context(nc.semaphore("d%d" % i))
        for e in ("pe", "act", "dve", "pool"):
            c = 0
            for o in self.ops[e]:
                if o.is_dma:
                    continue
                if o.sig:
                    c += 1
                    o.sigval = c
        out_ops = self.out_dma_ops

        def run(engh, ename):
            waited = {}

            def wait(sem, val):
                k = id(sem)
                if waited.get(k, 0) >= val:
                    return
                waited[k] = val
                engh.wait_ge(sem, val)

            for o in self.ops[ename]:
                for d in o.deps:
                    if d.is_dma:
                        wait(d.dma_sem.dma_sem, d.dma_val)
                    else:
                        wait(sems[d.eng], d.sigval)
                ins = o.fn(engh)
                if o.is_dma:
                    ins.then_inc(o.dma_sem.dma_sem, 16)
                elif o.sig:
                    ins.then_inc(sems[ename], 1)
            if ename == "sp":
                for o in out_ops:
                    wait(o.dma_sem.dma_sem, o.dma_val)

        block = stack.enter_context(nc.Block())

        @block.tensor
        def _(e):
            run(e, "pe")

        @block.scalar
        def _(e):
            run(e, "act")

        @block.vector
        def _(e):
            run(e, "dve")

        @block.gpsimd
        def _(e):
            run(e, "pool")

        @block.sync
        def _(e):
            run(e, "sp")


def ptab_layout():
    off = {}
    c = 0
    for i in range(DEPTH):
        for nm in ("mix_pre", "mix_post", "ffn_pre", "ffn_post", "ple_g"):
            off[(nm, i)] = c
            c += 8
        off[("conv", i)] = c
        c += 132
    for j in range(2):
        off[("a_lng", j)] = c
        c += 16
        off[("a_lnb", j)] = c
        c += 16
    off["c_dw"] = c
    c += 248
    off["c_dwb"] = c
    c += 8
    off["c_lng"] = c
    c += 8
    off["c_lnb"] = c
    c += 8
    return off, c


class StopBuild(Exception):
    pass


class Stream:
    def __init__(self, B, name, n, free, srcs):
        self.B = B
        self.n = n
        self.srcs = list(srcs)
        self.aps = [B.A.alloc(BF16, *free) for _ in range(n)]
        self.bufs = [Buf("%s%d" % (name, i)) for i in range(n)]
        self.issued = 0
        self.taken = 0
        for _ in range(n - 1):
            self._issue()

    def _issue(self):
        k = self.issued
        if k >= len(self.srcs):
            return
        self.issued += 1
        ap, bf = self.aps[k % self.n], self.bufs[k % self.n]
        flat = ap.rearrange("p a b -> p (a b)") if len(ap.shape) == 3 else ap
        if "fastdma" in self.B.dbg:
            n8 = flat.shape[-1] // 8
            self.B.dma_cast(flat[:, 0:n8], self.srcs[k][:, 0:n8], [], [bf])
            return
        self.B.dma_cast(flat, self.srcs[k], [], [bf])

    def next(self):
        k = self.taken
        self.taken += 1
        self._issue()
        return self.aps[k % self.n], self.bufs[k % self.n]


class Arena:
    def __init__(self, ap_all, base, limit):
        self.all = ap_all
        self.off = base
        self.limit = limit

    def mark(self):
        return self.off

    def reset(self, m):
        self.off = m

    def alloc(self, dtype, *free):
        isz = 4 if dtype == F32 else 2
        n = 1
        for f in free:
            n *= f
        nb = n * isz
        off = (self.off + 63) // 64 * 64
        assert off + nb <= self.limit, ("SBUF arena overflow", off + nb, self.limit)
        self.off = off + nb
        v = self.all[:, off // 2:(off + nb) // 2]
        if dtype == F32:
            v = v.bitcast(F32)
        if len(free) == 2:
            v = v.rearrange("p (a b) -> p a b", a=free[0])
        elif len(free) == 3:
            v = v.rearrange("p (a b c) -> p a b c", a=free[0], b=free[1])
        return v


class Builder:
    def __init__(self, layers, dbg=()):
        self.layers = list(layers)
        self.dbg = set(dbg)
        self.nc = bass.Bass("TRN2", target_bir_lowering=False)
        Buf.fence = {}
        self.P = Prog(self.nc)
        self.poff, self.pcols = ptab_layout()
        self._bank = 0
        self._stat = 0

    def mm(self, out, lhsT, rhs, start, stop, r, w, **kw):
        self.P.op("pe", lambda e: e.matmul(out, lhsT=lhsT, rhs=rhs, start=start, stop=stop, **kw), r, w)

    def act(self, out, in_, func, r, w, bias=None, scale=None, accum=None):
        kw = {}
        if bias is not None:
            kw["bias"] = bias
        if scale is not None:
            kw["scale"] = scale
        if accum is not None:
            kw["accum_out"] = accum
        self.P.op("act", lambda e: e.activation(out=out, in_=in_, func=func, **kw), r, w)

    def ts(self, eng, out, in0, s1, s2, op0, op1, r, w):
        if op1 is None and eng == "pool":
            s2, op1 = 1.0, ALU.mult
        if op1 is None:
            self.P.op(eng, lambda e: e.tensor_scalar(out=out, in0=in0, scalar1=s1, scalar2=None, op0=op0), r, w)
        else:
            self.P.op(eng, lambda e: e.tensor_scalar(out=out, in0=in0, scalar1=s1, scalar2=s2, op0=op0, op1=op1), r, w)

    def stt(self, out, in0, scalar, in1, op0, op1, r, w):
        self.P.op("dve", lambda e: e.scalar_tensor_tensor(out=out, in0=in0, scalar=scalar, in1=in1, op0=op0, op1=op1), r, w)

    def tt(self, eng, out, in0, in1, op, r, w):
        self.P.op(eng, lambda e: e.tensor_tensor(out=out, in0=in0, in1=in1, op=op), r, w)

    def cp(self, eng, out, in_, r, w):
        if eng == "act":
            self.P.op("act", lambda e: e.copy(out=out, in_=in_), r, w)
        else:
            self.P.op(eng, lambda e: e.tensor_copy(out=out, in_=in_), r, w)

    def recip(self, out, in_, r, w):
        self.P.op("dve", lambda e: e.reciprocal(out=out, in_=in_), r, w)

    def memset(self, eng, ap, val, w):
        self.P.op(eng, lambda e: e.memset(ap, val), [], w)

    def dma_cast(self, out, in_, r, w):
        self.P.dma("pool", lambda e: e.dma_start(out=out, in_=in_), r, w)

    def dma_plain(self, out, in_, r, w, is_output=False):
        self.P.dma("sp", lambda e: e.dma_start(out=out, in_=in_), r, w, is_output=is_output)

    def bank(self):
        b = self._bank
        self._bank = (b + 1) % 6
        return b

    def statbank(self):
        b = 6 + self._stat
        self._stat ^= 1
        return b

    def psb(self, b, lo=0, hi=512):
        return self.ps[:, b * 512 + lo:b * 512 + hi]

    def build(self):
        nc = self.nc
        st = ExitStack()
        with st:
            self.declare_dram()
            self.sb_all = st.enter_context(nc.sbuf_tensor("sb_all", [128, 106300], BF16))
            self.ps = st.enter_context(nc.psum_tensor("ps_all", [128, 4096], F32))
            self.PBK = [Buf("psk%d" % i, excl=True) for i in range(32)]
            self.PB = [self.PBK[4 * i:4 * i + 4] for i in range(8)]
            self.A = Arena(self.sb_all, 0, 106300 * 2)
            self.setup_persistent()
            for i in self.layers:
                self.layer(i)
            self.store_output()
            self.P.emit(st)
        return nc

    def declare_dram(self):
        nc = self.nc
        dt = lambda n, s: nc.dram_tensor(n, s, F32, kind="ExternalInput").ap()
        self.d = {}
        self.d["xT"] = dt("xT", [8, 128, S])
        self.d["ptab"] = dt("ptab", [128, self.pcols])
        self.d["ident"] = dt("ident", [128, 128])
        for i in self.layers:
            self.d["pT%d" % i] = dt("pT%d" % i, [2, 128, S])
            self.d["wup%d" % i] = dt("wup%d" % i, [NFC, 128, 8 * 256])
            self.d["wdn%d" % i] = dt("wdn%d" % i, [8, 128, NFC * 128])
            self.d["wg%d" % i] = dt("wg%d" % i, [8, 128, 8 * 128])
            self.d["wp%d" % i] = dt("wp%d" % i, [128, 2 * D])
            kind, j = i % 3, i // 3
            if kind == 0:
                self.d["a_wu%d" % j] = dt("a_wu%d" % j, [16, 128, 8 * 128])
                self.d["a_wv%d" % j] = dt("a_wv%d" % j, [4, 128, 8 * 512])
                self.d["a_wo%d" % j] = dt("a_wo%d" % j, [8, 128, 16 * 128])
                self.d["a_ws%d" % j] = dt("a_ws%d" % j, [128, 8 * 128])
                self.d["a_bs%d" % j] = dt("a_bs%d" % j, [128, 8 * 128])
            elif kind == 1:
                self.d["b_wqk"] = dt("b_wqk", [8, 128, 8 * 256])
                self.d["b_wv"] = dt("b_wv", [4, 128, 8 * 256])
                self.d["b_wo"] = dt("b_wo", [8, 128, 8 * 128])
                self.d["b_bias"] = dt("b_bias", [128, 16 * 640])
            else:
                self.d["c_wi"] = dt("c_wi", [8, 128, 8 * 256])
                self.d["c_wo"] = dt("c_wo", [8, 128, 8 * 128])
        self.yT = nc.dram_tensor("yT", [8, 128, S], F32, kind="ExternalOutput").ap()

    def setup_persistent(self):
        A = self.A
        self.h = A.alloc(F32, 8, S)
        self.Bh = [[Buf("h%d_%d" % (c, t)) for t in range(NT)] for c in range(8)]
        self.ptab = A.alloc(F32, self.pcols)
        self.Bptab = Buf("ptab", const=True)
        self.ident = A.alloc(BF16, 128)
        self.onesb = A.alloc(BF16, 128)
        self.epsc = A.alloc(F32, 1)
        self.dummy = A.alloc(F32, 8)
        self.Bconst = Buf("const", const=True)
        self.sq = [A.alloc(BF16, T) for _ in range(3)]
        self.Bsq = [Buf("sq%d" % i) for i in range(3)]
        self._sq = 0
        self.rstds = [A.alloc(F32, T) for _ in range(3)]
        self.Brstds = [Buf("rstd%d" % k) for k in range(3)]
        self.rstd, self.Brstd = self.rstds[0], self.Brstds[0]
        self.mres = A.alloc(F32, 8, T)
        self.Bmres = [Buf("mres%d" % c) for c in range(8)]
        self.hns = [A.alloc(BF16, 8, T) for _ in range(2)]
        self.hn1_off = A.off - 8 * T * 2
        self.Bhns = [[Buf("hn%d_%d" % (k, c)) for c in range(8)] for k in range(2)]
        self.hn, self.Bhn = self.hns[0], self.Bhns[0]
        self.rtmp = [A.alloc(F32, T) for _ in range(3)]
        self.Brtmp = [Buf("rtmp%d" % i) for i in range(3)]
        self._rt = 0
        self.phase_mark = A.mark()
        self.dma_plain(self.ptab, self.d["ptab"], [], [self.Bptab])
        for c in range(8):
            self.dma_plain(self.h[:, c, :], self.d["xT"][c], [], self.Bh[c])
        self.memset("pool", self.onesb, 1.0, [self.Bconst])
        self.memset("pool", self.epsc, EPS, [self.Bconst])
        self.dma_cast(self.ident, self.d["ident"], [], [self.Bconst])

    def pcol(self, key, c, n=1):
        o = self.poff[key] + c
        return self.ptab[:, o:o + n]

    def nextsq(self):
        i = self._sq
        self._sq = (i + 1) % 3
        return self.sq[i], self.Bsq[i]

    def nextrt(self):
        i = self._rt
        self._rt = (i + 1) % 3
        return self.rtmp[i], self.Brtmp[i]

    def defer_mm(self, *args, **kw):
        self.flush_mm()
        self._pend_mm = (args, kw)

    def flush_mm(self):
        p = getattr(self, "_pend_mm", None)
        if p is not None:
            self._pend_mm = None
            self.mm(*p[0], **p[1])

    def stat_add(self, sb, src, src_bufs, c, n=8):
        sq, Bsq = self.nextsq()
        self.act(sq, src, AF.Square, src_bufs, [Bsq])
        self.defer_mm(self.psb(sb), self.onesb, sq, c == 0, c == n - 1, [Bsq, self.Bconst], [self.PB[sb]])

    def finish_rstd(self, sb, dim=D, role=0):
        self.flush_mm()
        rstd, Brstd = self.rstds[role], self.Brstds[role]
        self.act(rstd, self.psb(sb), AF.Sqrt, [self.PB[sb], self.Bconst], [Brstd], bias=self.epsc, scale=1.0 / dim)
        self.recip(rstd, rstd, [Brstd], [Brstd])
        return rstd, Brstd

    def pre_norm_gen(self, t, gkey, k=0):
        hn, Bhn = self.hns[k], self.Bhns[k]
        sb = self.statbank()
        tsl = slice(t * T, (t + 1) * T)
        for c in range(8):
            self.stat_add(sb, self.h[:, c, tsl], [self.Bh[c][t]], c)
            yield
        rstd, Brstd = self.finish_rstd(sb, role=0)
        yield
        for c in range(8):
            self.stt(hn[:, c, :], self.h[:, c, tsl], self.pcol(gkey, c), rstd, ALU.mult, ALU.mult,
                     [self.Bh[c][t], Brstd, self.Bptab], [Bhn[c]])
            if c % 2:
                yield

    def pre_norm(self, t, gkey, k=0):
        for _ in self.pre_norm_gen(t, gkey, k):
            pass

    def post_norm_gen(self, t, gkey, sb, role, after=None):
        rstd, Brstd = self.finish_rstd(sb, role=role)
        tsl = slice(t * T, (t + 1) * T)
        yield
        rts = {}
        for i in range(10):
            if i < 8:
                rt, Brt = self.nextrt()
                rts[i] = (rt, Brt)
                self.stt(rt, self.mres[:, i, :], self.pcol(gkey, i), rstd, ALU.mult, ALU.mult,
                         [self.Bmres[i], Brstd, self.Bptab], [Brt])
            if 1 <= i < 9:
                c = i - 1
                rt, Brt = rts.pop(c)
                self.tt("pool", self.h[:, c, tsl], self.h[:, c, tsl], rt, ALU.add, [self.Bh[c][t], Brt], [self.Bh[c][t]])
            if 2 <= i < 10 and after is not None:
                after(i - 2)
            yield

    def post_norm_residual(self, t, gkey, sb, role=1):
        for _ in self.post_norm_gen(t, gkey, sb, role):
            pass

    def evac_mres(self, b, dc, sb):
        self.cp("dve", self.mres[:, dc, :], self.psb(b), [self.PB[b]], [self.Bmres[dc]])
        self.stat_add(sb, self.mres[:, dc, :], [self.Bmres[dc]], dc)

    def make_slots(self, name, n, *free):
        aps = [self.A.alloc(BF16, *free) for _ in range(n)]
        bufs = [Buf("%s%d" % (name, i)) for i in range(n)]
        return {"aps": aps, "bufs": bufs, "i": 0, "n": n}

    def load_slot(self, slots, src):
        i = slots["i"]
        slots["i"] = (i + 1) % slots["n"]
        ap, bf = slots["aps"][i], slots["bufs"][i]
        flat = ap
        if len(ap.shape) == 3:
            flat = ap.rearrange("p a b -> p (a b)")
        self.dma_cast(flat, src, [], [bf])
        return ap, bf

    def stream(self, name, n, free, srcs):
        return Stream(self, name, n, free, srcs)

    def stop(self, tag):
        if tag in self.dbg:
            raise StopBuild()

    def layer(self, i):
        try:
            self._layer(i)
        except StopBuild:
            pass

    def new_phase(self):
        self.flush_mm()
        self.A.reset(self.phase_mark)
        f = {}
        for e in ("pe", "act", "dve", "pool"):
            for o in reversed(self.P.ops[e]):
                if not o.is_dma:
                    f[e] = o
                    break
        Buf.fence = f

    def _layer(self, i):
        kind, j = i % 3, i // 3
        A = self.A
        self.new_phase()
        if "prenorm" in self.dbg:
            self.pre_norm(0, ("mix_pre", i))
            return
        if "nomix" in self.dbg:
            pass
        elif kind == 0:
            self.mixer_a(i, j)
        elif kind == 1:
            self.mixer_b(i)
        else:
            self.mixer_c(i)
        self.new_phase()
        if "noffn" not in self.dbg:
            self.ffn_phase(i)

    def ffn_phase(self, i):
        A = self.A
        actb = A.alloc(BF16, NFC, T)
        Bact = [Buf("act%d" % f) for f in range(NFC)]
        wupS = self.stream("wup", 4, (8, 256), [self.d["wup%d" % i][fc] for _t in range(NT) for fc in range(NFC)])
        wdnS = self.stream("wdn", 2, (NFC, 128), [self.d["wdn%d" % i][dc] for _t in range(NT) for dc in range(8)])
        wgS = self.stream("wg", 2, (8, 128), [self.d["wg%d" % i][dc] for _t in range(NT) for dc in range(8)])
        cg = [A.alloc(F32, T) for _ in range(2)]
        cv = [A.alloc(F32, T) for _ in range(2)]
        sg = [A.alloc(F32, T) for _ in range(2)]
        Bcg = [Buf("cg%d" % k) for k in range(2)]
        Bcv = [Buf("cv%d" % k) for k in range(2)]
        Bsg = [Buf("sg%d" % k) for k in range(2)]
        halo = [A.alloc(F32, 2 * NFC, 2) for _ in range(2)]
        Bhalo = [[Buf("halo%d_%d" % (k, q)) for q in range(2 * NFC)] for k in range(2)]
        bnd = [A.alloc(F32, 3, 2 * NFC) for _ in range(2)]
        Bbnd = [Buf("bnd%d" % k) for k in range(2)]
        hb = A.alloc(BF16, 8, T)
        Bhb = [Buf("hb%d" % c) for c in range(8)]
        pt = [A.alloc(BF16, 2, T) for _ in range(2)]
        Bpt = [Buf("pt%d" % k) for k in range(2)]
        wp = A.alloc(BF16, 2, D)
        Bwp = Buf("wp")
        gate = [A.alloc(F32, T) for _ in range(2)]
        Bgate = [Buf("gate%d" % k) for k in range(2)]
        self.dma_cast(wp.rearrange("p a b -> p (a b)"), self.d["wp%d" % i], [], [Bwp])
        cbase = self.poff[("conv", i)]

        def tap(jj, q):
            o = cbase + jj * 44 + q
            return self.ptab[:, o:o + 1]

        def step(bg):
            for g in list(bg):
                try:
                    next(g)
                except StopIteration:
                    bg.remove(g)

        def drain(bg):
            while bg:
                step(bg)

        def load_pt(t):
            ptt, Bptt = pt[t % 2], Bpt[t % 2]
            tsl = slice(t * T, (t + 1) * T)
            for kc in range(2):
                self.dma_cast(ptt[:, kc, :], self.d["pT%d" % i][kc][:, tsl], [], [Bptt])

        def stage_A(t):
            return self.pre_norm_gen(t, ("ffn_pre", i), k=t % 2)

        def stage_B(t, bg):
            hn, Bhn = self.hns[t % 2], self.Bhns[t % 2]
            if t > 0:
                ho, Bho = halo[(t - 1) % 2], Bhalo[(t - 1) % 2]
                bd, Bbd = bnd[t % 2], Bbnd[t % 2]
                W0 = self.ptab[:, cbase:cbase + 44]
                W1 = self.ptab[:, cbase + 44:cbase + 88]
                self.tt("dve", bd[:, 2, :], ho[:, :, 1], W1, ALU.mult, Bho + [self.Bptab], [Bbd])
                self.tt("dve", bd[:, 0, :], ho[:, :, 0], W0, ALU.mult, Bho + [self.Bptab], [Bbd])
                self.tt("dve", bd[:, 0, :], bd[:, 0, :], bd[:, 2, :], ALU.add, [Bbd], [Bbd])
                self.tt("dve", bd[:, 1, :], ho[:, :, 1], W0, ALU.mult, Bho + [self.Bptab], [Bbd])
            for fc in range(NFC):
                w, Bw = wupS.next()
                k2 = fc % 2
                bg_ = self.bank()
                bv_ = self.bank()
                for kc in range(8):
                    self.mm(self.psb(bg_), w[:, kc, 0:128], hn[:, kc, :], kc == 0, kc == 7, [Bw, Bhn[kc]], [self.PB[bg_]])
                for kc in range(8):
                    self.mm(self.psb(bv_), w[:, kc, 128:256], hn[:, kc, :], kc == 0, kc == 7, [Bw, Bhn[kc]], [self.PB[bv_]])
                for (b, q, cbuf, Bc) in ((bg_, fc, cg[k2], Bcg[k2]), (bv_, NFC + fc, cv[k2], Bcv[k2])):
                    pb = self.PB[b]
                    if t == 0:
                        self.act(cbuf, self.psb(b), AF.Copy, [pb, self.Bptab], [Bc], scale=tap(2, q))
                    else:
                        bd, Bbd = bnd[t % 2], Bbnd[t % 2]
                        self.act(cbuf[:, 2:T], self.psb(b, 2, T), AF.Copy, [pb, self.Bptab], [Bc], scale=tap(2, q))
                        self.act(cbuf[:, 0:1], self.psb(b, 0, 1), AF.Identity, [pb, self.Bptab, Bbd], [Bc],
                                 scale=tap(2, q), bias=bd[:, 0, q:q + 1])
                        self.act(cbuf[:, 1:2], self.psb(b, 1, 2), AF.Identity, [pb, self.Bptab, Bbd], [Bc],
                                 scale=tap(2, q), bias=bd[:, 1, q:q + 1])
                    if t < NT - 1:
                        self.cp("act", halo[t % 2][:, q, 0:2], self.psb(b, T - 2, T), [pb], [Bhalo[t % 2][q]])
                    self.stt(cbuf[:, 1:T], self.psb(b, 0, T - 1), tap(1, q), cbuf[:, 1:T], ALU.mult, ALU.add,
                             [pb, Bc, self.Bptab], [Bc])
                    self.stt(cbuf[:, 2:T], self.psb(b, 0, T - 2), tap(0, q), cbuf[:, 2:T], ALU.mult, ALU.add,
                             [pb, Bc, self.Bptab], [Bc])
                self.act(sg[k2], cg[k2], AF.Silu, [Bcg[k2]], [Bsg[k2]])
                self.tt("pool", actb[:, fc, :], sg[k2], cv[k2], ALU.mult, [Bsg[k2], Bcv[k2]], [Bact[fc]])
                step(bg)
            drain(bg)

        def stage_C(t, bg):
            sb = self.statbank()
            for dc in range(8):
                w, Bw = wdnS.next()
                b = self.bank()
                for fc in range(NFC):
                    self.mm(self.psb(b), w[:, fc, :], actb[:, fc, :], fc == 0, fc == NFC - 1, [Bw, Bact[fc]], [self.PB[b]])
                step(bg)
                self.evac_mres(b, dc, sb)
            drain(bg)
            return sb

        def stage_D(t, sb):
            tsl = slice(t * T, (t + 1) * T)

            def after(c):
                self.cp("act", hb[:, c, :], self.h[:, c, tsl], [self.Bh[c][t]], [Bhb[c]])
            g = self.post_norm_gen(t, ("ffn_post", i), sb, 1, after=after)
            next(g)
            return g

        def stage_E(t):
            ptt, Bptt = pt[t % 2], Bpt[t % 2]
            if t + 1 < NT:
                load_pt(t + 1)
            sb = self.statbank()
            for dc in range(8):
                w, Bw = wgS.next()
                bgt = self.bank()
                be = self.bank()
                for kc in range(8):
                    self.mm(self.psb(bgt), w[:, kc, :], hb[:, kc, :], kc == 0, kc == 7, [Bw, Bhb[kc]], [self.PB[bgt]])
                for kc in range(2):
                    self.mm(self.psb(be), wp[:, kc, dc * 128:(dc + 1) * 128], ptt[:, kc, :], kc == 0, kc == 1,
                            [Bwp, Bptt], [self.PB[be]])
                k2 = dc % 2
                self.act(gate[k2], self.psb(bgt), AF.Sigmoid, [self.PB[bgt]], [Bgate[k2]])
                self.tt("dve", self.mres[:, dc, :], gate[k2], self.psb(be), ALU.mult, [Bgate[k2], self.PB[be]],
                        [self.Bmres[dc]])
                self.stat_add(sb, self.mres[:, dc, :], [self.Bmres[dc]], dc)
            return sb

        load_pt(0)
        drain([stage_A(0)])
        stage_B(0, [stage_A(1)])
        pend = []
        for t in range(NT):
            sb = stage_C(t, pend)
            pend = []
            bgl = [stage_D(t, sb)]
            if t + 2 < NT:
                bgl.append(stage_A(t + 2))
            if t + 1 < NT:
                stage_B(t + 1, bgl)
            else:
                drain(bgl)
            sbe = stage_E(t)
            g = self.post_norm_gen(t, ("ple_g", i), sbe, 2)
            next(g)
            pend = [g]
        drain(pend)
        self.flush_mm()

    def mixer_c(self, i):
        A = self.A
        ybuf = A.alloc(BF16, 8, 30 + T)
        Byb = [Buf("yb%d" % c) for c in range(8)]
        z = A.alloc(F32, 8, T)
        Bz = [Buf("z%d" % c) for c in range(8)]
        zb = [A.alloc(BF16, T) for _ in range(2)]
        Bzb = [Buf("zb%d" % k) for k in range(2)]
        actc = A.alloc(BF16, 8, T)
        Bac = [Buf("actc%d" % c) for c in range(8)]
        dg = [A.alloc(BF16, 31, 128) for _ in range(2)]
        Bdg = [Buf("dg%d" % k) for k in range(2)]
        wi = self.stream("cwi", 3, (8, 256), [self.d["c_wi"][c] for _t in range(NT) for c in range(8)])
        wo = self.stream("cwo", 3, (8, 128), [self.d["c_wo"][c] for _t in range(NT) for c in range(8)])
        sgm = [A.alloc(F32, T) for _ in range(2)]
        Bsgm = [Buf("sgm%d" % k) for k in range(2)]
        mean = A.alloc(F32, T)
        msq = A.alloc(F32, T)
        var = A.alloc(F32, T)
        nmr = A.alloc(F32, T)
        Bmean, Bmsq, Bvar, Bnmr = Buf("mean"), Buf("msq"), Buf("var"), Buf("nmr")
        t1 = [A.alloc(F32, T) for _ in range(2)]
        Bt1 = [Buf("ct1_%d" % k) for k in range(2)]
        for c in range(8):
            self.memset("pool", ybuf[:, c, 0:30], 0.0, [Byb[c]])

        def build_diag(c):
            k2_ = c % 2
            o_ = self.poff["c_dw"] + c * 31
            in1 = self.ptab[:, o_:o_ + 31].unsqueeze(2).to_broadcast([128, 31, 128])
            in0 = self.ident.unsqueeze(1).to_broadcast([128, 31, 128])
            self.tt("dve", dg[k2_], in0, in1, ALU.mult, [self.Bconst, self.Bptab], [Bdg[k2_]])

        self.pre_norm(0, ("mix_pre", i), k=0)
        for t in range(NT):
            hn, Bhn = self.hns[t % 2], self.Bhns[t % 2]
            sb1 = self.statbank()
            sb2 = self.statbank()
            build_diag(0)
            for c in range(8):
                if c + 1 < 8:
                    build_diag(c + 1)
                w, Bw = wi.next()
                ba = self.bank()
                bg = self.bank()
                for kc in range(8):
                    self.mm(self.psb(ba), w[:, kc, 0:128], hn[:, kc, :], kc == 0, kc == 7, [Bw, Bhn[kc]], [self.PB[ba]])
                for kc in range(8):
                    self.mm(self.psb(bg), w[:, kc, 128:256], hn[:, kc, :], kc == 0, kc == 7, [Bw, Bhn[kc]], [self.PB[bg]])
                k2 = c % 2
                self.act(sgm[k2], self.psb(bg), AF.Sigmoid, [self.PB[bg]], [Bsgm[k2]])
                self.tt("dve", ybuf[:, c, 30:30 + T], sgm[k2], self.psb(ba), ALU.mult, [Bsgm[k2], self.PB[ba]], [Byb[c]])
                bc = self.bank()
                for jj in range(31):
                    self.mm(self.psb(bc), dg[k2][:, jj, :], ybuf[:, c, jj:jj + T], jj == 0, jj == 30, [Bdg[k2], Byb[c]], [self.PB[bc]])
                bias = self.pcol("c_dwb", c)
                self.act(z[:, c, :], self.psb(bc), AF.Identity, [self.PB[bc], self.Bptab], [Bz[c]], bias=bias)
                self.act(zb[k2], self.psb(bc), AF.Identity, [self.PB[bc], self.Bptab], [Bzb[k2]], bias=bias)
                self.flush_mm()
                self.mm(self.psb(sb1), self.onesb, zb[k2], c == 0, c == 7, [Bzb[k2], self.Bconst], [self.PB[sb1]])
                sq, Bsq = self.nextsq()
                self.act(sq, self.psb(bc), AF.Square, [self.PB[bc], self.Bptab], [Bsq], bias=bias)
                self.defer_mm(self.psb(sb2), self.onesb, sq, c == 0, c == 7, [Bsq, self.Bconst], [self.PB[sb2]])
                if t < NT - 1:
                    self.cp("pool", ybuf[:, c, 0:30], ybuf[:, c, T:T + 30], [Byb[c]], [Byb[c]])
            self.flush_mm()
            self.ts("dve", mean, self.psb(sb1), 1.0 / D, None, ALU.mult, None, [self.PB[sb1]], [Bmean])
            self.act(msq, mean, AF.Square, [Bmean], [Bmsq])
            self.stt(var, self.psb(sb2), 1.0 / D, msq, ALU.mult, ALU.subtract, [self.PB[sb2], Bmsq], [Bvar])
            self.act(self.rstd, var, AF.Sqrt, [Bvar, self.Bconst], [self.Brstd], bias=self.epsc)
            self.recip(self.rstd, self.rstd, [self.Brstd], [self.Brstd])
            self.stt(nmr, mean, -1.0, self.rstd, ALU.mult, ALU.mult, [Bmean, self.Brstd], [Bnmr])
            for c in range(8):
                k2 = c % 2
                self.tt("dve", t1[k2], z[:, c, :], self.rstd, ALU.mult, [Bz[c], self.Brstd], [Bt1[k2]])
                self.tt("pool", t1[k2], t1[k2], nmr, ALU.add, [Bt1[k2], Bnmr], [Bt1[k2]])
                self.act(actc[:, c, :], t1[k2], AF.Silu, [Bt1[k2], self.Bptab], [Bac[c]],
                         scale=self.pcol("c_lng", c), bias=self.pcol("c_lnb", c))
            if t + 1 < NT:
                self.pre_norm(t + 1, ("mix_pre", i), k=(t + 1) % 2)
            sb = self.statbank()
            for dc in range(8):
                w, Bw = wo.next()
                b = self.bank()
                for c in range(8):
                    self.mm(self.psb(b), w[:, c, :], actc[:, c, :], c == 0, c == 7, [Bw, Bac[c]], [self.PB[b]])
                self.evac_mres(b, dc, sb)
            self.post_norm_residual(t, ("mix_post", i), sb)

    def mixer_a(self, i, j):
        A = self.A
        u = A.alloc(BF16, 16, T)
        Bu = [Buf("u%d" % c) for c in range(16)]
        vgb = A.alloc(BF16, 4, 2048)
        Bvgb = [Buf("vgb%d" % b) for b in range(4)]
        wv = self.stream("awv", 2, (8, 512), [self.d["a_wv%d" % j][q] for _t in range(NT) for q in range(4)])
        wu = self.stream("awu", 3, (8, 128), [self.d["a_wu%d" % j][q] for _t in range(NT) for q in range(16)])
        wo = self.stream("awo", 2, (16, 128), [self.d["a_wo%d" % j][q] for _t in range(NT) for q in range(8)])
        bst = A.alloc(F32, 4, 4, 6)
        Bbst = [Buf("bst%d" % b) for b in range(4)]
        mv = A.alloc(F32, 4, 2)
        rs = A.alloc(F32, 4, 1)
        vpe = A.alloc(F32, 4, 1)
        Bmv = [Buf("mv%d" % b) for b in range(4)]
        Brs = [Buf("rs%d" % b) for b in range(4)]
        Bvpe = [Buf("vpe%d" % b) for b in range(4)]
        nmh = A.alloc(F32, 1)
        wsT = A.alloc(BF16, 8, 128)
        Bws = Buf("wsT")
        Cc = A.alloc(F32, 16, 128)
        BCc = Buf("Cc")
        bsb = A.alloc(F32, 8, 128)
        Bbsb = Buf("bsb")
        rw = A.alloc(F32, 8, 128)
        Brw = Buf("rw")
        t1 = [A.alloc(F32, 4, 128) for _ in range(3)]
        Bt1 = [Buf("at1_%d" % k) for k in range(3)]
        self.memset("pool", nmh, -0.5, [self.Bconst])
        self.dma_cast(wsT.rearrange("p a b -> p (a b)"), self.d["a_ws%d" % j], [], [Bws])
        self.memset("pool", wsT[64:128, :, 0:64], 0.0, [Bws])
        self.dma_plain(bsb.rearrange("p a b -> p (a b)"), self.d["a_bs%d" % j], [], [Bbsb])
        b0 = self.bank()
        b1 = self.bank()
        for g in range(8):
            bb = b0 if g < 4 else b1
            lo = (g % 4) * 128
            self.mm(self.psb(bb, lo, lo + 128), self.onesb, wsT[:, g, :], True, True, [Bws, self.Bconst], [self.PB[bb]])
        self.cp("act", rw[:, 0:4, :].rearrange("p a b -> p (a b)"), self.psb(b0), [self.PB[b0]], [Brw])
        self.cp("act", rw[:, 4:8, :].rearrange("p a b -> p (a b)"), self.psb(b1), [self.PB[b1]], [Brw])
        for uc in range(16):
            g = uc // 2
            self.stt(Cc[:, uc, :], rw[:, g, :], self.pcol(("a_lnb", j), uc), bsb[:, g, :], ALU.mult, ALU.add,
                     [Brw, Bbsb, self.Bptab], [BCc])
        self.pre_norm(0, ("mix_pre", i), k=0)
        for t in range(NT):
            hn, Bhn = self.hns[t % 2], self.Bhns[t % 2]
            for vq in range(4):
                w, Bw = wv.next()
                for blk in range(4):
                    b = self.bank()
                    for kc in range(8):
                        self.mm(self.psb(b), hn[:, kc, blk * 128:(blk + 1) * 128], w[:, kc, :], kc == 0, kc == 7,
                                [Bw, Bhn[kc]], [self.PB[b]])
                    dst = vgb[:, blk, vq * 512:(vq + 1) * 512]
                    self.act(dst, self.psb(b), AF.Gelu_apprx_tanh, [self.PB[b]], [Bvgb[blk]])
                    bo = bst[:, blk, vq, :]
                    self.P.op("dve", lambda e, bo=bo, src=dst: e.bn_stats(out=bo, in_=src), [Bvgb[blk]], [Bbst[blk]])
            for blk in range(4):
                mo = mv[:, blk, :]
                bi = bst[:, blk, :, :]
                self.P.op("dve", lambda e, mo=mo, bi=bi: e.bn_aggr(out=mo, in_=bi), [Bbst[blk]], [Bmv[blk]])
                self.ts("pool", vpe[:, blk, :], mv[:, blk, 1:2], EPS, None, ALU.add, None, [Bmv[blk]], [Bvpe[blk]])
                self.tt("pool", rs[:, blk, :], vpe[:, blk, :], nmh, ALU.pow, [Bvpe[blk], self.Bconst], [Brs[blk]])
                self.ts("dve", vgb[:, blk, :], vgb[:, blk, :], mv[:, blk, 0:1], rs[:, blk, :], ALU.subtract, ALU.mult,
                        [Bvgb[blk], Bmv[blk], Brs[blk]], [Bvgb[blk]])
            kk = 0
            for u4 in range(4):
                for j4 in range(4):
                    uc = u4 * 4 + j4
                    w, Bw = wu.next()
                    b = self.bank()
                    for kc in range(8):
                        self.mm(self.psb(b), w[:, kc, :], hn[:, kc, :], kc == 0, kc == 7, [Bw, Bhn[kc]], [self.PB[b]])
                    self.act(u[:, uc, :], self.psb(b), AF.Gelu_apprx_tanh, [self.PB[b]], [Bu[uc]])
                for blk in range(4):
                    bsl = slice(blk * 128, (blk + 1) * 128)
                    b = self.bank()
                    for j4 in range(4):
                        uc = u4 * 4 + j4
                        self.mm(self.psb(b, j4 * 128, (j4 + 1) * 128), vgb[:, blk, uc * 128:(uc + 1) * 128], wsT[:, uc // 2, :], True, True,
                                [Bvgb[blk], Bws], [self.PB[b]])
                    k3 = kk % 3
                    kk += 1
                    for j4 in range(4):
                        uc = u4 * 4 + j4
                        self.stt(t1[k3][:, j4, :], self.psb(b, j4 * 128, (j4 + 1) * 128), self.pcol(("a_lng", j), uc), Cc[:, uc, :],
                                 ALU.mult, ALU.add, [self.PB[b], BCc, self.Bptab], [Bt1[k3]])
                    uv = u[:, u4 * 4:(u4 + 1) * 4, bsl]
                    self.tt("dve", uv, t1[k3], uv, ALU.mult, [Bt1[k3]] + Bu[u4 * 4:(u4 + 1) * 4], Bu[u4 * 4:(u4 + 1) * 4])
            if t + 1 < NT:
                self.pre_norm(t + 1, ("mix_pre", i), k=(t + 1) % 2)
            sb = self.statbank()
            for dc in range(8):
                w, Bw = wo.next()
                b = self.bank()
                for uc in range(16):
                    self.mm(self.psb(b), w[:, uc, :], u[:, uc, :], uc == 0, uc == 15, [Bw, Bu[uc]], [self.PB[b]])
                self.evac_mres(b, dc, sb)
            self.post_norm_residual(t, ("mix_post", i), sb)

    def mixer_b(self, i):
        A = self.A
        kT = A.alloc(BF16, 8, 1024)
        BkT = [[Buf("kT%d_%d" % (c, s_)) for s_ in range(2)] for c in range(8)]
        V = A.alloc(BF16, 8, 1024)
        BV = [Buf("V%d" % b) for b in range(8)]
        qz = A.alloc(BF16, 8, 2, T)
        Bq = [Buf("qz%d" % c) for c in range(8)]
        Bb = A.alloc(BF16, 16, 640)
        BBb = Buf("Bb")
        A2 = Arena(self.sb_all, self.hn1_off, self.hn1_off + 8 * T * 2)
        Pb = [A2.alloc(BF16, 640) for _ in range(3)]
        BPb = [Buf("Pb%d" % k) for k in range(3)]
        PTb = [A2.alloc(BF16, 640) for _ in range(3)]
        BPTb = [Buf("PTb%d" % k) for k in range(3)]
        dgr = [A.alloc(BF16, 128) for _ in range(3)]
        Bdgr = [Buf("dgr%d" % k) for k in range(3)]
        st3 = [A.alloc(F32, 4) for _ in range(4)]
        Bnm = [Buf("nm%d" % k) for k in range(4)]
        Brs = [Buf("rsum%d" % k) for k in range(4)]
        Bri = [Buf("rinv%d" % k) for k in range(4)]
        wqk = self.stream("bwqk", 2, (8, 256), [self.d["b_wqk"][q] for _t in range(NT) for q in range(8)])
        wvs = self.stream("bwv", 2, (8, 256), [self.d["b_wv"][q] for _t in range(NT) for q in range(4)])
        wos = self.stream("bwo", 2, (8, 128), [self.d["b_wo"][q] for _t in range(NT) for q in range(8)])
        oT, BoT = self.hn, self.Bhn
        ps = self.ps
        self.dma_cast(Bb.rearrange("p a b -> p (a b)"), self.d["b_bias"], [], [BBb])
        self.memset("pool", Bb[64:128, :, 0:64], NEG, [BBb])
        self.memset("pool", Bb[0:64, :, 576:640], NEG, [BBb])
        for c in range(8):
            self.memset("pool", qz[:, c, :, :], 0.0, [Bq[c]])
        unit = 0
        for t in range(NT):
            slot = t % 2
            self.pre_norm(t, ("mix_pre", i))
            for c in range(8):
                w, Bw = wqk.next()
                bq = self.bank()
                bk = self.bank()
                for kc in range(8):
                    self.mm(self.psb(bq), w[:, kc, 0:128], self.hn[:, kc, :], kc == 0, kc == 7, [Bw, self.Bhn[kc]], [self.PB[bq]])
                for kc in range(8):
                    self.mm(self.psb(bk), w[:, kc, 128:256], self.hn[:, kc, :], kc == 0, kc == 7, [Bw, self.Bhn[kc]], [self.PB[bk]])
                for hh in range(2):
                    rows = slice(hh * 64, hh * 64 + 64)
                    self.act(qz[rows, c, hh, :], ps[rows, bq * 512:(bq + 1) * 512], AF.Copy, [self.PB[bq]], [Bq[c]], scale=0.125)
                self.cp("dve", kT[:, c, slot * 512:(slot + 1) * 512], self.psb(bk), [self.PB[bk]], [BkT[c][slot]])
            for qt in range(4):
                w, Bw = wvs.next()
                for blk in range(4):
                    b = self.bank()
                    for kc in range(8):
                        self.mm(self.psb(b, 0, 256), self.hn[:, kc, blk * 128:(blk + 1) * 128], w[:, kc, :], kc == 0, kc == 7,
                                [Bw, self.Bhn[kc]], [self.PB[b]])
                    rb = (4 * t + blk) % 8
                    self.cp("act" if (blk % 2) else "dve", V[:, rb, qt * 256:(qt + 1) * 256], self.psb(b, 0, 256), [self.PB[b]], [BV[rb]])
            units = [(c, jq, hh) for c in range(8) for jq in range(4) for hh in range(2)]
            NU = len(units)

            def geom(u):
                c, jq, hh = units[u]
                jb = 4 * t + jq
                return c, jq, hh, jb, max(0, 4 - jb)

            def st_scores(u):
                c, jq, hh, jb, i0 = geom(u)
                hd = 2 * c + hh
                sl = u % 2
                sbase = sl * 1024
                groups = []
                ii = i0
                while ii < 5:
                    n = 1
                    while (ii + n < 5 and (ii + n) != 4 and ((jb - 4 + ii + n) % 8) == ((jb - 4 + ii) % 8) + n):
                        n += 1
                    groups.append((ii, n))
                    ii += n
                started = set()
                for (ii, n) in groups:
                    kb = jb - 4 + ii
                    rc = (kb % 8) * 128
                    ksl = sorted(set(((kb + m) // 4) % 2 for m in range(n)))
                    sap = ps[:, sbase + ii * 128: sbase + (ii + n) * 128]
                    bk = 0 if ii < 4 else 1
                    wb = self.PB[2 * sl + bk]
                    self.mm(sap, qz[:, c, hh, jq * 128:(jq + 1) * 128], kT[:, c, rc:rc + n * 128], bk not in started, False,
                            [Bq[c]] + [BkT[c][k_] for k_ in ksl], wb)
                    started.add(bk)
                if i0 < 4:
                    self.mm(ps[:, sbase + i0 * 128: sbase + 512], self.ident, Bb[:, hd, i0 * 128:512], False, True,
                            [self.Bconst, BBb], self.PB[2 * sl])
                self.mm(ps[:, sbase + 512: sbase + 640], self.ident, Bb[:, hd, 512:640], False, True,
                        [self.Bconst, BBb], self.PB[2 * sl + 1])
                sbufs = [self.PB[2 * sl], self.PB[2 * sl + 1]]
                sfull = ps[:, sbase + i0 * 128: sbase + 640]
                k4 = u % 4
                nmax = st3[k4][:, 0:1]
                self.P.op("dve", lambda e, nmax=nmax, sfull=sfull: e.tensor_reduce(out=nmax, in_=sfull, axis=AX.X, op=ALU.max, negate=True),
                          sbufs, [Bnm[k4]])

            def st_exp(u):
                c, jq, hh, jb, i0 = geom(u)
                sl = u % 2
                k4 = u % 4
                sbase = sl * 1024
                sbufs = [self.PB[2 * sl], self.PB[2 * sl + 1]]
                sfull = ps[:, sbase + i0 * 128: sbase + 640]
                nmax, rsum = st3[k4][:, 0:1], st3[k4][:, 1:2]
                self.act(Pb[u % 3][:, i0 * 128:640], sfull, AF.Exp, sbufs + [Bnm[k4]], [BPb[u % 3], Brs[k4]], bias=nmax, accum=rsum)

            def st_rd(u):
                k4 = u % 4
                k3 = u % 3
                rsum, rinv = st3[k4][:, 1:2], st3[k4][:, 2:3]
                self.recip(rinv, rsum, [Brs[k4]], [Bri[k4]])
                self.ts("pool", dgr[k3], self.ident, rinv, None, ALU.mult, None, [self.Bconst, Bri[k4]], [Bdgr[k3]])

            def st_pt(u):
                c, jq, hh, jb, i0 = geom(u)
                sl = u % 2
                k3 = u % 3
                pbase = 2048
                for ii in range(i0, 5):
                    self.mm(ps[:, pbase + ii * 128: pbase + (ii + 1) * 128], Pb[k3][:, ii * 128:(ii + 1) * 128], dgr[k3], True, True,
                            [BPb[k3], Bdgr[k3]], [self.PB[4] if ii < 4 else self.PB[5]])
                self.cp("act" if (u % 2) else "dve", PTb[k3][:, i0 * 128:640], ps[:, pbase + i0 * 128: pbase + 640],
                        [self.PB[4], self.PB[5]], [BPTb[k3]])

            def st_pv(u):
                c, jq, hh, jb, i0 = geom(u)
                k3 = u % 3
                ob = 6 + hh
                for ii in range(i0, 5):
                    kb = jb - 4 + ii
                    self.mm(ps[:, ob * 512 + jq * 128: ob * 512 + (jq + 1) * 128], V[:, kb % 8, c * 128:(c + 1) * 128],
                            PTb[k3][:, ii * 128:(ii + 1) * 128], ii == i0, ii == 4, [BV[kb % 8], BPTb[k3]], [self.PB[ob]])
                if jq == 3 and hh == 1:
                    for h2 in range(2):
                        rows = slice(h2 * 64, h2 * 64 + 64)
                        o2 = 6 + h2
                        self.cp("dve" if h2 else "act", oT[rows, c, :], ps[rows, o2 * 512:(o2 + 1) * 512], self.PB[o2], [BoT[c]])

            for u in range(NU + 4):
                if u < NU:
                    st_scores(u)
                if 0 <= u - 1 < NU:
                    st_exp(u - 1)
                if 0 <= u - 2 < NU:
                    st_rd(u - 2)
                if 0 <= u - 3 < NU:
                    st_pt(u - 3)
                if 0 <= u - 4 < NU:
                    st_pv(u - 4)
            sb = self.statbank()
            for dc in range(8):
                w, Bw = wos.next()
                b = self.bank()
                for c in range(8):
                    self.mm(self.psb(b), w[:, c, :], oT[:, c, :], c == 0, c == 7, [Bw, BoT[c]], [self.PB[b]])
                self.evac_mres(b, dc, sb)
            self.post_norm_residual(t, ("mix_post", i), sb)

    def store_output(self):
        for c in range(8):
            self.dma_plain(self.yT[c], self.h[:, c, :], self.Bh[c], [Buf("y%d" % c)], is_output=True)


def _cols(v, n):
    return np.ascontiguousarray(np.asarray(v, np.float32).reshape(n, 128).T)


def _kc_tile(w, ncols_per_block):
    K, N = w.shape
    nb = N // ncols_per_block
    x = w.reshape(K // 128, 128, nb, ncols_per_block)
    x = x.transpose(2, 1, 0, 3)
    return np.ascontiguousarray(x).reshape(nb, 128, (K // 128) * ncols_per_block)


def host_shared(inp, layers):
    off, R = ptab_layout()
    ptab = np.zeros((128, R), np.float32)

    def put(key, arr):
        ptab[:, off[key]:off[key] + arr.shape[1]] = arr

    for i in range(DEPTH):
        put(("mix_pre", i), _cols(inp["mix_pre_g"][i], 8))
        put(("mix_post", i), _cols(inp["mix_post_g"][i], 8))
        put(("ffn_pre", i), _cols(inp["ffn_pre_g"][i], 8))
        put(("ffn_post", i), _cols(inp["ffn_post_g"][i], 8))
        put(("ple_g", i), _cols(inp["ple_norm_g"][i], 8))
        cv = np.concatenate([_cols(inp["ffn_conv"][i][jj], 44) for jj in range(3)], axis=1)
        put(("conv", i), cv)
    for j in range(2):
        put(("a_lng", j), _cols(inp["a_ln_g"][j], 16))
        put(("a_lnb", j), _cols(inp["a_ln_b"][j], 16))
    dw = np.asarray(inp["c_dw"][0], np.float32)
    dwc = dw.reshape(31, 8, 128).transpose(2, 1, 0).reshape(128, 248)
    put("c_dw", np.ascontiguousarray(dwc))
    put("c_dwb", _cols(inp["c_dw_b"][0], 8))
    put("c_lng", _cols(inp["c_ln_g"][0], 8))
    put("c_lnb", _cols(inp["c_ln_b"][0], 8))
    sh = {"ptab": ptab, "ident": np.eye(128, dtype=np.float32)}
    for i in layers:
        wu = np.asarray(inp["ffn_w_up"][i], np.float32)
        g = wu[:, :FF].reshape(D, NFC, 128)
        v = wu[:, FF:].reshape(D, NFC, 128)
        gv = np.concatenate([g, v], axis=2).reshape(D, NFC * 256)
        sh["wup%d" % i] = _kc_tile(gv, 256)
        wd = np.asarray(inp["ffn_w_down"][i], np.float32)
        x = wd.reshape(NFC, 128, 8, 128).transpose(2, 1, 0, 3)
        sh["wdn%d" % i] = np.ascontiguousarray(x).reshape(8, 128, NFC * 128)
        sh["wg%d" % i] = _kc_tile(np.asarray(inp["ple_w_gate"][i], np.float32), 128)
        wp = np.asarray(inp["ple_w_proj"][i], np.float32)
        sh["wp%d" % i] = np.ascontiguousarray(wp.reshape(2, 128, D).transpose(1, 0, 2)).reshape(128, 2 * D)
        kind, j = i % 3, i // 3
        if kind == 0:
            win = np.asarray(inp["a_w_in"][j], np.float32)
            sh["a_wu%d" % j] = _kc_tile(win[:, :2048], 128)
            sh["a_wv%d" % j] = _kc_tile(win[:, 2048:], 512)
            wo = np.asarray(inp["a_w_out"][j], np.float32)
            x = wo.reshape(16, 128, 8, 128).transpose(2, 1, 0, 3)
            sh["a_wo%d" % j] = np.ascontiguousarray(x).reshape(8, 128, 16 * 128)
            ws = np.asarray(inp["a_w_s"][j], np.float32)
            sh["a_ws%d" % j] = np.ascontiguousarray(ws.transpose(2, 0, 1)).reshape(128, 8 * 128)
            bs = np.asarray(inp["a_b_s"][j], np.float32).reshape(1, 8 * 128)
            sh["a_bs%d" % j] = np.ascontiguousarray(np.broadcast_to(bs, (128, 8 * 128)))
        elif kind == 1:
            wq = np.asarray(inp["b_w_qkv"][0], np.float32)
            q = wq[:, :D].reshape(D, 8, 128)
            k = wq[:, D:2 * D].reshape(D, 8, 128)
            qk = np.concatenate([q, k], axis=2).reshape(D, 8 * 256)
            sh["b_wqk"] = _kc_tile(qk, 256)
            sh["b_wv"] = _kc_tile(np.ascontiguousarray(wq[:, 2 * D:]), 256)
            wo = np.asarray(inp["b_w_out"][0], np.float32)
            x = wo.reshape(8, 128, 8, 128).transpose(2, 1, 0, 3)
            sh["b_wo"] = np.ascontiguousarray(x).reshape(8, 128, 8 * 128)
            rb = np.asarray(inp["b_rel_bias"][0], np.float32)
            qq = np.arange(128)[:, None]
            kk = np.arange(640)[None, :]
            idx = np.clip(qq + 512 - kk, -128, 128) + 128
            bfull = rb[:, idx]
            sh["b_bias"] = np.ascontiguousarray(bfull.transpose(1, 0, 2)).reshape(128, 16 * 640)
        else:
            wi = np.asarray(inp["c_w_in"][0], np.float32)
            a = wi[:, :D].reshape(D, 8, 128)
            g = wi[:, D:].reshape(D, 8, 128)
            ag = np.concatenate([a, g], axis=2).reshape(D, 8 * 256)
            sh["c_wi"] = _kc_tile(ag, 256)
            wo = np.asarray(inp["c_w_out"][0], np.float32)
            x = wo.reshape(8, 128, 8, 128).transpose(2, 1, 0, 3)
            sh["c_wo"] = np.ascontiguousarray(x).reshape(8, 128, 8 * 128)
    return sh


def run_layers(hT_in, p, shared, layers, trace=False, dbg=()):
    nc = Builder(layers, dbg).build()
    in_maps = []
    for b in range(8):
        m = dict(shared)
        m["xT"] = hT_in[b]
        for i in layers:
            m["pT%d" % i] = np.ascontiguousarray(p[i, b].T).reshape(2, 128, S)
        in_maps.append(m)
    res = run_bass_kernel_spmd(nc, in_maps, core_ids=list(range(8)), trace=trace)
    out = np.stack([res.results[b]["yT"] for b in range(8)])
    return out, res


def kernel(**inputs):
    inp = {k: np.asarray(v) for k, v in inputs.items()}
    layers = list(range(DEPTH))
    x = inp["x"].astype(np.float32, copy=False)
    hT = np.ascontiguousarray(x.transpose(0, 2, 1)).reshape(8, 8, 128, S)
    shared = host_shared(inp, layers)
    out, _ = run_layers(hT, inp["p"].astype(np.float32, copy=False), shared, layers)
    y = out.reshape(8, D, S).transpose(0, 2, 1)
    return np.ascontiguousarray(y).astype(np.float32, copy=False)
```

```python
import numpy as np
from contextlib import ExitStack
import concourse.bass as bass
import concourse.mybir as mybir
from concourse.bass_utils import run_bass_kernel_spmd

F32 = mybir.dt.float32
BF16 = mybir.dt.bfloat16
AF = mybir.ActivationFunctionType
ALU = mybir.AluOpType
AX = mybir.AxisListType

D = 1024
S = 2048
T = 512
NT = S // T
FF = 2816
NFC = FF // 128
DEPTH = 4
EPS = 1e-6
NEG = -1e30


class Buf:
    __slots__ = ("name", "last_w", "readers", "dma_readers", "dma_sem", "dma_cnt", "const", "excl")

    fence = {}

    def __init__(self, name, const=False, excl=False):
        self.name = name
        self.excl = excl
        self.last_w = None
        self.readers = dict(Buf.fence)
        self.dma_readers = []
        self.dma_sem = None
        self.dma_cnt = 0
        self.const = const


class SemSlot:
    __slots__ = ("sem", "cnt")

    def __init__(self):
        self.sem = None
        self.cnt = 0


class Op:
    __slots__ = ("eng", "fn", "deps", "sig", "sigval", "is_dma", "dma_sem", "dma_val", "idx")

    def __init__(self, eng, fn, is_dma):
        self.eng = eng
        self.fn = fn
        self.deps = []
        self.sig = False
        self.sigval = 0
        self.is_dma = is_dma
        self.dma_sem = None
        self.dma_val = 0


class Prog:
    ENGS = ("pe", "act", "dve", "pool", "sp")

    def __init__(self, nc):
        self.nc = nc
        self.ops = {e: [] for e in self.ENGS}
        self.n = 0
        self.semslots = {}
        self.out_dma_ops = []

    def _add(self, op, reads, writes):
        deps = []
        for b in reads:
            w = b.last_w
            if w is not None:
                deps.append((w, True))
            if b.excl:
                for e_, r in b.readers.items():
                    if e_ != op.eng:
                        deps.append((r, False))
        for b in writes:
            w = b.last_w
            if w is not None and not (op.is_dma and w.is_dma):
                deps.append((w, False))
            for r in b.readers.values():
                deps.append((r, False))
            for r in b.dma_readers:
                deps.append((r, False))
        seen = set()
        for d, raw in deps:
            if d is op:
                continue
            if (not d.is_dma) and (not op.is_dma) and d.eng == op.eng:
                if op.eng == "pe":
                    continue
            k = id(d)
            if k in seen:
                continue
            seen.add(k)
            op.deps.append(d)
            if not d.is_dma:
                d.sig = True
        for b in writes:
            b.last_w = op
            b.readers = {}
            b.dma_readers = []
        for b in reads:
            if b.const or b in writes:
                continue
            if op.is_dma:
                b.dma_readers.append(op)
            else:
                b.readers[op.eng] = op
        op.idx = self.n
        self.n += 1
        self.ops[op.eng].append(op)
        return op

    @staticmethod
    def _flat(xs):
        out = []
        for x in xs:
            if isinstance(x, (list, tuple)):
                out.extend(Prog._flat(x))
            else:
                out.append(x)
        return out

    def op(self, eng, fn, reads=(), writes=()):
        return self._add(Op(eng, fn, False), self._flat(reads), self._flat(writes))

    def dma(self, eng, fn, reads=(), writes=(), is_output=False):
        o = Op(eng, fn, True)
        reads, writes = self._flat(reads), self._flat(writes)
        dst = writes[0]
        slot = self.semslots.get(dst.name)
        if slot is None:
            slot = self.semslots[dst.name] = SemSlot()
        slot.cnt += 16
        o.dma_sem = slot
        o.dma_val = slot.cnt
        self._add(o, list(reads), list(writes))
        if is_output:
            self.out_dma_ops.append(o)
        return o

    def emit(self, stack):
        nc = self.nc
        sems = {}
        for e in ("pe", "act", "dve", "pool"):
            sems[e] = stack.enter_context(nc.semaphore("s_" + e))
        for i, slot in enumerate(self.semslots.values()):
            slot.sem = stack.enter_context(nc.semaphore("d%d" % i))
        for e in ("pe", "act", "dve", "pool"):
            c = 0
            for o in self.ops[e]:
                if o.is_dma:
                    continue
                if o.sig:
                    c += 1
                    o.sigval = c
        out_ops = self.out_dma_ops

        def run(engh, ename):
            waited = {}

            def wait(sem, val):
                k = id(sem)
                if waited.get(k, 0) >= val:
                    return
                waited[k] = val
                engh.wait_ge(sem, val)

            for o in self.ops[ename]:
                for d in o.deps:
                    if d.is_dma:
                        wait(d.dma_sem.sem, d.dma_val)
                    else:
                        wait(sems[d.eng], d.sigval)
                ins = o.fn(engh)
                if o.is_dma:
                    ins.then_inc(o.dma_sem.sem, 16)
                elif o.sig:
                    ins.then_inc(sems[ename], 1)
            if ename == "sp":
                for o in out_ops:
                    wait(o.dma_sem.sem, o.dma_val)

        block = stack.enter_context(nc.Block())

        @block.tensor
        def _(e):
            run(e, "pe")

        @block.scalar
        def _(e):
            run(e, "act")

        @block.vector
        def _(e):
            run(e, "dve")

        @block.gpsimd
        def _(e):
            run(e, "pool")

        @block.sync
        def _(e):
            run(e, "sp")


def ptab_layout():
    off = {}
    c = 0
    for i in range(DEPTH):
        for nm in ("mix_pre", "mix_post", "ffn_pre", "ffn_post", "ple_g"):
            off[(nm, i)] = c
            c += 8
        off[("conv", i)] = c
        c += 132
    for j in range(2):
        off[("a_lng", j)] = c
        c += 16
        off[("a_lnb", j)] = c
        c += 16
    off["c_dw"] = c
    c += 248
    off["c_dwb"] = c
    c += 8
    off["c_lng"] = c
    c += 8
    off["c_lnb"] = c
    c += 8
    return off, c


class StopBuild(Exception):
    pass


class Stream:
    def __init__(self, B, name, n, free, srcs):
        self.B = B
        self.n = n
        self.srcs = list(srcs)
        self.aps = [B.A.alloc(BF16, *free) for _ in range(n)]
        self.bufs = [Buf("%s%d" % (name, i)) for i in range(n)]
        self.issued = 0
        self.taken = 0
        for _ in range(n - 1):
            self._issue()

    def _issue(self):
        k = self.issued
        if k >= len(self.srcs):
            return
        self.issued += 1
        ap, bf = self.aps[k % self.n], self.bufs[k % self.n]
        flat = ap.rearrange("p a b -> p (a b)") if len(ap.shape) == 3 else ap
        if "fastdma" in self.B.dbg:
            n8 = flat.shape[-1] // 8
            self.B.dma_cast(flat[:, 0:n8], self.srcs[k][:, 0:n8], [], [bf])
            return
        self.B.dma_cast(flat, self.srcs[k], [], [bf])

    def next(self):
        k = self.taken
        self.taken += 1
        self._issue()
        return self.aps[k % self.n], self.bufs[k % self.n]


class Arena:
    def __init__(self, ap_all, base, limit):
        self.all = ap_all
        self.off = base
        self.limit = limit

    def mark(self):
        return self.off

    def reset(self, m):
        self.off = m

    def alloc(self, dtype, *free):
        isz = 4 if dtype == F32 else 2
        n = 1
        for f in free:
            n *= f
        nb = n * isz
        off = (self.off + 63) // 64 * 64
        assert off + nb <= self.limit, ("SBUF arena overflow", off + nb, self.limit)
        self.off = off + nb
        v = self.all[:, off // 2:(off + nb) // 2]
        if dtype == F32:
            v = v.bitcast(F32)
        if len(free) == 2:
            v = v.rearrange("p (a b) -> p a b", a=free[0])
        elif len(free) == 3:
            v = v.rearrange("p (a b c) -> p a b c", a=free[0], b=free[1])
        return v


class Builder:
    def __init__(self, layers, dbg=()):
        self.layers = list(layers)
        self.dbg = set(dbg)
        self.nc = bass.Bass("TRN2", target_bir_lowering=False)
        Buf.fence = {}
        self.P = Prog(self.nc)
        self.poff, self.pcols = ptab_layout()
        self._bank = 0
        self._stat = 0

    def mm(self, out, lhsT, rhs, start, stop, r, w, **kw):
        self.P.op("pe", lambda e: e.matmul(out, lhsT=lhsT, rhs=rhs, start=start, stop=stop, **kw), r, w)

    def act(self, out, in_, func, r, w, bias=None, scale=None, accum=None):
        kw = {}
        if bias is not None:
            kw["bias"] = bias
        if scale is not None:
            kw["scale"] = scale
        if accum is not None:
            kw["accum_out"] = accum
        self.P.op("act", lambda e: e.activation(out=out, in_=in_, func=func, **kw), r, w)

    def ts(self, eng, out, in0, s1, s2, op0, op1, r, w):
        if op1 is None and eng == "pool":
            s2, op1 = 1.0, ALU.mult
        if op1 is None:
            self.P.op(eng, lambda e: e.tensor_scalar(out=out, in0=in0, scalar1=s1, scalar2=None, op0=op0), r, w)
        else:
            self.P.op(eng, lambda e: e.tensor_scalar(out=out, in0=in0, scalar1=s1, scalar2=s2, op0=op0, op1=op1), r, w)

    def stt(self, out, in0, scalar, in1, op0, op1, r, w):
        self.P.op("dve", lambda e: e.scalar_tensor_tensor(out=out, in0=in0, scalar=scalar, in1=in1, op0=op0, op1=op1), r, w)

    def tt(self, eng, out, in0, in1, op, r, w):
        self.P.op(eng, lambda e: e.tensor_tensor(out=out, in0=in0, in1=in1, op=op), r, w)

    def cp(self, eng, out, in_, r, w):
        if eng == "act":
            self.P.op("act", lambda e: e.copy(out=out, in_=in_), r, w)
        else:
            self.P.op(eng, lambda e: e.tensor_copy(out=out, in_=in_), r, w)

    def recip(self, out, in_, r, w):
        self.P.op("dve", lambda e: e.reciprocal(out=out, in_=in_), r, w)

    def memset(self, eng, ap, val, w):
        self.P.op(eng, lambda e: e.memset(ap, val), [], w)

    def dma_cast(self, out, in_, r, w):
        self.P.dma("pool", lambda e: e.dma_start(out=out, in_=in_), r, w)

    def dma_plain(self, out, in_, r, w, is_output=False):
        self.P.dma("sp", lambda e: e.dma_start(out=out, in_=in_), r, w, is_output=is_output)

    def bank(self):
        b = self._bank
        self._bank = (b + 1) % 6
        return b

    def statbank(self):
        b = 6 + self._stat
        self._stat ^= 1
        return b

    def psb(self, b, lo=0, hi=512):
        return self.ps[:, b * 512 + lo:b * 512 + hi]

    def build(self):
        nc = self.nc
        st = ExitStack()
        with st:
            self.declare_dram()
            self.sb_all = st.enter_context(nc.sbuf_tensor("sb_all", [128, 106300], BF16))
            self.ps = st.enter_context(nc.psum_tensor("ps_all", [128, 4096], F32))
            self.PBK = [Buf("psk%d" % i, excl=True) for i in range(32)]
            self.PB = [self.PBK[4 * i:4 * i + 4] for i in range(8)]
            self.A = Arena(self.sb_all, 0, 106300 * 2)
            self.setup_persistent()
            for i in self.layers:
                self.layer(i)
            self.store_output()
            self.P.emit(st)
        return nc

    def declare_dram(self):
        nc = self.nc
        dt = lambda n, s: nc.dram_tensor(n, s, F32, kind="ExternalInput").ap()
        self.d = {}
        self.d["xT"] = dt("xT", [8, 128, S])
        self.d["ptab"] = dt("ptab", [128, self.pcols])
        self.d["ident"] = dt("ident", [128, 128])
        for i in self.layers:
            self.d["pT%d" % i] = dt("pT%d" % i, [2, 128, S])
            self.d["wup%d" % i] = dt("wup%d" % i, [NFC, 128, 8 * 256])
            self.d["wdn%d" % i] = dt("wdn%d" % i, [8, 128, NFC * 128])
            self.d["wg%d" % i] = dt("wg%d" % i, [8, 128, 8 * 128])
            self.d["wp%d" % i] = dt("wp%d" % i, [128, 2 * D])
            kind, j = i % 3, i // 3
            if kind == 0:
                self.d["a_wu%d" % j] = dt("a_wu%d" % j, [16, 128, 8 * 128])
                self.d["a_wv%d" % j] = dt("a_wv%d" % j, [4, 128, 8 * 512])
                self.d["a_wo%d" % j] = dt("a_wo%d" % j, [8, 128, 16 * 128])
                self.d["a_ws%d" % j] = dt("a_ws%d" % j, [128, 8 * 128])
                self.d["a_bs%d" % j] = dt("a_bs%d" % j, [128, 8 * 128])
            elif kind == 1:
                self.d["b_wqk"] = dt("b_wqk", [8, 128, 8 * 256])
                self.d["b_wv"] = dt("b_wv", [4, 128, 8 * 256])
                self.d["b_wo"] = dt("b_wo", [8, 128, 8 * 128])
                self.d["b_bias"] = dt("b_bias", [128, 16 * 640])
            else:
                self.d["c_wi"] = dt("c_wi", [8, 128, 8 * 256])
                self.d["c_wo"] = dt("c_wo", [8, 128, 8 * 128])
        self.yT = nc.dram_tensor("yT", [8, 128, S], F32, kind="ExternalOutput").ap()

    def setup_persistent(self):
        A = self.A
        self.h = A.alloc(F32, 8, S)
        self.Bh = [[Buf("h%d_%d" % (c, t)) for t in range(NT)] for c in range(8)]
        self.ptab = A.alloc(F32, self.pcols)
        self.Bptab = Buf("ptab", const=True)
        self.ident = A.alloc(BF16, 128)
        self.onesb = A.alloc(BF16, 128)
        self.epsc = A.alloc(F32, 1)
        self.dummy = A.alloc(F32, 8)
        self.Bconst = Buf("const", const=True)
        self.sq = [A.alloc(BF16, T) for _ in range(3)]
        self.Bsq = [Buf("sq%d" % i) for i in range(3)]
        self._sq = 0
        self.rstds = [A.alloc(F32, T) for _ in range(3)]
        self.Brstds = [Buf("rstd%d" % k) for k in range(3)]
        self.rstd, self.Brstd = self.rstds[0], self.Brstds[0]
        self.mres = A.alloc(F32, 8, T)
        self.Bmres = [Buf("mres%d" % c) for c in range(8)]
        self.hns = [A.alloc(BF16, 8, T) for _ in range(2)]
        self.hn1_off = A.off - 8 * T * 2
        self.Bhns = [[Buf("hn%d_%d" % (k, c)) for c in range(8)] for k in range(2)]
        self.hn, self.Bhn = self.hns[0], self.Bhns[0]
        self.rtmp = [A.alloc(F32, T) for _ in range(3)]
        self.Brtmp = [Buf("rtmp%d" % i) for i in range(3)]
        self._rt = 0
        self.phase_mark = A.mark()
        self.dma_plain(self.ptab, self.d["ptab"], [], [self.Bptab])
        for c in range(8):
            self.dma_plain(self.h[:, c, :], self.d["xT"][c], [], self.Bh[c])
        self.memset("pool", self.onesb, 1.0, [self.Bconst])
        self.memset("pool", self.epsc, EPS, [self.Bconst])
        self.dma_cast(self.ident, self.d["ident"], [], [self.Bconst])

    def pcol(self, key, c, n=1):
        o = self.poff[key] + c
        return self.ptab[:, o:o + n]

    def nextsq(self):
        i = self._sq
        self._sq = (i + 1) % 3
        return self.sq[i], self.Bsq[i]

    def nextrt(self):
        i = self._rt
        self._rt = (i + 1) % 3
        return self.rtmp[i], self.Brtmp[i]

    def defer_mm(self, *args, **kw):
        self.flush_mm()
        self._pend_mm = (args, kw)

    def flush_mm(self):
        p = getattr(self, "_pend_mm", None)
        if p is not None:
            self._pend_mm = None
            self.mm(*p[0], **p[1])

    def stat_add(self, sb, src, src_bufs, c, n=8):
        sq, Bsq = self.nextsq()
        self.act(sq, src, AF.Square, src_bufs, [Bsq])
        self.defer_mm(self.psb(sb), self.onesb, sq, c == 0, c == n - 1, [Bsq, self.Bconst], [self.PB[sb]])

    def finish_rstd(self, sb, dim=D, role=0):
        self.flush_mm()
        rstd, Brstd = self.rstds[role], self.Brstds[role]
        self.act(rstd, self.psb(sb), AF.Sqrt, [self.PB[sb], self.Bconst], [Brstd], bias=self.epsc, scale=1.0 / dim)
        self.recip(rstd, rstd, [Brstd], [Brstd])
        return rstd, Brstd

    def pre_norm_gen(self, t, gkey, k=0):
        hn, Bhn = self.hns[k], self.Bhns[k]
        sb = self.statbank()
        tsl = slice(t * T, (t + 1) * T)
        for c in range(8):
            self.stat_add(sb, self.h[:, c, tsl], [self.Bh[c][t]], c)
            yield
        rstd, Brstd = self.finish_rstd(sb, role=0)
        yield
        for c in range(8):
            self.stt(hn[:, c, :], self.h[:, c, tsl], self.pcol(gkey, c), rstd, ALU.mult, ALU.mult,
                     [self.Bh[c][t], Brstd, self.Bptab], [Bhn[c]])
            if c % 2:
                yield

    def pre_norm(self, t, gkey, k=0):
        for _ in self.pre_norm_gen(t, gkey, k):
            pass

    def post_norm_gen(self, t, gkey, sb, role, after=None):
        rstd, Brstd = self.finish_rstd(sb, role=role)
        tsl = slice(t * T, (t + 1) * T)
        yield
        rts = {}
        for i in range(10):
            if i < 8:
                rt, Brt = self.nextrt()
                rts[i] = (rt, Brt)
                self.stt(rt, self.mres[:, i, :], self.pcol(gkey, i), rstd, ALU.mult, ALU.mult,
                         [self.Bmres[i], Brstd, self.Bptab], [Brt])
            if 1 <= i < 9:
                c = i - 1
                rt, Brt = rts.pop(c)
                self.tt("pool", self.h[:, c, tsl], self.h[:, c, tsl], rt, ALU.add, [self.Bh[c][t], Brt], [self.Bh[c][t]])
            if 2 <= i < 10 and after is not None:
                after(i - 2)
            yield

    def post_norm_residual(self, t, gkey, sb, role=1):
        for _ in self.post_norm_gen(t, gkey, sb, role):
            pass

    def evac_mres(self, b, dc, sb):
        self.cp("dve", self.mres[:, dc, :], self.psb(b), [self.PB[b]], [self.Bmres[dc]])
        self.stat_add(sb, self.mres[:, dc, :], [self.Bmres[dc]], dc)

    def make_slots(self, name, n, *free):
        aps = [self.A.alloc(BF16, *free) for _ in range(n)]
        bufs = [Buf("%s%d" % (name, i)) for i in range(n)]
        return {"aps": aps, "bufs": bufs, "i": 0, "n": n}

    def load_slot(self, slots, src):
        i = slots["i"]
        slots["i"] = (i + 1) % slots["n"]
        ap, bf = slots["aps"][i], slots["bufs"][i]
        flat = ap
        if len(ap.shape) == 3:
            flat = ap.rearrange("p a b -> p (a b)")
        self.dma_cast(flat, src, [], [bf])
        return ap, bf

    def stream(self, name, n, free, srcs):
        return Stream(self, name, n, free, srcs)

    def stop(self, tag):
        if tag in self.dbg:
            raise StopBuild()

    def layer(self, i):
        try:
            self._layer(i)
        except StopBuild:
            pass

    def new_phase(self):
        self.flush_mm()
        self.A.reset(self.phase_mark)
        f = {}
        for e in ("pe", "act", "dve", "pool"):
            for o in reversed(self.P.ops[e]):
                if not o.is_dma:
                    f[e] = o
                    break
        Buf.fence = f

    def _layer(self, i):
        kind, j = i % 3, i // 3
        A = self.A
        self.new_phase()
        if "prenorm" in self.dbg:
            self.pre_norm(0, ("mix_pre", i))
            return
        if "nomix" in self.dbg:
            pass
        elif kind == 0:
            self.mixer_a(i, j)
        elif kind == 1:
            self.mixer_b(i)
        else:
            self.mixer_c(i)
        self.new_phase()
        if "noffn" not in self.dbg:
            self.ffn_phase(i)

    def ffn_phase(self, i):
        A = self.A
        actb = A.alloc(BF16, NFC, T)
        Bact = [Buf("act%d" % f) for f in range(NFC)]
        wupS = self.stream("wup", 4, (8, 256), [self.d["wup%d" % i][fc] for _t in range(NT) for fc in range(NFC)])
        HF = NFC // 2
        wdnS = self.stream("wdn", 5, (HF, 128), [self.d["wdn%d" % i][dc][:, hf * HF * 128:(hf + 1) * HF * 128]
                                                  for _t in range(NT) for dc in range(8) for hf in range(2)])
        wgS = self.stream("wg", 4, (8, 128), [self.d["wg%d" % i][dc] for _t in range(NT) for dc in range(8)])
        cg = [A.alloc(F32, T) for _ in range(2)]
        cv = [A.alloc(F32, T) for _ in range(2)]
        sg = [A.alloc(F32, T) for _ in range(2)]
        Bcg = [Buf("cg%d" % k) for k in range(2)]
        Bcv = [Buf("cv%d" % k) for k in range(2)]
        Bsg = [Buf("sg%d" % k) for k in range(2)]
        halo = [A.alloc(F32, 2 * NFC, 2) for _ in range(2)]
        Bhalo = [[Buf("halo%d_%d" % (k, q)) for q in range(2 * NFC)] for k in range(2)]
        bnd = [A.alloc(F32, 3, 2 * NFC) for _ in range(2)]
        Bbnd = [Buf("bnd%d" % k) for k in range(2)]
        hb = A.alloc(BF16, 8, T)
        Bhb = [Buf("hb%d" % c) for c in range(8)]
        pt = [A.alloc(BF16, 2, T) for _ in range(2)]
        Bpt = [Buf("pt%d" % k) for k in range(2)]
        wp = A.alloc(BF16, 2, D)
        Bwp = Buf("wp")
        gate = [A.alloc(F32, T) for _ in range(1)]
        Bgate = [Buf("gate%d" % k) for k in range(1)]
        self.dma_cast(wp.rearrange("p a b -> p (a b)"), self.d["wp%d" % i], [], [Bwp])
        cbase = self.poff[("conv", i)]

        def tap(jj, q):
            o = cbase + jj * 44 + q
            return self.ptab[:, o:o + 1]

        def step(bg):
            for g in list(bg):
                try:
                    next(g)
                except StopIteration:
                    bg.remove(g)

        def drain(bg):
            while bg:
                step(bg)

        def load_pt(t):
            ptt, Bptt = pt[t % 2], Bpt[t % 2]
            tsl = slice(t * T, (t + 1) * T)
            for kc in range(2):
                self.dma_cast(ptt[:, kc, :], self.d["pT%d" % i][kc][:, tsl], [], [Bptt])

        def stage_A(t):
            return self.pre_norm_gen(t, ("ffn_pre", i), k=t % 2)

        def stage_B(t, bg):
            hn, Bhn = self.hns[t % 2], self.Bhns[t % 2]
            if t > 0:
                ho, Bho = halo[(t - 1) % 2], Bhalo[(t - 1) % 2]
                bd, Bbd = bnd[t % 2], Bbnd[t % 2]
                W0 = self.ptab[:, cbase:cbase + 44]
                W1 = self.ptab[:, cbase + 44:cbase + 88]
                self.tt("dve", bd[:, 2, :], ho[:, :, 1], W1, ALU.mult, Bho + [self.Bptab], [Bbd])
                self.tt("dve", bd[:, 0, :], ho[:, :, 0], W0, ALU.mult, Bho + [self.Bptab], [Bbd])
                self.tt("dve", bd[:, 0, :], bd[:, 0, :], bd[:, 2, :], ALU.add, [Bbd], [Bbd])
                self.tt("dve", bd[:, 1, :], ho[:, :, 1], W0, ALU.mult, Bho + [self.Bptab], [Bbd])
            for fc in range(NFC):
                w, Bw = wupS.next()
                k2 = fc % 2
                bg_ = self.bank()
                bv_ = self.bank()
                for kc in range(8):
                    self.mm(self.psb(bg_), w[:, kc, 0:128], hn[:, kc, :], kc == 0, kc == 7, [Bw, Bhn[kc]], [self.PB[bg_]])
                for kc in range(8):
                    self.mm(self.psb(bv_), w[:, kc, 128:256], hn[:, kc, :], kc == 0, kc == 7, [Bw, Bhn[kc]], [self.PB[bv_]])
                for (b, q, cbuf, Bc) in ((bg_, fc, cg[k2], Bcg[k2]), (bv_, NFC + fc, cv[k2], Bcv[k2])):
                    pb = self.PB[b]
                    if t == 0:
                        self.act(cbuf, self.psb(b), AF.Copy, [pb, self.Bptab], [Bc], scale=tap(2, q))
                    else:
                        bd, Bbd = bnd[t % 2], Bbnd[t % 2]
                        self.act(cbuf[:, 2:T], self.psb(b, 2, T), AF.Copy, [pb, self.Bptab], [Bc], scale=tap(2, q))
                        self.act(cbuf[:, 0:1], self.psb(b, 0, 1), AF.Identity, [pb, self.Bptab, Bbd], [Bc],
                                 scale=tap(2, q), bias=bd[:, 0, q:q + 1])
                        self.act(cbuf[:, 1:2], self.psb(b, 1, 2), AF.Identity, [pb, self.Bptab, Bbd], [Bc],
                                 scale=tap(2, q), bias=bd[:, 1, q:q + 1])
                    if t < NT - 1:
                        self.cp("act", halo[t % 2][:, q, 0:2], self.psb(b, T - 2, T), [pb], [Bhalo[t % 2][q]])
                    self.stt(cbuf[:, 1:T], self.psb(b, 0, T - 1), tap(1, q), cbuf[:, 1:T], ALU.mult, ALU.add,
                             [pb, Bc, self.Bptab], [Bc])
                    self.stt(cbuf[:, 2:T], self.psb(b, 0, T - 2), tap(0, q), cbuf[:, 2:T], ALU.mult, ALU.add,
                             [pb, Bc, self.Bptab], [Bc])
                self.act(sg[k2], cg[k2], AF.Silu, [Bcg[k2]], [Bsg[k2]])
                self.tt("pool", actb[:, fc, :], sg[k2], cv[k2], ALU.mult, [Bsg[k2], Bcv[k2]], [Bact[fc]])
                step(bg)
            drain(bg)

        def stage_C(t, bg):
            sb = self.statbank()
            for dc in range(8):
                b = self.bank()
                for hf in range(2):
                    w, Bw = wdnS.next()
                    for f2 in range(HF):
                        fc = hf * HF + f2
                        self.mm(self.psb(b), w[:, f2, :], actb[:, fc, :], fc == 0, fc == NFC - 1, [Bw, Bact[fc]], [self.PB[b]])
                step(bg)
                self.evac_mres(b, dc, sb)
            drain(bg)
            return sb

        def stage_D(t, sb):
            tsl = slice(t * T, (t + 1) * T)

            def after(c):
                self.cp("act", hb[:, c, :], self.h[:, c, tsl], [self.Bh[c][t]], [Bhb[c]])
            g = self.post_norm_gen(t, ("ffn_post", i), sb, 1, after=after)
            next(g)
            return g

        def stage_E(t):
            ptt, Bptt = pt[t % 2], Bpt[t % 2]
            if t + 1 < NT:
                load_pt(t + 1)
            sb = self.statbank()
            for dc in range(8):
                w, Bw = wgS.next()
                bgt = self.bank()
                be = self.bank()
                for kc in range(8):
                    self.mm(self.psb(bgt), w[:, kc, :], hb[:, kc, :], kc == 0, kc == 7, [Bw, Bhb[kc]], [self.PB[bgt]])
                for kc in range(2):
                    self.mm(self.psb(be), wp[:, kc, dc * 128:(dc + 1) * 128], ptt[:, kc, :], kc == 0, kc == 1,
                            [Bwp, Bptt], [self.PB[be]])
                k2 = 0
                self.act(gate[k2], self.psb(bgt), AF.Sigmoid, [self.PB[bgt]], [Bgate[k2]])
                self.tt("dve", self.mres[:, dc, :], gate[k2], self.psb(be), ALU.mult, [Bgate[k2], self.PB[be]],
                        [self.Bmres[dc]])
                self.stat_add(sb, self.mres[:, dc, :], [self.Bmres[dc]], dc)
            return sb

        load_pt(0)
        drain([stage_A(0)])
        stage_B(0, [stage_A(1)])
        pend = []
        for t in range(NT):
            sb = stage_C(t, pend)
            pend = []
            bgl = [stage_D(t, sb)]
            if t + 2 < NT:
                bgl.append(stage_A(t + 2))
            if t + 1 < NT:
                stage_B(t + 1, bgl)
            else:
                drain(bgl)
            sbe = stage_E(t)
            g = self.post_norm_gen(t, ("ple_g", i), sbe, 2)
            next(g)
            pend = [g]
        drain(pend)
        self.flush_mm()

    def mixer_c(self, i):
        A = self.A
        ybuf = A.alloc(BF16, 8, 30 + T)
        Byb = [Buf("yb%d" % c) for c in range(8)]
        z = A.alloc(F32, 8, T)
        Bz = [Buf("z%d" % c) for c in range(8)]
        zb = [A.alloc(BF16, T) for _ in range(2)]
        Bzb = [Buf("zb%d" % k) for k in range(2)]
        actc = A.alloc(BF16, 8, T)
        Bac = [Buf("actc%d" % c) for c in range(8)]
        dg = [A.alloc(BF16, 31, 128) for _ in range(2)]
        Bdg = [Buf("dg%d" % k) for k in range(2)]
        wi = self.stream("cwi", 3, (8, 256), [self.d["c_wi"][c] for _t in range(NT) for c in range(8)])
        wo = self.stream("cwo", 3, (8, 128), [self.d["c_wo"][c] for _t in range(NT) for c in range(8)])
        sgm = [A.alloc(F32, T) for _ in range(2)]
        Bsgm = [Buf("sgm%d" % k) for k in range(2)]
        mean = A.alloc(F32, T)
        msq = A.alloc(F32, T)
        var = A.alloc(F32, T)
        nmr = A.alloc(F32, T)
        Bmean, Bmsq, Bvar, Bnmr = Buf("mean"), Buf("msq"), Buf("var"), Buf("nmr")
        t1 = [A.alloc(F32, T) for _ in range(2)]
        Bt1 = [Buf("ct1_%d" % k) for k in range(2)]
        for c in range(8):
            self.memset("pool", ybuf[:, c, 0:30], 0.0, [Byb[c]])

        def build_diag(c):
            k2_ = c % 2
            o_ = self.poff["c_dw"] + c * 31
            in1 = self.ptab[:, o_:o_ + 31].unsqueeze(2).to_broadcast([128, 31, 128])
            in0 = self.ident.unsqueeze(1).to_broadcast([128, 31, 128])
            self.tt("dve", dg[k2_], in0, in1, ALU.mult, [self.Bconst, self.Bptab], [Bdg[k2_]])

        self.pre_norm(0, ("mix_pre", i), k=0)
        for t in range(NT):
            hn, Bhn = self.hns[t % 2], self.Bhns[t % 2]
            sb1 = self.statbank()
            sb2 = self.statbank()
            build_diag(0)
            for c in range(8):
                if c + 1 < 8:
                    build_diag(c + 1)
                w, Bw = wi.next()
                ba = self.bank()
                bg = self.bank()
                for kc in range(8):
                    self.mm(self.psb(ba), w[:, kc, 0:128], hn[:, kc, :], kc == 0, kc == 7, [Bw, Bhn[kc]], [self.PB[ba]])
                for kc in range(8):
                    self.mm(self.psb(bg), w[:, kc, 128:256], hn[:, kc, :], kc == 0, kc == 7, [Bw, Bhn[kc]], [self.PB[bg]])
                k2 = c % 2
                self.act(sgm[k2], self.psb(bg), AF.Sigmoid, [self.PB[bg]], [Bsgm[k2]])
                self.tt("dve", ybuf[:, c, 30:30 + T], sgm[k2], self.psb(ba), ALU.mult, [Bsgm[k2], self.PB[ba]], [Byb[c]])
                bc = self.bank()
                for jj in range(31):
                    self.mm(self.psb(bc), dg[k2][:, jj, :], ybuf[:, c, jj:jj + T], jj == 0, jj == 30, [Bdg[k2], Byb[c]], [self.PB[bc]])
                bias = self.pcol("c_dwb", c)
                self.act(z[:, c, :], self.psb(bc), AF.Identity, [self.PB[bc], self.Bptab], [Bz[c]], bias=bias)
                self.act(zb[k2], self.psb(bc), AF.Identity, [self.PB[bc], self.Bptab], [Bzb[k2]], bias=bias)
                self.flush_mm()
                self.mm(self.psb(sb1), self.onesb, zb[k2], c == 0, c == 7, [Bzb[k2], self.Bconst], [self.PB[sb1]])
                sq, Bsq = self.nextsq()
                self.act(sq, self.psb(bc), AF.Square, [self.PB[bc], self.Bptab], [Bsq], bias=bias)
                self.defer_mm(self.psb(sb2), self.onesb, sq, c == 0, c == 7, [Bsq, self.Bconst], [self.PB[sb2]])
                if t < NT - 1:
                    self.cp("pool", ybuf[:, c, 0:30], ybuf[:, c, T:T + 30], [Byb[c]], [Byb[c]])
            self.flush_mm()
            self.ts("dve", mean, self.psb(sb1), 1.0 / D, None, ALU.mult, None, [self.PB[sb1]], [Bmean])
            self.act(msq, mean, AF.Square, [Bmean], [Bmsq])
            self.stt(var, self.psb(sb2), 1.0 / D, msq, ALU.mult, ALU.subtract, [self.PB[sb2], Bmsq], [Bvar])
            self.act(self.rstd, var, AF.Sqrt, [Bvar, self.Bconst], [self.Brstd], bias=self.epsc)
            self.recip(self.rstd, self.rstd, [self.Brstd], [self.Brstd])
            self.stt(nmr, mean, -1.0, self.rstd, ALU.mult, ALU.mult, [Bmean, self.Brstd], [Bnmr])
            for c in range(8):
                k2 = c % 2
                self.tt("dve", t1[k2], z[:, c, :], self.rstd, ALU.mult, [Bz[c], self.Brstd], [Bt1[k2]])
                self.tt("pool", t1[k2], t1[k2], nmr, ALU.add, [Bt1[k2], Bnmr], [Bt1[k2]])
                self.act(actc[:, c, :], t1[k2], AF.Silu, [Bt1[k2], self.Bptab], [Bac[c]],
                         scale=self.pcol("c_lng", c), bias=self.pcol("c_lnb", c))
            if t + 1 < NT:
                self.pre_norm(t + 1, ("mix_pre", i), k=(t + 1) % 2)
            sb = self.statbank()
            for dc in range(8):
                w, Bw = wo.next()
                b = self.bank()
                for c in range(8):
                    self.mm(self.psb(b), w[:, c, :], actc[:, c, :], c == 0, c == 7, [Bw, Bac[c]], [self.PB[b]])
                self.evac_mres(b, dc, sb)
            self.post_norm_residual(t, ("mix_post", i), sb)

    def mixer_a(self, i, j):
        A = self.A
        u = A.alloc(BF16, 16, T)
        Bu = [Buf("u%d" % c) for c in range(16)]
        vgb = A.alloc(BF16, 4, 2048)
        Bvgb = [Buf("vgb%d" % b) for b in range(4)]
        wv = self.stream("awv", 3, (8, 512), [self.d["a_wv%d" % j][q] for _t in range(NT) for q in range(4)])
        wu = self.stream("awu", 3, (8, 128), [self.d["a_wu%d" % j][q] for _t in range(NT) for q in range(16)])
        wo = self.stream("awo", 2, (16, 128), [self.d["a_wo%d" % j][q] for _t in range(NT) for q in range(8)])
        bst = A.alloc(F32, 4, 4, 6)
        Bbst = [Buf("bst%d" % b) for b in range(4)]
        mv = A.alloc(F32, 4, 2)
        rs = A.alloc(F32, 4, 1)
        vpe = A.alloc(F32, 4, 1)
        Bmv = [Buf("mv%d" % b) for b in range(4)]
        Brs = [Buf("rs%d" % b) for b in range(4)]
        Bvpe = [Buf("vpe%d" % b) for b in range(4)]
        nmh = A.alloc(F32, 1)
        wsT = A.alloc(BF16, 8, 128)
        Bws = Buf("wsT")
        Cc = A.alloc(F32, 16, 128)
        BCc = Buf("Cc")
        bsb = A.alloc(F32, 8, 128)
        Bbsb = Buf("bsb")
        rw = A.alloc(F32, 8, 128)
        Brw = Buf("rw")
        t1 = [A.alloc(F32, 4, 128) for _ in range(1)]
        Bt1 = [Buf("at1_%d" % k) for k in range(1)]
        self.memset("pool", nmh, -0.5, [self.Bconst])
        self.dma_cast(wsT.rearrange("p a b -> p (a b)"), self.d["a_ws%d" % j], [], [Bws])
        self.memset("pool", wsT[64:128, :, 0:64], 0.0, [Bws])
        self.dma_plain(bsb.rearrange("p a b -> p (a b)"), self.d["a_bs%d" % j], [], [Bbsb])
        b0 = self.bank()
        b1 = self.bank()
        for g in range(8):
            bb = b0 if g < 4 else b1
            lo = (g % 4) * 128
            self.mm(self.psb(bb, lo, lo + 128), self.onesb, wsT[:, g, :], True, True, [Bws, self.Bconst], [self.PB[bb]])
        self.cp("act", rw[:, 0:4, :].rearrange("p a b -> p (a b)"), self.psb(b0), [self.PB[b0]], [Brw])
        self.cp("act", rw[:, 4:8, :].rearrange("p a b -> p (a b)"), self.psb(b1), [self.PB[b1]], [Brw])
        for uc in range(16):
            g = uc // 2
            self.stt(Cc[:, uc, :], rw[:, g, :], self.pcol(("a_lnb", j), uc), bsb[:, g, :], ALU.mult, ALU.add,
                     [Brw, Bbsb, self.Bptab], [BCc])
        self.pre_norm(0, ("mix_pre", i), k=0)
        for t in range(NT):
            hn, Bhn = self.hns[t % 2], self.Bhns[t % 2]
            for vq in range(4):
                w, Bw = wv.next()
                for blk in range(4):
                    b = self.bank()
                    for kc in range(8):
                        self.mm(self.psb(b), hn[:, kc, blk * 128:(blk + 1) * 128], w[:, kc, :], kc == 0, kc == 7,
                                [Bw, Bhn[kc]], [self.PB[b]])
                    dst = vgb[:, blk, vq * 512:(vq + 1) * 512]
                    self.act(dst, self.psb(b), AF.Gelu_apprx_tanh, [self.PB[b]], [Bvgb[blk]])
                    bo = bst[:, blk, vq, :]
                    self.P.op("dve", lambda e, bo=bo, src=dst: e.bn_stats(out=bo, in_=src), [Bvgb[blk]], [Bbst[blk]])
            for blk in range(4):
                mo = mv[:, blk, :]
                bi = bst[:, blk, :, :]
                self.P.op("dve", lambda e, mo=mo, bi=bi: e.bn_aggr(out=mo, in_=bi), [Bbst[blk]], [Bmv[blk]])
                self.ts("pool", vpe[:, blk, :], mv[:, blk, 1:2], EPS, None, ALU.add, None, [Bmv[blk]], [Bvpe[blk]])
                self.tt("pool", rs[:, blk, :], vpe[:, blk, :], nmh, ALU.pow, [Bvpe[blk], self.Bconst], [Brs[blk]])
                self.ts("dve", vgb[:, blk, :], vgb[:, blk, :], mv[:, blk, 0:1], rs[:, blk, :], ALU.subtract, ALU.mult,
                        [Bvgb[blk], Bmv[blk], Brs[blk]], [Bvgb[blk]])
            kk = 0
            for u4 in range(4):
                for j4 in range(4):
                    uc = u4 * 4 + j4
                    w, Bw = wu.next()
                    b = self.bank()
                    for kc in range(8):
                        self.mm(self.psb(b), w[:, kc, :], hn[:, kc, :], kc == 0, kc == 7, [Bw, Bhn[kc]], [self.PB[b]])
                    self.act(u[:, uc, :], self.psb(b), AF.Gelu_apprx_tanh, [self.PB[b]], [Bu[uc]])
                for blk in range(4):
                    bsl = slice(blk * 128, (blk + 1) * 128)
                    b = self.bank()
                    for j4 in range(4):
                        uc = u4 * 4 + j4
                        self.mm(self.psb(b, j4 * 128, (j4 + 1) * 128), vgb[:, blk, uc * 128:(uc + 1) * 128], wsT[:, uc // 2, :], True, True,
                                [Bvgb[blk], Bws], [self.PB[b]])
                    k3 = 0
                    kk += 1
                    for j4 in range(4):
                        uc = u4 * 4 + j4
                        self.stt(t1[k3][:, j4, :], self.psb(b, j4 * 128, (j4 + 1) * 128), self.pcol(("a_lng", j), uc), Cc[:, uc, :],
                                 ALU.mult, ALU.add, [self.PB[b], BCc, self.Bptab], [Bt1[k3]])
                    uv = u[:, u4 * 4:(u4 + 1) * 4, bsl]
                    self.tt("dve", uv, t1[k3], uv, ALU.mult, [Bt1[k3]] + Bu[u4 * 4:(u4 + 1) * 4], Bu[u4 * 4:(u4 + 1) * 4])
            if t + 1 < NT:
                self.pre_norm(t + 1, ("mix_pre", i), k=(t + 1) % 2)
            sb = self.statbank()
            for dc in range(8):
                w, Bw = wo.next()
                b = self.bank()
                for uc in range(16):
                    self.mm(self.psb(b), w[:, uc, :], u[:, uc, :], uc == 0, uc == 15, [Bw, Bu[uc]], [self.PB[b]])
                self.evac_mres(b, dc, sb)
            self.post_norm_residual(t, ("mix_post", i), sb)

    def mixer_b(self, i):
        A = self.A
        kT = A.alloc(BF16, 8, 1024)
        BkT = [[Buf("kT%d_%d" % (c, s_)) for s_ in range(2)] for c in range(8)]
        V = A.alloc(BF16, 8, 1024)
        BV = [Buf("V%d" % b) for b in range(8)]
        qz = A.alloc(BF16, 8, 2, T)
        Bq = [Buf("qz%d" % c) for c in range(8)]
        Bb = A.alloc(BF16, 16, 640)
        BBb = Buf("Bb")
        A2 = Arena(self.sb_all, self.hn1_off, self.hn1_off + 8 * T * 2)
        Pb = [A2.alloc(BF16, 640) for _ in range(3)]
        BPb = [Buf("Pb%d" % k) for k in range(3)]
        PTb = [A2.alloc(BF16, 640) for _ in range(3)]
        BPTb = [Buf("PTb%d" % k) for k in range(3)]
        dgr = [A.alloc(BF16, 128) for _ in range(3)]
        Bdgr = [Buf("dgr%d" % k) for k in range(3)]
        st3 = [A.alloc(F32, 4) for _ in range(4)]
        Bnm = [Buf("nm%d" % k) for k in range(4)]
        Brs = [Buf("rsum%d" % k) for k in range(4)]
        Bri = [Buf("rinv%d" % k) for k in range(4)]
        wqk = self.stream("bwqk", 2, (8, 256), [self.d["b_wqk"][q] for _t in range(NT) for q in range(8)])
        wvs = self.stream("bwv", 2, (8, 256), [self.d["b_wv"][q] for _t in range(NT) for q in range(4)])
        wos = self.stream("bwo", 2, (8, 128), [self.d["b_wo"][q] for _t in range(NT) for q in range(8)])
        oT, BoT = self.hn, self.Bhn
        ps = self.ps
        self.dma_cast(Bb.rearrange("p a b -> p (a b)"), self.d["b_bias"], [], [BBb])
        self.memset("pool", Bb[64:128, :, 0:64], NEG, [BBb])
        self.memset("pool", Bb[0:64, :, 576:640], NEG, [BBb])
        for c in range(8):
            self.memset("pool", qz[:, c, :, :], 0.0, [Bq[c]])
        unit = 0
        for t in range(NT):
            slot = t % 2
            self.pre_norm(t, ("mix_pre", i))
            for c in range(8):
                w, Bw = wqk.next()
                bq = self.bank()
                bk = self.bank()
                for kc in range(8):
                    self.mm(self.psb(bq), w[:, kc, 0:128], self.hn[:, kc, :], kc == 0, kc == 7, [Bw, self.Bhn[kc]], [self.PB[bq]])
                for kc in range(8):
                    self.mm(self.psb(bk), w[:, kc, 128:256], self.hn[:, kc, :], kc == 0, kc == 7, [Bw, self.Bhn[kc]], [self.PB[bk]])
                for hh in range(2):
                    rows = slice(hh * 64, hh * 64 + 64)
                    self.act(qz[rows, c, hh, :], ps[rows, bq * 512:(bq + 1) * 512], AF.Copy, [self.PB[bq]], [Bq[c]], scale=0.125)
                self.cp("dve", kT[:, c, slot * 512:(slot + 1) * 512], self.psb(bk), [self.PB[bk]], [BkT[c][slot]])
            for qt in range(4):
                w, Bw = wvs.next()
                for blk in range(4):
                    b = self.bank()
                    for kc in range(8):
                        self.mm(self.psb(b, 0, 256), self.hn[:, kc, blk * 128:(blk + 1) * 128], w[:, kc, :], kc == 0, kc == 7,
                                [Bw, self.Bhn[kc]], [self.PB[b]])
                    rb = (4 * t + blk) % 8
                    self.cp("act" if (blk % 2) else "dve", V[:, rb, qt * 256:(qt + 1) * 256], self.psb(b, 0, 256), [self.PB[b]], [BV[rb]])
            units = [(c, jq, hh) for c in range(8) for jq in range(4) for hh in range(2)]
            NU = len(units)

            def geom(u):
                c, jq, hh = units[u]
                jb = 4 * t + jq
                return c, jq, hh, jb, max(0, 4 - jb)

            def st_scores(u):
                c, jq, hh, jb, i0 = geom(u)
                hd = 2 * c + hh
                sl = u % 2
                sbase = sl * 1024
                groups = []
                ii = i0
                while ii < 5:
                    n = 1
                    while (ii + n < 5 and (ii + n) != 4 and ((jb - 4 + ii + n) % 8) == ((jb - 4 + ii) % 8) + n):
                        n += 1
                    groups.append((ii, n))
                    ii += n
                started = set()
                for (ii, n) in groups:
                    kb = jb - 4 + ii
                    rc = (kb % 8) * 128
                    ksl = sorted(set(((kb + m) // 4) % 2 for m in range(n)))
                    sap = ps[:, sbase + ii * 128: sbase + (ii + n) * 128]
                    bk = 0 if ii < 4 else 1
                    wb = self.PB[2 * sl + bk]
                    self.mm(sap, qz[:, c, hh, jq * 128:(jq + 1) * 128], kT[:, c, rc:rc + n * 128], bk not in started, False,
                            [Bq[c]] + [BkT[c][k_] for k_ in ksl], wb)
                    started.add(bk)
                if i0 < 4:
                    self.mm(ps[:, sbase + i0 * 128: sbase + 512], self.ident, Bb[:, hd, i0 * 128:512], False, True,
                            [self.Bconst, BBb], self.PB[2 * sl])
                self.mm(ps[:, sbase + 512: sbase + 640], self.ident, Bb[:, hd, 512:640], False, True,
                        [self.Bconst, BBb], self.PB[2 * sl + 1])
                sbufs = [self.PB[2 * sl], self.PB[2 * sl + 1]]
                sfull = ps[:, sbase + i0 * 128: sbase + 640]
                k4 = u % 4
                nmax = st3[k4][:, 0:1]
                self.P.op("dve", lambda e, nmax=nmax, sfull=sfull: e.tensor_reduce(out=nmax, in_=sfull, axis=AX.X, op=ALU.max, negate=True),
                          sbufs, [Bnm[k4]])

            def st_exp(u):
                c, jq, hh, jb, i0 = geom(u)
                sl = u % 2
                k4 = u % 4
                sbase = sl * 1024
                sbufs = [self.PB[2 * sl], self.PB[2 * sl + 1]]
                sfull = ps[:, sbase + i0 * 128: sbase + 640]
                nmax, rsum = st3[k4][:, 0:1], st3[k4][:, 1:2]
                self.act(Pb[u % 3][:, i0 * 128:640], sfull, AF.Exp, sbufs + [Bnm[k4]], [BPb[u % 3], Brs[k4]], bias=nmax, accum=rsum)

            def st_rd(u):
                k4 = u % 4
                k3 = u % 3
                rsum, rinv = st3[k4][:, 1:2], st3[k4][:, 2:3]
                self.recip(rinv, rsum, [Brs[k4]], [Bri[k4]])
                self.ts("pool", dgr[k3], self.ident, rinv, None, ALU.mult, None, [self.Bconst, Bri[k4]], [Bdgr[k3]])

            def st_pt(u):
                c, jq, hh, jb, i0 = geom(u)
                sl = u % 2
                k3 = u % 3
                pbase = 2048
                for ii in range(i0, 5):
                    self.mm(ps[:, pbase + ii * 128: pbase + (ii + 1) * 128], Pb[k3][:, ii * 128:(ii + 1) * 128], dgr[k3], True, True,
                            [BPb[k3], Bdgr[k3]], [self.PB[4] if ii < 4 else self.PB[5]])
                self.cp("act" if (u % 2) else "dve", PTb[k3][:, i0 * 128:640], ps[:, pbase + i0 * 128: pbase + 640],
                        [self.PB[4], self.PB[5]], [BPTb[k3]])

            def st_pv(u):
                c, jq, hh, jb, i0 = geom(u)
                k3 = u % 3
                ob = 6 + hh
                for ii in range(i0, 5):
                    kb = jb - 4 + ii
                    self.mm(ps[:, ob * 512 + jq * 128: ob * 512 + (jq + 1) * 128], V[:, kb % 8, c * 128:(c + 1) * 128],
                            PTb[k3][:, ii * 128:(ii + 1) * 128], ii == i0, ii == 4, [BV[kb % 8], BPTb[k3]], [self.PB[ob]])
                if jq == 3 and hh == 1:
                    for h2 in range(2):
                        rows = slice(h2 * 64, h2 * 64 + 64)
                        o2 = 6 + h2
                        self.cp("dve" if h2 else "act", oT[rows, c, :], ps[rows, o2 * 512:(o2 + 1) * 512], self.PB[o2], [BoT[c]])

            for u in range(NU + 4):
                if u < NU:
                    st_scores(u)
                if 0 <= u - 1 < NU:
                    st_exp(u - 1)
                if 0 <= u - 2 < NU:
                    st_rd(u - 2)
                if 0 <= u - 3 < NU:
                    st_pt(u - 3)
                if 0 <= u - 4 < NU:
                    st_pv(u - 4)
            sb = self.statbank()
            for dc in range(8):
                w, Bw = wos.next()
                b = self.bank()
                for c in range(8):
                    self.mm(self.psb(b), w[:, c, :], oT[:, c, :], c == 0, c == 7, [Bw, BoT[c]], [self.PB[b]])
                self.evac_mres(b, dc, sb)
            self.post_norm_residual(t, ("mix_post", i), sb)

    def store_output(self):
        for c in range(8):
            self.dma_plain(self.yT[c], self.h[:, c, :], self.Bh[c], [Buf("y%d" % c)], is_output=True)


def _cols(v, n):
    return np.ascontiguousarray(np.asarray(v, np.float32).reshape(n, 128).T)


def _kc_tile(w, ncols_per_block):
    K, N = w.shape
    nb = N // ncols_per_block
    x = w.reshape(K // 128, 128, nb, ncols_per_block)
    x = x.transpose(2, 1, 0, 3)
    return np.ascontiguousarray(x).reshape(nb, 128, (K // 128) * ncols_per_block)


def host_shared(inp, layers):
    off, R = ptab_layout()
    ptab = np.zeros((128, R), np.float32)

    def put(key, arr):
        ptab[:, off[key]:off[key] + arr.shape[1]] = arr

    for i in range(DEPTH):
        put(("mix_pre", i), _cols(inp["mix_pre_g"][i], 8))
        put(("mix_post", i), _cols(inp["mix_post_g"][i], 8))
        put(("ffn_pre", i), _cols(inp["ffn_pre_g"][i], 8))
        put(("ffn_post", i), _cols(inp["ffn_post_g"][i], 8))
        put(("ple_g", i), _cols(inp["ple_norm_g"][i], 8))
        cv = np.concatenate([_cols(inp["ffn_conv"][i][jj], 44) for jj in range(3)], axis=1)
        put(("conv", i), cv)
    for j in range(2):
        put(("a_lng", j), _cols(inp["a_ln_g"][j], 16))
        put(("a_lnb", j), _cols(inp["a_ln_b"][j], 16))
    dw = np.asarray(inp["c_dw"][0], np.float32)
    dwc = dw.reshape(31, 8, 128).transpose(2, 1, 0).reshape(128, 248)
    put("c_dw", np.ascontiguousarray(dwc))
    put("c_dwb", _cols(inp["c_dw_b"][0], 8))
    put("c_lng", _cols(inp["c_ln_g"][0], 8))
    put("c_lnb", _cols(inp["c_ln_b"][0], 8))
    sh = {"ptab": ptab, "ident": np.eye(128, dtype=np.float32)}
    for i in layers:
        wu = np.asarray(inp["ffn_w_up"][i], np.float32)
        g = wu[:, :FF].reshape(D, NFC, 128)
        v = wu[:, FF:].reshape(D, NFC, 128)
        gv = np.concatenate([g, v], axis=2).reshape(D, NFC * 256)
        sh["wup%d" % i] = _kc_tile(gv, 256)
        wd = np.asarray(inp["ffn_w_down"][i], np.float32)
        x = wd.reshape(NFC, 128, 8, 128).transpose(2, 1, 0, 3)
        sh["wdn%d" % i] = np.ascontiguousarray(x).reshape(8, 128, NFC * 128)
        sh["wg%d" % i] = _kc_tile(np.asarray(inp["ple_w_gate"][i], np.float32), 128)
        wp = np.asarray(inp["ple_w_proj"][i], np.float32)
        sh["wp%d" % i] = np.ascontiguousarray(wp.reshape(2, 128, D).transpose(1, 0, 2)).reshape(128, 2 * D)
        kind, j = i % 3, i // 3
        if kind == 0:
            win = np.asarray(inp["a_w_in"][j], np.float32)
            sh["a_wu%d" % j] = _kc_tile(win[:, :2048], 128)
            sh["a_wv%d" % j] = _kc_tile(win[:, 2048:], 512)
            wo = np.asarray(inp["a_w_out"][j], np.float32)
            x = wo.reshape(16, 128, 8, 128).transpose(2, 1, 0, 3)
            sh["a_wo%d" % j] = np.ascontiguousarray(x).reshape(8, 128, 16 * 128)
            ws = np.asarray(inp["a_w_s"][j], np.float32)
            sh["a_ws%d" % j] = np.ascontiguousarray(ws.transpose(2, 0, 1)).reshape(128, 8 * 128)
            bs = np.asarray(inp["a_b_s"][j], np.float32).reshape(1, 8 * 128)
            sh["a_bs%d" % j] = np.ascontiguousarray(np.broadcast_to(bs, (128, 8 * 128)))
        elif kind == 1:
            wq = np.asarray(inp["b_w_qkv"][0], np.float32)
            q = wq[:, :D].reshape(D, 8, 128)
            k = wq[:, D:2 * D].reshape(D, 8, 128)
            qk = np.concatenate([q, k], axis=2).reshape(D, 8 * 256)
            sh["b_wqk"] = _kc_tile(qk, 256)
            sh["b_wv"] = _kc_tile(np.ascontiguousarray(wq[:, 2 * D:]), 256)
            wo = np.asarray(inp["b_w_out"][0], np.float32)
            x = wo.reshape(8, 128, 8, 128).transpose(2, 1, 0, 3)
            sh["b_wo"] = np.ascontiguousarray(x).reshape(8, 128, 8 * 128)
            rb = np.asarray(inp["b_rel_bias"][0], np.float32)
            qq = np.arange(128)[:, None]
            kk = np.arange(640)[None, :]
            idx = np.clip(qq + 512 - kk, -128, 128) + 128
            bfull = rb[:, idx]
            sh["b_bias"] = np.ascontiguousarray(bfull.transpose(1, 0, 2)).reshape(128, 16 * 640)
        else:
            wi = np.asarray(inp["c_w_in"][0], np.float32)
            a = wi[:, :D].reshape(D, 8, 128)
            g = wi[:, D:].reshape(D, 8, 128)
            ag = np.concatenate([a, g], axis=2).reshape(D, 8 * 256)
            sh["c_wi"] = _kc_tile(ag, 256)
            wo = np.asarray(inp["c_w_out"][0], np.float32)
            x = wo.reshape(8, 128, 8, 128).transpose(2, 1, 0, 3)
            sh["c_wo"] = np.ascontiguousarray(x).reshape(8, 128, 8 * 128)
    return sh


def run_layers(hT_in, p, shared, layers, trace=False, dbg=()):
    nc = Builder(layers, dbg).build()
    in_maps = []
    for b in range(8):
        m = dict(shared)
        m["xT"] = hT_in[b]
        for i in layers:
            m["pT%d" % i] = np.ascontiguousarray(p[i, b].T).reshape(2, 128, S)
        in_maps.append(m)
    res = run_bass_kernel_spmd(nc, in_maps, core_ids=list(range(8)), trace=trace)
    out = np.stack([res.results[b]["yT"] for b in range(8)])
    return out, res


def kernel(**inputs):
    inp = {k: np.asarray(v) for k, v in inputs.items()}
    layers = list(range(DEPTH))
    x = inp["x"].astype(np.float32, copy=False)
    hT = np.ascontiguousarray(x.transpose(0, 2, 1)).reshape(8, 8, 128, S)
    shared = host_shared(inp, layers)
    out, _ = run_layers(hT, inp["p"].astype(np.float32, copy=False), shared, layers)
    y = out.reshape(8, D, S).transpose(0, 2, 1)
    return np.ascontiguousarray(y).astype(np.float32, copy=False)
```

```python
import numpy as np
from contextlib import ExitStack
import concourse.bass as bass
import concourse.mybir as mybir
from concourse.bass_utils import run_bass_kernel_spmd

F32 = mybir.dt.float32
BF16 = mybir.dt.bfloat16
AF = mybir.ActivationFunctionType
ALU = mybir.AluOpType
AX = mybir.AxisListType

D = 1024
S = 2048
T = 512
NT = S // T
FF = 2816
NFC = FF // 128
DEPTH = 4
EPS = 1e-6
NEG = -1e30


class Buf:
    __slots__ = ("name", "last_w", "readers", "dma_readers", "dma_sem", "dma_cnt", "const", "excl")

    fence = {}

    def __init__(self, name, const=False, excl=False):
        self.name = name
        self.excl = excl
        self.last_w = None
        self.readers = dict(Buf.fence)
        self.dma_readers = []
        self.dma_sem = None
        self.dma_cnt = 0
        self.const = const


class SemSlot:
    __slots__ = ("sem", "cnt")

    def __init__(self):
        self.sem = None
        self.cnt = 0


class Op:
    __slots__ = ("eng", "fn", "deps", "sig", "sigval", "is_dma", "dma_sem", "dma_val", "idx")

    def __init__(self, eng, fn, is_dma):
        self.eng = eng
        self.fn = fn
        self.deps = []
        self.sig = False
        self.sigval = 0
        self.is_dma = is_dma
        self.dma_sem = None
        self.dma_val = 0


class Prog:
    ENGS = ("pe", "act", "dve", "pool", "sp")

    def __init__(self, nc):
        self.nc = nc
        self.ops = {e: [] for e in self.ENGS}
        self.n = 0
        self.semslots = {}
        self.out_dma_ops = []

    def _add(self, op, reads, writes):
        deps = []
        for b in reads:
            w = b.last_w
            if w is not None:
                deps.append((w, True))
            if b.excl:
                for e_, r in b.readers.items():
                    if e_ != op.eng:
                        deps.append((r, False))
        for b in writes:
            w = b.last_w
            if w is not None and not (op.is_dma and w.is_dma):
                deps.append((w, False))
            for r in b.readers.values():
                deps.append((r, False))
            for r in b.dma_readers:
                deps.append((r, False))
        seen = set()
        for d, raw in deps:
            if d is op:
                continue
            if (not d.is_dma) and (not op.is_dma) and d.eng == op.eng:
                if op.eng == "pe":
                    continue
            k = id(d)
            if k in seen:
                continue
            seen.add(k)
            op.deps.append(d)
            if not d.is_dma:
                d.sig = True
        for b in writes:
            b.last_w = op
            b.readers = {}
            b.dma_readers = []
        for b in reads:
            if b.const or b in writes:
                continue
            if op.is_dma:
                b.dma_readers.append(op)
            else:
                b.readers[op.eng] = op
        op.idx = self.n
        self.n += 1
        self.ops[op.eng].append(op)
        return op

    @staticmethod
    def _flat(xs):
        out = []
        for x in xs:
            if isinstance(x, (list, tuple)):
                out.extend(Prog._flat(x))
            else:
                out.append(x)
        return out

    def op(self, eng, fn, reads=(), writes=()):
        return self._add(Op(eng, fn, False), self._flat(reads), self._flat(writes))

    def dma(self, eng, fn, reads=(), writes=(), is_output=False):
        o = Op(eng, fn, True)
        reads, writes = self._flat(reads), self._flat(writes)
        dst = writes[0]
        slot = self.semslots.get(dst.name)
        if slot is None:
            slot = self.semslots[dst.name] = SemSlot()
        slot.cnt += 16
        o.dma_sem = slot
        o.dma_val = slot.cnt
        self._add(o, list(reads), list(writes))
        if is_output:
            self.out_dma_ops.append(o)
        return o

    def emit(self, stack):
        nc = self.nc
        sems = {}
        for e in ("pe", "act", "dve", "pool"):
            sems[e] = stack.enter_context(nc.semaphore("s_" + e))
        for i, slot in enumerate(self.semslots.values()):
            slot.sem = stack.enter_context(nc.semaphore("d%d" % i))
        for e in ("pe", "act", "dve", "pool"):
            c = 0
            for o in self.ops[e]:
                if o.is_dma:
                    continue
                if o.sig:
                    c += 1
                    o.sigval = c
        out_ops = self.out_dma_ops

        def run(engh, ename):
            waited = {}

            def wait(sem, val):
                k = id(sem)
                if waited.get(k, 0) >= val:
                    return
                waited[k] = val
                engh.wait_ge(sem, val)

            for o in self.ops[ename]:
                for d in o.deps:
                    if d.is_dma:
                        wait(d.dma_sem.sem, d.dma_val)
                    else:
                        wait(sems[d.eng], d.sigval)
                ins = o.fn(engh)
                if o.is_dma:
                    ins.then_inc(o.dma_sem.sem, 16)
                elif o.sig:
                    ins.then_inc(sems[ename], 1)
            if ename == "sp":
                for o in out_ops:
                    wait(o.dma_sem.sem, o.dma_val)

        block = stack.enter_context(nc.Block())

        @block.tensor
        def _(e):
            run(e, "pe")

        @block.scalar
        def _(e):
            run(e, "act")

        @block.vector
        def _(e):
            run(e, "dve")

        @block.gpsimd
        def _(e):
            run(e, "pool")

        @block.sync
        def _(e):
            run(e, "sp")


def ptab_layout():
    off = {}
    c = 0
    for i in range(DEPTH):
        for nm in ("mix_pre", "mix_post", "ffn_pre", "ffn_post", "ple_g"):
            off[(nm, i)] = c
            c += 8
        off[("conv", i)] = c
        c += 132
    for j in range(2):
        off[("a_lng", j)] = c
        c += 16
        off[("a_lnb", j)] = c
        c += 16
    off["c_dw"] = c
    c += 248
    off["c_dwb"] = c
    c += 8
    off["c_lng"] = c
    c += 8
    off["c_lnb"] = c
    c += 8
    return off, c


class StopBuild(Exception):
    pass


class Stream:
    def __init__(self, B, name, n, free, srcs):
        self.B = B
        self.n = n
        self.srcs = list(srcs)
        self.aps = [B.A.alloc(BF16, *free) for _ in range(n)]
        self.bufs = [Buf("%s%d" % (name, i)) for i in range(n)]
        self.issued = 0
        self.taken = 0
        for _ in range(n - 1):
            self._issue()

    def _issue(self):
        k = self.issued
        if k >= len(self.srcs):
            return
        self.issued += 1
        ap, bf = self.aps[k % self.n], self.bufs[k % self.n]
        flat = ap.rearrange("p a b -> p (a b)") if len(ap.shape) == 3 else ap
        if "fastdma" in self.B.dbg:
            n8 = flat.shape[-1] // 8
            self.B.dma_cast(flat[:, 0:n8], self.srcs[k][:, 0:n8], [], [bf])
            return
        self.B.dma_cast(flat, self.srcs[k], [], [bf])

    def next(self):
        k = self.taken
        self.taken += 1
        self._issue()
        return self.aps[k % self.n], self.bufs[k % self.n]


class Arena:
    def __init__(self, ap_all, base, limit):
        self.all = ap_all
        self.off = base
        self.limit = limit

    def mark(self):
        return self.off

    def reset(self, m):
        self.off = m

    def alloc(self, dtype, *free):
        isz = 4 if dtype == F32 else 2
        n = 1
        for f in free:
            n *= f
        nb = n * isz
        off = (self.off + 63) // 64 * 64
        assert off + nb <= self.limit, ("SBUF arena overflow", off + nb, self.limit)
        self.off = off + nb
        v = self.all[:, off // 2:(off + nb) // 2]
        if dtype == F32:
            v = v.bitcast(F32)
        if len(free) == 2:
            v = v.rearrange("p (a b) -> p a b", a=free[0])
        elif len(free) == 3:
            v = v.rearrange("p (a b c) -> p a b c", a=free[0], b=free[1])
        return v


class Builder:
    def __init__(self, layers, dbg=()):
        self.layers = list(layers)
        self.dbg = set(dbg)
        self.nc = bass.Bass("TRN2", target_bir_lowering=False)
        Buf.fence = {}
        self.P = Prog(self.nc)
        self.poff, self.pcols = ptab_layout()
        self._bank = 0
        self._stat = 0

    def mm(self, out, lhsT, rhs, start, stop, r, w, **kw):
        self.P.op("pe", lambda e: e.matmul(out, lhsT=lhsT, rhs=rhs, start=start, stop=stop, **kw), r, w)

    def act(self, out, in_, func, r, w, bias=None, scale=None, accum=None):
        kw = {}
        if bias is not None:
            kw["bias"] = bias
        if scale is not None:
            kw["scale"] = scale
        if accum is not None:
            kw["accum_out"] = accum
        self.P.op("act", lambda e: e.activation(out=out, in_=in_, func=func, **kw), r, w)

    def ts(self, eng, out, in0, s1, s2, op0, op1, r, w):
        if op1 is None and eng == "pool":
            s2, op1 = 1.0, ALU.mult
        if op1 is None:
            self.P.op(eng, lambda e: e.tensor_scalar(out=out, in0=in0, scalar1=s1, scalar2=None, op0=op0), r, w)
        else:
            self.P.op(eng, lambda e: e.tensor_scalar(out=out, in0=in0, scalar1=s1, scalar2=s2, op0=op0, op1=op1), r, w)

    def stt(self, out, in0, scalar, in1, op0, op1, r, w):
        self.P.op("dve", lambda e: e.scalar_tensor_tensor(out=out, in0=in0, scalar=scalar, in1=in1, op0=op0, op1=op1), r, w)

    def tt(self, eng, out, in0, in1, op, r, w):
        self.P.op(eng, lambda e: e.tensor_tensor(out=out, in0=in0, in1=in1, op=op), r, w)

    def cp(self, eng, out, in_, r, w):
        if eng == "act":
            self.P.op("act", lambda e: e.copy(out=out, in_=in_), r, w)
        else:
            self.P.op(eng, lambda e: e.tensor_copy(out=out, in_=in_), r, w)

    def recip(self, out, in_, r, w):
        self.P.op("dve", lambda e: e.reciprocal(out=out, in_=in_), r, w)

    def memset(self, eng, ap, val, w):
        self.P.op(eng, lambda e: e.memset(ap, val), [], w)

    def dma_cast(self, out, in_, r, w):
        self.P.dma("pool", lambda e: e.dma_start(out=out, in_=in_), r, w)

    def dma_plain(self, out, in_, r, w, is_output=False):
        self.P.dma("sp", lambda e: e.dma_start(out=out, in_=in_), r, w, is_output=is_output)

    def bank(self):
        b = self._bank
        self._bank = (b + 1) % 6
        return b

    def statbank(self):
        b = 6 + self._stat
        self._stat ^= 1
        return b

    def psb(self, b, lo=0, hi=512):
        return self.ps[:, b * 512 + lo:b * 512 + hi]

    def build(self):
        nc = self.nc
        st = ExitStack()
        with st:
            self.declare_dram()
            self.sb_all = st.enter_context(nc.sbuf_tensor("sb_all", [128, 106300], BF16))
            self.ps = st.enter_context(nc.psum_tensor("ps_all", [128, 4096], F32))
            self.PBK = [Buf("psk%d" % i, excl=True) for i in range(32)]
            self.PB = [self.PBK[4 * i:4 * i + 4] for i in range(8)]
            self.A = Arena(self.sb_all, 0, 106300 * 2)
            self.setup_persistent()
            for i in self.layers:
                self.layer(i)
            self.store_output()
            self.P.emit(st)
        return nc

    def declare_dram(self):
        nc = self.nc
        dt = lambda n, s: nc.dram_tensor(n, s, F32, kind="ExternalInput").ap()
        self.d = {}
        self.d["xT"] = dt("xT", [8, 128, S])
        self.d["ptab"] = dt("ptab", [128, self.pcols])
        self.d["ident"] = dt("ident", [128, 128])
        for i in self.layers:
            self.d["pT%d" % i] = dt("pT%d" % i, [2, 128, S])
            self.d["wup%d" % i] = dt("wup%d" % i, [NFC, 128, 8 * 256])
            self.d["wdn%d" % i] = dt("wdn%d" % i, [8, 128, NFC * 128])
            self.d["wg%d" % i] = dt("wg%d" % i, [8, 128, 8 * 128])
            self.d["wp%d" % i] = dt("wp%d" % i, [128, 2 * D])
            kind, j = i % 3, i // 3
            if kind == 0:
                self.d["a_wu%d" % j] = dt("a_wu%d" % j, [16, 128, 8 * 128])
                self.d["a_wv%d" % j] = dt("a_wv%d" % j, [4, 128, 8 * 512])
                self.d["a_wo%d" % j] = dt("a_wo%d" % j, [8, 128, 16 * 128])
                self.d["a_ws%d" % j] = dt("a_ws%d" % j, [128, 8 * 128])
                self.d["a_bs%d" % j] = dt("a_bs%d" % j, [128, 8 * 128])
            elif kind == 1:
                self.d["b_wqk"] = dt("b_wqk", [8, 128, 8 * 256])
                self.d["b_wv"] = dt("b_wv", [4, 128, 8 * 256])
                self.d["b_wo"] = dt("b_wo", [8, 128, 8 * 128])
                self.d["b_bias"] = dt("b_bias", [128, 16 * 640])
            else:
                self.d["c_wi"] = dt("c_wi", [8, 128, 8 * 256])
                self.d["c_wo"] = dt("c_wo", [8, 128, 8 * 128])
        self.yT = nc.dram_tensor("yT", [8, 128, S], F32, kind="ExternalOutput").ap()

    def setup_persistent(self):
        A = self.A
        self.h = A.alloc(F32, 8, S)
        self.Bh = [[Buf("h%d_%d" % (c, t)) for t in range(NT)] for c in range(8)]
        self.ptab = A.alloc(F32, self.pcols)
        self.Bptab = Buf("ptab", const=True)
        self.ident = A.alloc(BF16, 128)
        self.onesb = A.alloc(BF16, 128)
        self.epsc = A.alloc(F32, 1)
        self.dummy = A.alloc(F32, 8)
        self.Bconst = Buf("const", const=True)
        self.sq = [A.alloc(BF16, T) for _ in range(3)]
        self.Bsq = [Buf("sq%d" % i) for i in range(3)]
        self._sq = 0
        self.rstds = [A.alloc(F32, T) for _ in range(3)]
        self.Brstds = [Buf("rstd%d" % k) for k in range(3)]
        self.rstd, self.Brstd = self.rstds[0], self.Brstds[0]
        self.mres = A.alloc(F32, 8, T)
        self.Bmres = [Buf("mres%d" % c) for c in range(8)]
        self.hns = [A.alloc(BF16, 8, T) for _ in range(2)]
        self.hn1_off = A.off - 8 * T * 2
        self.Bhns = [[Buf("hn%d_%d" % (k, c)) for c in range(8)] for k in range(2)]
        self.hn, self.Bhn = self.hns[0], self.Bhns[0]
        self.rtmp = [A.alloc(F32, T) for _ in range(3)]
        self.Brtmp = [Buf("rtmp%d" % i) for i in range(3)]
        self._rt = 0
        self.phase_mark = A.mark()
        self.dma_plain(self.ptab, self.d["ptab"], [], [self.Bptab])
        for c in range(8):
            self.dma_plain(self.h[:, c, :], self.d["xT"][c], [], self.Bh[c])
        self.memset("pool", self.onesb, 1.0, [self.Bconst])
        self.memset("pool", self.epsc, EPS, [self.Bconst])
        self.dma_cast(self.ident, self.d["ident"], [], [self.Bconst])

    def pcol(self, key, c, n=1):
        o = self.poff[key] + c
        return self.ptab[:, o:o + n]

    def nextsq(self):
        i = self._sq
        self._sq = (i + 1) % 3
        return self.sq[i], self.Bsq[i]

    def nextrt(self):
        i = self._rt
        self._rt = (i + 1) % 3
        return self.rtmp[i], self.Brtmp[i]

    def defer_mm(self, *args, **kw):
        self.flush_mm()
        self._pend_mm = (args, kw)

    def flush_mm(self):
        p = getattr(self, "_pend_mm", None)
        if p is not None:
            self._pend_mm = None
            self.mm(*p[0], **p[1])

    def stat_add(self, sb, src, src_bufs, c, n=8):
        sq, Bsq = self.nextsq()
        self.act(sq, src, AF.Square, src_bufs, [Bsq])
        self.defer_mm(self.psb(sb), self.onesb, sq, c == 0, c == n - 1, [Bsq, self.Bconst], [self.PB[sb]])

    def finish_rstd(self, sb, dim=D, role=0):
        self.flush_mm()
        rstd, Brstd = self.rstds[role], self.Brstds[role]
        self.act(rstd, self.psb(sb), AF.Sqrt, [self.PB[sb], self.Bconst], [Brstd], bias=self.epsc, scale=1.0 / dim)
        self.recip(rstd, rstd, [Brstd], [Brstd])
        return rstd, Brstd

    def pre_norm_gen(self, t, gkey, k=0):
        hn, Bhn = self.hns[k], self.Bhns[k]
        sb = self.statbank()
        tsl = slice(t * T, (t + 1) * T)
        for c in range(8):
            self.stat_add(sb, self.h[:, c, tsl], [self.Bh[c][t]], c)
            yield
        rstd, Brstd = self.finish_rstd(sb, role=0)
        yield
        for c in range(8):
            self.stt(hn[:, c, :], self.h[:, c, tsl], self.pcol(gkey, c), rstd, ALU.mult, ALU.mult,
                     [self.Bh[c][t], Brstd, self.Bptab], [Bhn[c]])
            if c % 2:
                yield

    def pre_norm(self, t, gkey, k=0):
        for _ in self.pre_norm_gen(t, gkey, k):
            pass

    def post_norm_gen(self, t, gkey, sb, role, after=None):
        rstd, Brstd = self.finish_rstd(sb, role=role)
        tsl = slice(t * T, (t + 1) * T)
        yield
        rts = {}
        for i in range(10):
            if i < 8:
                rt, Brt = self.nextrt()
                rts[i] = (rt, Brt)
                self.stt(rt, self.mres[:, i, :], self.pcol(gkey, i), rstd, ALU.mult, ALU.mult,
                         [self.Bmres[i], Brstd, self.Bptab], [Brt])
            if 1 <= i < 9:
                c = i - 1
                rt, Brt = rts.pop(c)
                self.tt("pool", self.h[:, c, tsl], self.h[:, c, tsl], rt, ALU.add, [self.Bh[c][t], Brt], [self.Bh[c][t]])
            if 2 <= i < 10 and after is not None:
                after(i - 2)
            yield

    def bg_start(self, t, gkey, sb, role=1):
        g = self.post_norm_gen(t, gkey, sb, role)
        next(g)
        self._bg = getattr(self, "_bg", [])
        self._bg.append(g)

    def bg_step(self):
        for g in list(getattr(self, "_bg", [])):
            try:
                next(g)
            except StopIteration:
                self._bg.remove(g)

    def bg_drain(self):
        while getattr(self, "_bg", []):
            self.bg_step()

    def post_norm_residual(self, t, gkey, sb, role=1):
        for _ in self.post_norm_gen(t, gkey, sb, role):
            pass

    def evac_mres(self, b, dc, sb):
        self.cp("dve", self.mres[:, dc, :], self.psb(b), [self.PB[b]], [self.Bmres[dc]])
        self.stat_add(sb, self.mres[:, dc, :], [self.Bmres[dc]], dc)

    def make_slots(self, name, n, *free):
        aps = [self.A.alloc(BF16, *free) for _ in range(n)]
        bufs = [Buf("%s%d" % (name, i)) for i in range(n)]
        return {"aps": aps, "bufs": bufs, "i": 0, "n": n}

    def load_slot(self, slots, src):
        i = slots["i"]
        slots["i"] = (i + 1) % slots["n"]
        ap, bf = slots["aps"][i], slots["bufs"][i]
        flat = ap
        if len(ap.shape) == 3:
            flat = ap.rearrange("p a b -> p (a b)")
        self.dma_cast(flat, src, [], [bf])
        return ap, bf

    def stream(self, name, n, free, srcs):
        return Stream(self, name, n, free, srcs)

    def stop(self, tag):
        if tag in self.dbg:
            raise StopBuild()

    def layer(self, i):
        try:
            self._layer(i)
        except StopBuild:
            pass

    def new_phase(self):
        self.bg_drain()
        self.flush_mm()
        self.A.reset(self.phase_mark)
        f = {}
        for e in ("pe", "act", "dve", "pool"):
            for o in reversed(self.P.ops[e]):
                if not o.is_dma:
                    f[e] = o
                    break
        Buf.fence = f

    def _layer(self, i):
        kind, j = i % 3, i // 3
        A = self.A
        self.new_phase()
        if "prenorm" in self.dbg:
            self.pre_norm(0, ("mix_pre", i))
            return
        if "nomix" in self.dbg:
            pass
        elif kind == 0:
            self.mixer_a(i, j)
        elif kind == 1:
            self.mixer_b(i)
        else:
            self.mixer_c(i)
        self.new_phase()
        if "noffn" not in self.dbg:
            self.ffn_phase(i)

    def ffn_phase(self, i):
        A = self.A
        actb = A.alloc(BF16, NFC, T)
        Bact = [Buf("act%d" % f) for f in range(NFC)]
        wupS = self.stream("wup", 4, (8, 256), [self.d["wup%d" % i][fc] for _t in range(NT) for fc in range(NFC)])
        HF = NFC // 2
        wdnS = self.stream("wdn", 5, (HF, 128), [self.d["wdn%d" % i][dc][:, hf * HF * 128:(hf + 1) * HF * 128]
                                                  for _t in range(NT) for dc in range(8) for hf in range(2)])
        wgS = self.stream("wg", 4, (8, 128), [self.d["wg%d" % i][dc] for _t in range(NT) for dc in range(8)])
        cg = [A.alloc(F32, T) for _ in range(2)]
        cv = [A.alloc(F32, T) for _ in range(2)]
        sg = [A.alloc(F32, T) for _ in range(2)]
        Bcg = [Buf("cg%d" % k) for k in range(2)]
        Bcv = [Buf("cv%d" % k) for k in range(2)]
        Bsg = [Buf("sg%d" % k) for k in range(2)]
        halo = [A.alloc(F32, 2 * NFC, 2) for _ in range(2)]
        Bhalo = [[Buf("halo%d_%d" % (k, q)) for q in range(2 * NFC)] for k in range(2)]
        bnd = [A.alloc(F32, 3, 2 * NFC) for _ in range(2)]
        Bbnd = [Buf("bnd%d" % k) for k in range(2)]
        hb = A.alloc(BF16, 8, T)
        Bhb = [Buf("hb%d" % c) for c in range(8)]
        pt = [A.alloc(BF16, 2, T) for _ in range(2)]
        Bpt = [Buf("pt%d" % k) for k in range(2)]
        wp = A.alloc(BF16, 2, D)
        Bwp = Buf("wp")
        gate = [A.alloc(F32, T) for _ in range(1)]
        Bgate = [Buf("gate%d" % k) for k in range(1)]
        self.dma_cast(wp.rearrange("p a b -> p (a b)"), self.d["wp%d" % i], [], [Bwp])
        cbase = self.poff[("conv", i)]

        def tap(jj, q):
            o = cbase + jj * 44 + q
            return self.ptab[:, o:o + 1]

        def step(bg):
            for g in list(bg):
                try:
                    next(g)
                except StopIteration:
                    bg.remove(g)

        def drain(bg):
            while bg:
                step(bg)

        def load_pt(t):
            ptt, Bptt = pt[t % 2], Bpt[t % 2]
            tsl = slice(t * T, (t + 1) * T)
            for kc in range(2):
                self.dma_cast(ptt[:, kc, :], self.d["pT%d" % i][kc][:, tsl], [], [Bptt])

        def stage_A(t):
            return self.pre_norm_gen(t, ("ffn_pre", i), k=t % 2)

        def stage_B(t, bg):
            hn, Bhn = self.hns[t % 2], self.Bhns[t % 2]
            if t > 0:
                ho, Bho = halo[(t - 1) % 2], Bhalo[(t - 1) % 2]
                bd, Bbd = bnd[t % 2], Bbnd[t % 2]
                W0 = self.ptab[:, cbase:cbase + 44]
                W1 = self.ptab[:, cbase + 44:cbase + 88]
                self.tt("dve", bd[:, 2, :], ho[:, :, 1], W1, ALU.mult, Bho + [self.Bptab], [Bbd])
                self.tt("dve", bd[:, 0, :], ho[:, :, 0], W0, ALU.mult, Bho + [self.Bptab], [Bbd])
                self.tt("dve", bd[:, 0, :], bd[:, 0, :], bd[:, 2, :], ALU.add, [Bbd], [Bbd])
                self.tt("dve", bd[:, 1, :], ho[:, :, 1], W0, ALU.mult, Bho + [self.Bptab], [Bbd])
            for fc in range(NFC):
                w, Bw = wupS.next()
                k2 = fc % 2
                bg_ = self.bank()
                bv_ = self.bank()
                for kc in range(8):
                    self.mm(self.psb(bg_), w[:, kc, 0:128], hn[:, kc, :], kc == 0, kc == 7, [Bw, Bhn[kc]], [self.PB[bg_]])
                for kc in range(8):
                    self.mm(self.psb(bv_), w[:, kc, 128:256], hn[:, kc, :], kc == 0, kc == 7, [Bw, Bhn[kc]], [self.PB[bv_]])
                for (b, q, cbuf, Bc) in ((bg_, fc, cg[k2], Bcg[k2]), (bv_, NFC + fc, cv[k2], Bcv[k2])):
                    pb = self.PB[b]
                    if t == 0:
                        self.act(cbuf, self.psb(b), AF.Copy, [pb, self.Bptab], [Bc], scale=tap(2, q))
                    else:
                        bd, Bbd = bnd[t % 2], Bbnd[t % 2]
                        self.act(cbuf[:, 2:T], self.psb(b, 2, T), AF.Copy, [pb, self.Bptab], [Bc], scale=tap(2, q))
                        self.act(cbuf[:, 0:1], self.psb(b, 0, 1), AF.Identity, [pb, self.Bptab, Bbd], [Bc],
                                 scale=tap(2, q), bias=bd[:, 0, q:q + 1])
                        self.act(cbuf[:, 1:2], self.psb(b, 1, 2), AF.Identity, [pb, self.Bptab, Bbd], [Bc],
                                 scale=tap(2, q), bias=bd[:, 1, q:q + 1])
                    if t < NT - 1:
                        self.cp("act", halo[t % 2][:, q, 0:2], self.psb(b, T - 2, T), [pb], [Bhalo[t % 2][q]])
                    self.stt(cbuf[:, 1:T], self.psb(b, 0, T - 1), tap(1, q), cbuf[:, 1:T], ALU.mult, ALU.add,
                             [pb, Bc, self.Bptab], [Bc])
                    self.stt(cbuf[:, 2:T], self.psb(b, 0, T - 2), tap(0, q), cbuf[:, 2:T], ALU.mult, ALU.add,
                             [pb, Bc, self.Bptab], [Bc])
                self.act(sg[k2], cg[k2], AF.Silu, [Bcg[k2]], [Bsg[k2]])
                self.tt("pool", actb[:, fc, :], sg[k2], cv[k2], ALU.mult, [Bsg[k2], Bcv[k2]], [Bact[fc]])
                step(bg)
            drain(bg)

        def stage_C(t, bg):
            sb = self.statbank()
            for dc in range(8):
                b = self.bank()
                for hf in range(2):
                    w, Bw = wdnS.next()
                    for f2 in range(HF):
                        fc = hf * HF + f2
                        self.mm(self.psb(b), w[:, f2, :], actb[:, fc, :], fc == 0, fc == NFC - 1, [Bw, Bact[fc]], [self.PB[b]])
                step(bg)
                self.evac_mres(b, dc, sb)
            drain(bg)
            return sb

        def stage_D(t, sb):
            tsl = slice(t * T, (t + 1) * T)

            def after(c):
                self.cp("act", hb[:, c, :], self.h[:, c, tsl], [self.Bh[c][t]], [Bhb[c]])
            g = self.post_norm_gen(t, ("ffn_post", i), sb, 1, after=after)
            next(g)
            return g

        def stage_E(t):
            ptt, Bptt = pt[t % 2], Bpt[t % 2]
            if t + 1 < NT:
                load_pt(t + 1)
            sb = self.statbank()
            for dc in range(8):
                w, Bw = wgS.next()
                bgt = self.bank()
                be = self.bank()
                for kc in range(8):
                    self.mm(self.psb(bgt), w[:, kc, :], hb[:, kc, :], kc == 0, kc == 7, [Bw, Bhb[kc]], [self.PB[bgt]])
                for kc in range(2):
                    self.mm(self.psb(be), wp[:, kc, dc * 128:(dc + 1) * 128], ptt[:, kc, :], kc == 0, kc == 1,
                            [Bwp, Bptt], [self.PB[be]])
                k2 = 0
                self.act(gate[k2], self.psb(bgt), AF.Sigmoid, [self.PB[bgt]], [Bgate[k2]])
                self.tt("dve", self.mres[:, dc, :], gate[k2], self.psb(be), ALU.mult, [Bgate[k2], self.PB[be]],
                        [self.Bmres[dc]])
                self.stat_add(sb, self.mres[:, dc, :], [self.Bmres[dc]], dc)
            return sb

        load_pt(0)
        drain([stage_A(0)])
        stage_B(0, [stage_A(1)])
        pend = []
        for t in range(NT):
            sb = stage_C(t, pend)
            pend = []
            bgl = [stage_D(t, sb)]
            if t + 2 < NT:
                bgl.append(stage_A(t + 2))
            if t + 1 < NT:
                stage_B(t + 1, bgl)
            else:
                drain(bgl)
            sbe = stage_E(t)
            g = self.post_norm_gen(t, ("ple_g", i), sbe, 2)
            next(g)
            pend = [g]
        drain(pend)
        self.flush_mm()

    def mixer_c(self, i):
        A = self.A
        ybuf = A.alloc(BF16, 8, 30 + T)
        Byb = [Buf("yb%d" % c) for c in range(8)]
        z = A.alloc(F32, 8, T)
        Bz = [Buf("z%d" % c) for c in range(8)]
        zb = [A.alloc(BF16, T) for _ in range(2)]
        Bzb = [Buf("zb%d" % k) for k in range(2)]
        actc = A.alloc(BF16, 8, T)
        Bac = [Buf("actc%d" % c) for c in range(8)]
        dg = [A.alloc(BF16, 31, 128) for _ in range(2)]
        Bdg = [Buf("dg%d" % k) for k in range(2)]
        wi = self.stream("cwi", 3, (8, 256), [self.d["c_wi"][c] for _t in range(NT) for c in range(8)])
        wo = self.stream("cwo", 3, (8, 128), [self.d["c_wo"][c] for _t in range(NT) for c in range(8)])
        sgm = [A.alloc(F32, T) for _ in range(2)]
        Bsgm = [Buf("sgm%d" % k) for k in range(2)]
        mean = A.alloc(F32, T)
        msq = A.alloc(F32, T)
        var = A.alloc(F32, T)
        nmr = A.alloc(F32, T)
        Bmean, Bmsq, Bvar, Bnmr = Buf("mean"), Buf("msq"), Buf("var"), Buf("nmr")
        t1 = [A.alloc(F32, T) for _ in range(2)]
        Bt1 = [Buf("ct1_%d" % k) for k in range(2)]
        for c in range(8):
            self.memset("pool", ybuf[:, c, 0:30], 0.0, [Byb[c]])

        def build_diag(c):
            k2_ = c % 2
            o_ = self.poff["c_dw"] + c * 31
            in1 = self.ptab[:, o_:o_ + 31].unsqueeze(2).to_broadcast([128, 31, 128])
            in0 = self.ident.unsqueeze(1).to_broadcast([128, 31, 128])
            self.tt("dve", dg[k2_], in0, in1, ALU.mult, [self.Bconst, self.Bptab], [Bdg[k2_]])

        self.pre_norm(0, ("mix_pre", i), k=0)
        for t in range(NT):
            hn, Bhn = self.hns[t % 2], self.Bhns[t % 2]
            sb1 = self.statbank()
            sb2 = self.statbank()
            if t == 0:
                build_diag(0)
            for c in range(8):
                if c + 1 < 8:
                    build_diag(c + 1)
                self.bg_step()
                w, Bw = wi.next()
                ba = self.bank()
                bg = self.bank()
                for kc in range(8):
                    self.mm(self.psb(ba), w[:, kc, 0:128], hn[:, kc, :], kc == 0, kc == 7, [Bw, Bhn[kc]], [self.PB[ba]])
                for kc in range(8):
                    self.mm(self.psb(bg), w[:, kc, 128:256], hn[:, kc, :], kc == 0, kc == 7, [Bw, Bhn[kc]], [self.PB[bg]])
                k2 = c % 2
                self.act(sgm[k2], self.psb(bg), AF.Sigmoid, [self.PB[bg]], [Bsgm[k2]])
                self.tt("dve", ybuf[:, c, 30:30 + T], sgm[k2], self.psb(ba), ALU.mult, [Bsgm[k2], self.PB[ba]], [Byb[c]])
                bc = self.bank()
                for jj in range(31):
                    self.mm(self.psb(bc), dg[k2][:, jj, :], ybuf[:, c, jj:jj + T], jj == 0, jj == 30, [Bdg[k2], Byb[c]], [self.PB[bc]])
                bias = self.pcol("c_dwb", c)
                self.act(z[:, c, :], self.psb(bc), AF.Identity, [self.PB[bc], self.Bptab], [Bz[c]], bias=bias)
                self.act(zb[k2], self.psb(bc), AF.Identity, [self.PB[bc], self.Bptab], [Bzb[k2]], bias=bias)
                self.flush_mm()
                self.mm(self.psb(sb1), self.onesb, zb[k2], c == 0, c == 7, [Bzb[k2], self.Bconst], [self.PB[sb1]])
                sq, Bsq = self.nextsq()
                self.act(sq, self.psb(bc), AF.Square, [self.PB[bc], self.Bptab], [Bsq], bias=bias)
                self.defer_mm(self.psb(sb2), self.onesb, sq, c == 0, c == 7, [Bsq, self.Bconst], [self.PB[sb2]])
                if t < NT - 1:
                    self.cp("pool", ybuf[:, c, 0:30], ybuf[:, c, T:T + 30], [Byb[c]], [Byb[c]])
            self.bg_drain()
            if t + 1 < NT:
                build_diag(0)
            self.flush_mm()
            self.ts("dve", mean, self.psb(sb1), 1.0 / D, None, ALU.mult, None, [self.PB[sb1]], [Bmean])
            self.act(msq, mean, AF.Square, [Bmean], [Bmsq])
            self.stt(var, self.psb(sb2), 1.0 / D, msq, ALU.mult, ALU.subtract, [self.PB[sb2], Bmsq], [Bvar])
            self.act(self.rstd, var, AF.Sqrt, [Bvar, self.Bconst], [self.Brstd], bias=self.epsc)
            self.recip(self.rstd, self.rstd, [self.Brstd], [self.Brstd])
            self.stt(nmr, mean, -1.0, self.rstd, ALU.mult, ALU.mult, [Bmean, self.Brstd], [Bnmr])
            for c in range(8):
                k2 = c % 2
                self.tt("dve", t1[k2], z[:, c, :], self.rstd, ALU.mult, [Bz[c], self.Brstd], [Bt1[k2]])
                self.tt("pool", t1[k2], t1[k2], nmr, ALU.add, [Bt1[k2], Bnmr], [Bt1[k2]])
                self.act(actc[:, c, :], t1[k2], AF.Silu, [Bt1[k2], self.Bptab], [Bac[c]],
                         scale=self.pcol("c_lng", c), bias=self.pcol("c_lnb", c))
            if t + 1 < NT:
                self.pre_norm(t + 1, ("mix_pre", i), k=(t + 1) % 2)
            sb = self.statbank()
            for dc in range(8):
                w, Bw = wo.next()
                b = self.bank()
                for c in range(8):
                    self.mm(self.psb(b), w[:, c, :], actc[:, c, :], c == 0, c == 7, [Bw, Bac[c]], [self.PB[b]])
                self.evac_mres(b, dc, sb)
            self.bg_start(t, ("mix_post", i), sb)
            if t == NT - 1:
                self.bg_drain()

    def mixer_a(self, i, j):
        A = self.A
        u = A.alloc(BF16, 16, T)
        Bu = [Buf("u%d" % c) for c in range(16)]
        vgb = A.alloc(BF16, 4, 2048)
        Bvgb = [Buf("vgb%d" % b) for b in range(4)]
        wv = self.stream("awv", 3, (8, 512), [self.d["a_wv%d" % j][q] for _t in range(NT) for q in range(4)])
        wu = self.stream("awu", 3, (8, 128), [self.d["a_wu%d" % j][q] for _t in range(NT) for q in range(16)])
        wo = self.stream("awo", 2, (16, 128), [self.d["a_wo%d" % j][q] for _t in range(NT) for q in range(8)])
        bst = A.alloc(F32, 4, 4, 6)
        Bbst = [Buf("bst%d" % b) for b in range(4)]
        mv = A.alloc(F32, 4, 2)
        rs = A.alloc(F32, 4, 1)
        vpe = A.alloc(F32, 4, 1)
        Bmv = [Buf("mv%d" % b) for b in range(4)]
        Brs = [Buf("rs%d" % b) for b in range(4)]
        Bvpe = [Buf("vpe%d" % b) for b in range(4)]
        nmh = A.alloc(F32, 1)
        wsT = A.alloc(BF16, 8, 128)
        Bws = Buf("wsT")
        Cc = A.alloc(F32, 16, 128)
        BCc = Buf("Cc")
        bsb = A.alloc(F32, 8, 128)
        Bbsb = Buf("bsb")
        rw = A.alloc(F32, 8, 128)
        Brw = Buf("rw")
        t1 = [A.alloc(F32, 4, 128) for _ in range(1)]
        Bt1 = [Buf("at1_%d" % k) for k in range(1)]
        self.memset("pool", nmh, -0.5, [self.Bconst])
        self.dma_cast(wsT.rearrange("p a b -> p (a b)"), self.d["a_ws%d" % j], [], [Bws])
        self.memset("pool", wsT[64:128, :, 0:64], 0.0, [Bws])
        self.dma_plain(bsb.rearrange("p a b -> p (a b)"), self.d["a_bs%d" % j], [], [Bbsb])
        b0 = self.bank()
        b1 = self.bank()
        for g in range(8):
            bb = b0 if g < 4 else b1
            lo = (g % 4) * 128
            self.mm(self.psb(bb, lo, lo + 128), self.onesb, wsT[:, g, :], True, True, [Bws, self.Bconst], [self.PB[bb]])
        self.cp("act", rw[:, 0:4, :].rearrange("p a b -> p (a b)"), self.psb(b0), [self.PB[b0]], [Brw])
        self.cp("act", rw[:, 4:8, :].rearrange("p a b -> p (a b)"), self.psb(b1), [self.PB[b1]], [Brw])
        for uc in range(16):
            g = uc // 2
            self.stt(Cc[:, uc, :], rw[:, g, :], self.pcol(("a_lnb", j), uc), bsb[:, g, :], ALU.mult, ALU.add,
                     [Brw, Bbsb, self.Bptab], [BCc])
        self.pre_norm(0, ("mix_pre", i), k=0)
        for t in range(NT):
            hn, Bhn = self.hns[t % 2], self.Bhns[t % 2]
            for vq in range(4):
                w, Bw = wv.next()
                for blk in range(4):
                    b = self.bank()
                    for kc in range(8):
                        self.mm(self.psb(b), hn[:, kc, blk * 128:(blk + 1) * 128], w[:, kc, :], kc == 0, kc == 7,
                                [Bw, Bhn[kc]], [self.PB[b]])
                    dst = vgb[:, blk, vq * 512:(vq + 1) * 512]
                    self.act(dst, self.psb(b), AF.Gelu_apprx_tanh, [self.PB[b]], [Bvgb[blk]])
                    bo = bst[:, blk, vq, :]
                    self.P.op("dve", lambda e, bo=bo, src=dst: e.bn_stats(out=bo, in_=src), [Bvgb[blk]], [Bbst[blk]])
                    self.bg_step()
            self.bg_drain()
            for blk in range(4):
                mo = mv[:, blk, :]
                bi = bst[:, blk, :, :]
                self.P.op("dve", lambda e, mo=mo, bi=bi: e.bn_aggr(out=mo, in_=bi), [Bbst[blk]], [Bmv[blk]])
                self.ts("pool", vpe[:, blk, :], mv[:, blk, 1:2], EPS, None, ALU.add, None, [Bmv[blk]], [Bvpe[blk]])
                self.tt("pool", rs[:, blk, :], vpe[:, blk, :], nmh, ALU.pow, [Bvpe[blk], self.Bconst], [Brs[blk]])
                self.ts("dve", vgb[:, blk, :], vgb[:, blk, :], mv[:, blk, 0:1], rs[:, blk, :], ALU.subtract, ALU.mult,
                        [Bvgb[blk], Bmv[blk], Brs[blk]], [Bvgb[blk]])
            kk = 0
            for u4 in range(4):
                for j4 in range(4):
                    uc = u4 * 4 + j4
                    w, Bw = wu.next()
                    b = self.bank()
                    for kc in range(8):
                        self.mm(self.psb(b), w[:, kc, :], hn[:, kc, :], kc == 0, kc == 7, [Bw, Bhn[kc]], [self.PB[b]])
                    self.act(u[:, uc, :], self.psb(b), AF.Gelu_apprx_tanh, [self.PB[b]], [Bu[uc]])
                for blk in range(4):
                    bsl = slice(blk * 128, (blk + 1) * 128)
                    b = self.bank()
                    for j4 in range(4):
                        uc = u4 * 4 + j4
                        self.mm(self.psb(b, j4 * 128, (j4 + 1) * 128), vgb[:, blk, uc * 128:(uc + 1) * 128], wsT[:, uc // 2, :], True, True,
                                [Bvgb[blk], Bws], [self.PB[b]])
                    k3 = 0
                    kk += 1
                    for j4 in range(4):
                        uc = u4 * 4 + j4
                        self.stt(t1[k3][:, j4, :], self.psb(b, j4 * 128, (j4 + 1) * 128), self.pcol(("a_lng", j), uc), Cc[:, uc, :],
                                 ALU.mult, ALU.add, [self.PB[b], BCc, self.Bptab], [Bt1[k3]])
                    uv = u[:, u4 * 4:(u4 + 1) * 4, bsl]
                    self.tt("dve", uv, t1[k3], uv, ALU.mult, [Bt1[k3]] + Bu[u4 * 4:(u4 + 1) * 4], Bu[u4 * 4:(u4 + 1) * 4])
            if t + 1 < NT:
                self.pre_norm(t + 1, ("mix_pre", i), k=(t + 1) % 2)
            sb = self.statbank()
            for dc in range(8):
                w, Bw = wo.next()
                b = self.bank()
                for uc in range(16):
                    self.mm(self.psb(b), w[:, uc, :], u[:, uc, :], uc == 0, uc == 15, [Bw, Bu[uc]], [self.PB[b]])
                self.evac_mres(b, dc, sb)
            self.bg_start(t, ("mix_post", i), sb)
            if t == NT - 1:
                self.bg_drain()

    def mixer_b(self, i):
        A = self.A
        kT = A.alloc(BF16, 8, 1024)
        BkT = [[Buf("kT%d_%d" % (c, s_)) for s_ in range(2)] for c in range(8)]
        V = A.alloc(BF16, 8, 1024)
        BV = [Buf("V%d" % b) for b in range(8)]
        qz = A.alloc(BF16, 8, 2, T)
        Bq = [Buf("qz%d" % c) for c in range(8)]
        Bb = A.alloc(BF16, 16, 640)
        BBb = Buf("Bb")
        A2 = Arena(self.sb_all, self.hn1_off, self.hn1_off + 8 * T * 2)
        Pb = [A2.alloc(BF16, 640) for _ in range(3)]
        BPb = [Buf("Pb%d" % k) for k in range(3)]
        PTb = [A2.alloc(BF16, 640) for _ in range(3)]
        BPTb = [Buf("PTb%d" % k) for k in range(3)]
        dgr = [A.alloc(BF16, 128) for _ in range(3)]
        Bdgr = [Buf("dgr%d" % k) for k in range(3)]
        st3 = [A.alloc(F32, 4) for _ in range(4)]
        Bnm = [Buf("nm%d" % k) for k in range(4)]
        Brs = [Buf("rsum%d" % k) for k in range(4)]
        Bri = [Buf("rinv%d" % k) for k in range(4)]
        wqk = self.stream("bwqk", 2, (8, 256), [self.d["b_wqk"][q] for _t in range(NT) for q in range(8)])
        wvs = self.stream("bwv", 2, (8, 256), [self.d["b_wv"][q] for _t in range(NT) for q in range(4)])
        wos = self.stream("bwo", 2, (8, 128), [self.d["b_wo"][q] for _t in range(NT) for q in range(8)])
        oT, BoT = self.hn, self.Bhn
        ps = self.ps
        self.dma_cast(Bb.rearrange("p a b -> p (a b)"), self.d["b_bias"], [], [BBb])
        self.memset("pool", Bb[64:128, :, 0:64], NEG, [BBb])
        self.memset("pool", Bb[0:64, :, 576:640], NEG, [BBb])
        for c in range(8):
            self.memset("pool", qz[:, c, :, :], 0.0, [Bq[c]])
        unit = 0
        for t in range(NT):
            slot = t % 2
            self.pre_norm(t, ("mix_pre", i))
            for c in range(8):
                w, Bw = wqk.next()
                bq = self.bank()
                bk = self.bank()
                for kc in range(8):
                    self.mm(self.psb(bq), w[:, kc, 0:128], self.hn[:, kc, :], kc == 0, kc == 7, [Bw, self.Bhn[kc]], [self.PB[bq]])
                for kc in range(8):
                    self.mm(self.psb(bk), w[:, kc, 128:256], self.hn[:, kc, :], kc == 0, kc == 7, [Bw, self.Bhn[kc]], [self.PB[bk]])
                for hh in range(2):
                    rows = slice(hh * 64, hh * 64 + 64)
                    self.act(qz[rows, c, hh, :], ps[rows, bq * 512:(bq + 1) * 512], AF.Copy, [self.PB[bq]], [Bq[c]], scale=0.125)
                self.cp("dve", kT[:, c, slot * 512:(slot + 1) * 512], self.psb(bk), [self.PB[bk]], [BkT[c][slot]])
                self.bg_step()
            for qt in range(4):
                w, Bw = wvs.next()
                for blk in range(4):
                    b = self.bank()
                    for kc in range(8):
                        self.mm(self.psb(b, 0, 256), self.hn[:, kc, blk * 128:(blk + 1) * 128], w[:, kc, :], kc == 0, kc == 7,
                                [Bw, self.Bhn[kc]], [self.PB[b]])
                    rb = (4 * t + blk) % 8
                    self.cp("act" if (blk % 2) else "dve", V[:, rb, qt * 256:(qt + 1) * 256], self.psb(b, 0, 256), [self.PB[b]], [BV[rb]])
            self.bg_drain()
            units = [(c, jq, hh) for c in range(8) for jq in range(4) for hh in range(2)]
            NU = len(units)

            def geom(u):
                c, jq, hh = units[u]
                jb = 4 * t + jq
                return c, jq, hh, jb, max(0, 4 - jb)

            def st_scores(u):
                c, jq, hh, jb, i0 = geom(u)
                hd = 2 * c + hh
                sl = u % 2
                sbase = sl * 1024
                groups = []
                ii = i0
                while ii < 5:
                    n = 1
                    while (ii + n < 5 and (ii + n) != 4 and ((jb - 4 + ii + n) % 8) == ((jb - 4 + ii) % 8) + n):
                        n += 1
                    groups.append((ii, n))
                    ii += n
                started = set()
                for (ii, n) in groups:
                    kb = jb - 4 + ii
                    rc = (kb % 8) * 128
                    ksl = sorted(set(((kb + m) // 4) % 2 for m in range(n)))
                    sap = ps[:, sbase + ii * 128: sbase + (ii + n) * 128]
                    bk = 0 if ii < 4 else 1
                    wb = self.PB[2 * sl + bk]
                    self.mm(sap, qz[:, c, hh, jq * 128:(jq + 1) * 128], kT[:, c, rc:rc + n * 128], bk not in started, False,
                            [Bq[c]] + [BkT[c][k_] for k_ in ksl], wb)
                    started.add(bk)
                if i0 < 4:
                    self.mm(ps[:, sbase + i0 * 128: sbase + 512], self.ident, Bb[:, hd, i0 * 128:512], False, True,
                            [self.Bconst, BBb], self.PB[2 * sl])
                self.mm(ps[:, sbase + 512: sbase + 640], self.ident, Bb[:, hd, 512:640], False, True,
                        [self.Bconst, BBb], self.PB[2 * sl + 1])
                sbufs = [self.PB[2 * sl], self.PB[2 * sl + 1]]
                sfull = ps[:, sbase + i0 * 128: sbase + 640]
                k4 = u % 4
                nmax = st3[k4][:, 0:1]
                self.P.op("dve", lambda e, nmax=nmax, sfull=sfull: e.tensor_reduce(out=nmax, in_=sfull, axis=AX.X, op=ALU.max, negate=True),
                          sbufs, [Bnm[k4]])

            def st_exp(u):
                c, jq, hh, jb, i0 = geom(u)
                sl = u % 2
                k4 = u % 4
                sbase = sl * 1024
                sbufs = [self.PB[2 * sl], self.PB[2 * sl + 1]]
                sfull = ps[:, sbase + i0 * 128: sbase + 640]
                nmax, rsum = st3[k4][:, 0:1], st3[k4][:, 1:2]
                self.act(Pb[u % 3][:, i0 * 128:640], sfull, AF.Exp, sbufs + [Bnm[k4]], [BPb[u % 3], Brs[k4]], bias=nmax, accum=rsum)

            def st_rd(u):
                k4 = u % 4
                k3 = u % 3
                rsum, rinv = st3[k4][:, 1:2], st3[k4][:, 2:3]
                self.recip(rinv, rsum, [Brs[k4]], [Bri[k4]])
                self.ts("pool", dgr[k3], self.ident, rinv, None, ALU.mult, None, [self.Bconst, Bri[k4]], [Bdgr[k3]])

            def st_pt(u):
                c, jq, hh, jb, i0 = geom(u)
                sl = u % 2
                k3 = u % 3
                pbase = 2048
                for ii in range(i0, 5):
                    self.mm(ps[:, pbase + ii * 128: pbase + (ii + 1) * 128], Pb[k3][:, ii * 128:(ii + 1) * 128], dgr[k3], True, True,
                            [BPb[k3], Bdgr[k3]], [self.PB[4] if ii < 4 else self.PB[5]])
                self.cp("act" if (u % 2) else "dve", PTb[k3][:, i0 * 128:640], ps[:, pbase + i0 * 128: pbase + 640],
                        [self.PB[4], self.PB[5]], [BPTb[k3]])

            def st_pv(u):
                c, jq, hh, jb, i0 = geom(u)
                k3 = u % 3
                ob = 6 + hh
                for ii in range(i0, 5):
                    kb = jb - 4 + ii
                    self.mm(ps[:, ob * 512 + jq * 128: ob * 512 + (jq + 1) * 128], V[:, kb % 8, c * 128:(c + 1) * 128],
                            PTb[k3][:, ii * 128:(ii + 1) * 128], ii == i0, ii == 4, [BV[kb % 8], BPTb[k3]], [self.PB[ob]])
                if jq == 3 and hh == 1:
                    for h2 in range(2):
                        rows = slice(h2 * 64, h2 * 64 + 64)
                        o2 = 6 + h2
                        self.cp("dve" if h2 else "act", oT[rows, c, :], ps[rows, o2 * 512:(o2 + 1) * 512], self.PB[o2], [BoT[c]])

            for u in range(NU + 4):
                if u < NU:
                    st_scores(u)
                if 0 <= u - 1 < NU:
                    st_exp(u - 1)
                if 0 <= u - 2 < NU:
                    st_rd(u - 2)
                if 0 <= u - 3 < NU:
                    st_pt(u - 3)
                if 0 <= u - 4 < NU:
                    st_pv(u - 4)
            sb = self.statbank()
            for dc in range(8):
                w, Bw = wos.next()
                b = self.bank()
                for c in range(8):
                    self.mm(self.psb(b), w[:, c, :], oT[:, c, :], c == 0, c == 7, [Bw, BoT[c]], [self.PB[b]])
                self.evac_mres(b, dc, sb)
            self.bg_start(t, ("mix_post", i), sb)
            if t == NT - 1:
                self.bg_drain()

    def store_output(self):
        for c in range(8):
            self.dma_plain(self.yT[c], self.h[:, c, :], self.Bh[c], [Buf("y%d" % c)], is_output=True)


def _cols(v, n):
    return np.ascontiguousarray(np.asarray(v, np.float32).reshape(n, 128).T)


def _kc_tile(w, ncols_per_block):
    K, N = w.shape
    nb = N // ncols_per_block
    x = w.reshape(K // 128, 128, nb, ncols_per_block)
    x = x.transpose(2, 1, 0, 3)
    return np.ascontiguousarray(x).reshape(nb, 128, (K // 128) * ncols_per_block)


def host_shared(inp, layers):
    off, R = ptab_layout()
    ptab = np.zeros((128, R), np.float32)

    def put(key, arr):
        ptab[:, off[key]:off[key] + arr.shape[1]] = arr

    for i in range(DEPTH):
        put(("mix_pre", i), _cols(inp["mix_pre_g"][i], 8))
        put(("mix_post", i), _cols(inp["mix_post_g"][i], 8))
        put(("ffn_pre", i), _cols(inp["ffn_pre_g"][i], 8))
        put(("ffn_post", i), _cols(inp["ffn_post_g"][i], 8))
        put(("ple_g", i), _cols(inp["ple_norm_g"][i], 8))
        cv = np.concatenate([_cols(inp["ffn_conv"][i][jj], 44) for jj in range(3)], axis=1)
        put(("conv", i), cv)
    for j in range(2):
        put(("a_lng", j), _cols(inp["a_ln_g"][j], 16))
        put(("a_lnb", j), _cols(inp["a_ln_b"][j], 16))
    dw = np.asarray(inp["c_dw"][0], np.float32)
    dwc = dw.reshape(31, 8, 128).transpose(2, 1, 0).reshape(128, 248)
    put("c_dw", np.ascontiguousarray(dwc))
    put("c_dwb", _cols(inp["c_dw_b"][0], 8))
    put("c_lng", _cols(inp["c_ln_g"][0], 8))
    put("c_lnb", _cols(inp["c_ln_b"][0], 8))
    sh = {"ptab": ptab, "ident": np.eye(128, dtype=np.float32)}
    for i in layers:
        wu = np.asarray(inp["ffn_w_up"][i], np.float32)
        g = wu[:, :FF].reshape(D, NFC, 128)
        v = wu[:, FF:].reshape(D, NFC, 128)
        gv = np.concatenate([g, v], axis=2).reshape(D, NFC * 256)
        sh["wup%d" % i] = _kc_tile(gv, 256)
        wd = np.asarray(inp["ffn_w_down"][i], np.float32)
        x = wd.reshape(NFC, 128, 8, 128).transpose(2, 1, 0, 3)
        sh["wdn%d" % i] = np.ascontiguousarray(x).reshape(8, 128, NFC * 128)
        sh["wg%d" % i] = _kc_tile(np.asarray(inp["ple_w_gate"][i], np.float32), 128)
        wp = np.asarray(inp["ple_w_proj"][i], np.float32)
        sh["wp%d" % i] = np.ascontiguousarray(wp.reshape(2, 128, D).transpose(1, 0, 2)).reshape(128, 2 * D)
        kind, j = i % 3, i // 3
        if kind == 0:
            win = np.asarray(inp["a_w_in"][j], np.float32)
            sh["a_wu%d" % j] = _kc_tile(win[:, :2048], 128)
            sh["a_wv%d" % j] = _kc_tile(win[:, 2048:], 512)
            wo = np.asarray(inp["a_w_out"][j], np.float32)
            x = wo.reshape(16, 128, 8, 128).transpose(2, 1, 0, 3)
            sh["a_wo%d" % j] = np.ascontiguousarray(x).reshape(8, 128, 16 * 128)
            ws = np.asarray(inp["a_w_s"][j], np.float32)
            sh["a_ws%d" % j] = np.ascontiguousarray(ws.transpose(2, 0, 1)).reshape(128, 8 * 128)
            bs = np.asarray(inp["a_b_s"][j], np.float32).reshape(1, 8 * 128)
            sh["a_bs%d" % j] = np.ascontiguousarray(np.broadcast_to(bs, (128, 8 * 128)))
        elif kind == 1:
            wq = np.asarray(inp["b_w_qkv"][0], np.float32)
            q = wq[:, :D].reshape(D, 8, 128)
            k = wq[:, D:2 * D].reshape(D, 8, 128)
            qk = np.concatenate([q, k], axis=2).reshape(D, 8 * 256)
            sh["b_wqk"] = _kc_tile(qk, 256)
            sh["b_wv"] = _kc_tile(np.ascontiguousarray(wq[:, 2 * D:]), 256)
            wo = np.asarray(inp["b_w_out"][0], np.float32)
            x = wo.reshape(8, 128, 8, 128).transpose(2, 1, 0, 3)
            sh["b_wo"] = np.ascontiguousarray(x).reshape(8, 128, 8 * 128)
            rb = np.asarray(inp["b_rel_bias"][0], np.float32)
            qq = np.arange(128)[:, None]
            kk = np.arange(640)[None, :]
            idx = np.clip(qq + 512 - kk, -128, 128) + 128
            bfull = rb[:, idx]
            sh["b_bias"] = np.ascontiguousarray(bfull.transpose(1, 0, 2)).reshape(128, 16 * 640)
        else:
            wi = np.asarray(inp["c_w_in"][0], np.float32)
            a = wi[:, :D].reshape(D, 8, 128)
            g = wi[:, D:].reshape(D, 8, 128)
            ag = np.concatenate([a, g], axis=2).reshape(D, 8 * 256)
            sh["c_wi"] = _kc_tile(ag, 256)
            wo = np.asarray(inp["c_w_out"][0], np.float32)
            x = wo.reshape(8, 128, 8, 128).transpose(2, 1, 0, 3)
            sh["c_wo"] = np.ascontiguousarray(x).reshape(8, 128, 8 * 128)
    return sh


def run_layers(hT_in, p, shared, layers, trace=False, dbg=()):
    nc = Builder(layers, dbg).build()
    in_maps = []
    for b in range(8):
        m = dict(shared)
        m["xT"] = hT_in[b]
        for i in layers:
            m["pT%d" % i] = np.ascontiguousarray(p[i, b].T).reshape(2, 128, S)
        in_maps.append(m)
    res = run_bass_kernel_spmd(nc, in_maps, core_ids=list(range(8)), trace=trace)
    out = np.stack([res.results[b]["yT"] for b in range(8)])
    return out, res


def kernel(**inputs):
    inp = {k: np.asarray(v) for k, v in inputs.items()}
    layers = list(range(DEPTH))
    x = inp["x"].astype(np.float32, copy=False)
    hT = np.ascontiguousarray(x.transpose(0, 2, 1)).reshape(8, 8, 128, S)
    shared = host_shared(inp, layers)
    out, _ = run_layers(hT, inp["p"].astype(np.float32, copy=False), shared, layers)
    y = out.reshape(8, D, S).transpose(0, 2, 1)
    return np.ascontiguousarray(y).astype(np.float32, copy=False)
```

```python
import numpy as np
from contextlib import ExitStack
import concourse.bass as bass
import concourse.mybir as mybir
from concourse.bass_utils import run_bass_kernel_spmd

F32 = mybir.dt.float32
BF16 = mybir.dt.bfloat16
AF = mybir.ActivationFunctionType
ALU = mybir.AluOpType
AX = mybir.AxisListType

D = 1024
S = 2048
T = 512
NT = S // T
FF = 2816
NFC = FF // 128
DEPTH = 4
EPS = 1e-6
NEG = -1e30


class Buf:
    __slots__ = ("name", "last_w", "readers", "dma_readers", "dma_sem", "dma_cnt", "const", "excl")

    fence = {}

    def __init__(self, name, const=False, excl=False):
        self.name = name
        self.excl = excl
        self.last_w = None
        self.readers = dict(Buf.fence)
        self.dma_readers = []
        self.dma_sem = None
        self.dma_cnt = 0
        self.const = const


class SemSlot:
    __slots__ = ("sem", "cnt")

    def __init__(self):
        self.sem = None
        self.cnt = 0


class Op:
    __slots__ = ("eng", "fn", "deps", "sig", "sigval", "is_dma", "dma_sem", "dma_val", "idx")

    def __init__(self, eng, fn, is_dma):
        self.eng = eng
        self.fn = fn
        self.deps = []
        self.sig = False
        self.sigval = 0
        self.is_dma = is_dma
        self.dma_sem = None
        self.dma_val = 0


class Prog:
    ENGS = ("pe", "act", "dve", "pool", "sp")

    def __init__(self, nc):
        self.nc = nc
        self.ops = {e: [] for e in self.ENGS}
        self.n = 0
        self.semslots = {}
        self.out_dma_ops = []

    def _add(self, op, reads, writes):
        deps = []
        for b in reads:
            w = b.last_w
            if w is not None:
                deps.append((w, True))
            if b.excl:
                for e_, r in b.readers.items():
                    if e_ != op.eng:
                        deps.append((r, False))
        for b in writes:
            w = b.last_w
            if w is not None and not (op.is_dma and w.is_dma):
                deps.append((w, False))
            for r in b.readers.values():
                deps.append((r, False))
            for r in b.dma_readers:
                deps.append((r, False))
        seen = set()
        for d, raw in deps:
            if d is op:
                continue
            if (not d.is_dma) and (not op.is_dma) and d.eng == op.eng:
                if op.eng == "pe":
                    continue
            k = id(d)
            if k in seen:
                continue
            seen.add(k)
            op.deps.append(d)
            if not d.is_dma:
                d.sig = True
        for b in writes:
            b.last_w = op
            b.readers = {}
            b.dma_readers = []
        for b in reads:
            if b.const or b in writes:
                continue
            if op.is_dma:
                b.dma_readers.append(op)
            else:
                b.readers[op.eng] = op
        op.idx = self.n
        self.n += 1
        self.ops[op.eng].append(op)
        return op

    @staticmethod
    def _flat(xs):
        out = []
        for x in xs:
            if isinstance(x, (list, tuple)):
                out.extend(Prog._flat(x))
            else:
                out.append(x)
        return out

    def op(self, eng, fn, reads=(), writes=()):
        return self._add(Op(eng, fn, False), self._flat(reads), self._flat(writes))

    def dma(self, eng, fn, reads=(), writes=(), is_output=False):
        o = Op(eng, fn, True)
        reads, writes = self._flat(reads), self._flat(writes)
        dst = writes[0]
        slot = self.semslots.get(dst.name)
        if slot is None:
            slot = self.semslots[dst.name] = SemSlot()
        slot.cnt += 16
        o.dma_sem = slot
        o.dma_val = slot.cnt
        self._add(o, list(reads), list(writes))
        if is_output:
            self.out_dma_ops.append(o)
        return o

    def emit(self, stack):
        nc = self.nc
        sems = {}
        for e in ("pe", "act", "dve", "pool"):
            sems[e] = stack.enter_context(nc.semaphore("s_" + e))
        for i, slot in enumerate(self.semslots.values()):
            slot.sem = stack.enter_context(nc.semaphore("d%d" % i))
        for e in ("pe", "act", "dve", "pool"):
            c = 0
            for o in self.ops[e]:
                if o.is_dma:
                    continue
                if o.sig:
                    c += 1
                    o.sigval = c
        out_ops = self.out_dma_ops

        def run(engh, ename):
            waited = {}

            def wait(sem, val):
                k = id(sem)
                if waited.get(k, 0) >= val:
                    return
                waited[k] = val
                engh.wait_ge(sem, val)

            for o in self.ops[ename]:
                for d in o.deps:
                    if d.is_dma:
                        wait(d.dma_sem.sem, d.dma_val)
                    else:
                        wait(sems[d.eng], d.sigval)
                ins = o.fn(engh)
                if o.is_dma:
                    ins.then_inc(o.dma_sem.sem, 16)
                elif o.sig:
                    ins.then_inc(sems[ename], 1)
            if ename == "sp":
                for o in out_ops:
                    wait(o.dma_sem.sem, o.dma_val)

        block = stack.enter_context(nc.Block())

        @block.tensor
        def _(e):
            run(e, "pe")

        @block.scalar
        def _(e):
            run(e, "act")

        @block.vector
        def _(e):
            run(e, "dve")

        @block.gpsimd
        def _(e):
            run(e, "pool")

        @block.sync
        def _(e):
            run(e, "sp")


def ptab_layout():
    off = {}
    c = 0
    for i in range(DEPTH):
        for nm in ("mix_pre", "mix_post", "ffn_pre", "ffn_post", "ple_g"):
            off[(nm, i)] = c
            c += 8
        off[("conv", i)] = c
        c += 132
    for j in range(2):
        off[("a_lng", j)] = c
        c += 16
        off[("a_lnb", j)] = c
        c += 16
    off["c_dw"] = c
    c += 248
    off["c_dwb"] = c
    c += 8
    off["c_lng"] = c
    c += 8
    off["c_lnb"] = c
    c += 8
    return off, c


class StopBuild(Exception):
    pass


class Stream:
    def __init__(self, B, name, n, free, srcs):
        self.B = B
        self.n = n
        self.srcs = list(srcs)
        self.aps = [B.A.alloc(BF16, *free) for _ in range(n)]
        self.bufs = [Buf("%s%d" % (name, i)) for i in range(n)]
        self.issued = 0
        self.taken = 0
        for _ in range(n - 1):
            self._issue()

    def _issue(self):
        k = self.issued
        if k >= len(self.srcs):
            return
        self.issued += 1
        ap, bf = self.aps[k % self.n], self.bufs[k % self.n]
        flat = ap.rearrange("p a b -> p (a b)") if len(ap.shape) == 3 else ap
        if "fastdma" in self.B.dbg:
            n8 = flat.shape[-1] // 8
            self.B.dma_cast(flat[:, 0:n8], self.srcs[k][:, 0:n8], [], [bf])
            return
        self.B.dma_cast(flat, self.srcs[k], [], [bf])

    def next(self):
        k = self.taken
        self.taken += 1
        self._issue()
        return self.aps[k % self.n], self.bufs[k % self.n]


class Arena:
    def __init__(self, ap_all, base, limit):
        self.all = ap_all
        self.off = base
        self.limit = limit

    def mark(self):
        return self.off

    def reset(self, m):
        self.off = m

    def alloc(self, dtype, *free):
        isz = 4 if dtype == F32 else 2
        n = 1
        for f in free:
            n *= f
        nb = n * isz
        off = (self.off + 63) // 64 * 64
        assert off + nb <= self.limit, ("SBUF arena overflow", off + nb, self.limit)
        self.off = off + nb
        v = self.all[:, off // 2:(off + nb) // 2]
        if dtype == F32:
            v = v.bitcast(F32)
        if len(free) == 2:
            v = v.rearrange("p (a b) -> p a b", a=free[0])
        elif len(free) == 3:
            v = v.rearrange("p (a b c) -> p a b c", a=free[0], b=free[1])
        return v


class Builder:
    def __init__(self, layers, dbg=()):
        self.layers = list(layers)
        self.dbg = set(dbg)
        self.nc = bass.Bass("TRN2", target_bir_lowering=False)
        Buf.fence = {}
        self.P = Prog(self.nc)
        self.poff, self.pcols = ptab_layout()
        self._bank = 0
        self._stat = 0

    def mm(self, out, lhsT, rhs, start, stop, r, w, **kw):
        self.P.op("pe", lambda e: e.matmul(out, lhsT=lhsT, rhs=rhs, start=start, stop=stop, **kw), r, w)

    def act(self, out, in_, func, r, w, bias=None, scale=None, accum=None):
        kw = {}
        if bias is not None:
            kw["bias"] = bias
        if scale is not None:
            kw["scale"] = scale
        if accum is not None:
            kw["accum_out"] = accum
        self.P.op("act", lambda e: e.activation(out=out, in_=in_, func=func, **kw), r, w)

    def ts(self, eng, out, in0, s1, s2, op0, op1, r, w):
        if op1 is None and eng == "pool":
            s2, op1 = 1.0, ALU.mult
        if op1 is None:
            self.P.op(eng, lambda e: e.tensor_scalar(out=out, in0=in0, scalar1=s1, scalar2=None, op0=op0), r, w)
        else:
            self.P.op(eng, lambda e: e.tensor_scalar(out=out, in0=in0, scalar1=s1, scalar2=s2, op0=op0, op1=op1), r, w)

    def stt(self, out, in0, scalar, in1, op0, op1, r, w):
        self.P.op("dve", lambda e: e.scalar_tensor_tensor(out=out, in0=in0, scalar=scalar, in1=in1, op0=op0, op1=op1), r, w)

    def tt(self, eng, out, in0, in1, op, r, w):
        self.P.op(eng, lambda e: e.tensor_tensor(out=out, in0=in0, in1=in1, op=op), r, w)

    def cp(self, eng, out, in_, r, w):
        if eng == "act":
            self.P.op("act", lambda e: e.copy(out=out, in_=in_), r, w)
        else:
            self.P.op(eng, lambda e: e.tensor_copy(out=out, in_=in_), r, w)

    def recip(self, out, in_, r, w):
        self.P.op("dve", lambda e: e.reciprocal(out=out, in_=in_), r, w)

    def memset(self, eng, ap, val, w):
        self.P.op(eng, lambda e: e.memset(ap, val), [], w)

    def dma_cast(self, out, in_, r, w):
        self.P.dma("pool", lambda e: e.dma_start(out=out, in_=in_), r, w)

    def dma_plain(self, out, in_, r, w, is_output=False):
        self.P.dma("sp", lambda e: e.dma_start(out=out, in_=in_), r, w, is_output=is_output)

    def bank(self):
        b = self._bank
        self._bank = (b + 1) % 6
        return b

    def statbank(self):
        b = 6 + self._stat
        self._stat ^= 1
        return b

    def psb(self, b, lo=0, hi=512):
        return self.ps[:, b * 512 + lo:b * 512 + hi]

    def build(self):
        nc = self.nc
        st = ExitStack()
        with st:
            self.declare_dram()
            self.sb_all = st.enter_context(nc.sbuf_tensor("sb_all", [128, 106300], BF16))
            self.ps = st.enter_context(nc.psum_tensor("ps_all", [128, 4096], F32))
            self.PBK = [Buf("psk%d" % i, excl=True) for i in range(32)]
            self.PB = [self.PBK[4 * i:4 * i + 4] for i in range(8)]
            self.A = Arena(self.sb_all, 0, 106300 * 2)
            self.setup_persistent()
            for i in self.layers:
                self.layer(i)
            self.store_output()
            self.P.emit(st)
        return nc

    def declare_dram(self):
        nc = self.nc
        dt = lambda n, s: nc.dram_tensor(n, s, F32, kind="ExternalInput").ap()
        self.d = {}
        self.d["xT"] = dt("xT", [8, 128, S])
        self.d["ptab"] = dt("ptab", [128, self.pcols])
        self.d["ident"] = dt("ident", [128, 128])
        for i in self.layers:
            self.d["pT%d" % i] = dt("pT%d" % i, [2, 128, S])
            self.d["wup%d" % i] = dt("wup%d" % i, [NFC, 128, 8 * 256])
            self.d["wdn%d" % i] = dt("wdn%d" % i, [8, 128, NFC * 128])
            self.d["wg%d" % i] = dt("wg%d" % i, [8, 128, 8 * 128])
            self.d["wp%d" % i] = dt("wp%d" % i, [128, 2 * D])
            kind, j = i % 3, i // 3
            if kind == 0:
                self.d["a_wu%d" % j] = dt("a_wu%d" % j, [16, 128, 8 * 128])
                self.d["a_wv%d" % j] = dt("a_wv%d" % j, [4, 128, 8 * 512])
                self.d["a_wo%d" % j] = dt("a_wo%d" % j, [8, 128, 16 * 128])
                self.d["a_ws%d" % j] = dt("a_ws%d" % j, [128, 8 * 128])
                self.d["a_bs%d" % j] = dt("a_bs%d" % j, [128, 8 * 128])
            elif kind == 1:
                self.d["b_wqk"] = dt("b_wqk", [8, 128, 8 * 256])
                self.d["b_wv"] = dt("b_wv", [4, 128, 8 * 256])
                self.d["b_wo"] = dt("b_wo", [8, 128, 8 * 128])
                self.d["b_bias"] = dt("b_bias", [128, 16 * 640])
            else:
                self.d["c_wi"] = dt("c_wi", [8, 128, 8 * 256])
                self.d["c_wo"] = dt("c_wo", [8, 128, 8 * 128])
        self.yT = nc.dram_tensor("yT", [8, 128, S], F32, kind="ExternalOutput").ap()

    def setup_persistent(self):
        A = self.A
        self.h = A.alloc(F32, 8, S)
        self.Bh = [[Buf("h%d_%d" % (c, t)) for t in range(NT)] for c in range(8)]
        self.ptab = A.alloc(F32, self.pcols)
        self.Bptab = Buf("ptab", const=True)
        self.ident = A.alloc(BF16, 128)
        self.onesb = A.alloc(BF16, 128)
        self.epsc = A.alloc(F32, 1)
        self.Bconst = Buf("const", const=True)
        self.sq = [A.alloc(BF16, T) for _ in range(3)]
        self.Bsq = [Buf("sq%d" % i) for i in range(3)]
        self._sq = 0
        self.rstds = [A.alloc(F32, T) for _ in range(3)]
        self.Brstds = [Buf("rstd%d" % k) for k in range(3)]
        self.rstd, self.Brstd = self.rstds[0], self.Brstds[0]
        self.mres = A.alloc(F32, 8, T)
        self.Bmres = [Buf("mres%d" % c) for c in range(8)]
        self.hns = [A.alloc(BF16, 8, T) for _ in range(2)]
        self.hn1_off = A.off - 8 * T * 2
        self.Bhns = [[Buf("hn%d_%d" % (k, c)) for c in range(8)] for k in range(2)]
        self.hn, self.Bhn = self.hns[0], self.Bhns[0]
        self.rtmp = [A.alloc(F32, T) for _ in range(3)]
        self.Brtmp = [Buf("rtmp%d" % i) for i in range(3)]
        self._rt = 0
        self.phase_mark = A.mark()
        self.dma_plain(self.ptab, self.d["ptab"], [], [self.Bptab])
        for c in range(8):
            self.dma_plain(self.h[:, c, :], self.d["xT"][c], [], self.Bh[c])
        self.memset("pool", self.onesb, 1.0, [self.Bconst])
        self.memset("pool", self.epsc, EPS, [self.Bconst])
        self.dma_cast(self.ident, self.d["ident"], [], [self.Bconst])

    def pcol(self, key, c, n=1):
        o = self.poff[key] + c
        return self.ptab[:, o:o + n]

    def nextsq(self):
        i = self._sq
        self._sq = (i + 1) % 3
        return self.sq[i], self.Bsq[i]

    def nextrt(self):
        i = self._rt
        self._rt = (i + 1) % 3
        return self.rtmp[i], self.Brtmp[i]

    def defer_mm(self, *args, **kw):
        self.flush_mm()
        self._pend_mm = (args, kw)

    def flush_mm(self):
        p = getattr(self, "_pend_mm", None)
        if p is not None:
            self._pend_mm = None
            self.mm(*p[0], **p[1])

    def stat_add(self, sb, src, src_bufs, c, n=8):
        sq, Bsq = self.nextsq()
        self.act(sq, src, AF.Square, src_bufs, [Bsq])
        self.defer_mm(self.psb(sb), self.onesb, sq, c == 0, c == n - 1, [Bsq, self.Bconst], [self.PB[sb]])

    def finish_rstd(self, sb, dim=D, role=0):
        self.flush_mm()
        rstd, Brstd = self.rstds[role], self.Brstds[role]
        self.act(rstd, self.psb(sb), AF.Sqrt, [self.PB[sb], self.Bconst], [Brstd], bias=self.epsc, scale=1.0 / dim)
        self.recip(rstd, rstd, [Brstd], [Brstd])
        return rstd, Brstd

    def pre_norm_gen(self, t, gkey, k=0):
        hn, Bhn = self.hns[k], self.Bhns[k]
        sb = self.statbank()
        tsl = slice(t * T, (t + 1) * T)
        for c in range(8):
            self.stat_add(sb, self.h[:, c, tsl], [self.Bh[c][t]], c)
            yield
        rstd, Brstd = self.finish_rstd(sb, role=0)
        yield
        for c in range(8):
            self.stt(hn[:, c, :], self.h[:, c, tsl], self.pcol(gkey, c), rstd, ALU.mult, ALU.mult,
                     [self.Bh[c][t], Brstd, self.Bptab], [Bhn[c]])
            if c % 2:
                yield

    def pre_norm(self, t, gkey, k=0):
        for _ in self.pre_norm_gen(t, gkey, k):
            pass

    def post_norm_gen(self, t, gkey, sb, role, after=None):
        rstd, Brstd = self.finish_rstd(sb, role=role)
        tsl = slice(t * T, (t + 1) * T)
        yield
        rts = {}
        for i in range(10):
            if i < 8:
                rt, Brt = self.nextrt()
                rts[i] = (rt, Brt)
                self.stt(rt, self.mres[:, i, :], self.pcol(gkey, i), rstd, ALU.mult, ALU.mult,
                         [self.Bmres[i], Brstd, self.Bptab], [Brt])
            if 1 <= i < 9:
                c = i - 1
                rt, Brt = rts.pop(c)
                self.tt("pool", self.h[:, c, tsl], self.h[:, c, tsl], rt, ALU.add, [self.Bh[c][t], Brt], [self.Bh[c][t]])
            if 2 <= i < 10 and after is not None:
                after(i - 2)
            yield

    def bg_start(self, t, gkey, sb, role=1):
        g = self.post_norm_gen(t, gkey, sb, role)
        next(g)
        self._bg = getattr(self, "_bg", [])
        self._bg.append(g)

    def bg_step(self):
        for g in list(getattr(self, "_bg", [])):
            try:
                next(g)
            except StopIteration:
                self._bg.remove(g)

    def bg_drain(self):
        while getattr(self, "_bg", []):
            self.bg_step()

    def post_norm_residual(self, t, gkey, sb, role=1):
        for _ in self.post_norm_gen(t, gkey, sb, role):
            pass

    def evac_mres(self, b, dc, sb):
        self.cp("dve", self.mres[:, dc, :], self.psb(b), [self.PB[b]], [self.Bmres[dc]])
        self.stat_add(sb, self.mres[:, dc, :], [self.Bmres[dc]], dc)

    def make_slots(self, name, n, *free):
        aps = [self.A.alloc(BF16, *free) for _ in range(n)]
        bufs = [Buf("%s%d" % (name, i)) for i in range(n)]
        return {"aps": aps, "bufs": bufs, "i": 0, "n": n}

    def load_slot(self, slots, src):
        i = slots["i"]
        slots["i"] = (i + 1) % slots["n"]
        ap, bf = slots["aps"][i], slots["bufs"][i]
        flat = ap
        if len(ap.shape) == 3:
            flat = ap.rearrange("p a b -> p (a b)")
        self.dma_cast(flat, src, [], [bf])
        return ap, bf

    def stream(self, name, n, free, srcs):
        return Stream(self, name, n, free, srcs)

    def stop(self, tag):
        if tag in self.dbg:
            raise StopBuild()

    def layer(self, i):
        try:
            self._layer(i)
        except StopBuild:
            pass

    def new_phase(self):
        self.bg_drain()
        self.flush_mm()
        self.A.reset(self.phase_mark)
        f = {}
        for e in ("pe", "act", "dve", "pool"):
            for o in reversed(self.P.ops[e]):
                if not o.is_dma:
                    f[e] = o
                    break
        Buf.fence = f

    def _layer(self, i):
        kind, j = i % 3, i // 3
        A = self.A
        self.new_phase()
        if "prenorm" in self.dbg:
            self.pre_norm(0, ("mix_pre", i))
            return
        if "nomix" in self.dbg:
            pass
        elif kind == 0:
            self.mixer_a(i, j)
        elif kind == 1:
            self.mixer_b(i)
        else:
            self.mixer_c(i)
        self.new_phase()
        if "noffn" not in self.dbg:
            self.ffn_phase(i)

    def ffn_phase(self, i):
        A = self.A
        actb = A.alloc(BF16, NFC, T)
        Bact = [Buf("act%d" % f) for f in range(NFC)]
        wupS = self.stream("wup", 4, (8, 256), [self.d["wup%d" % i][fc] for _t in range(NT) for fc in range(NFC)])
        HF = NFC // 2
        wdnS = self.stream("wdn", 5, (HF, 128), [self.d["wdn%d" % i][dc][:, hf * HF * 128:(hf + 1) * HF * 128]
                                                  for _t in range(NT) for dc in range(8) for hf in range(2)])
        wgS = self.stream("wg", 4, (8, 128), [self.d["wg%d" % i][dc] for _t in range(NT) for dc in range(8)])
        cg = [A.alloc(F32, T) for _ in range(2)]
        cv = [A.alloc(F32, T) for _ in range(2)]
        sg = [A.alloc(F32, T) for _ in range(2)]
        Bcg = [Buf("cg%d" % k) for k in range(2)]
        Bcv = [Buf("cv%d" % k) for k in range(2)]
        Bsg = [Buf("sg%d" % k) for k in range(2)]
        halo = [A.alloc(F32, 2 * NFC, 2) for _ in range(2)]
        Bhalo = [[Buf("halo%d_%d" % (k, q)) for q in range(2 * NFC)] for k in range(2)]
        bnd = [A.alloc(F32, 3, 2 * NFC) for _ in range(1)] * 2
        Bbnd = [Buf("bnd")] * 2
        x01 = A.alloc(F32, 2 * NFC, 2)
        c01 = A.alloc(F32, 2 * NFC, 2)
        s01 = A.alloc(F32, NFC, 2)
        Bx01, Bc01, Bs01 = Buf("x01"), Buf("c01"), Buf("s01")
        hb = A.alloc(BF16, 8, T)
        Bhb = [Buf("hb%d" % c) for c in range(8)]
        pt = [A.alloc(BF16, 2, T) for _ in range(2)]
        Bpt = [Buf("pt%d" % k) for k in range(2)]
        wp = A.alloc(BF16, 2, D)
        Bwp = Buf("wp")
        gate = [A.alloc(F32, T) for _ in range(1)]
        Bgate = [Buf("gate%d" % k) for k in range(1)]
        self.dma_cast(wp.rearrange("p a b -> p (a b)"), self.d["wp%d" % i], [], [Bwp])
        cbase = self.poff[("conv", i)]

        def tap(jj, q):
            o = cbase + jj * 44 + q
            return self.ptab[:, o:o + 1]

        def step(bg):
            for g in list(bg):
                try:
                    next(g)
                except StopIteration:
                    bg.remove(g)

        def drain(bg):
            while bg:
                step(bg)

        def load_pt(t):
            ptt, Bptt = pt[t % 2], Bpt[t % 2]
            tsl = slice(t * T, (t + 1) * T)
            for kc in range(2):
                self.dma_cast(ptt[:, kc, :], self.d["pT%d" % i][kc][:, tsl], [], [Bptt])

        def stage_A(t):
            return self.pre_norm_gen(t, ("ffn_pre", i), k=t % 2)

        def stage_B(t, bg):
            hn, Bhn = self.hns[t % 2], self.Bhns[t % 2]
            if t > 0:
                ho, Bho = halo[(t - 1) % 2], Bhalo[(t - 1) % 2]
                bd, Bbd = bnd[t % 2], Bbnd[t % 2]
                W0 = self.ptab[:, cbase:cbase + 44]
                W1 = self.ptab[:, cbase + 44:cbase + 88]
                self.tt("dve", bd[:, 2, :], ho[:, :, 1], W1, ALU.mult, Bho + [self.Bptab], [Bbd])
                self.tt("dve", bd[:, 0, :], ho[:, :, 0], W0, ALU.mult, Bho + [self.Bptab], [Bbd])
                self.tt("dve", bd[:, 0, :], bd[:, 0, :], bd[:, 2, :], ALU.add, [Bbd], [Bbd])
                self.tt("dve", bd[:, 1, :], ho[:, :, 1], W0, ALU.mult, Bho + [self.Bptab], [Bbd])
            for fc in range(NFC):
                w, Bw = wupS.next()
                k2 = fc % 2
                bg_ = self.bank()
                bv_ = self.bank()
                for kc in range(8):
                    self.mm(self.psb(bg_), w[:, kc, 0:128], hn[:, kc, :], kc == 0, kc == 7, [Bw, Bhn[kc]], [self.PB[bg_]])
                for kc in range(8):
                    self.mm(self.psb(bv_), w[:, kc, 128:256], hn[:, kc, :], kc == 0, kc == 7, [Bw, Bhn[kc]], [self.PB[bv_]])
                lo = 0 if t == 0 else 2
                for (b, q, cbuf, Bc) in ((bg_, fc, cg[k2], Bcg[k2]), (bv_, NFC + fc, cv[k2], Bcv[k2])):
                    pb = self.PB[b]
                    self.act(cbuf[:, lo:T], self.psb(b, lo, T), AF.Copy, [pb, self.Bptab], [Bc], scale=tap(2, q))
                    if t > 0:
                        self.cp("act", x01[:, q, 0:2], self.psb(b, 0, 2), [pb], [Bx01])
                    if t < NT - 1:
                        self.cp("act", halo[t % 2][:, q, 0:2], self.psb(b, T - 2, T), [pb], [Bhalo[t % 2][q]])
                    self.stt(cbuf[:, lo + 1 - lo // 2:T], self.psb(b, lo // 2, T - 1), tap(1, q), cbuf[:, lo + 1 - lo // 2:T],
                             ALU.mult, ALU.add, [pb, Bc, self.Bptab], [Bc])
                    self.stt(cbuf[:, 2:T], self.psb(b, 0, T - 2), tap(0, q), cbuf[:, 2:T], ALU.mult, ALU.add,
                             [pb, Bc, self.Bptab], [Bc])
                self.act(sg[k2][:, lo:T], cg[k2][:, lo:T], AF.Silu, [Bcg[k2]], [Bsg[k2]])
                self.tt("pool", actb[:, fc, lo:T], sg[k2][:, lo:T], cv[k2][:, lo:T], ALU.mult, [Bsg[k2], Bcv[k2]], [Bact[fc]])
                step(bg)
            if t > 0:
                bd, Bbd = bnd[0], Bbnd[0]
                W0 = self.ptab[:, cbase:cbase + 44]
                W1 = self.ptab[:, cbase + 44:cbase + 88]
                W2 = self.ptab[:, cbase + 88:cbase + 132]
                rP = [Bx01, self.Bptab, Bbd]
                self.tt("dve", c01[:, :, 0], x01[:, :, 0], W2, ALU.mult, rP, [Bc01])
                self.tt("dve", c01[:, :, 0], c01[:, :, 0], bd[:, 0, :], ALU.add, rP + [Bc01], [Bc01])
                self.tt("dve", c01[:, :, 1], x01[:, :, 1], W2, ALU.mult, rP, [Bc01])
                self.tt("dve", bd[:, 2, :], x01[:, :, 0], W1, ALU.mult, rP, [Bbd])
                self.tt("dve", c01[:, :, 1], c01[:, :, 1], bd[:, 2, :], ALU.add, rP + [Bc01], [Bc01])
                self.tt("dve", c01[:, :, 1], c01[:, :, 1], bd[:, 1, :], ALU.add, rP + [Bc01], [Bc01])
                self.act(s01, c01[:, 0:NFC, :], AF.Silu, [Bc01], [Bs01])
                self.tt("dve", actb[:, :, 0:2], s01, c01[:, NFC:2 * NFC, :], ALU.mult, [Bs01, Bc01], Bact)
            drain(bg)

        def stage_C(t, bg):
            sb = self.statbank()
            for dc in range(8):
                b = self.bank()
                for hf in range(2):
                    w, Bw = wdnS.next()
                    for f2 in range(HF):
                        fc = hf * HF + f2
                        self.mm(self.psb(b), w[:, f2, :], actb[:, fc, :], fc == 0, fc == NFC - 1, [Bw, Bact[fc]], [self.PB[b]])
                step(bg)
                self.evac_mres(b, dc, sb)
            drain(bg)
            return sb

        def stage_D(t, sb):
            tsl = slice(t * T, (t + 1) * T)

            def after(c):
                self.cp("act", hb[:, c, :], self.h[:, c, tsl], [self.Bh[c][t]], [Bhb[c]])
            g = self.post_norm_gen(t, ("ffn_post", i), sb, 1, after=after)
            next(g)
            return g

        def stage_E(t):
            ptt, Bptt = pt[t % 2], Bpt[t % 2]
            if t + 1 < NT:
                load_pt(t + 1)
            sb = self.statbank()
            for dc in range(8):
                w, Bw = wgS.next()
                bgt = self.bank()
                be = self.bank()
                for kc in range(8):
                    self.mm(self.psb(bgt), w[:, kc, :], hb[:, kc, :], kc == 0, kc == 7, [Bw, Bhb[kc]], [self.PB[bgt]])
                for kc in range(2):
                    self.mm(self.psb(be), wp[:, kc, dc * 128:(dc + 1) * 128], ptt[:, kc, :], kc == 0, kc == 1,
                            [Bwp, Bptt], [self.PB[be]])
                k2 = 0
                self.act(gate[k2], self.psb(bgt), AF.Sigmoid, [self.PB[bgt]], [Bgate[k2]])
                self.tt("dve", self.mres[:, dc, :], gate[k2], self.psb(be), ALU.mult, [Bgate[k2], self.PB[be]],
                        [self.Bmres[dc]])
                self.stat_add(sb, self.mres[:, dc, :], [self.Bmres[dc]], dc)
            return sb

        load_pt(0)
        drain([stage_A(0)])
        stage_B(0, [stage_A(1)])
        pend = []
        for t in range(NT):
            sb = stage_C(t, pend)
            pend = []
            bgl = [stage_D(t, sb)]
            if t + 2 < NT:
                bgl.append(stage_A(t + 2))
            if t + 1 < NT:
                stage_B(t + 1, bgl)
            else:
                drain(bgl)
            sbe = stage_E(t)
            g = self.post_norm_gen(t, ("ple_g", i), sbe, 2)
            next(g)
            pend = [g]
        drain(pend)
        self.flush_mm()

    def mixer_c(self, i):
        A = self.A
        ybuf = A.alloc(BF16, 8, 30 + T)
        Byb = [Buf("yb%d" % c) for c in range(8)]
        z = A.alloc(F32, 8, T)
        Bz = [Buf("z%d" % c) for c in range(8)]
        zb = [A.alloc(BF16, T) for _ in range(2)]
        Bzb = [Buf("zb%d" % k) for k in range(2)]
        actc = A.alloc(BF16, 8, T)
        Bac = [Buf("actc%d" % c) for c in range(8)]
        dg = [A.alloc(BF16, 31, 128) for _ in range(2)]
        Bdg = [Buf("dg%d" % k) for k in range(2)]
        wi = self.stream("cwi", 3, (8, 256), [self.d["c_wi"][c] for _t in range(NT) for c in range(8)])
        wo = self.stream("cwo", 3, (8, 128), [self.d["c_wo"][c] for _t in range(NT) for c in range(8)])
        sgm = [A.alloc(F32, T) for _ in range(2)]
        Bsgm = [Buf("sgm%d" % k) for k in range(2)]
        mean = A.alloc(F32, T)
        msq = A.alloc(F32, T)
        var = A.alloc(F32, T)
        nmr = A.alloc(F32, T)
        Bmean, Bmsq, Bvar, Bnmr = Buf("mean"), Buf("msq"), Buf("var"), Buf("nmr")
        t1 = [A.alloc(F32, T) for _ in range(2)]
        Bt1 = [Buf("ct1_%d" % k) for k in range(2)]
        for c in range(8):
            self.memset("pool", ybuf[:, c, 0:30], 0.0, [Byb[c]])

        def build_diag(c):
            k2_ = c % 2
            o_ = self.poff["c_dw"] + c * 31
            in1 = self.ptab[:, o_:o_ + 31].unsqueeze(2).to_broadcast([128, 31, 128])
            in0 = self.ident.unsqueeze(1).to_broadcast([128, 31, 128])
            self.tt("dve", dg[k2_], in0, in1, ALU.mult, [self.Bconst, self.Bptab], [Bdg[k2_]])

        self.pre_norm(0, ("mix_pre", i), k=0)
        for t in range(NT):
            hn, Bhn = self.hns[t % 2], self.Bhns[t % 2]
            sb1 = self.statbank()
            sb2 = self.statbank()
            if t == 0:
                build_diag(0)
            for c in range(8):
                if c + 1 < 8:
                    build_diag(c + 1)
                self.bg_step()
                w, Bw = wi.next()
                ba = self.bank()
                bg = self.bank()
                for kc in range(8):
                    self.mm(self.psb(ba), w[:, kc, 0:128], hn[:, kc, :], kc == 0, kc == 7, [Bw, Bhn[kc]], [self.PB[ba]])
                for kc in range(8):
                    self.mm(self.psb(bg), w[:, kc, 128:256], hn[:, kc, :], kc == 0, kc == 7, [Bw, Bhn[kc]], [self.PB[bg]])
                k2 = c % 2
                self.act(sgm[k2], self.psb(bg), AF.Sigmoid, [self.PB[bg]], [Bsgm[k2]])
                self.tt("dve", ybuf[:, c, 30:30 + T], sgm[k2], self.psb(ba), ALU.mult, [Bsgm[k2], self.PB[ba]], [Byb[c]])
                bc = self.bank()
                for jj in range(31):
                    self.mm(self.psb(bc), dg[k2][:, jj, :], ybuf[:, c, jj:jj + T], jj == 0, jj == 30, [Bdg[k2], Byb[c]], [self.PB[bc]])
                bias = self.pcol("c_dwb", c)
                self.act(z[:, c, :], self.psb(bc), AF.Identity, [self.PB[bc], self.Bptab], [Bz[c]], bias=bias)
                self.act(zb[k2], self.psb(bc), AF.Identity, [self.PB[bc], self.Bptab], [Bzb[k2]], bias=bias)
                self.flush_mm()
                self.mm(self.psb(sb1), self.onesb, zb[k2], c == 0, c == 7, [Bzb[k2], self.Bconst], [self.PB[sb1]])
                sq, Bsq = self.nextsq()
                self.act(sq, self.psb(bc), AF.Square, [self.PB[bc], self.Bptab], [Bsq], bias=bias)
                self.defer_mm(self.psb(sb2), self.onesb, sq, c == 0, c == 7, [Bsq, self.Bconst], [self.PB[sb2]])
                if t < NT - 1:
                    self.cp("pool", ybuf[:, c, 0:30], ybuf[:, c, T:T + 30], [Byb[c]], [Byb[c]])
            self.bg_drain()
            if t + 1 < NT:
                build_diag(0)
            self.flush_mm()
            self.ts("dve", mean, self.psb(sb1), 1.0 / D, None, ALU.mult, None, [self.PB[sb1]], [Bmean])
            self.act(msq, mean, AF.Square, [Bmean], [Bmsq])
            self.stt(var, self.psb(sb2), 1.0 / D, msq, ALU.mult, ALU.subtract, [self.PB[sb2], Bmsq], [Bvar])
            self.act(self.rstd, var, AF.Sqrt, [Bvar, self.Bconst], [self.Brstd], bias=self.epsc)
            self.recip(self.rstd, self.rstd, [self.Brstd], [self.Brstd])
            self.stt(nmr, mean, -1.0, self.rstd, ALU.mult, ALU.mult, [Bmean, self.Brstd], [Bnmr])
            for c in range(8):
                k2 = c % 2
                self.tt("dve", t1[k2], z[:, c, :], self.rstd, ALU.mult, [Bz[c], self.Brstd], [Bt1[k2]])
                self.tt("pool", t1[k2], t1[k2], nmr, ALU.add, [Bt1[k2], Bnmr], [Bt1[k2]])
                self.act(actc[:, c, :], t1[k2], AF.Silu, [Bt1[k2], self.Bptab], [Bac[c]],
                         scale=self.pcol("c_lng", c), bias=self.pcol("c_lnb", c))
            if t + 1 < NT:
                self.pre_norm(t + 1, ("mix_pre", i), k=(t + 1) % 2)
            sb = self.statbank()
            for dc in range(8):
                w, Bw = wo.next()
                b = self.bank()
                for c in range(8):
                    self.mm(self.psb(b), w[:, c, :], actc[:, c, :], c == 0, c == 7, [Bw, Bac[c]], [self.PB[b]])
                self.evac_mres(b, dc, sb)
            self.bg_start(t, ("mix_post", i), sb)
            if t == NT - 1:
                self.bg_drain()

    def mixer_a(self, i, j):
        A = self.A
        u = A.alloc(BF16, 16, T)
        Bu = [Buf("u%d" % c) for c in range(16)]
        vgb = A.alloc(BF16, 4, 2048)
        Bvgb = [Buf("vgb%d" % b) for b in range(4)]
        wv = self.stream("awv", 3, (8, 512), [self.d["a_wv%d" % j][q] for _t in range(NT) for q in range(4)])
        wu = self.stream("awu", 3, (8, 128), [self.d["a_wu%d" % j][q] for _t in range(NT) for q in range(16)])
        wo = self.stream("awo", 2, (16, 128), [self.d["a_wo%d" % j][q] for _t in range(NT) for q in range(8)])
        bst = A.alloc(F32, 4, 4, 6)
        Bbst = [Buf("bst%d" % b) for b in range(4)]
        mv = A.alloc(F32, 4, 2)
        rs = A.alloc(F32, 4, 1)
        vpe = A.alloc(F32, 4, 1)
        Bmv = [Buf("mv%d" % b) for b in range(4)]
        Brs = [Buf("rs%d" % b) for b in range(4)]
        Bvpe = [Buf("vpe%d" % b) for b in range(4)]
        nmh = A.alloc(F32, 1)
        wsT = A.alloc(BF16, 8, 128)
        Bws = Buf("wsT")
        Cc = A.alloc(F32, 16, 128)
        BCc = Buf("Cc")
        bsb = A.alloc(F32, 8, 128)
        Bbsb = Buf("bsb")
        rw = A.alloc(F32, 8, 128)
        Brw = Buf("rw")
        t1 = [A.alloc(F32, 4, 128) for _ in range(1)]
        Bt1 = [Buf("at1_%d" % k) for k in range(1)]
        self.memset("pool", nmh, -0.5, [self.Bconst])
        self.dma_cast(wsT.rearrange("p a b -> p (a b)"), self.d["a_ws%d" % j], [], [Bws])
        self.memset("pool", wsT[64:128, :, 0:64], 0.0, [Bws])
        self.dma_plain(bsb.rearrange("p a b -> p (a b)"), self.d["a_bs%d" % j], [], [Bbsb])
        b0 = self.bank()
        b1 = self.bank()
        for g in range(8):
            bb = b0 if g < 4 else b1
            lo = (g % 4) * 128
            self.mm(self.psb(bb, lo, lo + 128), self.onesb, wsT[:, g, :], True, True, [Bws, self.Bconst], [self.PB[bb]])
        self.cp("act", rw[:, 0:4, :].rearrange("p a b -> p (a b)"), self.psb(b0), [self.PB[b0]], [Brw])
        self.cp("act", rw[:, 4:8, :].rearrange("p a b -> p (a b)"), self.psb(b1), [self.PB[b1]], [Brw])
        for uc in range(16):
            g = uc // 2
            self.stt(Cc[:, uc, :], rw[:, g, :], self.pcol(("a_lnb", j), uc), bsb[:, g, :], ALU.mult, ALU.add,
                     [Brw, Bbsb, self.Bptab], [BCc])
        self.pre_norm(0, ("mix_pre", i), k=0)
        for t in range(NT):
            hn, Bhn = self.hns[t % 2], self.Bhns[t % 2]
            for vq in range(4):
                w, Bw = wv.next()
                for blk in range(4):
                    b = self.bank()
                    for kc in range(8):
                        self.mm(self.psb(b), hn[:, kc, blk * 128:(blk + 1) * 128], w[:, kc, :], kc == 0, kc == 7,
                                [Bw, Bhn[kc]], [self.PB[b]])
                    dst = vgb[:, blk, vq * 512:(vq + 1) * 512]
                    self.act(dst, self.psb(b), AF.Gelu_apprx_tanh, [self.PB[b]], [Bvgb[blk]])
                    bo = bst[:, blk, vq, :]
                    self.P.op("dve", lambda e, bo=bo, src=dst: e.bn_stats(out=bo, in_=src), [Bvgb[blk]], [Bbst[blk]])
                    self.bg_step()
            self.bg_drain()
            for blk in range(4):
                mo = mv[:, blk, :]
                bi = bst[:, blk, :, :]
                self.P.op("dve", lambda e, mo=mo, bi=bi: e.bn_aggr(out=mo, in_=bi), [Bbst[blk]], [Bmv[blk]])
                self.ts("pool", vpe[:, blk, :], mv[:, blk, 1:2], EPS, None, ALU.add, None, [Bmv[blk]], [Bvpe[blk]])
                self.tt("pool", rs[:, blk, :], vpe[:, blk, :], nmh, ALU.pow, [Bvpe[blk], self.Bconst], [Brs[blk]])
                self.ts("dve", vgb[:, blk, :], vgb[:, blk, :], mv[:, blk, 0:1], rs[:, blk, :], ALU.subtract, ALU.mult,
                        [Bvgb[blk], Bmv[blk], Brs[blk]], [Bvgb[blk]])
            kk = 0
            for u4 in range(4):
                for j4 in range(4):
                    uc = u4 * 4 + j4
                    w, Bw = wu.next()
                    b = self.bank()
                    for kc in range(8):
                        self.mm(self.psb(b), w[:, kc, :], hn[:, kc, :], kc == 0, kc == 7, [Bw, Bhn[kc]], [self.PB[b]])
                    self.act(u[:, uc, :], self.psb(b), AF.Gelu_apprx_tanh, [self.PB[b]], [Bu[uc]])
                for blk in range(4):
                    bsl = slice(blk * 128, (blk + 1) * 128)
                    b = self.bank()
                    for j4 in range(4):
                        uc = u4 * 4 + j4
                        self.mm(self.psb(b, j4 * 128, (j4 + 1) * 128), vgb[:, blk, uc * 128:(uc + 1) * 128], wsT[:, uc // 2, :], True, True,
                                [Bvgb[blk], Bws], [self.PB[b]])
                    k3 = 0
                    kk += 1
                    for j4 in range(4):
                        uc = u4 * 4 + j4
                        self.stt(t1[k3][:, j4, :], self.psb(b, j4 * 128, (j4 + 1) * 128), self.pcol(("a_lng", j), uc), Cc[:, uc, :],
                                 ALU.mult, ALU.add, [self.PB[b], BCc, self.Bptab], [Bt1[k3]])
                    uv = u[:, u4 * 4:(u4 + 1) * 4, bsl]
                    self.tt("dve", uv, t1[k3], uv, ALU.mult, [Bt1[k3]] + Bu[u4 * 4:(u4 + 1) * 4], Bu[u4 * 4:(u4 + 1) * 4])
            if t + 1 < NT:
                self.pre_norm(t + 1, ("mix_pre", i), k=(t + 1) % 2)
            sb = self.statbank()
            for dc in range(8):
                w, Bw = wo.next()
                b = self.bank()
                for uc in range(16):
                    self.mm(self.psb(b), w[:, uc, :], u[:, uc, :], uc == 0, uc == 15, [Bw, Bu[uc]], [self.PB[b]])
                self.evac_mres(b, dc, sb)
            self.bg_start(t, ("mix_post", i), sb)
            if t == NT - 1:
                self.bg_drain()

    def mixer_b(self, i):
        A = self.A
        kT = A.alloc(BF16, 8, 1024)
        BkT = [[Buf("kT%d_%d" % (c, s_)) for s_ in range(2)] for c in range(8)]
        V = A.alloc(BF16, 8, 1024)
        BV = [Buf("V%d" % b) for b in range(8)]
        qz = A.alloc(BF16, 8, 2, T)
        Bq = [Buf("qz%d" % c) for c in range(8)]
        Bb = A.alloc(BF16, 16, 640)
        BBb = Buf("Bb")
        A2 = Arena(self.sb_all, self.hn1_off, self.hn1_off + 8 * T * 2)
        Pb = [A2.alloc(BF16, 640) for _ in range(3)]
        BPb = [Buf("Pb%d" % k) for k in range(3)]
        PTb = [A2.alloc(BF16, 640) for _ in range(3)]
        BPTb = [Buf("PTb%d" % k) for k in range(3)]
        dgr = [A.alloc(BF16, 128) for _ in range(3)]
        Bdgr = [Buf("dgr%d" % k) for k in range(3)]
        st3 = [A.alloc(F32, 4) for _ in range(4)]
        Bnm = [Buf("nm%d" % k) for k in range(4)]
        Brs = [Buf("rsum%d" % k) for k in range(4)]
        Bri = [Buf("rinv%d" % k) for k in range(4)]
        wqk = self.stream("bwqk", 2, (8, 256), [self.d["b_wqk"][q] for _t in range(NT) for q in range(8)])
        wvs = self.stream("bwv", 2, (8, 256), [self.d["b_wv"][q] for _t in range(NT) for q in range(4)])
        wos = self.stream("bwo", 2, (8, 128), [self.d["b_wo"][q] for _t in range(NT) for q in range(8)])
        oT, BoT = self.hn, self.Bhn
        ps = self.ps
        self.dma_cast(Bb.rearrange("p a b -> p (a b)"), self.d["b_bias"], [], [BBb])
        self.memset("pool", Bb[64:128, :, 0:64], NEG, [BBb])
        self.memset("pool", Bb[0:64, :, 576:640], NEG, [BBb])
        for c in range(8):
            self.memset("pool", qz[:, c, :, :], 0.0, [Bq[c]])
        unit = 0
        for t in range(NT):
            slot = t % 2
            self.pre_norm(t, ("mix_pre", i))
            for c in range(8):
                w, Bw = wqk.next()
                bq = self.bank()
                bk = self.bank()
                for kc in range(8):
                    self.mm(self.psb(bq), w[:, kc, 0:128], self.hn[:, kc, :], kc == 0, kc == 7, [Bw, self.Bhn[kc]], [self.PB[bq]])
                for kc in range(8):
                    self.mm(self.psb(bk), w[:, kc, 128:256], self.hn[:, kc, :], kc == 0, kc == 7, [Bw, self.Bhn[kc]], [self.PB[bk]])
                for hh in range(2):
                    rows = slice(hh * 64, hh * 64 + 64)
                    self.act(qz[rows, c, hh, :], ps[rows, bq * 512:(bq + 1) * 512], AF.Copy, [self.PB[bq]], [Bq[c]], scale=0.125)
                self.cp("dve", kT[:, c, slot * 512:(slot + 1) * 512], self.psb(bk), [self.PB[bk]], [BkT[c][slot]])
                self.bg_step()
            for qt in range(4):
                w, Bw = wvs.next()
                for blk in range(4):
                    b = self.bank()
                    for kc in range(8):
                        self.mm(self.psb(b, 0, 256), self.hn[:, kc, blk * 128:(blk + 1) * 128], w[:, kc, :], kc == 0, kc == 7,
                                [Bw, self.Bhn[kc]], [self.PB[b]])
                    rb = (4 * t + blk) % 8
                    self.cp("act" if (blk % 2) else "dve", V[:, rb, qt * 256:(qt + 1) * 256], self.psb(b, 0, 256), [self.PB[b]], [BV[rb]])
            self.bg_drain()
            units = [(c, jq, hh) for c in range(8) for jq in range(4) for hh in range(2)]
            NU = len(units)

            def geom(u):
                c, jq, hh = units[u]
                jb = 4 * t + jq
                return c, jq, hh, jb, max(0, 4 - jb)

            def st_scores(u):
                c, jq, hh, jb, i0 = geom(u)
                hd = 2 * c + hh
                sl = u % 2
                sbase = sl * 1024
                groups = []
                ii = i0
                while ii < 5:
                    n = 1
                    while (ii + n < 5 and (ii + n) != 4 and ((jb - 4 + ii + n) % 8) == ((jb - 4 + ii) % 8) + n):
                        n += 1
                    groups.append((ii, n))
                    ii += n
                started = set()
                for (ii, n) in groups:
                    kb = jb - 4 + ii
                    rc = (kb % 8) * 128
                    ksl = sorted(set(((kb + m) // 4) % 2 for m in range(n)))
                    sap = ps[:, sbase + ii * 128: sbase + (ii + n) * 128]
                    bk = 0 if ii < 4 else 1
                    wb = self.PB[2 * sl + bk]
                    self.mm(sap, qz[:, c, hh, jq * 128:(jq + 1) * 128], kT[:, c, rc:rc + n * 128], bk not in started, False,
                            [Bq[c]] + [BkT[c][k_] for k_ in ksl], wb)
                    started.add(bk)
                if i0 < 4:
                    self.mm(ps[:, sbase + i0 * 128: sbase + 512], self.ident, Bb[:, hd, i0 * 128:512], False, True,
                            [self.Bconst, BBb], self.PB[2 * sl])
                self.mm(ps[:, sbase + 512: sbase + 640], self.ident, Bb[:, hd, 512:640], False, True,
                        [self.Bconst, BBb], self.PB[2 * sl + 1])
                sbufs = [self.PB[2 * sl], self.PB[2 * sl + 1]]
                sfull = ps[:, sbase + i0 * 128: sbase + 640]
                k4 = u % 4
                nmax = st3[k4][:, 0:1]
                self.P.op("dve", lambda e, nmax=nmax, sfull=sfull: e.tensor_reduce(out=nmax, in_=sfull, axis=AX.X, op=ALU.max, negate=True),
                          sbufs, [Bnm[k4]])

            def st_exp(u):
                c, jq, hh, jb, i0 = geom(u)
                sl = u % 2
                k4 = u % 4
                sbase = sl * 1024
                sbufs = [self.PB[2 * sl], self.PB[2 * sl + 1]]
                sfull = ps[:, sbase + i0 * 128: sbase + 640]
                nmax, rsum = st3[k4][:, 0:1], st3[k4][:, 1:2]
                self.act(Pb[u % 3][:, i0 * 128:640], sfull, AF.Exp, sbufs + [Bnm[k4]], [BPb[u % 3], Brs[k4]], bias=nmax, accum=rsum)

            def st_rd(u):
                k4 = u % 4
                k3 = u % 3
                rsum, rinv = st3[k4][:, 1:2], st3[k4][:, 2:3]
                self.recip(rinv, rsum, [Brs[k4]], [Bri[k4]])
                self.ts("pool", dgr[k3], self.ident, rinv, None, ALU.mult, None, [self.Bconst, Bri[k4]], [Bdgr[k3]])

            def st_pt(u):
                c, jq, hh, jb, i0 = geom(u)
                sl = u % 2
                k3 = u % 3
                pbase = 2048
                for ii in range(i0, 5):
                    self.mm(ps[:, pbase + ii * 128: pbase + (ii + 1) * 128], Pb[k3][:, ii * 128:(ii + 1) * 128], dgr[k3], True, True,
                            [BPb[k3], Bdgr[k3]], [self.PB[4] if ii < 4 else self.PB[5]])
                self.cp("act" if (u % 2) else "dve", PTb[k3][:, i0 * 128:640], ps[:, pbase + i0 * 128: pbase + 640],
                        [self.PB[4], self.PB[5]], [BPTb[k3]])

            def st_pv(u):
                c, jq, hh, jb, i0 = geom(u)
                k3 = u % 3
                ob = 6 + hh
                for ii in range(i0, 5):
                    kb = jb - 4 + ii
                    self.mm(ps[:, ob * 512 + jq * 128: ob * 512 + (jq + 1) * 128], V[:, kb % 8, c * 128:(c + 1) * 128],
                            PTb[k3][:, ii * 128:(ii + 1) * 128], ii == i0, ii == 4, [BV[kb % 8], BPTb[k3]], [self.PB[ob]])
                if jq == 3 and hh == 1:
                    for h2 in range(2):
                        rows = slice(h2 * 64, h2 * 64 + 64)
                        o2 = 6 + h2
                        self.cp("dve" if h2 else "act", oT[rows, c, :], ps[rows, o2 * 512:(o2 + 1) * 512], self.PB[o2], [BoT[c]])

            for u in range(NU + 4):
                if u < NU:
                    st_scores(u)
                if 0 <= u - 1 < NU:
                    st_exp(u - 1)
                if 0 <= u - 2 < NU:
                    st_rd(u - 2)
                if 0 <= u - 3 < NU:
                    st_pt(u - 3)
                if 0 <= u - 4 < NU:
                    st_pv(u - 4)
            sb = self.statbank()
            for dc in range(8):
                w, Bw = wos.next()
                b = self.bank()
                for c in range(8):
                    self.mm(self.psb(b), w[:, c, :], oT[:, c, :], c == 0, c == 7, [Bw, BoT[c]], [self.PB[b]])
                self.evac_mres(b, dc, sb)
            self.bg_start(t, ("mix_post", i), sb)
            if t == NT - 1:
                self.bg_drain()

    def store_output(self):
        for c in range(8):
            self.dma_plain(self.yT[c], self.h[:, c, :], self.Bh[c], [Buf("y%d" % c)], is_output=True)


def _cols(v, n):
    return np.ascontiguousarray(np.asarray(v, np.float32).reshape(n, 128).T)


def _kc_tile(w, ncols_per_block):
    K, N = w.shape
    nb = N // ncols_per_block
    x = w.reshape(K // 128, 128, nb, ncols_per_block)
    x = x.transpose(2, 1, 0, 3)
    return np.ascontiguousarray(x).reshape(nb, 128, (K // 128) * ncols_per_block)


def host_shared(inp, layers):
    off, R = ptab_layout()
    ptab = np.zeros((128, R), np.float32)

    def put(key, arr):
        ptab[:, off[key]:off[key] + arr.shape[1]] = arr

    for i in range(DEPTH):
        put(("mix_pre", i), _cols(inp["mix_pre_g"][i], 8))
        put(("mix_post", i), _cols(inp["mix_post_g"][i], 8))
        put(("ffn_pre", i), _cols(inp["ffn_pre_g"][i], 8))
        put(("ffn_post", i), _cols(inp["ffn_post_g"][i], 8))
        put(("ple_g", i), _cols(inp["ple_norm_g"][i], 8))
        cv = np.concatenate([_cols(inp["ffn_conv"][i][jj], 44) for jj in range(3)], axis=1)
        put(("conv", i), cv)
    for j in range(2):
        put(("a_lng", j), _cols(inp["a_ln_g"][j], 16))
        put(("a_lnb", j), _cols(inp["a_ln_b"][j], 16))
    dw = np.asarray(inp["c_dw"][0], np.float32)
    dwc = dw.reshape(31, 8, 128).transpose(2, 1, 0).reshape(128, 248)
    put("c_dw", np.ascontiguousarray(dwc))
    put("c_dwb", _cols(inp["c_dw_b"][0], 8))
    put("c_lng", _cols(inp["c_ln_g"][0], 8))
    put("c_lnb", _cols(inp["c_ln_b"][0], 8))
    sh = {"ptab": ptab, "ident": np.eye(128, dtype=np.float32)}
    for i in layers:
        wu = np.asarray(inp["ffn_w_up"][i], np.float32)
        g = wu[:, :FF].reshape(D, NFC, 128)
        v = wu[:, FF:].reshape(D, NFC, 128)
        gv = np.concatenate([g, v], axis=2).reshape(D, NFC * 256)
        sh["wup%d" % i] = _kc_tile(gv, 256)
        wd = np.asarray(inp["ffn_w_down"][i], np.float32)
        x = wd.reshape(NFC, 128, 8, 128).transpose(2, 1, 0, 3)
        sh["wdn%d" % i] = np.ascontiguousarray(x).reshape(8, 128, NFC * 128)
        sh["wg%d" % i] = _kc_tile(np.asarray(inp["ple_w_gate"][i], np.float32), 128)
        wp = np.asarray(inp["ple_w_proj"][i], np.float32)
        sh["wp%d" % i] = np.ascontiguousarray(wp.reshape(2, 128, D).transpose(1, 0, 2)).reshape(128, 2 * D)
        kind, j = i % 3, i // 3
        if kind == 0:
            win = np.asarray(inp["a_w_in"][j], np.float32)
            sh["a_wu%d" % j] = _kc_tile(win[:, :2048], 128)
            sh["a_wv%d" % j] = _kc_tile(win[:, 2048:], 512)
            wo = np.asarray(inp["a_w_out"][j], np.float32)
            x = wo.reshape(16, 128, 8, 128).transpose(2, 1, 0, 3)
            sh["a_wo%d" % j] = np.ascontiguousarray(x).reshape(8, 128, 16 * 128)
            ws = np.asarray(inp["a_w_s"][j], np.float32)
            sh["a_ws%d" % j] = np.ascontiguousarray(ws.transpose(2, 0, 1)).reshape(128, 8 * 128)
            bs = np.asarray(inp["a_b_s"][j], np.float32).reshape(1, 8 * 128)
            sh["a_bs%d" % j] = np.ascontiguousarray(np.broadcast_to(bs, (128, 8 * 128)))
        elif kind == 1:
            wq = np.asarray(inp["b_w_qkv"][0], np.float32)
            q = wq[:, :D].reshape(D, 8, 128)
            k = wq[:, D:2 * D].reshape(D, 8, 128)
            qk = np.concatenate([q, k], axis=2).reshape(D, 8 * 256)
            sh["b_wqk"] = _kc_tile(qk, 256)
            sh["b_wv"] = _kc_tile(np.ascontiguousarray(wq[:, 2 * D:]), 256)
            wo = np.asarray(inp["b_w_out"][0], np.float32)
            x = wo.reshape(8, 128, 8, 128).transpose(2, 1, 0, 3)
            sh["b_wo"] = np.ascontiguousarray(x).reshape(8, 128, 8 * 128)
            rb = np.asarray(inp["b_rel_bias"][0], np.float32)
            qq = np.arange(128)[:, None]
            kk = np.arange(640)[None, :]
            idx = np.clip(qq + 512 - kk, -128, 128) + 128
            bfull = rb[:, idx]
            sh["b_bias"] = np.ascontiguousarray(bfull.transpose(1, 0, 2)).reshape(128, 16 * 640)
        else:
            wi = np.asarray(inp["c_w_in"][0], np.float32)
            a = wi[:, :D].reshape(D, 8, 128)
            g = wi[:, D:].reshape(D, 8, 128)
            ag = np.concatenate([a, g], axis=2).reshape(D, 8 * 256)
            sh["c_wi"] = _kc_tile(ag, 256)
            wo = np.asarray(inp["c_w_out"][0], np.float32)
            x = wo.reshape(8, 128, 8, 128).transpose(2, 1, 0, 3)
            sh["c_wo"] = np.ascontiguousarray(x).reshape(8, 128, 8 * 128)
    return sh


def run_layers(hT_in, p, shared, layers, trace=False, dbg=()):
    nc = Builder(layers, dbg).build()
    in_maps = []
    for b in range(8):
        m = dict(shared)
        m["xT"] = hT_in[b]
        for i in layers:
            m["pT%d" % i] = np.ascontiguousarray(p[i, b].T).reshape(2, 128, S)
        in_maps.append(m)
    res = run_bass_kernel_spmd(nc, in_maps, core_ids=list(range(8)), trace=trace)
    out = np.stack([res.results[b]["yT"] for b in range(8)])
    return out, res


def kernel(**inputs):
    inp = {k: np.asarray(v) for k, v in inputs.items()}
    layers = list(range(DEPTH))
    x = inp["x"].astype(np.float32, copy=False)
    hT = np.ascontiguousarray(x.transpose(0, 2, 1)).reshape(8, 8, 128, S)
    shared = host_shared(inp, layers)
    out, _ = run_layers(hT, inp["p"].astype(np.float32, copy=False), shared, layers)
    y = out.reshape(8, D, S).transpose(0, 2, 1)
    return np.ascontiguousarray(y).astype(np.float32, copy=False)
```

```python
import numpy as np
from contextlib import ExitStack
import concourse.bass as bass
import concourse.mybir as mybir
from concourse.bass_utils import run_bass_kernel_spmd

F32 = mybir.dt.float32
BF16 = mybir.dt.bfloat16
AF = mybir.ActivationFunctionType
ALU = mybir.AluOpType
AX = mybir.AxisListType

D = 1024
S = 2048
T = 512
NT = S // T
FF = 2816
NFC = FF // 128
DEPTH = 4
EPS = 1e-6
NEG = -1e30


class Buf:
    __slots__ = ("name", "last_w", "readers", "dma_readers", "dma_sem", "dma_cnt", "const", "excl")

    fence = {}

    def __init__(self, name, const=False, excl=False):
        self.name = name
        self.excl = excl
        self.last_w = None
        self.readers = dict(Buf.fence)
        self.dma_readers = []
        self.dma_sem = None
        self.dma_cnt = 0
        self.const = const


class SemSlot:
    __slots__ = ("sem", "cnt")

    def __init__(self):
        self.sem = None
        self.cnt = 0


class Op:
    __slots__ = ("eng", "fn", "deps", "sig", "sigval", "is_dma", "dma_sem", "dma_val", "idx")

    def __init__(self, eng, fn, is_dma):
        self.eng = eng
        self.fn = fn
        self.deps = []
        self.sig = False
        self.sigval = 0
        self.is_dma = is_dma
        self.dma_sem = None
        self.dma_val = 0


class Prog:
    ENGS = ("pe", "act", "dve", "pool", "sp")

    def __init__(self, nc):
        self.nc = nc
        self.ops = {e: [] for e in self.ENGS}
        self.n = 0
        self.semslots = {}
        self.out_dma_ops = []

    def _add(self, op, reads, writes):
        deps = []
        for b in reads:
            w = b.last_w
            if w is not None:
                deps.append((w, True))
            if b.excl:
                for e_, r in b.readers.items():
                    if e_ != op.eng:
                        deps.append((r, False))
        for b in writes:
            w = b.last_w
            if w is not None and not (op.is_dma and w.is_dma):
                deps.append((w, False))
            for r in b.readers.values():
                deps.append((r, False))
            for r in b.dma_readers:
                deps.append((r, False))
        seen = set()
        for d, raw in deps:
            if d is op:
                continue
            if (not d.is_dma) and (not op.is_dma) and d.eng == op.eng:
                if op.eng == "pe":
                    continue
            k = id(d)
            if k in seen:
                continue
            seen.add(k)
            op.deps.append(d)
            if not d.is_dma:
                d.sig = True
        for b in writes:
            b.last_w = op
            b.readers = {}
            b.dma_readers = []
        for b in reads:
            if b.const or b in writes:
                continue
            if op.is_dma:
                b.dma_readers.append(op)
            else:
                b.readers[op.eng] = op
        op.idx = self.n
        self.n += 1
        self.ops[op.eng].append(op)
        return op

    @staticmethod
    def _flat(xs):
        out = []
        for x in xs:
            if isinstance(x, (list, tuple)):
                out.extend(Prog._flat(x))
            else:
                out.append(x)
        return out

    def op(self, eng, fn, reads=(), writes=()):
        return self._add(Op(eng, fn, False), self._flat(reads), self._flat(writes))

    def dma(self, eng, fn, reads=(), writes=(), is_output=False):
        o = Op(eng, fn, True)
        reads, writes = self._flat(reads), self._flat(writes)
        dst = writes[0]
        slot = self.semslots.get(dst.name)
        if slot is None:
            slot = self.semslots[dst.name] = SemSlot()
        slot.cnt += 16
        o.dma_sem = slot
        o.dma_val = slot.cnt
        self._add(o, list(reads), list(writes))
        if is_output:
            self.out_dma_ops.append(o)
        return o

    def emit(self, stack):
        nc = self.nc
        sems = {}
        for e in ("pe", "act", "dve", "pool"):
            sems[e] = stack.enter_context(nc.semaphore("s_" + e))
        for i, slot in enumerate(self.semslots.values()):
            slot.sem = stack.enter_context(nc.semaphore("d%d" % i))
        for e in ("pe", "act", "dve", "pool"):
            c = 0
            for o in self.ops[e]:
                if o.is_dma:
                    continue
                if o.sig:
                    c += 1
                    o.sigval = c
        out_ops = self.out_dma_ops

        def run(engh, ename):
            waited = {}

            def wait(sem, val):
                k = id(sem)
                if waited.get(k, 0) >= val:
                    return
                waited[k] = val
                engh.wait_ge(sem, val)

            for o in self.ops[ename]:
                for d in o.deps:
                    if d.is_dma:
                        wait(d.dma_sem.sem, d.dma_val)
                    else:
                        wait(sems[d.eng], d.sigval)
                ins = o.fn(engh)
                if o.is_dma:
                    ins.then_inc(o.dma_sem.sem, 16)
                elif o.sig:
                    ins.then_inc(sems[ename], 1)
            if ename == "sp":
                for o in out_ops:
                    wait(o.dma_sem.sem, o.dma_val)

        block = stack.enter_context(nc.Block())

        @block.tensor
        def _(e):
            run(e, "pe")

        @block.scalar
        def _(e):
            run(e, "act")

        @block.vector
        def _(e):
            run(e, "dve")

        @block.gpsimd
        def _(e):
            run(e, "pool")

        @block.sync
        def _(e):
            run(e, "sp")


def ptab_layout():
    off = {}
    c = 0
    for i in range(DEPTH):
        for nm in ("mix_pre", "mix_post", "ffn_pre", "ffn_post", "ple_g"):
            off[(nm, i)] = c
            c += 8
        off[("conv", i)] = c
        c += 132
    for j in range(2):
        off[("a_lng", j)] = c
        c += 16
        off[("a_lnb", j)] = c
        c += 16
    off["c_dw"] = c
    c += 248
    off["c_dwb"] = c
    c += 8
    off["c_lng"] = c
    c += 8
    off["c_lnb"] = c
    c += 8
    return off, c


class StopBuild(Exception):
    pass


class Stream:
    def __init__(self, B, name, n, free, srcs):
        self.B = B
        self.n = n
        self.srcs = list(srcs)
        self.aps = [B.A.alloc(BF16, *free) for _ in range(n)]
        self.bufs = [Buf("%s%d" % (name, i)) for i in range(n)]
        self.issued = 0
        self.taken = 0
        for _ in range(n - 1):
            self._issue()

    def _issue(self):
        k = self.issued
        if k >= len(self.srcs):
            return
        self.issued += 1
        ap, bf = self.aps[k % self.n], self.bufs[k % self.n]
        flat = ap.rearrange("p a b -> p (a b)") if len(ap.shape) == 3 else ap
        if "fastdma" in self.B.dbg:
            n8 = flat.shape[-1] // 8
            self.B.dma_cast(flat[:, 0:n8], self.srcs[k][:, 0:n8], [], [bf])
            return
        self.B.dma_cast(flat, self.srcs[k], [], [bf])

    def next(self):
        k = self.taken
        self.taken += 1
        self._issue()
        return self.aps[k % self.n], self.bufs[k % self.n]


class Arena:
    def __init__(self, ap_all, base, limit):
        self.all = ap_all
        self.off = base
        self.limit = limit

    def mark(self):
        return self.off

    def reset(self, m):
        self.off = m

    def alloc(self, dtype, *free):
        isz = 4 if dtype == F32 else 2
        n = 1
        for f in free:
            n *= f
        nb = n * isz
        off = (self.off + 63) // 64 * 64
        assert off + nb <= self.limit, ("SBUF arena overflow", off + nb, self.limit)
        self.off = off + nb
        v = self.all[:, off // 2:(off + nb) // 2]
        if dtype == F32:
            v = v.bitcast(F32)
        if len(free) == 2:
            v = v.rearrange("p (a b) -> p a b", a=free[0])
        elif len(free) == 3:
            v = v.rearrange("p (a b c) -> p a b c", a=free[0], b=free[1])
        return v


class Builder:
    def __init__(self, layers, dbg=()):
        self.layers = list(layers)
        self.dbg = set(dbg)
        self.nc = bass.Bass("TRN2", target_bir_lowering=False)
        Buf.fence = {}
        self.P = Prog(self.nc)
        self.poff, self.pcols = ptab_layout()
        self._bank = 0
        self._stat = 0

    def mm(self, out, lhsT, rhs, start, stop, r, w, **kw):
        self.P.op("pe", lambda e: e.matmul(out, lhsT=lhsT, rhs=rhs, start=start, stop=stop, **kw), r, w)

    def act(self, out, in_, func, r, w, bias=None, scale=None, accum=None):
        kw = {}
        if bias is not None:
            kw["bias"] = bias
        if scale is not None:
            kw["scale"] = scale
        if accum is not None:
            kw["accum_out"] = accum
        self.P.op("act", lambda e: e.activation(out=out, in_=in_, func=func, **kw), r, w)

    def ts(self, eng, out, in0, s1, s2, op0, op1, r, w):
        if op1 is None and eng == "pool":
            s2, op1 = 1.0, ALU.mult
        if op1 is None:
            self.P.op(eng, lambda e: e.tensor_scalar(out=out, in0=in0, scalar1=s1, scalar2=None, op0=op0), r, w)
        else:
            self.P.op(eng, lambda e: e.tensor_scalar(out=out, in0=in0, scalar1=s1, scalar2=s2, op0=op0, op1=op1), r, w)

    def stt(self, out, in0, scalar, in1, op0, op1, r, w):
        self.P.op("dve", lambda e: e.scalar_tensor_tensor(out=out, in0=in0, scalar=scalar, in1=in1, op0=op0, op1=op1), r, w)

    def tt(self, eng, out, in0, in1, op, r, w):
        self.P.op(eng, lambda e: e.tensor_tensor(out=out, in0=in0, in1=in1, op=op), r, w)

    def cp(self, eng, out, in_, r, w):
        if eng == "act":
            self.P.op("act", lambda e: e.copy(out=out, in_=in_), r, w)
        else:
            self.P.op(eng, lambda e: e.tensor_copy(out=out, in_=in_), r, w)

    def recip(self, out, in_, r, w):
        self.P.op("dve", lambda e: e.reciprocal(out=out, in_=in_), r, w)

    def memset(self, eng, ap, val, w):
        self.P.op(eng, lambda e: e.memset(ap, val), [], w)

    def dma_cast(self, out, in_, r, w):
        self.P.dma("pool", lambda e: e.dma_start(out=out, in_=in_), r, w)

    def dma_plain(self, out, in_, r, w, is_output=False):
        self.P.dma("sp", lambda e: e.dma_start(out=out, in_=in_), r, w, is_output=is_output)

    def bank(self):
        b = self._bank
        self._bank = (b + 1) % 6
        return b

    def statbank(self):
        b = 6 + self._stat
        self._stat ^= 1
        return b

    def psb(self, b, lo=0, hi=512):
        return self.ps[:, b * 512 + lo:b * 512 + hi]

    def build(self):
        nc = self.nc
        st = ExitStack()
        with st:
            self.declare_dram()
            self.sb_all = st.enter_context(nc.sbuf_tensor("sb_all", [128, 106300], BF16))
            self.ps = st.enter_context(nc.psum_tensor("ps_all", [128, 4096], F32))
            self.PBK = [Buf("psk%d" % i, excl=True) for i in range(32)]
            self.PB = [self.PBK[4 * i:4 * i + 4] for i in range(8)]
            self.A = Arena(self.sb_all, 0, 106300 * 2)
            self.setup_persistent()
            for i in self.layers:
                self.layer(i)
            self.store_output()
            self.P.emit(st)
        return nc

    def declare_dram(self):
        nc = self.nc
        dt = lambda n, s: nc.dram_tensor(n, s, F32, kind="ExternalInput").ap()
        self.d = {}
        self.d["xT"] = dt("xT", [8, 128, S])
        self.d["ptab"] = dt("ptab", [128, self.pcols])
        self.d["ident"] = dt("ident", [128, 128])
        for i in self.layers:
            self.d["pT%d" % i] = dt("pT%d" % i, [2, 128, S])
            self.d["wup%d" % i] = dt("wup%d" % i, [NFC, 2, 128, 8 * 128])
            self.d["wdn%d" % i] = dt("wdn%d" % i, [8, 128, NFC * 128])
            self.d["wg%d" % i] = dt("wg%d" % i, [8, 128, 8 * 128])
            self.d["wp%d" % i] = dt("wp%d" % i, [128, 2 * D])
            kind, j = i % 3, i // 3
            if kind == 0:
                self.d["a_wu%d" % j] = dt("a_wu%d" % j, [16, 128, 8 * 128])
                self.d["a_wv%d" % j] = dt("a_wv%d" % j, [4, 128, 8 * 512])
                self.d["a_wo%d" % j] = dt("a_wo%d" % j, [8, 128, 16 * 128])
                self.d["a_ws%d" % j] = dt("a_ws%d" % j, [128, 8 * 128])
                self.d["a_bs%d" % j] = dt("a_bs%d" % j, [128, 8 * 128])
            elif kind == 1:
                self.d["b_wqk"] = dt("b_wqk", [8, 128, 8 * 256])
                self.d["b_wv"] = dt("b_wv", [4, 128, 8 * 256])
                self.d["b_wo"] = dt("b_wo", [8, 128, 8 * 128])
                self.d["b_bias"] = dt("b_bias", [128, 16 * 640])
            else:
                self.d["c_wi"] = dt("c_wi", [8, 128, 8 * 256])
                self.d["c_wo"] = dt("c_wo", [8, 128, 8 * 128])
        self.yT = nc.dram_tensor("yT", [8, 128, S], F32, kind="ExternalOutput").ap()

    def setup_persistent(self):
        A = self.A
        self.h = A.alloc(F32, 8, S)
        self.Bh = [[Buf("h%d_%d" % (c, t)) for t in range(NT)] for c in range(8)]
        self.ptab = A.alloc(F32, self.pcols)
        self.Bptab = Buf("ptab", const=True)
        self.ident = A.alloc(BF16, 128)
        self.onesb = A.alloc(BF16, 128)
        self.epsc = A.alloc(F32, 1)
        self.Bconst = Buf("const", const=True)
        self.sq = [A.alloc(BF16, T) for _ in range(3)]
        self.Bsq = [Buf("sq%d" % i) for i in range(3)]
        self._sq = 0
        self.rstds = [A.alloc(F32, T) for _ in range(3)]
        self.Brstds = [Buf("rstd%d" % k) for k in range(3)]
        self.rstd, self.Brstd = self.rstds[0], self.Brstds[0]
        self.mres = A.alloc(F32, 8, T)
        self.Bmres = [Buf("mres%d" % c) for c in range(8)]
        self.hns = [A.alloc(BF16, 8, T) for _ in range(2)]
        self.hn1_off = A.off - 8 * T * 2
        self.Bhns = [[Buf("hn%d_%d" % (k, c)) for c in range(8)] for k in range(2)]
        self.hn, self.Bhn = self.hns[0], self.Bhns[0]
        self.rtmp = [A.alloc(F32, T) for _ in range(3)]
        self.Brtmp = [Buf("rtmp%d" % i) for i in range(3)]
        self._rt = 0
        self.phase_mark = A.mark()
        self.dma_plain(self.ptab, self.d["ptab"], [], [self.Bptab])
        for c in range(8):
            self.dma_plain(self.h[:, c, :], self.d["xT"][c], [], self.Bh[c])
        self.memset("pool", self.onesb, 1.0, [self.Bconst])
        self.memset("pool", self.epsc, EPS, [self.Bconst])
        self.dma_cast(self.ident, self.d["ident"], [], [self.Bconst])

    def pcol(self, key, c, n=1):
        o = self.poff[key] + c
        return self.ptab[:, o:o + n]

    def nextsq(self):
        i = self._sq
        self._sq = (i + 1) % 3
        return self.sq[i], self.Bsq[i]

    def nextrt(self):
        i = self._rt
        self._rt = (i + 1) % 3
        return self.rtmp[i], self.Brtmp[i]

    def defer_mm(self, *args, **kw):
        self.flush_mm()
        self._pend_mm = (args, kw)

    def flush_mm(self):
        p = getattr(self, "_pend_mm", None)
        if p is not None:
            self._pend_mm = None
            self.mm(*p[0], **p[1])

    def stat_add(self, sb, src, src_bufs, c, n=8):
        sq, Bsq = self.nextsq()
        self.act(sq, src, AF.Square, src_bufs, [Bsq])
        self.defer_mm(self.psb(sb), self.onesb, sq, c == 0, c == n - 1, [Bsq, self.Bconst], [self.PB[sb]])

    def finish_rstd(self, sb, dim=D, role=0):
        self.flush_mm()
        rstd, Brstd = self.rstds[role], self.Brstds[role]
        self.act(rstd, self.psb(sb), AF.Sqrt, [self.PB[sb], self.Bconst], [Brstd], bias=self.epsc, scale=1.0 / dim)
        self.recip(rstd, rstd, [Brstd], [Brstd])
        return rstd, Brstd

    def pre_norm_gen(self, t, gkey, k=0):
        hn, Bhn = self.hns[k], self.Bhns[k]
        sb = self.statbank()
        tsl = slice(t * T, (t + 1) * T)
        for c in range(8):
            self.stat_add(sb, self.h[:, c, tsl], [self.Bh[c][t]], c)
            yield
        rstd, Brstd = self.finish_rstd(sb, role=0)
        yield
        for c in range(8):
            self.stt(hn[:, c, :], self.h[:, c, tsl], self.pcol(gkey, c), rstd, ALU.mult, ALU.mult,
                     [self.Bh[c][t], Brstd, self.Bptab], [Bhn[c]])
            if c % 2:
                yield

    def pre_norm(self, t, gkey, k=0):
        for _ in self.pre_norm_gen(t, gkey, k):
            pass

    def post_norm_gen(self, t, gkey, sb, role, after=None):
        rstd, Brstd = self.finish_rstd(sb, role=role)
        tsl = slice(t * T, (t + 1) * T)
        yield
        rts = {}
        for i in range(10):
            if i < 8:
                rt, Brt = self.nextrt()
                rts[i] = (rt, Brt)
                self.stt(rt, self.mres[:, i, :], self.pcol(gkey, i), rstd, ALU.mult, ALU.mult,
                         [self.Bmres[i], Brstd, self.Bptab], [Brt])
            if 1 <= i < 9:
                c = i - 1
                rt, Brt = rts.pop(c)
                self.tt("pool", self.h[:, c, tsl], self.h[:, c, tsl], rt, ALU.add, [self.Bh[c][t], Brt], [self.Bh[c][t]])
            if 2 <= i < 10 and after is not None:
                after(i - 2)
            yield

    def bg_start(self, t, gkey, sb, role=1):
        g = self.post_norm_gen(t, gkey, sb, role)
        next(g)
        self._bg = getattr(self, "_bg", [])
        self._bg.append(g)

    def bg_step(self):
        for g in list(getattr(self, "_bg", [])):
            try:
                next(g)
            except StopIteration:
                self._bg.remove(g)

    def bg_drain(self):
        while getattr(self, "_bg", []):
            self.bg_step()

    def post_norm_residual(self, t, gkey, sb, role=1):
        for _ in self.post_norm_gen(t, gkey, sb, role):
            pass

    def evac_mres(self, b, dc, sb):
        self.cp("dve", self.mres[:, dc, :], self.psb(b), [self.PB[b]], [self.Bmres[dc]])
        self.stat_add(sb, self.mres[:, dc, :], [self.Bmres[dc]], dc)

    def make_slots(self, name, n, *free):
        aps = [self.A.alloc(BF16, *free) for _ in range(n)]
        bufs = [Buf("%s%d" % (name, i)) for i in range(n)]
        return {"aps": aps, "bufs": bufs, "i": 0, "n": n}

    def load_slot(self, slots, src):
        i = slots["i"]
        slots["i"] = (i + 1) % slots["n"]
        ap, bf = slots["aps"][i], slots["bufs"][i]
        flat = ap
        if len(ap.shape) == 3:
            flat = ap.rearrange("p a b -> p (a b)")
        self.dma_cast(flat, src, [], [bf])
        return ap, bf

    def stream(self, name, n, free, srcs):
        return Stream(self, name, n, free, srcs)

    def stop(self, tag):
        if tag in self.dbg:
            raise StopBuild()

    def layer(self, i):
        try:
            self._layer(i)
        except StopBuild:
            pass

    def new_phase(self):
        self.bg_drain()
        self.flush_mm()
        self.A.reset(self.phase_mark)
        f = {}
        for e in ("pe", "act", "dve", "pool"):
            for o in reversed(self.P.ops[e]):
                if not o.is_dma:
                    f[e] = o
                    break
        Buf.fence = f

    def _layer(self, i):
        kind, j = i % 3, i // 3
        A = self.A
        self.new_phase()
        if "prenorm" in self.dbg:
            self.pre_norm(0, ("mix_pre", i))
            return
        if "nomix" in self.dbg:
            pass
        elif kind == 0:
            self.mixer_a(i, j)
        elif kind == 1:
            self.mixer_b(i)
        else:
            self.mixer_c(i)
        self.new_phase()
        if "noffn" not in self.dbg:
            self.ffn_phase(i)

    def ffn_phase(self, i):
        A = self.A
        actb = A.alloc(BF16, NFC, T)
        Bact = [Buf("act%d" % f) for f in range(NFC)]
        wupS = self.stream("wup", 8, (8, 128), [self.d["wup%d" % i][fc][gv] for _t in range(NT) for fc in range(NFC) for gv in range(2)])
        HF = NFC // 2
        wdnS = self.stream("wdn", 5, (HF, 128), [self.d["wdn%d" % i][dc][:, hf * HF * 128:(hf + 1) * HF * 128]
                                                  for _t in range(NT) for dc in range(8) for hf in range(2)])
        wgS = self.stream("wg", 4, (8, 128), [self.d["wg%d" % i][dc] for _t in range(NT) for dc in range(8)])
        cg = [A.alloc(F32, T) for _ in range(2)]
        cv = [A.alloc(F32, T) for _ in range(2)]
        sg = [A.alloc(F32, T) for _ in range(2)]
        Bcg = [Buf("cg%d" % k) for k in range(2)]
        Bcv = [Buf("cv%d" % k) for k in range(2)]
        Bsg = [Buf("sg%d" % k) for k in range(2)]
        halo = [A.alloc(F32, 2 * NFC, 2) for _ in range(2)]
        Bhalo = [[Buf("halo%d_%d" % (k, q)) for q in range(2 * NFC)] for k in range(2)]
        bnd = [A.alloc(F32, 3, 2 * NFC) for _ in range(1)] * 2
        Bbnd = [Buf("bnd")] * 2
        x01 = A.alloc(F32, 2 * NFC, 2)
        c01 = A.alloc(F32, 2 * NFC, 2)
        s01 = A.alloc(F32, NFC, 2)
        Bx01, Bc01, Bs01 = Buf("x01"), Buf("c01"), Buf("s01")
        hb = A.alloc(BF16, 8, T)
        Bhb = [Buf("hb%d" % c) for c in range(8)]
        pt = [A.alloc(BF16, 2, T) for _ in range(2)]
        Bpt = [Buf("pt%d" % k) for k in range(2)]
        wp = A.alloc(BF16, 2, D)
        Bwp = Buf("wp")
        gate = [A.alloc(F32, T) for _ in range(1)]
        Bgate = [Buf("gate%d" % k) for k in range(1)]
        self.dma_cast(wp.rearrange("p a b -> p (a b)"), self.d["wp%d" % i], [], [Bwp])
        cbase = self.poff[("conv", i)]

        def tap(jj, q):
            o = cbase + jj * 44 + q
            return self.ptab[:, o:o + 1]

        def step(bg):
            for g in list(bg):
                try:
                    next(g)
                except StopIteration:
                    bg.remove(g)

        def drain(bg):
            while bg:
                step(bg)

        def load_pt(t):
            ptt, Bptt = pt[t % 2], Bpt[t % 2]
            tsl = slice(t * T, (t + 1) * T)
            for kc in range(2):
                self.dma_cast(ptt[:, kc, :], self.d["pT%d" % i][kc][:, tsl], [], [Bptt])

        def stage_A(t):
            return self.pre_norm_gen(t, ("ffn_pre", i), k=t % 2)

        def stage_B(t, bg):
            hn, Bhn = self.hns[t % 2], self.Bhns[t % 2]
            if t > 0:
                ho, Bho = halo[(t - 1) % 2], Bhalo[(t - 1) % 2]
                bd, Bbd = bnd[t % 2], Bbnd[t % 2]
                W0 = self.ptab[:, cbase:cbase + 44]
                W1 = self.ptab[:, cbase + 44:cbase + 88]
                self.tt("dve", bd[:, 2, :], ho[:, :, 1], W1, ALU.mult, Bho + [self.Bptab], [Bbd])
                self.tt("dve", bd[:, 0, :], ho[:, :, 0], W0, ALU.mult, Bho + [self.Bptab], [Bbd])
                self.tt("dve", bd[:, 0, :], bd[:, 0, :], bd[:, 2, :], ALU.add, [Bbd], [Bbd])
                self.tt("dve", bd[:, 1, :], ho[:, :, 1], W0, ALU.mult, Bho + [self.Bptab], [Bbd])
            for fc in range(NFC):
                wg_, Bwg_ = wupS.next()
                k2 = fc % 2
                bg_ = self.bank()
                bv_ = self.bank()
                for kc in range(8):
                    self.mm(self.psb(bg_), wg_[:, kc, :], hn[:, kc, :], kc == 0, kc == 7, [Bwg_, Bhn[kc]], [self.PB[bg_]])
                wv_, Bwv_ = wupS.next()
                for kc in range(8):
                    self.mm(self.psb(bv_), wv_[:, kc, :], hn[:, kc, :], kc == 0, kc == 7, [Bwv_, Bhn[kc]], [self.PB[bv_]])
                lo = 0 if t == 0 else 2
                for (b, q, cbuf, Bc) in ((bg_, fc, cg[k2], Bcg[k2]), (bv_, NFC + fc, cv[k2], Bcv[k2])):
                    pb = self.PB[b]
                    self.act(cbuf[:, lo:T], self.psb(b, lo, T), AF.Copy, [pb, self.Bptab], [Bc], scale=tap(2, q))
                    if t > 0:
                        self.cp("act", x01[:, q, 0:2], self.psb(b, 0, 2), [pb], [Bx01])
                    if t < NT - 1:
                        self.cp("act", halo[t % 2][:, q, 0:2], self.psb(b, T - 2, T), [pb], [Bhalo[t % 2][q]])
                    self.stt(cbuf[:, lo + 1 - lo // 2:T], self.psb(b, lo // 2, T - 1), tap(1, q), cbuf[:, lo + 1 - lo // 2:T],
                             ALU.mult, ALU.add, [pb, Bc, self.Bptab], [Bc])
                    self.stt(cbuf[:, 2:T], self.psb(b, 0, T - 2), tap(0, q), cbuf[:, 2:T], ALU.mult, ALU.add,
                             [pb, Bc, self.Bptab], [Bc])
                self.act(sg[k2][:, lo:T], cg[k2][:, lo:T], AF.Silu, [Bcg[k2]], [Bsg[k2]])
                self.tt("pool", actb[:, fc, lo:T], sg[k2][:, lo:T], cv[k2][:, lo:T], ALU.mult, [Bsg[k2], Bcv[k2]], [Bact[fc]])
                step(bg)
            if t > 0:
                bd, Bbd = bnd[0], Bbnd[0]
                W0 = self.ptab[:, cbase:cbase + 44]
                W1 = self.ptab[:, cbase + 44:cbase + 88]
                W2 = self.ptab[:, cbase + 88:cbase + 132]
                rP = [Bx01, self.Bptab, Bbd]
                self.tt("dve", c01[:, :, 0], x01[:, :, 0], W2, ALU.mult, rP, [Bc01])
                self.tt("dve", c01[:, :, 0], c01[:, :, 0], bd[:, 0, :], ALU.add, rP + [Bc01], [Bc01])
                self.tt("dve", c01[:, :, 1], x01[:, :, 1], W2, ALU.mult, rP, [Bc01])
                self.tt("dve", bd[:, 2, :], x01[:, :, 0], W1, ALU.mult, rP, [Bbd])
                self.tt("dve", c01[:, :, 1], c01[:, :, 1], bd[:, 2, :], ALU.add, rP + [Bc01], [Bc01])
                self.tt("dve", c01[:, :, 1], c01[:, :, 1], bd[:, 1, :], ALU.add, rP + [Bc01], [Bc01])
                self.act(s01, c01[:, 0:NFC, :], AF.Silu, [Bc01], [Bs01])
                self.tt("dve", actb[:, :, 0:2], s01, c01[:, NFC:2 * NFC, :], ALU.mult, [Bs01, Bc01], Bact)
            drain(bg)

        def stage_C(t, bg):
            sb = self.statbank()
            for dc in range(8):
                b = self.bank()
                for hf in range(2):
                    w, Bw = wdnS.next()
                    for f2 in range(HF):
                        fc = hf * HF + f2
                        self.mm(self.psb(b), w[:, f2, :], actb[:, fc, :], fc == 0, fc == NFC - 1, [Bw, Bact[fc]], [self.PB[b]])
                step(bg)
                self.evac_mres(b, dc, sb)
            drain(bg)
            return sb

        def stage_D(t, sb):
            tsl = slice(t * T, (t + 1) * T)

            def after(c):
                self.cp("act", hb[:, c, :], self.h[:, c, tsl], [self.Bh[c][t]], [Bhb[c]])
            g = self.post_norm_gen(t, ("ffn_post", i), sb, 1, after=after)
            next(g)
            return g

        def stage_E(t):
            ptt, Bptt = pt[t % 2], Bpt[t % 2]
            if t + 1 < NT:
                load_pt(t + 1)
            sb = self.statbank()
            for dc in range(8):
                w, Bw = wgS.next()
                bgt = self.bank()
                be = self.bank()
                for kc in range(8):
                    self.mm(self.psb(bgt), w[:, kc, :], hb[:, kc, :], kc == 0, kc == 7, [Bw, Bhb[kc]], [self.PB[bgt]])
                for kc in range(2):
                    self.mm(self.psb(be), wp[:, kc, dc * 128:(dc + 1) * 128], ptt[:, kc, :], kc == 0, kc == 1,
                            [Bwp, Bptt], [self.PB[be]])
                k2 = 0
                self.act(gate[k2], self.psb(bgt), AF.Sigmoid, [self.PB[bgt]], [Bgate[k2]])
                self.tt("dve", self.mres[:, dc, :], gate[k2], self.psb(be), ALU.mult, [Bgate[k2], self.PB[be]],
                        [self.Bmres[dc]])
                self.stat_add(sb, self.mres[:, dc, :], [self.Bmres[dc]], dc)
            return sb

        load_pt(0)
        drain([stage_A(0)])
        stage_B(0, [stage_A(1)])
        pend = []
        for t in range(NT):
            sb = stage_C(t, pend)
            pend = []
            bgl = [stage_D(t, sb)]
            if t + 2 < NT:
                bgl.append(stage_A(t + 2))
            if t + 1 < NT:
                stage_B(t + 1, bgl)
            else:
                drain(bgl)
            sbe = stage_E(t)
            g = self.post_norm_gen(t, ("ple_g", i), sbe, 2)
            next(g)
            pend = [g]
        drain(pend)
        self.flush_mm()

    def mixer_c(self, i):
        A = self.A
        ybuf = A.alloc(BF16, 8, 30 + T)
        Byb = [Buf("yb%d" % c) for c in range(8)]
        z = A.alloc(F32, 8, T)
        Bz = [Buf("z%d" % c) for c in range(8)]
        zb = [A.alloc(BF16, T) for _ in range(2)]
        Bzb = [Buf("zb%d" % k) for k in range(2)]
        actc = A.alloc(BF16, 8, T)
        Bac = [Buf("actc%d" % c) for c in range(8)]
        dg = [A.alloc(BF16, 31, 128) for _ in range(2)]
        Bdg = [Buf("dg%d" % k) for k in range(2)]
        wi = self.stream("cwi", 3, (8, 256), [self.d["c_wi"][c] for _t in range(NT) for c in range(8)])
        wo = self.stream("cwo", 3, (8, 128), [self.d["c_wo"][c] for _t in range(NT) for c in range(8)])
        sgm = [A.alloc(F32, T) for _ in range(2)]
        Bsgm = [Buf("sgm%d" % k) for k in range(2)]
        mean = A.alloc(F32, T)
        msq = A.alloc(F32, T)
        var = A.alloc(F32, T)
        nmr = A.alloc(F32, T)
        Bmean, Bmsq, Bvar, Bnmr = Buf("mean"), Buf("msq"), Buf("var"), Buf("nmr")
        t1 = [A.alloc(F32, T) for _ in range(2)]
        Bt1 = [Buf("ct1_%d" % k) for k in range(2)]
        for c in range(8):
            self.memset("pool", ybuf[:, c, 0:30], 0.0, [Byb[c]])

        def build_diag(c):
            k2_ = c % 2
            o_ = self.poff["c_dw"] + c * 31
            in1 = self.ptab[:, o_:o_ + 31].unsqueeze(2).to_broadcast([128, 31, 128])
            in0 = self.ident.unsqueeze(1).to_broadcast([128, 31, 128])
            self.tt("dve", dg[k2_], in0, in1, ALU.mult, [self.Bconst, self.Bptab], [Bdg[k2_]])

        self.pre_norm(0, ("mix_pre", i), k=0)
        for t in range(NT):
            hn, Bhn = self.hns[t % 2], self.Bhns[t % 2]
            sb1 = self.statbank()
            sb2 = self.statbank()
            if t == 0:
                build_diag(0)
            for c in range(8):
                if c + 1 < 8:
                    build_diag(c + 1)
                self.bg_step()
                w, Bw = wi.next()
                ba = self.bank()
                bg = self.bank()
                for kc in range(8):
                    self.mm(self.psb(ba), w[:, kc, 0:128], hn[:, kc, :], kc == 0, kc == 7, [Bw, Bhn[kc]], [self.PB[ba]])
                for kc in range(8):
                    self.mm(self.psb(bg), w[:, kc, 128:256], hn[:, kc, :], kc == 0, kc == 7, [Bw, Bhn[kc]], [self.PB[bg]])
                k2 = c % 2
                self.act(sgm[k2], self.psb(bg), AF.Sigmoid, [self.PB[bg]], [Bsgm[k2]])
                self.tt("dve", ybuf[:, c, 30:30 + T], sgm[k2], self.psb(ba), ALU.mult, [Bsgm[k2], self.PB[ba]], [Byb[c]])
                bc = self.bank()
                for jj in range(31):
                    self.mm(self.psb(bc), dg[k2][:, jj, :], ybuf[:, c, jj:jj + T], jj == 0, jj == 30, [Bdg[k2], Byb[c]], [self.PB[bc]])
                bias = self.pcol("c_dwb", c)
                self.act(z[:, c, :], self.psb(bc), AF.Identity, [self.PB[bc], self.Bptab], [Bz[c]], bias=bias)
                self.act(zb[k2], self.psb(bc), AF.Identity, [self.PB[bc], self.Bptab], [Bzb[k2]], bias=bias)
                self.flush_mm()
                self.mm(self.psb(sb1), self.onesb, zb[k2], c == 0, c == 7, [Bzb[k2], self.Bconst], [self.PB[sb1]])
                sq, Bsq = self.nextsq()
                self.act(sq, self.psb(bc), AF.Square, [self.PB[bc], self.Bptab], [Bsq], bias=bias)
                self.defer_mm(self.psb(sb2), self.onesb, sq, c == 0, c == 7, [Bsq, self.Bconst], [self.PB[sb2]])
                if t < NT - 1:
                    self.cp("pool", ybuf[:, c, 0:30], ybuf[:, c, T:T + 30], [Byb[c]], [Byb[c]])
            self.bg_drain()
            if t + 1 < NT:
                build_diag(0)
            self.flush_mm()
            self.ts("dve", mean, self.psb(sb1), 1.0 / D, None, ALU.mult, None, [self.PB[sb1]], [Bmean])
            self.act(msq, mean, AF.Square, [Bmean], [Bmsq])
            self.stt(var, self.psb(sb2), 1.0 / D, msq, ALU.mult, ALU.subtract, [self.PB[sb2], Bmsq], [Bvar])
            self.act(self.rstd, var, AF.Sqrt, [Bvar, self.Bconst], [self.Brstd], bias=self.epsc)
            self.recip(self.rstd, self.rstd, [self.Brstd], [self.Brstd])
            self.stt(nmr, mean, -1.0, self.rstd, ALU.mult, ALU.mult, [Bmean, self.Brstd], [Bnmr])
            for c in range(8):
                k2 = c % 2
                self.tt("dve", t1[k2], z[:, c, :], self.rstd, ALU.mult, [Bz[c], self.Brstd], [Bt1[k2]])
                self.tt("pool", t1[k2], t1[k2], nmr, ALU.add, [Bt1[k2], Bnmr], [Bt1[k2]])
                self.act(actc[:, c, :], t1[k2], AF.Silu, [Bt1[k2], self.Bptab], [Bac[c]],
                         scale=self.pcol("c_lng", c), bias=self.pcol("c_lnb", c))
            if t + 1 < NT:
                self.pre_norm(t + 1, ("mix_pre", i), k=(t + 1) % 2)
            sb = self.statbank()
            for dc in range(8):
                w, Bw = wo.next()
                b = self.bank()
                for c in range(8):
                    self.mm(self.psb(b), w[:, c, :], actc[:, c, :], c == 0, c == 7, [Bw, Bac[c]], [self.PB[b]])
                self.evac_mres(b, dc, sb)
            self.bg_start(t, ("mix_post", i), sb)
            if t == NT - 1:
                self.bg_drain()

    def mixer_a(self, i, j):
        A = self.A
        u = A.alloc(BF16, 16, T)
        Bu = [Buf("u%d" % c) for c in range(16)]
        vgb = A.alloc(BF16, 4, 2048)
        Bvgb = [Buf("vgb%d" % b) for b in range(4)]
        wv = self.stream("awv", 3, (8, 512), [self.d["a_wv%d" % j][q] for _t in range(NT) for q in range(4)])
        wu = self.stream("awu", 3, (8, 128), [self.d["a_wu%d" % j][q] for _t in range(NT) for q in range(16)])
        wo = self.stream("awo", 2, (16, 128), [self.d["a_wo%d" % j][q] for _t in range(NT) for q in range(8)])
        bst = A.alloc(F32, 4, 4, 6)
        Bbst = [Buf("bst%d" % b) for b in range(4)]
        mv = A.alloc(F32, 4, 2)
        rs = A.alloc(F32, 4, 1)
        vpe = A.alloc(F32, 4, 1)
        Bmv = [Buf("mv%d" % b) for b in range(4)]
        Brs = [Buf("rs%d" % b) for b in range(4)]
        Bvpe = [Buf("vpe%d" % b) for b in range(4)]
        nmh = A.alloc(F32, 1)
        wsT = A.alloc(BF16, 8, 128)
        Bws = Buf("wsT")
        Cc = A.alloc(F32, 16, 128)
        BCc = Buf("Cc")
        bsb = A.alloc(F32, 8, 128)
        Bbsb = Buf("bsb")
        rw = A.alloc(F32, 8, 128)
        Brw = Buf("rw")
        t1 = [A.alloc(F32, 4, 128) for _ in range(1)]
        Bt1 = [Buf("at1_%d" % k) for k in range(1)]
        self.memset("pool", nmh, -0.5, [self.Bconst])
        self.dma_cast(wsT.rearrange("p a b -> p (a b)"), self.d["a_ws%d" % j], [], [Bws])
        self.memset("pool", wsT[64:128, :, 0:64], 0.0, [Bws])
        self.dma_plain(bsb.rearrange("p a b -> p (a b)"), self.d["a_bs%d" % j], [], [Bbsb])
        b0 = self.bank()
        b1 = self.bank()
        for g in range(8):
            bb = b0 if g < 4 else b1
            lo = (g % 4) * 128
            self.mm(self.psb(bb, lo, lo + 128), self.onesb, wsT[:, g, :], True, True, [Bws, self.Bconst], [self.PB[bb]])
        self.cp("act", rw[:, 0:4, :].rearrange("p a b -> p (a b)"), self.psb(b0), [self.PB[b0]], [Brw])
        self.cp("act", rw[:, 4:8, :].rearrange("p a b -> p (a b)"), self.psb(b1), [self.PB[b1]], [Brw])
        for uc in range(16):
            g = uc // 2
            self.stt(Cc[:, uc, :], rw[:, g, :], self.pcol(("a_lnb", j), uc), bsb[:, g, :], ALU.mult, ALU.add,
                     [Brw, Bbsb, self.Bptab], [BCc])
        self.pre_norm(0, ("mix_pre", i), k=0)
        for t in range(NT):
            hn, Bhn = self.hns[t % 2], self.Bhns[t % 2]
            for vq in range(4):
                w, Bw = wv.next()
                for blk in range(4):
                    b = self.bank()
                    for kc in range(8):
                        self.mm(self.psb(b), hn[:, kc, blk * 128:(blk + 1) * 128], w[:, kc, :], kc == 0, kc == 7,
                                [Bw, Bhn[kc]], [self.PB[b]])
                    dst = vgb[:, blk, vq * 512:(vq + 1) * 512]
                    self.act(dst, self.psb(b), AF.Gelu_apprx_tanh, [self.PB[b]], [Bvgb[blk]])
                    bo = bst[:, blk, vq, :]
                    self.P.op("dve", lambda e, bo=bo, src=dst: e.bn_stats(out=bo, in_=src), [Bvgb[blk]], [Bbst[blk]])
                    self.bg_step()
            self.bg_drain()
            for blk in range(4):
                mo = mv[:, blk, :]
                bi = bst[:, blk, :, :]
                self.P.op("dve", lambda e, mo=mo, bi=bi: e.bn_aggr(out=mo, in_=bi), [Bbst[blk]], [Bmv[blk]])
                self.ts("pool", vpe[:, blk, :], mv[:, blk, 1:2], EPS, None, ALU.add, None, [Bmv[blk]], [Bvpe[blk]])
                self.tt("pool", rs[:, blk, :], vpe[:, blk, :], nmh, ALU.pow, [Bvpe[blk], self.Bconst], [Brs[blk]])
                self.ts("dve", vgb[:, blk, :], vgb[:, blk, :], mv[:, blk, 0:1], rs[:, blk, :], ALU.subtract, ALU.mult,
                        [Bvgb[blk], Bmv[blk], Brs[blk]], [Bvgb[blk]])
            kk = 0
            for u4 in range(4):
                for j4 in range(4):
                    uc = u4 * 4 + j4
                    w, Bw = wu.next()
                    b = self.bank()
                    for kc in range(8):
                        self.mm(self.psb(b), w[:, kc, :], hn[:, kc, :], kc == 0, kc == 7, [Bw, Bhn[kc]], [self.PB[b]])
                    self.act(u[:, uc, :], self.psb(b), AF.Gelu_apprx_tanh, [self.PB[b]], [Bu[uc]])
                for blk in range(4):
                    bsl = slice(blk * 128, (blk + 1) * 128)
                    b = self.bank()
                    for j4 in range(4):
                        uc = u4 * 4 + j4
                        self.mm(self.psb(b, j4 * 128, (j4 + 1) * 128), vgb[:, blk, uc * 128:(uc + 1) * 128], wsT[:, uc // 2, :], True, True,
                                [Bvgb[blk], Bws], [self.PB[b]])
                    k3 = 0
                    kk += 1
                    for j4 in range(4):
                        uc = u4 * 4 + j4
                        self.stt(t1[k3][:, j4, :], self.psb(b, j4 * 128, (j4 + 1) * 128), self.pcol(("a_lng", j), uc), Cc[:, uc, :],
                                 ALU.mult, ALU.add, [self.PB[b], BCc, self.Bptab], [Bt1[k3]])
                    uv = u[:, u4 * 4:(u4 + 1) * 4, bsl]
                    self.tt("dve", uv, t1[k3], uv, ALU.mult, [Bt1[k3]] + Bu[u4 * 4:(u4 + 1) * 4], Bu[u4 * 4:(u4 + 1) * 4])
            if t + 1 < NT:
                self.pre_norm(t + 1, ("mix_pre", i), k=(t + 1) % 2)
            sb = self.statbank()
            for dc in range(8):
                w, Bw = wo.next()
                b = self.bank()
                for uc in range(16):
                    self.mm(self.psb(b), w[:, uc, :], u[:, uc, :], uc == 0, uc == 15, [Bw, Bu[uc]], [self.PB[b]])
                self.evac_mres(b, dc, sb)
            self.bg_start(t, ("mix_post", i), sb)
            if t == NT - 1:
                self.bg_drain()

    def mixer_b(self, i):
        A = self.A
        kT = A.alloc(BF16, 8, 1024)
        BkT = [[Buf("kT%d_%d" % (c, s_)) for s_ in range(2)] for c in range(8)]
        V = A.alloc(BF16, 8, 1024)
        BV = [Buf("V%d" % b) for b in range(8)]
        qz = A.alloc(BF16, 8, 2, T)
        Bq = [Buf("qz%d" % c) for c in range(8)]
        Bb = A.alloc(BF16, 16, 640)
        BBb = Buf("Bb")
        A2 = Arena(self.sb_all, self.hn1_off, self.hn1_off + 8 * T * 2)
        Pb = [A2.alloc(BF16, 640) for _ in range(3)]
        BPb = [Buf("Pb%d" % k) for k in range(3)]
        PTb = [A2.alloc(BF16, 640) for _ in range(3)]
        BPTb = [Buf("PTb%d" % k) for k in range(3)]
        dgr = [A.alloc(BF16, 128) for _ in range(3)]
        Bdgr = [Buf("dgr%d" % k) for k in range(3)]
        st3 = [A.alloc(F32, 4) for _ in range(4)]
        Bnm = [Buf("nm%d" % k) for k in range(4)]
        Brs = [Buf("rsum%d" % k) for k in range(4)]
        Bri = [Buf("rinv%d" % k) for k in range(4)]
        wqk = self.stream("bwqk", 2, (8, 256), [self.d["b_wqk"][q] for _t in range(NT) for q in range(8)])
        wvs = self.stream("bwv", 2, (8, 256), [self.d["b_wv"][q] for _t in range(NT) for q in range(4)])
        wos = self.stream("bwo", 2, (8, 128), [self.d["b_wo"][q] for _t in range(NT) for q in range(8)])
        oT, BoT = self.hn, self.Bhn
        ps = self.ps
        self.dma_cast(Bb.rearrange("p a b -> p (a b)"), self.d["b_bias"], [], [BBb])
        self.memset("pool", Bb[64:128, :, 0:64], NEG, [BBb])
        self.memset("pool", Bb[0:64, :, 576:640], NEG, [BBb])
        for c in range(8):
            self.memset("pool", qz[:, c, :, :], 0.0, [Bq[c]])
        unit = 0
        for t in range(NT):
            slot = t % 2
            self.pre_norm(t, ("mix_pre", i))
            for c in range(8):
                w, Bw = wqk.next()
                bq = self.bank()
                bk = self.bank()
                for kc in range(8):
                    self.mm(self.psb(bq), w[:, kc, 0:128], self.hn[:, kc, :], kc == 0, kc == 7, [Bw, self.Bhn[kc]], [self.PB[bq]])
                for kc in range(8):
                    self.mm(self.psb(bk), w[:, kc, 128:256], self.hn[:, kc, :], kc == 0, kc == 7, [Bw, self.Bhn[kc]], [self.PB[bk]])
                for hh in range(2):
                    rows = slice(hh * 64, hh * 64 + 64)
                    self.act(qz[rows, c, hh, :], ps[rows, bq * 512:(bq + 1) * 512], AF.Copy, [self.PB[bq]], [Bq[c]], scale=0.125)
                self.cp("dve", kT[:, c, slot * 512:(slot + 1) * 512], self.psb(bk), [self.PB[bk]], [BkT[c][slot]])
                self.bg_step()
            for qt in range(4):
                w, Bw = wvs.next()
                for blk in range(4):
                    b = self.bank()
                    for kc in range(8):
                        self.mm(self.psb(b, 0, 256), self.hn[:, kc, blk * 128:(blk + 1) * 128], w[:, kc, :], kc == 0, kc == 7,
                                [Bw, self.Bhn[kc]], [self.PB[b]])
                    rb = (4 * t + blk) % 8
                    self.cp("act" if (blk % 2) else "dve", V[:, rb, qt * 256:(qt + 1) * 256], self.psb(b, 0, 256), [self.PB[b]], [BV[rb]])
            self.bg_drain()
            units = [(c, jq, hh) for c in range(8) for jq in range(4) for hh in range(2)]
            NU = len(units)

            def geom(u):
                c, jq, hh = units[u]
                jb = 4 * t + jq
                return c, jq, hh, jb, max(0, 4 - jb)

            def st_scores(u):
                c, jq, hh, jb, i0 = geom(u)
                hd = 2 * c + hh
                sl = u % 2
                sbase = sl * 1024
                groups = []
                ii = i0
                while ii < 5:
                    n = 1
                    while (ii + n < 5 and (ii + n) != 4 and ((jb - 4 + ii + n) % 8) == ((jb - 4 + ii) % 8) + n):
                        n += 1
                    groups.append((ii, n))
                    ii += n
                started = set()
                for (ii, n) in groups:
                    kb = jb - 4 + ii
                    rc = (kb % 8) * 128
                    ksl = sorted(set(((kb + m) // 4) % 2 for m in range(n)))
                    sap = ps[:, sbase + ii * 128: sbase + (ii + n) * 128]
                    bk = 0 if ii < 4 else 1
                    wb = self.PB[2 * sl + bk]
                    self.mm(sap, qz[:, c, hh, jq * 128:(jq + 1) * 128], kT[:, c, rc:rc + n * 128], bk not in started, False,
                            [Bq[c]] + [BkT[c][k_] for k_ in ksl], wb)
                    started.add(bk)
                if i0 < 4:
                    self.mm(ps[:, sbase + i0 * 128: sbase + 512], self.ident, Bb[:, hd, i0 * 128:512], False, True,
                            [self.Bconst, BBb], self.PB[2 * sl])
                self.mm(ps[:, sbase + 512: sbase + 640], self.ident, Bb[:, hd, 512:640], False, True,
                        [self.Bconst, BBb], self.PB[2 * sl + 1])
                sbufs = [self.PB[2 * sl], self.PB[2 * sl + 1]]
                sfull = ps[:, sbase + i0 * 128: sbase + 640]
                k4 = u % 4
                nmax = st3[k4][:, 0:1]
                self.P.op("dve", lambda e, nmax=nmax, sfull=sfull: e.tensor_reduce(out=nmax, in_=sfull, axis=AX.X, op=ALU.max, negate=True),
                          sbufs, [Bnm[k4]])

            def st_exp(u):
                c, jq, hh, jb, i0 = geom(u)
                sl = u % 2
                k4 = u % 4
                sbase = sl * 1024
                sbufs = [self.PB[2 * sl], self.PB[2 * sl + 1]]
                sfull = ps[:, sbase + i0 * 128: sbase + 640]
                nmax, rsum = st3[k4][:, 0:1], st3[k4][:, 1:2]
                self.act(Pb[u % 3][:, i0 * 128:640], sfull, AF.Exp, sbufs + [Bnm[k4]], [BPb[u % 3], Brs[k4]], bias=nmax, accum=rsum)

            def st_rd(u):
                k4 = u % 4
                k3 = u % 3
                rsum, rinv = st3[k4][:, 1:2], st3[k4][:, 2:3]
                self.recip(rinv, rsum, [Brs[k4]], [Bri[k4]])
                self.ts("pool", dgr[k3], self.ident, rinv, None, ALU.mult, None, [self.Bconst, Bri[k4]], [Bdgr[k3]])

            def st_pt(u):
                c, jq, hh, jb, i0 = geom(u)
                sl = u % 2
                k3 = u % 3
                pbase = 2048
                for ii in range(i0, 5):
                    self.mm(ps[:, pbase + ii * 128: pbase + (ii + 1) * 128], Pb[k3][:, ii * 128:(ii + 1) * 128], dgr[k3], True, True,
                            [BPb[k3], Bdgr[k3]], [self.PB[4] if ii < 4 else self.PB[5]])
                self.cp("act" if (u % 2) else "dve", PTb[k3][:, i0 * 128:640], ps[:, pbase + i0 * 128: pbase + 640],
                        [self.PB[4], self.PB[5]], [BPTb[k3]])

            def st_pv(u):
                c, jq, hh, jb, i0 = geom(u)
                k3 = u % 3
                ob = 6 + hh
                for ii in range(i0, 5):
                    kb = jb - 4 + ii
                    self.mm(ps[:, ob * 512 + jq * 128: ob * 512 + (jq + 1) * 128], V[:, kb % 8, c * 128:(c + 1) * 128],
                            PTb[k3][:, ii * 128:(ii + 1) * 128], ii == i0, ii == 4, [BV[kb % 8], BPTb[k3]], [self.PB[ob]])
                if jq == 3 and hh == 1:
                    for h2 in range(2):
                        rows = slice(h2 * 64, h2 * 64 + 64)
                        o2 = 6 + h2
                        self.cp("dve" if h2 else "act", oT[rows, c, :], ps[rows, o2 * 512:(o2 + 1) * 512], self.PB[o2], [BoT[c]])

            for u in range(NU + 4):
                if u < NU:
                    st_scores(u)
                if 0 <= u - 1 < NU:
                    st_exp(u - 1)
                if 0 <= u - 2 < NU:
                    st_rd(u - 2)
                if 0 <= u - 3 < NU:
                    st_pt(u - 3)
                if 0 <= u - 4 < NU:
                    st_pv(u - 4)
            sb = self.statbank()
            for dc in range(8):
                w, Bw = wos.next()
                b = self.bank()
                for c in range(8):
                    self.mm(self.psb(b), w[:, c, :], oT[:, c, :], c == 0, c == 7, [Bw, BoT[c]], [self.PB[b]])
                self.evac_mres(b, dc, sb)
            self.bg_start(t, ("mix_post", i), sb)
            if t == NT - 1:
                self.bg_drain()

    def store_output(self):
        for c in range(8):
            self.dma_plain(self.yT[c], self.h[:, c, :], self.Bh[c], [Buf("y%d" % c)], is_output=True)


def _cols(v, n):
    return np.ascontiguousarray(np.asarray(v, np.float32).reshape(n, 128).T)


def _kc_tile(w, ncols_per_block):
    K, N = w.shape
    nb = N // ncols_per_block
    x = w.reshape(K // 128, 128, nb, ncols_per_block)
    x = x.transpose(2, 1, 0, 3)
    return np.ascontiguousarray(x).reshape(nb, 128, (K // 128) * ncols_per_block)


def host_shared(inp, layers):
    off, R = ptab_layout()
    ptab = np.zeros((128, R), np.float32)

    def put(key, arr):
        ptab[:, off[key]:off[key] + arr.shape[1]] = arr

    for i in range(DEPTH):
        put(("mix_pre", i), _cols(inp["mix_pre_g"][i], 8))
        put(("mix_post", i), _cols(inp["mix_post_g"][i], 8))
        put(("ffn_pre", i), _cols(inp["ffn_pre_g"][i], 8))
        put(("ffn_post", i), _cols(inp["ffn_post_g"][i], 8))
        put(("ple_g", i), _cols(inp["ple_norm_g"][i], 8))
        cv = np.concatenate([_cols(inp["ffn_conv"][i][jj], 44) for jj in range(3)], axis=1)
        put(("conv", i), cv)
    for j in range(2):
        put(("a_lng", j), _cols(inp["a_ln_g"][j], 16))
        put(("a_lnb", j), _cols(inp["a_ln_b"][j], 16))
    dw = np.asarray(inp["c_dw"][0], np.float32)
    dwc = dw.reshape(31, 8, 128).transpose(2, 1, 0).reshape(128, 248)
    put("c_dw", np.ascontiguousarray(dwc))
    put("c_dwb", _cols(inp["c_dw_b"][0], 8))
    put("c_lng", _cols(inp["c_ln_g"][0], 8))
    put("c_lnb", _cols(inp["c_ln_b"][0], 8))
    sh = {"ptab": ptab, "ident": np.eye(128, dtype=np.float32)}
    for i in layers:
        wu = np.asarray(inp["ffn_w_up"][i], np.float32)
        g = _kc_tile(np.ascontiguousarray(wu[:, :FF]), 128)
        v = _kc_tile(np.ascontiguousarray(wu[:, FF:]), 128)
        sh["wup%d" % i] = np.ascontiguousarray(np.stack([g, v], axis=1))
        wd = np.asarray(inp["ffn_w_down"][i], np.float32)
        x = wd.reshape(NFC, 128, 8, 128).transpose(2, 1, 0, 3)
        sh["wdn%d" % i] = np.ascontiguousarray(x).reshape(8, 128, NFC * 128)
        sh["wg%d" % i] = _kc_tile(np.asarray(inp["ple_w_gate"][i], np.float32), 128)
        wp = np.asarray(inp["ple_w_proj"][i], np.float32)
        sh["wp%d" % i] = np.ascontiguousarray(wp.reshape(2, 128, D).transpose(1, 0, 2)).reshape(128, 2 * D)
        kind, j = i % 3, i // 3
        if kind == 0:
            win = np.asarray(inp["a_w_in"][j], np.float32)
            sh["a_wu%d" % j] = _kc_tile(win[:, :2048], 128)
            sh["a_wv%d" % j] = _kc_tile(win[:, 2048:], 512)
            wo = np.asarray(inp["a_w_out"][j], np.float32)
            x = wo.reshape(16, 128, 8, 128).transpose(2, 1, 0, 3)
            sh["a_wo%d" % j] = np.ascontiguousarray(x).reshape(8, 128, 16 * 128)
            ws = np.asarray(inp["a_w_s"][j], np.float32)
            sh["a_ws%d" % j] = np.ascontiguousarray(ws.transpose(2, 0, 1)).reshape(128, 8 * 128)
            bs = np.asarray(inp["a_b_s"][j], np.float32).reshape(1, 8 * 128)
            sh["a_bs%d" % j] = np.ascontiguousarray(np.broadcast_to(bs, (128, 8 * 128)))
        elif kind == 1:
            wq = np.asarray(inp["b_w_qkv"][0], np.float32)
            q = wq[:, :D].reshape(D, 8, 128)
            k = wq[:, D:2 * D].reshape(D, 8, 128)
            qk = np.concatenate([q, k], axis=2).reshape(D, 8 * 256)
            sh["b_wqk"] = _kc_tile(qk, 256)
            sh["b_wv"] = _kc_tile(np.ascontiguousarray(wq[:, 2 * D:]), 256)
            wo = np.asarray(inp["b_w_out"][0], np.float32)
            x = wo.reshape(8, 128, 8, 128).transpose(2, 1, 0, 3)
            sh["b_wo"] = np.ascontiguousarray(x).reshape(8, 128, 8 * 128)
            rb = np.asarray(inp["b_rel_bias"][0], np.float32)
            qq = np.arange(128)[:, None]
            kk = np.arange(640)[None, :]
            idx = np.clip(qq + 512 - kk, -128, 128) + 128
            bfull = rb[:, idx]
            sh["b_bias"] = np.ascontiguousarray(bfull.transpose(1, 0, 2)).reshape(128, 16 * 640)
        else:
            wi = np.asarray(inp["c_w_in"][0], np.float32)
            a = wi[:, :D].reshape(D, 8, 128)
            g = wi[:, D:].reshape(D, 8, 128)
            ag = np.concatenate([a, g], axis=2).reshape(D, 8 * 256)
            sh["c_wi"] = _kc_tile(ag, 256)
            wo = np.asarray(inp["c_w_out"][0], np.float32)
            x = wo.reshape(8, 128, 8, 128).transpose(2, 1, 0, 3)
            sh["c_wo"] = np.ascontiguousarray(x).reshape(8, 128, 8 * 128)
    return sh


def run_layers(hT_in, p, shared, layers, trace=False, dbg=()):
    nc = Builder(layers, dbg).build()
    in_maps = []
    for b in range(8):
        m = dict(shared)
        m["xT"] = hT_in[b]
        for i in layers:
            m["pT%d" % i] = np.ascontiguousarray(p[i, b].T).reshape(2, 128, S)
        in_maps.append(m)
    res = run_bass_kernel_spmd(nc, in_maps, core_ids=list(range(8)), trace=trace)
    out = np.stack([res.results[b]["yT"] for b in range(8)])
    return out, res


def kernel(**inputs):
    inp = {k: np.asarray(v) for k, v in inputs.items()}
    layers = list(range(DEPTH))
    x = inp["x"].astype(np.float32, copy=False)
    hT = np.ascontiguousarray(x.transpose(0, 2, 1)).reshape(8, 8, 128, S)
    shared = host_shared(inp, layers)
    out, _ = run_layers(hT, inp["p"].astype(np.float32, copy=False), shared, layers)
    y = out.reshape(8, D, S).transpose(0, 2, 1)
    return np.ascontiguousarray(y).astype(np.float32, copy=False)
```

```python
import numpy as np
from contextlib import ExitStack
import concourse.bass as bass
import concourse.mybir as mybir
from concourse.bass_utils import run_bass_kernel_spmd

F32 = mybir.dt.float32
BF16 = mybir.dt.bfloat16
AF = mybir.ActivationFunctionType
ALU = mybir.AluOpType
AX = mybir.AxisListType

D = 1024
S = 2048
T = 512
NT = S // T
FF = 2816
NFC = FF // 128
DEPTH = 4
EPS = 1e-6
NEG = -1e30


class Buf:
    __slots__ = ("name", "last_w", "readers", "dma_readers", "dma_sem", "dma_cnt", "const", "excl")

    fence = {}

    def __init__(self, name, const=False, excl=False):
        self.name = name
        self.excl = excl
        self.last_w = None
        self.readers = dict(Buf.fence)
        self.dma_readers = []
        self.dma_sem = None
        self.dma_cnt = 0
        self.const = const


class SemSlot:
    __slots__ = ("sem", "cnt")

    def __init__(self):
        self.sem = None
        self.cnt = 0


class Op:
    __slots__ = ("eng", "fn", "deps", "sig", "sigval", "is_dma", "dma_sem", "dma_val", "idx")

    def __init__(self, eng, fn, is_dma):
        self.eng = eng
        self.fn = fn
        self.deps = []
        self.sig = False
        self.sigval = 0
        self.is_dma = is_dma
        self.dma_sem = None
        self.dma_val = 0


class Prog:
    ENGS = ("pe", "act", "dve", "pool", "sp")

    def __init__(self, nc):
        self.nc = nc
        self.ops = {e: [] for e in self.ENGS}
        self.n = 0
        self.semslots = {}
        self.out_dma_ops = []

    def _add(self, op, reads, writes):
        deps = []
        for b in reads:
            w = b.last_w
            if w is not None:
                deps.append((w, True))
            if b.excl:
                for e_, r in b.readers.items():
                    if e_ != op.eng:
                        deps.append((r, False))
        for b in writes:
            w = b.last_w
            if w is not None and not (op.is_dma and w.is_dma):
                deps.append((w, False))
            for r in b.readers.values():
                deps.append((r, False))
            for r in b.dma_readers:
                deps.append((r, False))
        seen = set()
        for d, raw in deps:
            if d is op:
                continue
            if (not d.is_dma) and (not op.is_dma) and d.eng == op.eng:
                if op.eng == "pe":
                    continue
            k = id(d)
            if k in seen:
                continue
            seen.add(k)
            op.deps.append(d)
            if not d.is_dma:
                d.sig = True
        for b in writes:
            b.last_w = op
            b.readers = {}
            b.dma_readers = []
        for b in reads:
            if b.const or b in writes:
                continue
            if op.is_dma:
                b.dma_readers.append(op)
            else:
                b.readers[op.eng] = op
        op.idx = self.n
        self.n += 1
        self.ops[op.eng].append(op)
        return op

    @staticmethod
    def _flat(xs):
        out = []
        for x in xs:
            if isinstance(x, (list, tuple)):
                out.extend(Prog._flat(x))
            else:
                out.append(x)
        return out

    def op(self, eng, fn, reads=(), writes=()):
        return self._add(Op(eng, fn, False), self._flat(reads), self._flat(writes))

    def dma(self, eng, fn, reads=(), writes=(), is_output=False):
        o = Op(eng, fn, True)
        reads, writes = self._flat(reads), self._flat(writes)
        dst = writes[0]
        slot = self.semslots.get(dst.name)
        if slot is None:
            slot = self.semslots[dst.name] = SemSlot()
        slot.cnt += 16
        o.dma_sem = slot
        o.dma_val = slot.cnt
        self._add(o, list(reads), list(writes))
        if is_output:
            self.out_dma_ops.append(o)
        return o

    def emit(self, stack):
        nc = self.nc
        sems = {}
        for e in ("pe", "act", "dve", "pool"):
            sems[e] = stack.enter_context(nc.semaphore("s_" + e))
        for i, slot in enumerate(self.semslots.values()):
            slot.sem = stack.enter_context(nc.semaphore("d%d" % i))
        for e in ("pe", "act", "dve", "pool"):
            c = 0
            for o in self.ops[e]:
                if o.is_dma:
                    continue
                if o.sig:
                    c += 1
                    o.sigval = c
        out_ops = self.out_dma_ops

        def run(engh, ename):
            waited = {}

            def wait(sem, val):
                k = id(sem)
                if waited.get(k, 0) >= val:
                    return
                waited[k] = val
                engh.wait_ge(sem, val)

            for o in self.ops[ename]:
                for d in o.deps:
                    if d.is_dma:
                        wait(d.dma_sem.sem, d.dma_val)
                    else:
                        wait(sems[d.eng], d.sigval)
                ins = o.fn(engh)
                if o.is_dma:
                    ins.then_inc(o.dma_sem.sem, 16)
                elif o.sig:
                    ins.then_inc(sems[ename], 1)
            if ename == "sp":
                for o in out_ops:
                    wait(o.dma_sem.sem, o.dma_val)

        block = stack.enter_context(nc.Block())

        @block.tensor
        def _(e):
            run(e, "pe")

        @block.scalar
        def _(e):
            run(e, "act")

        @block.vector
        def _(e):
            run(e, "dve")

        @block.gpsimd
        def _(e):
            run(e, "pool")

        @block.sync
        def _(e):
            run(e, "sp")


def ptab_layout():
    off = {}
    c = 0
    for i in range(DEPTH):
        for nm in ("mix_pre", "mix_post", "ffn_pre", "ffn_post", "ple_g"):
            off[(nm, i)] = c
            c += 8
        off[("conv", i)] = c
        c += 132
    for j in range(2):
        off[("a_lng", j)] = c
        c += 16
        off[("a_lnb", j)] = c
        c += 16
    off["c_dw"] = c
    c += 248
    off["c_dwb"] = c
    c += 8
    off["c_lng"] = c
    c += 8
    off["c_lnb"] = c
    c += 8
    return off, c


class StopBuild(Exception):
    pass


class Stream:
    def __init__(self, B, name, n, free, srcs):
        self.B = B
        self.n = n
        self.srcs = list(srcs)
        self.aps = [B.A.alloc(BF16, *free) for _ in range(n)]
        self.bufs = [Buf("%s%d" % (name, i)) for i in range(n)]
        self.issued = 0
        self.taken = 0
        for _ in range(n - 1):
            self._issue()

    def _issue(self):
        k = self.issued
        if k >= len(self.srcs):
            return
        self.issued += 1
        ap, bf = self.aps[k % self.n], self.bufs[k % self.n]
        flat = ap.rearrange("p a b -> p (a b)") if len(ap.shape) == 3 else ap
        if "fastdma" in self.B.dbg:
            n8 = flat.shape[-1] // 8
            self.B.dma_cast(flat[:, 0:n8], self.srcs[k][:, 0:n8], [], [bf])
            return
        self.B.dma_cast(flat, self.srcs[k], [], [bf])

    def next(self):
        k = self.taken
        self.taken += 1
        self._issue()
        return self.aps[k % self.n], self.bufs[k % self.n]


class Arena:
    def __init__(self, ap_all, base, limit):
        self.all = ap_all
        self.off = base
        self.limit = limit

    def mark(self):
        return self.off

    def reset(self, m):
        self.off = m

    def alloc(self, dtype, *free):
        isz = 4 if dtype == F32 else 2
        n = 1
        for f in free:
            n *= f
        nb = n * isz
        off = (self.off + 63) // 64 * 64
        assert off + nb <= self.limit, ("SBUF arena overflow", off + nb, self.limit)
        self.off = off + nb
        v = self.all[:, off // 2:(off + nb) // 2]
        if dtype == F32:
            v = v.bitcast(F32)
        if len(free) == 2:
            v = v.rearrange("p (a b) -> p a b", a=free[0])
        elif len(free) == 3:
            v = v.rearrange("p (a b c) -> p a b c", a=free[0], b=free[1])
        return v


class Builder:
    def __init__(self, layers, dbg=()):
        self.layers = list(layers)
        self.dbg = set(dbg)
        self.nc = bass.Bass("TRN2", target_bir_lowering=False)
        Buf.fence = {}
        self.P = Prog(self.nc)
        self.poff, self.pcols = ptab_layout()
        self._bank = 0
        self._stat = 0

    def mm(self, out, lhsT, rhs, start, stop, r, w, **kw):
        self.P.op("pe", lambda e: e.matmul(out, lhsT=lhsT, rhs=rhs, start=start, stop=stop, **kw), r, w)

    def act(self, out, in_, func, r, w, bias=None, scale=None, accum=None):
        kw = {}
        if bias is not None:
            kw["bias"] = bias
        if scale is not None:
            kw["scale"] = scale
        if accum is not None:
            kw["accum_out"] = accum
        self.P.op("act", lambda e: e.activation(out=out, in_=in_, func=func, **kw), r, w)

    def ts(self, eng, out, in0, s1, s2, op0, op1, r, w):
        if op1 is None and eng == "pool":
            s2, op1 = 1.0, ALU.mult
        if op1 is None:
            self.P.op(eng, lambda e: e.tensor_scalar(out=out, in0=in0, scalar1=s1, scalar2=None, op0=op0), r, w)
        else:
            self.P.op(eng, lambda e: e.tensor_scalar(out=out, in0=in0, scalar1=s1, scalar2=s2, op0=op0, op1=op1), r, w)

    def stt(self, out, in0, scalar, in1, op0, op1, r, w):
        self.P.op("dve", lambda e: e.scalar_tensor_tensor(out=out, in0=in0, scalar=scalar, in1=in1, op0=op0, op1=op1), r, w)

    def tt(self, eng, out, in0, in1, op, r, w):
        self.P.op(eng, lambda e: e.tensor_tensor(out=out, in0=in0, in1=in1, op=op), r, w)

    def cp(self, eng, out, in_, r, w):
        if eng == "act":
            self.P.op("act", lambda e: e.copy(out=out, in_=in_), r, w)
        else:
            self.P.op(eng, lambda e: e.tensor_copy(out=out, in_=in_), r, w)

    def recip(self, out, in_, r, w):
        self.P.op("dve", lambda e: e.reciprocal(out=out, in_=in_), r, w)

    def memset(self, eng, ap, val, w):
        self.P.op(eng, lambda e: e.memset(ap, val), [], w)

    def dma_cast(self, out, in_, r, w):
        self.P.dma("pool", lambda e: e.dma_start(out=out, in_=in_), r, w)

    def dma_plain(self, out, in_, r, w, is_output=False):
        self.P.dma("sp", lambda e: e.dma_start(out=out, in_=in_), r, w, is_output=is_output)

    def bank(self):
        b = self._bank
        self._bank = (b + 1) % 6
        return b

    def statbank(self):
        b = 6 + self._stat
        self._stat ^= 1
        return b

    def psb(self, b, lo=0, hi=512):
        return self.ps[:, b * 512 + lo:b * 512 + hi]

    def build(self):
        nc = self.nc
        st = ExitStack()
        with st:
            self.declare_dram()
            self.sb_all = st.enter_context(nc.sbuf_tensor("sb_all", [128, 106300], BF16))
            self.ps = st.enter_context(nc.psum_tensor("ps_all", [128, 4096], F32))
            self.PBK = [Buf("psk%d" % i, excl=True) for i in range(32)]
            self.PB = [self.PBK[4 * i:4 * i + 4] for i in range(8)]
            self.A = Arena(self.sb_all, 0, 106300 * 2)
            self.setup_persistent()
            for i in self.layers:
                self.layer(i)
            self.store_output()
            self.P.emit(st)
        return nc

    def declare_dram(self):
        nc = self.nc
        dt = lambda n, s: nc.dram_tensor(n, s, F32, kind="ExternalInput").ap()
        self.d = {}
        self.d["xT"] = dt("xT", [8, 128, S])
        self.d["ptab"] = dt("ptab", [128, self.pcols])
        self.d["ident"] = dt("ident", [128, 128])
        for i in self.layers:
            self.d["pT%d" % i] = dt("pT%d" % i, [2, 128, S])
            self.d["wup%d" % i] = dt("wup%d" % i, [NFC, 2, 128, 8 * 128])
            self.d["wdn%d" % i] = dt("wdn%d" % i, [8, 128, NFC * 128])
            self.d["wg%d" % i] = dt("wg%d" % i, [8, 128, 8 * 128])
            self.d["wp%d" % i] = dt("wp%d" % i, [128, 2 * D])
            kind, j = i % 3, i // 3
            if kind == 0:
                self.d["a_wu%d" % j] = dt("a_wu%d" % j, [16, 128, 8 * 128])
                self.d["a_wv%d" % j] = dt("a_wv%d" % j, [4, 128, 8 * 512])
                self.d["a_wo%d" % j] = dt("a_wo%d" % j, [8, 128, 16 * 128])
                self.d["a_ws%d" % j] = dt("a_ws%d" % j, [128, 8 * 128])
                self.d["a_bs%d" % j] = dt("a_bs%d" % j, [128, 8 * 128])
            elif kind == 1:
                self.d["b_wqk"] = dt("b_wqk", [8, 128, 8 * 256])
                self.d["b_wv"] = dt("b_wv", [4, 128, 8 * 256])
                self.d["b_wo"] = dt("b_wo", [8, 128, 8 * 128])
                self.d["b_bias"] = dt("b_bias", [128, 16 * 640])
            else:
                self.d["c_wi"] = dt("c_wi", [8, 128, 8 * 256])
                self.d["c_wo"] = dt("c_wo", [8, 128, 8 * 128])
        self.yT = nc.dram_tensor("yT", [8, 128, S], F32, kind="ExternalOutput").ap()

    def setup_persistent(self):
        A = self.A
        self.h = A.alloc(F32, 8, S)
        self.Bh = [[Buf("h%d_%d" % (c, t)) for t in range(NT)] for c in range(8)]
        self.ptab = A.alloc(F32, self.pcols)
        self.Bptab = Buf("ptab", const=True)
        self.ident = A.alloc(BF16, 128)
        self.onesb = A.alloc(BF16, 128)
        self.epsc = A.alloc(F32, 1)
        self.Bconst = Buf("const", const=True)
        self.sq = [A.alloc(BF16, T) for _ in range(3)]
        self.Bsq = [Buf("sq%d" % i) for i in range(3)]
        self._sq = 0
        self.rstds = [A.alloc(F32, T) for _ in range(3)]
        self.Brstds = [Buf("rstd%d" % k) for k in range(3)]
        self.rstd, self.Brstd = self.rstds[0], self.Brstds[0]
        self.mres = A.alloc(F32, 8, T)
        self.Bmres = [Buf("mres%d" % c) for c in range(8)]
        self.hns = [A.alloc(BF16, 8, T) for _ in range(2)]
        self.hn1_off = A.off - 8 * T * 2
        self.Bhns = [[Buf("hn%d_%d" % (k, c)) for c in range(8)] for k in range(2)]
        self.hn, self.Bhn = self.hns[0], self.Bhns[0]
        self.rtmp = [A.alloc(F32, T) for _ in range(3)]
        self.Brtmp = [Buf("rtmp%d" % i) for i in range(3)]
        self._rt = 0
        self.phase_mark = A.mark()
        self.dma_plain(self.ptab, self.d["ptab"], [], [self.Bptab])
        for c in range(8):
            self.dma_plain(self.h[:, c, :], self.d["xT"][c], [], self.Bh[c])
        self.memset("pool", self.onesb, 1.0, [self.Bconst])
        self.memset("pool", self.epsc, EPS, [self.Bconst])
        self.dma_cast(self.ident, self.d["ident"], [], [self.Bconst])

    def pcol(self, key, c, n=1):
        o = self.poff[key] + c
        return self.ptab[:, o:o + n]

    def nextsq(self):
        i = self._sq
        self._sq = (i + 1) % 3
        return self.sq[i], self.Bsq[i]

    def nextrt(self):
        i = self._rt
        self._rt = (i + 1) % 3
        return self.rtmp[i], self.Brtmp[i]

    def defer_mm(self, *args, **kw):
        self.flush_mm()
        self._pend_mm = (args, kw)

    def flush_mm(self):
        p = getattr(self, "_pend_mm", None)
        if p is not None:
            self._pend_mm = None
            self.mm(*p[0], **p[1])

    def stat_add(self, sb, src, src_bufs, c, n=8):
        sq, Bsq = self.nextsq()
        self.act(sq, src, AF.Square, src_bufs, [Bsq])
        self.defer_mm(self.psb(sb), self.onesb, sq, c == 0, c == n - 1, [Bsq, self.Bconst], [self.PB[sb]])

    def finish_rstd(self, sb, dim=D, role=0):
        self.flush_mm()
        rstd, Brstd = self.rstds[role], self.Brstds[role]
        self.act(rstd, self.psb(sb), AF.Sqrt, [self.PB[sb], self.Bconst], [Brstd], bias=self.epsc, scale=1.0 / dim)
        self.recip(rstd, rstd, [Brstd], [Brstd])
        return rstd, Brstd

    def pre_norm_gen(self, t, gkey, k=0):
        hn, Bhn = self.hns[k], self.Bhns[k]
        sb = self.statbank()
        tsl = slice(t * T, (t + 1) * T)
        for c in range(8):
            self.stat_add(sb, self.h[:, c, tsl], [self.Bh[c][t]], c)
            yield
        rstd, Brstd = self.finish_rstd(sb, role=0)
        yield
        for c in range(8):
            self.stt(hn[:, c, :], self.h[:, c, tsl], self.pcol(gkey, c), rstd, ALU.mult, ALU.mult,
                     [self.Bh[c][t], Brstd, self.Bptab], [Bhn[c]])
            if c % 2:
                yield

    def pre_norm(self, t, gkey, k=0):
        for _ in self.pre_norm_gen(t, gkey, k):
            pass

    def post_norm_gen(self, t, gkey, sb, role, after=None):
        rstd, Brstd = self.finish_rstd(sb, role=role)
        tsl = slice(t * T, (t + 1) * T)
        yield
        rts = {}
        for i in range(10):
            if i < 8:
                rt, Brt = self.nextrt()
                rts[i] = (rt, Brt)
                self.stt(rt, self.mres[:, i, :], self.pcol(gkey, i), rstd, ALU.mult, ALU.mult,
                         [self.Bmres[i], Brstd, self.Bptab], [Brt])
            if 1 <= i < 9:
                c = i - 1
                rt, Brt = rts.pop(c)
                self.tt("pool", self.h[:, c, tsl], self.h[:, c, tsl], rt, ALU.add, [self.Bh[c][t], Brt], [self.Bh[c][t]])
            if 2 <= i < 10 and after is not None:
                after(i - 2)
            yield

    def bg_start(self, t, gkey, sb, role=1):
        g = self.post_norm_gen(t, gkey, sb, role)
        next(g)
        self._bg = getattr(self, "_bg", [])
        self._bg.append(g)

    def bg_step(self):
        for g in list(getattr(self, "_bg", [])):
            try:
                next(g)
            except StopIteration:
                self._bg.remove(g)

    def bg_drain(self):
        while getattr(self, "_bg", []):
            self.bg_step()

    def post_norm_residual(self, t, gkey, sb, role=1):
        for _ in self.post_norm_gen(t, gkey, sb, role):
            pass

    def evac_mres(self, b, dc, sb):
        self.cp("dve", self.mres[:, dc, :], self.psb(b), [self.PB[b]], [self.Bmres[dc]])
        self.stat_add(sb, self.mres[:, dc, :], [self.Bmres[dc]], dc)

    def make_slots(self, name, n, *free):
        aps = [self.A.alloc(BF16, *free) for _ in range(n)]
        bufs = [Buf("%s%d" % (name, i)) for i in range(n)]
        return {"aps": aps, "bufs": bufs, "i": 0, "n": n}

    def load_slot(self, slots, src):
        i = slots["i"]
        slots["i"] = (i + 1) % slots["n"]
        ap, bf = slots["aps"][i], slots["bufs"][i]
        flat = ap
        if len(ap.shape) == 3:
            flat = ap.rearrange("p a b -> p (a b)")
        self.dma_cast(flat, src, [], [bf])
        return ap, bf

    def stream(self, name, n, free, srcs):
        return Stream(self, name, n, free, srcs)

    def stop(self, tag):
        if tag in self.dbg:
            raise StopBuild()

    def layer(self, i):
        try:
            self._layer(i)
        except StopBuild:
            pass

    def new_phase(self):
        self.bg_drain()
        self.flush_mm()
        self.A.reset(self.phase_mark)
        f = {}
        for e in ("pe", "act", "dve", "pool"):
            for o in reversed(self.P.ops[e]):
                if not o.is_dma:
                    f[e] = o
                    break
        Buf.fence = f

    def _layer(self, i):
        kind, j = i % 3, i // 3
        A = self.A
        self.new_phase()
        if "prenorm" in self.dbg:
            self.pre_norm(0, ("mix_pre", i))
            return
        if "nomix" in self.dbg:
            pass
        elif kind == 0:
            self.mixer_a(i, j)
        elif kind == 1:
            self.mixer_b(i)
        else:
            self.mixer_c(i)
        self.new_phase()
        if "noffn" not in self.dbg:
            self.ffn_phase(i)

    def ffn_phase(self, i):
        A = self.A
        actb = A.alloc(BF16, NFC, T)
        Bact = [Buf("act%d" % f) for f in range(NFC)]
        wupS = self.stream("wup", 8, (8, 128), [self.d["wup%d" % i][fc][gv] for _t in range(NT) for fc in range(NFC) for gv in range(2)])
        HF = NFC // 2
        wdnS = self.stream("wdn", 5, (HF, 128), [self.d["wdn%d" % i][dc][:, hf * HF * 128:(hf + 1) * HF * 128]
                                                  for _t in range(NT) for dc in range(8) for hf in range(2)])
        wgS = self.stream("wg", 4, (8, 128), [self.d["wg%d" % i][dc] for _t in range(NT) for dc in range(8)])
        cg = [A.alloc(F32, T) for _ in range(2)]
        cv = [A.alloc(F32, T) for _ in range(2)]
        sg = [A.alloc(F32, T) for _ in range(2)]
        Bcg = [Buf("cg%d" % k) for k in range(2)]
        Bcv = [Buf("cv%d" % k) for k in range(2)]
        Bsg = [Buf("sg%d" % k) for k in range(2)]
        halo = [A.alloc(F32, 2 * NFC, 2) for _ in range(2)]
        Bhalo = [[Buf("halo%d_%d" % (k, q)) for q in range(2 * NFC)] for k in range(2)]
        bnd = [A.alloc(F32, 3, 2 * NFC) for _ in range(1)] * 2
        Bbnd = [Buf("bnd")] * 2
        x01 = A.alloc(F32, 2 * NFC, 2)
        c01 = A.alloc(F32, 2 * NFC, 2)
        s01 = A.alloc(F32, NFC, 2)
        Bx01, Bc01, Bs01 = Buf("x01"), Buf("c01"), Buf("s01")
        hb = A.alloc(BF16, 8, T)
        Bhb = [Buf("hb%d" % c) for c in range(8)]
        pt = [A.alloc(BF16, 2, T) for _ in range(2)]
        Bpt = [Buf("pt%d" % k) for k in range(2)]
        wp = A.alloc(BF16, 2, D)
        Bwp = Buf("wp")
        gate = [A.alloc(F32, T) for _ in range(1)]
        Bgate = [Buf("gate%d" % k) for k in range(1)]
        self.dma_cast(wp.rearrange("p a b -> p (a b)"), self.d["wp%d" % i], [], [Bwp])
        cbase = self.poff[("conv", i)]

        def tap(jj, q):
            o = cbase + jj * 44 + q
            return self.ptab[:, o:o + 1]

        def step(bg):
            for g in list(bg):
                try:
                    next(g)
                except StopIteration:
                    bg.remove(g)

        def drain(bg):
            while bg:
                step(bg)

        def load_pt(t):
            ptt, Bptt = pt[t % 2], Bpt[t % 2]
            tsl = slice(t * T, (t + 1) * T)
            for kc in range(2):
                self.dma_cast(ptt[:, kc, :], self.d["pT%d" % i][kc][:, tsl], [], [Bptt])

        def stage_A(t):
            return self.pre_norm_gen(t, ("ffn_pre", i), k=t % 2)

        def stage_B(t, bg):
            hn, Bhn = self.hns[t % 2], self.Bhns[t % 2]
            if t > 0:
                ho, Bho = halo[(t - 1) % 2], Bhalo[(t - 1) % 2]
                bd, Bbd = bnd[t % 2], Bbnd[t % 2]
                W0 = self.ptab[:, cbase:cbase + 44]
                W1 = self.ptab[:, cbase + 44:cbase + 88]
                self.tt("dve", bd[:, 2, :], ho[:, :, 1], W1, ALU.mult, Bho + [self.Bptab], [Bbd])
                self.tt("dve", bd[:, 0, :], ho[:, :, 0], W0, ALU.mult, Bho + [self.Bptab], [Bbd])
                self.tt("dve", bd[:, 0, :], bd[:, 0, :], bd[:, 2, :], ALU.add, [Bbd], [Bbd])
                self.tt("dve", bd[:, 1, :], ho[:, :, 1], W0, ALU.mult, Bho + [self.Bptab], [Bbd])
            for fc in range(NFC):
                wg_, Bwg_ = wupS.next()
                k2 = fc % 2
                bg_ = self.bank()
                bv_ = self.bank()
                for kc in range(8):
                    self.mm(self.psb(bg_), wg_[:, kc, :], hn[:, kc, :], kc == 0, kc == 7, [Bwg_, Bhn[kc]], [self.PB[bg_]])
                wv_, Bwv_ = wupS.next()
                for kc in range(8):
                    self.mm(self.psb(bv_), wv_[:, kc, :], hn[:, kc, :], kc == 0, kc == 7, [Bwv_, Bhn[kc]], [self.PB[bv_]])
                lo = 0 if t == 0 else 2
                for (b, q, cbuf, Bc) in ((bg_, fc, cg[k2], Bcg[k2]), (bv_, NFC + fc, cv[k2], Bcv[k2])):
                    pb = self.PB[b]
                    self.act(cbuf[:, lo:T], self.psb(b, lo, T), AF.Copy, [pb, self.Bptab], [Bc], scale=tap(2, q))
                    if t > 0:
                        self.cp("act", x01[:, q, 0:2], self.psb(b, 0, 2), [pb], [Bx01])
                    if t < NT - 1:
                        self.cp("act", halo[t % 2][:, q, 0:2], self.psb(b, T - 2, T), [pb], [Bhalo[t % 2][q]])
                    self.stt(cbuf[:, lo + 1 - lo // 2:T], self.psb(b, lo // 2, T - 1), tap(1, q), cbuf[:, lo + 1 - lo // 2:T],
                             ALU.mult, ALU.add, [pb, Bc, self.Bptab], [Bc])
                    self.stt(cbuf[:, 2:T], self.psb(b, 0, T - 2), tap(0, q), cbuf[:, 2:T], ALU.mult, ALU.add,
                             [pb, Bc, self.Bptab], [Bc])
                self.act(sg[k2][:, lo:T], cg[k2][:, lo:T], AF.Silu, [Bcg[k2]], [Bsg[k2]])
                self.tt("pool", actb[:, fc, lo:T], sg[k2][:, lo:T], cv[k2][:, lo:T], ALU.mult, [Bsg[k2], Bcv[k2]], [Bact[fc]])
                step(bg)
            if t > 0:
                bd, Bbd = bnd[0], Bbnd[0]
                W0 = self.ptab[:, cbase:cbase + 44]
                W1 = self.ptab[:, cbase + 44:cbase + 88]
                W2 = self.ptab[:, cbase + 88:cbase + 132]
                rP = [Bx01, self.Bptab, Bbd]
                self.tt("dve", c01[:, :, 0], x01[:, :, 0], W2, ALU.mult, rP, [Bc01])
                self.tt("dve", c01[:, :, 0], c01[:, :, 0], bd[:, 0, :], ALU.add, rP + [Bc01], [Bc01])
                self.tt("dve", c01[:, :, 1], x01[:, :, 1], W2, ALU.mult, rP, [Bc01])
                self.tt("dve", bd[:, 2, :], x01[:, :, 0], W1, ALU.mult, rP, [Bbd])
                self.tt("dve", c01[:, :, 1], c01[:, :, 1], bd[:, 2, :], ALU.add, rP + [Bc01], [Bc01])
                self.tt("dve", c01[:, :, 1], c01[:, :, 1], bd[:, 1, :], ALU.add, rP + [Bc01], [Bc01])
                self.act(s01, c01[:, 0:NFC, :], AF.Silu, [Bc01], [Bs01])
                self.tt("dve", actb[:, :, 0:2], s01, c01[:, NFC:2 * NFC, :], ALU.mult, [Bs01, Bc01], Bact)
            drain(bg)

        def stage_C(t, bg):
            sb = self.statbank()
            for dc in range(8):
                b = self.bank()
                for hf in range(2):
                    w, Bw = wdnS.next()
                    for f2 in range(HF):
                        fc = hf * HF + f2
                        self.mm(self.psb(b), w[:, f2, :], actb[:, fc, :], fc == 0, fc == NFC - 1, [Bw, Bact[fc]], [self.PB[b]])
                step(bg)
                self.evac_mres(b, dc, sb)
            drain(bg)
            return sb

        def stage_D(t, sb):
            tsl = slice(t * T, (t + 1) * T)

            def after(c):
                self.cp("act", hb[:, c, :], self.h[:, c, tsl], [self.Bh[c][t]], [Bhb[c]])
            g = self.post_norm_gen(t, ("ffn_post", i), sb, 1, after=after)
            next(g)
            return g

        def stage_E(t):
            ptt, Bptt = pt[t % 2], Bpt[t % 2]
            if t + 1 < NT:
                load_pt(t + 1)
            sb = self.statbank()
            for dc in range(8):
                w, Bw = wgS.next()
                bgt = self.bank()
                be = self.bank()
                for kc in range(8):
                    self.mm(self.psb(bgt), w[:, kc, :], hb[:, kc, :], kc == 0, kc == 7, [Bw, Bhb[kc]], [self.PB[bgt]])
                for kc in range(2):
                    self.mm(self.psb(be), wp[:, kc, dc * 128:(dc + 1) * 128], ptt[:, kc, :], kc == 0, kc == 1,
                            [Bwp, Bptt], [self.PB[be]])
                k2 = 0
                self.act(gate[k2], self.psb(bgt), AF.Sigmoid, [self.PB[bgt]], [Bgate[k2]])
                self.tt("dve", self.mres[:, dc, :], gate[k2], self.psb(be), ALU.mult, [Bgate[k2], self.PB[be]],
                        [self.Bmres[dc]])
                self.stat_add(sb, self.mres[:, dc, :], [self.Bmres[dc]], dc)
            return sb

        load_pt(0)
        drain([stage_A(0)])
        stage_B(0, [stage_A(1)])
        pend = []
        for t in range(NT):
            sb = stage_C(t, pend)
            pend = []
            bgl = [stage_D(t, sb)]
            if t + 2 < NT:
                bgl.append(stage_A(t + 2))
            if t + 1 < NT:
                stage_B(t + 1, bgl)
            else:
                drain(bgl)
            sbe = stage_E(t)
            g = self.post_norm_gen(t, ("ple_g", i), sbe, 2)
            next(g)
            pend = [g]
        drain(pend)
        self.flush_mm()

    def mixer_c(self, i):
        A = self.A
        ybuf = A.alloc(BF16, 8, 30 + T)
        Byb = [Buf("yb%d" % c) for c in range(8)]
        z = A.alloc(F32, 8, T)
        Bz = [Buf("z%d" % c) for c in range(8)]
        zb = [A.alloc(BF16, T) for _ in range(2)]
        Bzb = [Buf("zb%d" % k) for k in range(2)]
        actc = A.alloc(BF16, 8, T)
        Bac = [Buf("actc%d" % c) for c in range(8)]
        dg = [A.alloc(BF16, 31, 128) for _ in range(2)]
        Bdg = [Buf("dg%d" % k) for k in range(2)]
        wi = self.stream("cwi", 3, (8, 256), [self.d["c_wi"][c] for _t in range(NT) for c in range(8)])
        wo = self.stream("cwo", 3, (8, 128), [self.d["c_wo"][c] for _t in range(NT) for c in range(8)])
        sgm = [A.alloc(F32, T) for _ in range(2)]
        Bsgm = [Buf("sgm%d" % k) for k in range(2)]
        mean = A.alloc(F32, T)
        msq = A.alloc(F32, T)
        var = A.alloc(F32, T)
        nmr = A.alloc(F32, T)
        Bmean, Bmsq, Bvar, Bnmr = Buf("mean"), Buf("msq"), Buf("var"), Buf("nmr")
        t1 = [A.alloc(F32, T) for _ in range(2)]
        Bt1 = [Buf("ct1_%d" % k) for k in range(2)]
        for c in range(8):
            self.memset("pool", ybuf[:, c, 0:30], 0.0, [Byb[c]])

        def build_diag(c):
            k2_ = c % 2
            o_ = self.poff["c_dw"] + c * 31
            in1 = self.ptab[:, o_:o_ + 31].unsqueeze(2).to_broadcast([128, 31, 128])
            in0 = self.ident.unsqueeze(1).to_broadcast([128, 31, 128])
            self.tt("dve", dg[k2_], in0, in1, ALU.mult, [self.Bconst, self.Bptab], [Bdg[k2_]])

        self.pre_norm(0, ("mix_pre", i), k=0)
        for t in range(NT):
            hn, Bhn = self.hns[t % 2], self.Bhns[t % 2]
            sb1 = self.statbank()
            sb2 = self.statbank()
            if t == 0:
                build_diag(0)
            for c in range(8):
                if c + 1 < 8:
                    build_diag(c + 1)
                self.bg_step()
                w, Bw = wi.next()
                ba = self.bank()
                bg = self.bank()
                for kc in range(8):
                    self.mm(self.psb(ba), w[:, kc, 0:128], hn[:, kc, :], kc == 0, kc == 7, [Bw, Bhn[kc]], [self.PB[ba]])
                for kc in range(8):
                    self.mm(self.psb(bg), w[:, kc, 128:256], hn[:, kc, :], kc == 0, kc == 7, [Bw, Bhn[kc]], [self.PB[bg]])
                k2 = c % 2
                self.act(sgm[k2], self.psb(bg), AF.Sigmoid, [self.PB[bg]], [Bsgm[k2]])
                self.tt("dve", ybuf[:, c, 30:30 + T], sgm[k2], self.psb(ba), ALU.mult, [Bsgm[k2], self.PB[ba]], [Byb[c]])
                bc = self.bank()
                for jj in range(31):
                    self.mm(self.psb(bc), dg[k2][:, jj, :], ybuf[:, c, jj:jj + T], jj == 0, jj == 30, [Bdg[k2], Byb[c]], [self.PB[bc]])
                bias = self.pcol("c_dwb", c)
                self.act(z[:, c, :], self.psb(bc), AF.Identity, [self.PB[bc], self.Bptab], [Bz[c]], bias=bias)
                self.act(zb[k2], self.psb(bc), AF.Identity, [self.PB[bc], self.Bptab], [Bzb[k2]], bias=bias)
                self.flush_mm()
                self.mm(self.psb(sb1), self.onesb, zb[k2], c == 0, c == 7, [Bzb[k2], self.Bconst], [self.PB[sb1]])
                sq, Bsq = self.nextsq()
                self.act(sq, self.psb(bc), AF.Square, [self.PB[bc], self.Bptab], [Bsq], bias=bias)
                self.defer_mm(self.psb(sb2), self.onesb, sq, c == 0, c == 7, [Bsq, self.Bconst], [self.PB[sb2]])
                if t < NT - 1:
                    self.cp("pool", ybuf[:, c, 0:30], ybuf[:, c, T:T + 30], [Byb[c]], [Byb[c]])
            self.bg_drain()
            if t + 1 < NT:
                build_diag(0)
            self.flush_mm()
            self.ts("dve", mean, self.psb(sb1), 1.0 / D, None, ALU.mult, None, [self.PB[sb1]], [Bmean])
            self.act(msq, mean, AF.Square, [Bmean], [Bmsq])
            self.stt(var, self.psb(sb2), 1.0 / D, msq, ALU.mult, ALU.subtract, [self.PB[sb2], Bmsq], [Bvar])
            self.act(self.rstd, var, AF.Sqrt, [Bvar, self.Bconst], [self.Brstd], bias=self.epsc)
            self.recip(self.rstd, self.rstd, [self.Brstd], [self.Brstd])
            self.stt(nmr, mean, -1.0, self.rstd, ALU.mult, ALU.mult, [Bmean, self.Brstd], [Bnmr])
            for c in range(8):
                k2 = c % 2
                self.tt("dve", t1[k2], z[:, c, :], self.rstd, ALU.mult, [Bz[c], self.Brstd], [Bt1[k2]])
                self.tt("pool", t1[k2], t1[k2], nmr, ALU.add, [Bt1[k2], Bnmr], [Bt1[k2]])
                self.act(actc[:, c, :], t1[k2], AF.Silu, [Bt1[k2], self.Bptab], [Bac[c]],
                         scale=self.pcol("c_lng", c), bias=self.pcol("c_lnb", c))
            if t + 1 < NT:
                self.pre_norm(t + 1, ("mix_pre", i), k=(t + 1) % 2)
            sb = self.statbank()
            for dc in range(8):
                w, Bw = wo.next()
                b = self.bank()
                for c in range(8):
                    self.mm(self.psb(b), w[:, c, :], actc[:, c, :], c == 0, c == 7, [Bw, Bac[c]], [self.PB[b]])
                self.evac_mres(b, dc, sb)
            self.bg_start(t, ("mix_post", i), sb)
            if t == NT - 1:
                self.bg_drain()

    def mixer_a(self, i, j):
        A = self.A
        u = A.alloc(BF16, 16, T)
        Bu = [Buf("u%d" % c) for c in range(16)]
        vgb = A.alloc(BF16, 4, 2048)
        Bvgb = [Buf("vgb%d" % b) for b in range(4)]
        wv = self.stream("awv", 3, (8, 512), [self.d["a_wv%d" % j][q] for _t in range(NT) for q in range(4)])
        wu = self.stream("awu", 3, (8, 128), [self.d["a_wu%d" % j][q] for _t in range(NT) for q in range(16)])
        wo = self.stream("awo", 4, (8, 128), [self.d["a_wo%d" % j][q][:, hf * 1024:(hf + 1) * 1024]
                                              for _t in range(NT) for q in range(8) for hf in range(2)])
        bst = A.alloc(F32, 4, 4, 6)
        Bbst = [Buf("bst%d" % b) for b in range(4)]
        mv = A.alloc(F32, 4, 2)
        rs = A.alloc(F32, 4, 1)
        vpe = A.alloc(F32, 4, 1)
        Bmv = [Buf("mv%d" % b) for b in range(4)]
        Brs = [Buf("rs%d" % b) for b in range(4)]
        Bvpe = [Buf("vpe%d" % b) for b in range(4)]
        nmh = A.alloc(F32, 1)
        wsT = A.alloc(BF16, 8, 128)
        Bws = Buf("wsT")
        Cc = A.alloc(F32, 16, 128)
        BCc = Buf("Cc")
        bsb = A.alloc(F32, 8, 128)
        Bbsb = Buf("bsb")
        rw = A.alloc(F32, 8, 128)
        Brw = Buf("rw")
        t1 = [A.alloc(F32, 4, 128) for _ in range(1)]
        Bt1 = [Buf("at1_%d" % k) for k in range(1)]
        self.memset("pool", nmh, -0.5, [self.Bconst])
        self.dma_cast(wsT.rearrange("p a b -> p (a b)"), self.d["a_ws%d" % j], [], [Bws])
        self.memset("pool", wsT[64:128, :, 0:64], 0.0, [Bws])
        self.dma_plain(bsb.rearrange("p a b -> p (a b)"), self.d["a_bs%d" % j], [], [Bbsb])
        b0 = self.bank()
        b1 = self.bank()
        for g in range(8):
            bb = b0 if g < 4 else b1
            lo = (g % 4) * 128
            self.mm(self.psb(bb, lo, lo + 128), self.onesb, wsT[:, g, :], True, True, [Bws, self.Bconst], [self.PB[bb]])
        self.cp("act", rw[:, 0:4, :].rearrange("p a b -> p (a b)"), self.psb(b0), [self.PB[b0]], [Brw])
        self.cp("act", rw[:, 4:8, :].rearrange("p a b -> p (a b)"), self.psb(b1), [self.PB[b1]], [Brw])
        for uc in range(16):
            g = uc // 2
            self.stt(Cc[:, uc, :], rw[:, g, :], self.pcol(("a_lnb", j), uc), bsb[:, g, :], ALU.mult, ALU.add,
                     [Brw, Bbsb, self.Bptab], [BCc])
        self.pre_norm(0, ("mix_pre", i), k=0)
        for t in range(NT):
            hn, Bhn = self.hns[t % 2], self.Bhns[t % 2]
            for vq in range(4):
                w, Bw = wv.next()
                for blk in range(4):
                    b = self.bank()
                    for kc in range(8):
                        self.mm(self.psb(b), hn[:, kc, blk * 128:(blk + 1) * 128], w[:, kc, :], kc == 0, kc == 7,
                                [Bw, Bhn[kc]], [self.PB[b]])
                    dst = vgb[:, blk, vq * 512:(vq + 1) * 512]
                    self.act(dst, self.psb(b), AF.Gelu_apprx_tanh, [self.PB[b]], [Bvgb[blk]])
                    bo = bst[:, blk, vq, :]
                    self.P.op("dve", lambda e, bo=bo, src=dst: e.bn_stats(out=bo, in_=src), [Bvgb[blk]], [Bbst[blk]])
                    self.bg_step()
            self.bg_drain()
            for blk in range(4):
                mo = mv[:, blk, :]
                bi = bst[:, blk, :, :]
                self.P.op("dve", lambda e, mo=mo, bi=bi: e.bn_aggr(out=mo, in_=bi), [Bbst[blk]], [Bmv[blk]])
                self.ts("pool", vpe[:, blk, :], mv[:, blk, 1:2], EPS, None, ALU.add, None, [Bmv[blk]], [Bvpe[blk]])
                self.tt("pool", rs[:, blk, :], vpe[:, blk, :], nmh, ALU.pow, [Bvpe[blk], self.Bconst], [Brs[blk]])
                self.ts("dve", vgb[:, blk, :], vgb[:, blk, :], mv[:, blk, 0:1], rs[:, blk, :], ALU.subtract, ALU.mult,
                        [Bvgb[blk], Bmv[blk], Brs[blk]], [Bvgb[blk]])
            kk = 0
            for u4 in range(4):
                for j4 in range(4):
                    uc = u4 * 4 + j4
                    w, Bw = wu.next()
                    b = self.bank()
                    for kc in range(8):
                        self.mm(self.psb(b), w[:, kc, :], hn[:, kc, :], kc == 0, kc == 7, [Bw, Bhn[kc]], [self.PB[b]])
                    self.act(u[:, uc, :], self.psb(b), AF.Gelu_apprx_tanh, [self.PB[b]], [Bu[uc]])
                for blk in range(4):
                    bsl = slice(blk * 128, (blk + 1) * 128)
                    b = self.bank()
                    for j4 in range(4):
                        uc = u4 * 4 + j4
                        self.mm(self.psb(b, j4 * 128, (j4 + 1) * 128), vgb[:, blk, uc * 128:(uc + 1) * 128], wsT[:, uc // 2, :], True, True,
                                [Bvgb[blk], Bws], [self.PB[b]])
                    k3 = 0
                    kk += 1
                    for j4 in range(4):
                        uc = u4 * 4 + j4
                        self.stt(t1[k3][:, j4, :], self.psb(b, j4 * 128, (j4 + 1) * 128), self.pcol(("a_lng", j), uc), Cc[:, uc, :],
                                 ALU.mult, ALU.add, [self.PB[b], BCc, self.Bptab], [Bt1[k3]])
                    uv = u[:, u4 * 4:(u4 + 1) * 4, bsl]
                    self.tt("dve", uv, t1[k3], uv, ALU.mult, [Bt1[k3]] + Bu[u4 * 4:(u4 + 1) * 4], Bu[u4 * 4:(u4 + 1) * 4])
            if t + 1 < NT:
                self.pre_norm(t + 1, ("mix_pre", i), k=(t + 1) % 2)
            sb = self.statbank()
            for dc in range(8):
                b = self.bank()
                for hf in range(2):
                    w, Bw = wo.next()
                    for u8 in range(8):
                        uc = hf * 8 + u8
                        self.mm(self.psb(b), w[:, u8, :], u[:, uc, :], uc == 0, uc == 15, [Bw, Bu[uc]], [self.PB[b]])
                self.evac_mres(b, dc, sb)
            self.bg_start(t, ("mix_post", i), sb)
            if t == NT - 1:
                self.bg_drain()

    def mixer_b(self, i):
        A = self.A
        kT = A.alloc(BF16, 8, 1024)
        BkT = [[Buf("kT%d_%d" % (c, s_)) for s_ in range(2)] for c in range(8)]
        V = A.alloc(BF16, 8, 1024)
        BV = [Buf("V%d" % b) for b in range(8)]
        qz = A.alloc(BF16, 8, 2, T)
        Bq = [Buf("qz%d" % c) for c in range(8)]
        Bb = A.alloc(BF16, 16, 640)
        BBb = Buf("Bb")
        A2 = Arena(self.sb_all, self.hn1_off, self.hn1_off + 8 * T * 2)
        Pb = [A2.alloc(BF16, 640) for _ in range(3)]
        BPb = [Buf("Pb%d" % k) for k in range(3)]
        PTb = [A2.alloc(BF16, 640) for _ in range(3)]
        BPTb = [Buf("PTb%d" % k) for k in range(3)]
        dgr = [A.alloc(BF16, 128) for _ in range(3)]
        Bdgr = [Buf("dgr%d" % k) for k in range(3)]
        st3 = [A.alloc(F32, 4) for _ in range(4)]
        Bnm = [Buf("nm%d" % k) for k in range(4)]
        Brs = [Buf("rsum%d" % k) for k in range(4)]
        Bri = [Buf("rinv%d" % k) for k in range(4)]
        wqk = self.stream("bwqk", 2, (8, 256), [self.d["b_wqk"][q] for _t in range(NT) for q in range(8)])
        wvs = self.stream("bwv", 2, (8, 256), [self.d["b_wv"][q] for _t in range(NT) for q in range(4)])
        wos = self.stream("bwo", 2, (8, 128), [self.d["b_wo"][q] for _t in range(NT) for q in range(8)])
        oT, BoT = self.hn, self.Bhn
        ps = self.ps
        self.dma_cast(Bb.rearrange("p a b -> p (a b)"), self.d["b_bias"], [], [BBb])
        self.memset("pool", Bb[64:128, :, 0:64], NEG, [BBb])
        self.memset("pool", Bb[0:64, :, 576:640], NEG, [BBb])
        for c in range(8):
            self.memset("pool", qz[:, c, :, :], 0.0, [Bq[c]])
        unit = 0
        for t in range(NT):
            slot = t % 2
            self.pre_norm(t, ("mix_pre", i))
            for c in range(8):
                w, Bw = wqk.next()
                bq = self.bank()
                bk = self.bank()
                for kc in range(8):
                    self.mm(self.psb(bq), w[:, kc, 0:128], self.hn[:, kc, :], kc == 0, kc == 7, [Bw, self.Bhn[kc]], [self.PB[bq]])
                for kc in range(8):
                    self.mm(self.psb(bk), w[:, kc, 128:256], self.hn[:, kc, :], kc == 0, kc == 7, [Bw, self.Bhn[kc]], [self.PB[bk]])
                for hh in range(2):
                    rows = slice(hh * 64, hh * 64 + 64)
                    self.act(qz[rows, c, hh, :], ps[rows, bq * 512:(bq + 1) * 512], AF.Copy, [self.PB[bq]], [Bq[c]], scale=0.125)
                self.cp("dve", kT[:, c, slot * 512:(slot + 1) * 512], self.psb(bk), [self.PB[bk]], [BkT[c][slot]])
                self.bg_step()
            for qt in range(4):
                w, Bw = wvs.next()
                for blk in range(4):
                    b = self.bank()
                    for kc in range(8):
                        self.mm(self.psb(b, 0, 256), self.hn[:, kc, blk * 128:(blk + 1) * 128], w[:, kc, :], kc == 0, kc == 7,
                                [Bw, self.Bhn[kc]], [self.PB[b]])
                    rb = (4 * t + blk) % 8
                    self.cp("act" if (blk % 2) else "dve", V[:, rb, qt * 256:(qt + 1) * 256], self.psb(b, 0, 256), [self.PB[b]], [BV[rb]])
            self.bg_drain()
            units = [(c, jq, hh) for c in range(8) for jq in range(4) for hh in range(2)]
            NU = len(units)

            def geom(u):
                c, jq, hh = units[u]
                jb = 4 * t + jq
                return c, jq, hh, jb, max(0, 4 - jb)

            def st_scores(u):
                c, jq, hh, jb, i0 = geom(u)
                hd = 2 * c + hh
                sl = u % 2
                sbase = sl * 1024
                groups = []
                ii = i0
                while ii < 5:
                    n = 1
                    while (ii + n < 5 and (ii + n) != 4 and ((jb - 4 + ii + n) % 8) == ((jb - 4 + ii) % 8) + n):
                        n += 1
                    groups.append((ii, n))
                    ii += n
                started = set()
                for (ii, n) in groups:
                    kb = jb - 4 + ii
                    rc = (kb % 8) * 128
                    ksl = sorted(set(((kb + m) // 4) % 2 for m in range(n)))
                    sap = ps[:, sbase + ii * 128: sbase + (ii + n) * 128]
                    bk = 0 if ii < 4 else 1
                    wb = self.PB[2 * sl + bk]
                    self.mm(sap, qz[:, c, hh, jq * 128:(jq + 1) * 128], kT[:, c, rc:rc + n * 128], bk not in started, False,
                            [Bq[c]] + [BkT[c][k_] for k_ in ksl], wb)
                    started.add(bk)
                if i0 < 4:
                    self.mm(ps[:, sbase + i0 * 128: sbase + 512], self.ident, Bb[:, hd, i0 * 128:512], False, True,
                            [self.Bconst, BBb], self.PB[2 * sl])
                self.mm(ps[:, sbase + 512: sbase + 640], self.ident, Bb[:, hd, 512:640], False, True,
                        [self.Bconst, BBb], self.PB[2 * sl + 1])
                sbufs = [self.PB[2 * sl], self.PB[2 * sl + 1]]
                sfull = ps[:, sbase + i0 * 128: sbase + 640]
                k4 = u % 4
                nmax = st3[k4][:, 0:1]
                self.P.op("dve", lambda e, nmax=nmax, sfull=sfull: e.tensor_reduce(out=nmax, in_=sfull, axis=AX.X, op=ALU.max, negate=True),
                          sbufs, [Bnm[k4]])

            def st_exp(u):
                c, jq, hh, jb, i0 = geom(u)
                sl = u % 2
                k4 = u % 4
                sbase = sl * 1024
                sbufs = [self.PB[2 * sl], self.PB[2 * sl + 1]]
                sfull = ps[:, sbase + i0 * 128: sbase + 640]
                nmax, rsum = st3[k4][:, 0:1], st3[k4][:, 1:2]
                self.act(Pb[u % 3][:, i0 * 128:640], sfull, AF.Exp, sbufs + [Bnm[k4]], [BPb[u % 3], Brs[k4]], bias=nmax, accum=rsum)

            def st_rd(u):
                k4 = u % 4
                k3 = u % 3
                rsum, rinv = st3[k4][:, 1:2], st3[k4][:, 2:3]
                self.recip(rinv, rsum, [Brs[k4]], [Bri[k4]])
                self.ts("pool", dgr[k3], self.ident, rinv, None, ALU.mult, None, [self.Bconst, Bri[k4]], [Bdgr[k3]])

            def st_pt(u):
                c, jq, hh, jb, i0 = geom(u)
                sl = u % 2
                k3 = u % 3
                pbase = 2048
                for ii in range(i0, 5):
                    self.mm(ps[:, pbase + ii * 128: pbase + (ii + 1) * 128], Pb[k3][:, ii * 128:(ii + 1) * 128], dgr[k3], True, True,
                            [BPb[k3], Bdgr[k3]], [self.PB[4] if ii < 4 else self.PB[5]])
                self.cp("act" if (u % 2) else "dve", PTb[k3][:, i0 * 128:640], ps[:, pbase + i0 * 128: pbase + 640],
                        [self.PB[4], self.PB[5]], [BPTb[k3]])

            def st_pv(u):
                c, jq, hh, jb, i0 = geom(u)
                k3 = u % 3
                ob = 6 + hh
                for ii in range(i0, 5):
                    kb = jb - 4 + ii
                    self.mm(ps[:, ob * 512 + jq * 128: ob * 512 + (jq + 1) * 128], V[:, kb % 8, c * 128:(c + 1) * 128],
                            PTb[k3][:, ii * 128:(ii + 1) * 128], ii == i0, ii == 4, [BV[kb % 8], BPTb[k3]], [self.PB[ob]])
                if jq == 3 and hh == 1:
                    for h2 in range(2):
                        rows = slice(h2 * 64, h2 * 64 + 64)
                        o2 = 6 + h2
                        self.cp("dve" if h2 else "act", oT[rows, c, :], ps[rows, o2 * 512:(o2 + 1) * 512], self.PB[o2], [BoT[c]])

            for u in range(NU + 4):
                if u < NU:
                    st_scores(u)
                if 0 <= u - 1 < NU:
                    st_exp(u - 1)
                if 0 <= u - 2 < NU:
                    st_rd(u - 2)
                if 0 <= u - 3 < NU:
                    st_pt(u - 3)
                if 0 <= u - 4 < NU:
                    st_pv(u - 4)
            sb = self.statbank()
            for dc in range(8):
                w, Bw = wos.next()
                b = self.bank()
                for c in range(8):
                    self.mm(self.psb(b), w[:, c, :], oT[:, c, :], c == 0, c == 7, [Bw, BoT[c]], [self.PB[b]])
                self.evac_mres(b, dc, sb)
            self.bg_start(t, ("mix_post", i), sb)
            if t == NT - 1:
                self.bg_drain()

    def store_output(self):
        for c in range(8):
            self.dma_plain(self.yT[c], self.h[:, c, :], self.Bh[c], [Buf("y%d" % c)], is_output=True)


def _cols(v, n):
    return np.ascontiguousarray(np.asarray(v, np.float32).reshape(n, 128).T)


def _kc_tile(w, ncols_per_block):
    K, N = w.shape
    nb = N // ncols_per_block
    x = w.reshape(K // 128, 128, nb, ncols_per_block)
    x = x.transpose(2, 1, 0, 3)
    return np.ascontiguousarray(x).reshape(nb, 128, (K // 128) * ncols_per_block)


def host_shared(inp, layers):
    off, R = ptab_layout()
    ptab = np.zeros((128, R), np.float32)

    def put(key, arr):
        ptab[:, off[key]:off[key] + arr.shape[1]] = arr

    for i in range(DEPTH):
        put(("mix_pre", i), _cols(inp["mix_pre_g"][i], 8))
        put(("mix_post", i), _cols(inp["mix_post_g"][i], 8))
        put(("ffn_pre", i), _cols(inp["ffn_pre_g"][i], 8))
        put(("ffn_post", i), _cols(inp["ffn_post_g"][i], 8))
        put(("ple_g", i), _cols(inp["ple_norm_g"][i], 8))
        cv = np.concatenate([_cols(inp["ffn_conv"][i][jj], 44) for jj in range(3)], axis=1)
        put(("conv", i), cv)
    for j in range(2):
        put(("a_lng", j), _cols(inp["a_ln_g"][j], 16))
        put(("a_lnb", j), _cols(inp["a_ln_b"][j], 16))
    dw = np.asarray(inp["c_dw"][0], np.float32)
    dwc = dw.reshape(31, 8, 128).transpose(2, 1, 0).reshape(128, 248)
    put("c_dw", np.ascontiguousarray(dwc))
    put("c_dwb", _cols(inp["c_dw_b"][0], 8))
    put("c_lng", _cols(inp["c_ln_g"][0], 8))
    put("c_lnb", _cols(inp["c_ln_b"][0], 8))
    sh = {"ptab": ptab, "ident": np.eye(128, dtype=np.float32)}
    for i in layers:
        wu = np.asarray(inp["ffn_w_up"][i], np.float32)
        g = _kc_tile(np.ascontiguousarray(wu[:, :FF]), 128)
        v = _kc_tile(np.ascontiguousarray(wu[:, FF:]), 128)
        sh["wup%d" % i] = np.ascontiguousarray(np.stack([g, v], axis=1))
        wd = np.asarray(inp["ffn_w_down"][i], np.float32)
        x = wd.reshape(NFC, 128, 8, 128).transpose(2, 1, 0, 3)
        sh["wdn%d" % i] = np.ascontiguousarray(x).reshape(8, 128, NFC * 128)
        sh["wg%d" % i] = _kc_tile(np.asarray(inp["ple_w_gate"][i], np.float32), 128)
        wp = np.asarray(inp["ple_w_proj"][i], np.float32)
        sh["wp%d" % i] = np.ascontiguousarray(wp.reshape(2, 128, D).transpose(1, 0, 2)).reshape(128, 2 * D)
        kind, j = i % 3, i // 3
        if kind == 0:
            win = np.asarray(inp["a_w_in"][j], np.float32)
            sh["a_wu%d" % j] = _kc_tile(win[:, :2048], 128)
            sh["a_wv%d" % j] = _kc_tile(win[:, 2048:], 512)
            wo = np.asarray(inp["a_w_out"][j], np.float32)
            x = wo.reshape(16, 128, 8, 128).transpose(2, 1, 0, 3)
            sh["a_wo%d" % j] = np.ascontiguousarray(x).reshape(8, 128, 16 * 128)
            ws = np.asarray(inp["a_w_s"][j], np.float32)
            sh["a_ws%d" % j] = np.ascontiguousarray(ws.transpose(2, 0, 1)).reshape(128, 8 * 128)
            bs = np.asarray(inp["a_b_s"][j], np.float32).reshape(1, 8 * 128)
            sh["a_bs%d" % j] = np.ascontiguousarray(np.broadcast_to(bs, (128, 8 * 128)))
        elif kind == 1:
            wq = np.asarray(inp["b_w_qkv"][0], np.float32)
            q = wq[:, :D].reshape(D, 8, 128)
            k = wq[:, D:2 * D].reshape(D, 8, 128)
            qk = np.concatenate([q, k], axis=2).reshape(D, 8 * 256)
            sh["b_wqk"] = _kc_tile(qk, 256)
            sh["b_wv"] = _kc_tile(np.ascontiguousarray(wq[:, 2 * D:]), 256)
            wo = np.asarray(inp["b_w_out"][0], np.float32)
            x = wo.reshape(8, 128, 8, 128).transpose(2, 1, 0, 3)
            sh["b_wo"] = np.ascontiguousarray(x).reshape(8, 128, 8 * 128)
            rb = np.asarray(inp["b_rel_bias"][0], np.float32)
            qq = np.arange(128)[:, None]
            kk = np.arange(640)[None, :]
            idx = np.clip(qq + 512 - kk, -128, 128) + 128
            bfull = rb[:, idx]
            sh["b_bias"] = np.ascontiguousarray(bfull.transpose(1, 0, 2)).reshape(128, 16 * 640)
        else:
            wi = np.asarray(inp["c_w_in"][0], np.float32)
            a = wi[:, :D].reshape(D, 8, 128)
            g = wi[:, D:].reshape(D, 8, 128)
            ag = np.concatenate([a, g], axis=2).reshape(D, 8 * 256)
            sh["c_wi"] = _kc_tile(ag, 256)
            wo = np.asarray(inp["c_w_out"][0], np.float32)
            x = wo.reshape(8, 128, 8, 128).transpose(2, 1, 0, 3)
            sh["c_wo"] = np.ascontiguousarray(x).reshape(8, 128, 8 * 128)
    return sh


def run_layers(hT_in, p, shared, layers, trace=False, dbg=()):
    nc = Builder(layers, dbg).build()
    in_maps = []
    for b in range(8):
        m = dict(shared)
        m["xT"] = hT_in[b]
        for i in layers:
            m["pT%d" % i] = np.ascontiguousarray(p[i, b].T).reshape(2, 128, S)
        in_maps.append(m)
    res = run_bass_kernel_spmd(nc, in_maps, core_ids=list(range(8)), trace=trace)
    out = np.stack([res.results[b]["yT"] for b in range(8)])
    return out, res


def kernel(**inputs):
    inp = {k: np.asarray(v) for k, v in inputs.items()}
    layers = list(range(DEPTH))
    x = inp["x"].astype(np.float32, copy=False)
    hT = np.ascontiguousarray(x.transpose(0, 2, 1)).reshape(8, 8, 128, S)
    shared = host_shared(inp, layers)
    out, _ = run_layers(hT, inp["p"].astype(np.float32, copy=False), shared, layers)
    y = out.reshape(8, D, S).transpose(0, 2, 1)
    return np.ascontiguousarray(y).astype(np.float32, copy=False)
```
